# Optimizing a Trainium2 kernel written in Bass

```python
import math
import jax, jax.numpy as jnp
from jax import lax
import numpy as np

D_MODEL = 1024
BATCH = 2
SEQ = 16384
DEPTH = 2

F32 = jnp.float32
NEG = -1e30
LN_EPS = 1e-5
RMS_EPS = 1e-6
DN_ALPHA = (2.0 * DEPTH) ** 0.25
DN_BETA = (8.0 * DEPTH) ** -0.25

A_HEADS = 4
A_DK = 128
A_DV = 128
A_CHUNK = 64
A_KW = A_HEADS * A_DK
A_VW = A_HEADS * A_DV

B_WIDTH = 512
B_GROUP = 16
B_GROUPS = B_WIDTH // B_GROUP
B_STATE = 64
DT_MIN = 1e-3
DT_MAX = 1e-1

HEAD_DIM = 64
C_HEADS = 8
C_KV = 2
WINDOW = 128
D_HEADS = 8
D_KV = 2
MOBA_BLOCK = 256
MOBA_TOPK = 3
MOBA_QCHUNK = 64
C_QW = C_HEADS * HEAD_DIM
C_KVW = C_KV * HEAD_DIM
D_QW = D_HEADS * HEAD_DIM
D_KVW = D_KV * HEAD_DIM

EV_IN = 2 * A_KW + 2 * A_VW + B_WIDTH
EV_MIX = A_VW + B_WIDTH
OD_IN = C_QW + 2 * C_KVW + D_QW + 2 * D_KVW
OD_MIX = C_QW + D_QW

N_GROUPS = 4
EXP_PER_GROUP = 4
N_EXPERTS = N_GROUPS * EXP_PER_GROUP
EXP_HIDDEN = 256
EXP_TOPK = 2

kernel_name = 'hybrid_hgrn2_s5_swa_moba_hmoe'


def split_cols(p, widths):
    outs, start = [], 0
    for w in widths:
        outs.append(p[..., start:start + w])
        start += w
    return outs


def layer_norm(x, g, b):
    xf = x.astype(F32)
    mu = jnp.mean(xf, axis=-1, keepdims=True)
    var = jnp.mean(jnp.square(xf - mu), axis=-1, keepdims=True)
    return ((xf - mu) * lax.rsqrt(var + LN_EPS) * g.astype(F32) + b.astype(F32)).astype(x.dtype)


def hgrn2_mixer(q, f_logit, inp, gate, lower_bound, norm_g):
    Bsz, L, _ = q.shape
    n = L // A_CHUNK

    def heads(t, d):
        return t.astype(F32).reshape(Bsz, n, A_CHUNK, A_HEADS, d).transpose(0, 3, 1, 2, 4)

    lb = lower_bound.astype(F32)
    f = lb + (1.0 - lb) * jax.nn.sigmoid(f_logit.astype(F32))
    qh = heads(jax.nn.silu(q.astype(F32)), A_DK)
    kh = heads(1.0 - f, A_DK)
    lf = heads(jnp.log(f), A_DK)
    vh = heads(inp, A_DV)
    b = jnp.cumsum(lf, axis=3)
    b_last = b[:, :, :, -1:, :]
    q_dec = qh * jnp.exp(b)
    k_inv = kh * jnp.exp(-b)
    k_tail = kh * jnp.exp(b_last - b)
    causal = jnp.tril(jnp.ones((A_CHUNK, A_CHUNK), dtype=bool))
    att = jnp.where(causal, jnp.einsum('bhnck,bhnsk->bhncs', q_dec, k_inv), 0.0)
    o_intra = jnp.einsum('bhncs,bhnsv->bhncv', att, vh)
    d_state = jnp.einsum('bhnsk,bhnsv->bhnkv', k_tail, vh)
    chunk_decay = jnp.exp(b_last[:, :, :, 0, :])

    def step(S, xs):
        dec, ds = xs
        return dec[..., None] * S + ds, S

    S0 = jnp.zeros((Bsz, A_HEADS, A_DK, A_DV), F32)
    _, S_prev = lax.scan(step, S0, (jnp.moveaxis(chunk_decay, 2, 0), jnp.moveaxis(d_state, 2, 0)))
    S_prev = jnp.moveaxis(S_prev, 0, 2)
    o = o_intra + jnp.einsum('bhnck,bhnkv->bhncv', q_dec, S_prev)
    o = o * lax.rsqrt(jnp.mean(o * o, axis=-1, keepdims=True) + RMS_EPS)
    o = o.transpose(0, 2, 3, 1, 4).reshape(Bsz, L, A_VW)
    return (o * norm_g.astype(F32) * jax.nn.silu(gate.astype(F32))).astype(q.dtype)


def s5_mixer(u, a_re, a_im, log_dt, b_re, b_im, c_re, c_im, d_skip, w_glu):
    Bsz, L, _ = u.shape
    uf = u.astype(F32)
    ug = uf.reshape(Bsz, L, B_GROUPS, B_GROUP)
    dt = jnp.exp(log_dt.astype(F32))[:, None]
    ar, ai = a_re.astype(F32), a_im.astype(F32)
    mag = jnp.exp(dt * ar)
    abar_re, abar_im = mag * jnp.cos(dt * ai), mag * jnp.sin(dt * ai)
    den = ar * ar + ai * ai
    xr, xi = abar_re - 1.0, abar_im
    fr = (xr * ar + xi * ai) / den
    fi = (xi * ar - xr * ai) / den
    br, bi = b_re.astype(F32), b_im.astype(F32)
    bbar_re = fr[..., None] * br - fi[..., None] * bi
    bbar_im = fr[..., None] * bi + fi[..., None] * br
    drive_re = jnp.einsum('blgn,gpn->blgp', ug, bbar_re)
    drive_im = jnp.einsum('blgn,gpn->blgp', ug, bbar_im)
    a_full_re = jnp.broadcast_to(abar_re, drive_re.shape)
    a_full_im = jnp.broadcast_to(abar_im, drive_re.shape)

    def combine(e1, e2):
        a1r, a1i, b1r, b1i = e1
        a2r, a2i, b2r, b2i = e2
        return (a1r * a2r - a1i * a2i, a1r * a2i + a1i * a2r,
                a2r * b1r - a2i * b1i + b2r, a2r * b1i + a2i * b1r + b2i)

    _, _, h_re, h_im = lax.associative_scan(combine, (a_full_re, a_full_im, drive_re, drive_im), axis=1)
    y = (jnp.einsum('blgp,gnp->blgn', h_re, c_re.astype(F32))
         - jnp.einsum('blgp,gnp->blgn', h_im, c_im.astype(F32)))
    y = jax.nn.gelu(y.reshape(Bsz, L, B_WIDTH) + d_skip.astype(F32) * uf)
    z = jnp.einsum('blc,cf->blf', y, w_glu.astype(F32))
    return (z[..., :B_WIDTH] * jax.nn.sigmoid(z[..., B_WIDTH:])).astype(u.dtype)


def sliding_window_attention(q, k, v, sinks):
    Bsz, L, _ = q.shape
    nb = L // WINDOW
    G = C_HEADS // C_KV
    scale = HEAD_DIM ** -0.5
    qb = q.astype(F32).reshape(Bsz, nb, WINDOW, C_KV, G, HEAD_DIM) * scale
    kb = k.astype(F32).reshape(Bsz, nb, WINDOW, C_KV, HEAD_DIM)
    vb = v.astype(F32).reshape(Bsz, nb, WINDOW, C_KV, HEAD_DIM)
    prev = lambda t: jnp.concatenate([jnp.zeros_like(t[:, :1]), t[:, :-1]], axis=1)
    kk = jnp.concatenate([prev(kb), kb], axis=2)
    vv = jnp.concatenate([prev(vb), vb], axis=2)
    s = jnp.einsum('bnqhgd,bnkhd->bnhgqk', qb, kk)
    qpos = jnp.arange(WINDOW)[:, None] + WINDOW
    kpos = jnp.arange(2 * WINDOW)[None, :]
    band = (kpos <= qpos) & (qpos - kpos < WINDOW)
    first = (jnp.arange(nb) == 0)[:, None, None] & (kpos < WINDOW)[None]
    valid = band[None] & ~first
    s = jnp.where(valid[None, :, None, None], s, NEG)
    sink = sinks.astype(F32).reshape(C_KV, G)[None, None, :, :, None, None]
    m = jnp.maximum(jnp.max(s, axis=-1, keepdims=True), sink)
    p = jnp.exp(s - m)
    denom = jnp.sum(p, axis=-1, keepdims=True) + jnp.exp(sink - m)
    o = jnp.einsum('bnhgqk,bnkhd->bnqhgd', p / denom, vv)
    return o.reshape(Bsz, L, C_QW).astype(q.dtype)


def moba_attention(q, k, v):
    Bsz, L, _ = q.shape
    Lp = -(-L // MOBA_BLOCK) * MOBA_BLOCK
    nb = Lp // MOBA_BLOCK
    G = D_HEADS // D_KV
    scale = HEAD_DIM ** -0.5
    pad = ((0, 0), (0, Lp - L), (0, 0))
    qh = (jnp.pad(q.astype(F32), pad) * scale).reshape(Bsz, Lp, D_HEADS, HEAD_DIM).transpose(0, 2, 1, 3)
    kh = jnp.pad(k.astype(F32), pad).reshape(Bsz, Lp, D_KV, HEAD_DIM).transpose(0, 2, 1, 3)
    vh = jnp.pad(v.astype(F32), pad).reshape(Bsz, Lp, D_KV, HEAD_DIM).transpose(0, 2, 1, 3)
    kblk = kh.reshape(Bsz, D_KV, nb, MOBA_BLOCK, HEAD_DIM)
    vblk = vh.reshape(Bsz, D_KV, nb, MOBA_BLOCK, HEAD_DIM)
    qblk = qh.reshape(Bsz, D_KV, G, nb, MOBA_BLOCK, HEAD_DIM)
    causal = jnp.tril(jnp.ones((MOBA_BLOCK, MOBA_BLOCK), dtype=bool))
    s_own = jnp.where(causal, jnp.einsum('bhgnqd,bhnkd->bhgnqk', qblk, kblk), NEG)
    m_own = jnp.max(s_own, axis=-1)
    p_own = jnp.exp(s_own - m_own[..., None])
    l_own = jnp.sum(p_own, axis=-1).reshape(Bsz, D_HEADS, Lp)
    acc_own = jnp.einsum('bhgnqk,bhnkd->bhgnqd', p_own, vblk).reshape(Bsz, D_HEADS, Lp, HEAD_DIM)
    m_own = m_own.reshape(Bsz, D_HEADS, Lp)
    k_mean = jnp.mean(kblk, axis=3)
    cur_blk = jnp.arange(Lp) // MOBA_BLOCK
    past = jnp.arange(nb)[None, :] < cur_blk[:, None]
    gate = jnp.einsum('bhgtd,bhnd->bhgtn', qh.reshape(Bsz, D_KV, G, Lp, HEAD_DIM), k_mean)
    gate = jnp.where(past, gate, NEG).reshape(Bsz, D_HEADS, Lp, nb)
    n_sel = min(MOBA_TOPK, nb)
    _, sel = lax.top_k(gate, n_sel)
    sel_valid = sel < cur_blk[None, None, :, None]
    nc = Lp // MOBA_QCHUNK

    def chunks(t):
        return jnp.moveaxis(t.reshape(Bsz, D_HEADS, nc, MOBA_QCHUNK, *t.shape[3:]), 2, 0)

    bi = jnp.arange(Bsz)[:, None, None, None]
    hi = (jnp.arange(D_HEADS) // G)[None, :, None, None]

    def sel_chunk(args):
        qc, ic, vc, mo, lo, ao = args
        kg = kblk[bi, hi, ic]
        vg = vblk[bi, hi, ic]
        s = jnp.where(vc[..., None, None] if vc.ndim == 3 else vc[..., None],
                      jnp.einsum('bhqd,bhqsnd->bhqsn', qc, kg), NEG)
        m = jnp.maximum(mo, jnp.max(s, axis=(-2, -1)))
        p = jnp.exp(s - m[..., None, None])
        corr = jnp.exp(mo - m)
        l = lo * corr + jnp.sum(p, axis=(-2, -1))
        acc = ao * corr[..., None] + jnp.einsum('bhqsn,bhqsnd->bhqd', p, vg)
        return acc / l[..., None]

    out = lax.map(sel_chunk, (chunks(qh), chunks(sel), chunks(sel_valid),
                              chunks(m_own), chunks(l_own), chunks(acc_own)))
    out = jnp.moveaxis(out, 0, 2).reshape(Bsz, D_HEADS, Lp, HEAD_DIM).transpose(0, 2, 1, 3)
    return out.reshape(Bsz, Lp, D_QW)[:, :L].astype(q.dtype)


def hier_moe(x, w_group, b_group, w_expert, b_expert, w_gate_up, w_down):
    Bsz, L, D = x.shape
    xt = x.reshape(Bsz * L, D)
    g_prob = jax.nn.softmax((xt @ w_group).astype(F32) + b_group.astype(F32), axis=-1)
    g_top, g_idx = lax.top_k(g_prob, 1)
    e_logits = ((xt @ w_expert).astype(F32) + b_expert.astype(F32)).reshape(-1, N_GROUPS, EXP_PER_GROUP)
    e_in = jnp.take_along_axis(e_logits, g_idx[:, :, None], axis=1)[:, 0]
    e_top, e_idx = lax.top_k(e_in, EXP_TOPK)
    w_sel = jax.nn.softmax(e_top, axis=-1) * g_top
    eid = g_idx * EXP_PER_GROUP + e_idx
    gates = jnp.einsum('tk,tke->te', w_sel, jax.nn.one_hot(eid, N_EXPERTS, dtype=F32))
    y = jnp.zeros((Bsz * L, D), F32)
    for g in range(N_GROUPS):
        sl = slice(g * EXP_PER_GROUP, (g + 1) * EXP_PER_GROUP)
        hu = jnp.einsum('td,edf->tef', xt, w_gate_up[sl])
        h = jax.nn.silu(hu[..., :EXP_HIDDEN]) * hu[..., EXP_HIDDEN:]
        h = h * gates[:, sl, None].astype(h.dtype)
        y = y + jnp.einsum('tef,efd->td', h, w_down[sl])
    return y.reshape(Bsz, L, D).astype(x.dtype)


def setup_inputs(seed: int = 0) -> dict:
    key = jax.random.key(seed)
    ks = iter(jax.random.split(key, 48))
    nrm = lambda shape, s: jax.random.normal(next(ks), shape, F32) * s
    D = D_MODEL
    NEV = (DEPTH + 1) // 2
    NOD = DEPTH // 2
    sd = D ** -0.5
    x = nrm((BATCH, SEQ, D), 1.0)
    hgrn_lb_logits = nrm((DEPTH + 1, A_KW), 0.1)
    ev_w_in = jnp.concatenate([
        nrm((NEV, D, A_KW), sd),
        nrm((NEV, D, A_KW), sd),
        nrm((NEV, D, A_VW), sd * DN_BETA),
        nrm((NEV, D, A_VW), sd),
        nrm((NEV, D, B_WIDTH), sd * DN_BETA),
    ], axis=-1)
    ev_a_norm = 1.0 + nrm((NEV, A_VW), 0.02)
    ev_s5_a_re = -0.5 * jnp.exp(nrm((NEV, B_GROUPS, B_STATE), 0.02))
    ev_s5_a_im = math.pi * jnp.arange(B_STATE, dtype=F32) + nrm((NEV, B_GROUPS, B_STATE), 0.02)
    ev_s5_log_dt = jax.random.uniform(next(ks), (NEV, B_GROUPS), F32,
                                      minval=math.log(DT_MIN), maxval=math.log(DT_MAX))
    ev_s5_b_re = nrm((NEV, B_GROUPS, B_STATE, B_GROUP), (2.0 * B_GROUP) ** -0.5)
    ev_s5_b_im = nrm((NEV, B_GROUPS, B_STATE, B_GROUP), (2.0 * B_GROUP) ** -0.5)
    ev_s5_c_re = nrm((NEV, B_GROUPS, B_GROUP, B_STATE), (2.0 * B_STATE) ** -0.5)
    ev_s5_c_im = nrm((NEV, B_GROUPS, B_GROUP, B_STATE), (2.0 * B_STATE) ** -0.5)
    ev_s5_d = nrm((NEV, B_WIDTH), 1.0)
    ev_s5_w_glu = nrm((NEV, B_WIDTH, 2 * B_WIDTH), B_WIDTH ** -0.5)
    ev_w_out = nrm((NEV, EV_MIX, D), EV_MIX ** -0.5 * DN_BETA)
    od_w_in = jnp.concatenate([
        nrm((NOD, D, C_QW), sd), nrm((NOD, D, C_KVW), sd), nrm((NOD, D, C_KVW), sd * DN_BETA),
        nrm((NOD, D, D_QW), sd), nrm((NOD, D, D_KVW), sd), nrm((NOD, D, D_KVW), sd * DN_BETA),
    ], axis=-1)
    od_sinks = nrm((NOD, C_HEADS), 1.0)
    od_w_out = nrm((NOD, OD_MIX, D), OD_MIX ** -0.5 * DN_BETA)
    ln1_g = 1.0 + nrm((DEPTH, D), 0.02)
    ln1_b = nrm((DEPTH, D), 0.02)
    moe_w_group = nrm((DEPTH, D, N_GROUPS), sd)
    moe_b_group = nrm((DEPTH, N_GROUPS), 0.01)
    moe_w_expert = nrm((DEPTH, D, N_EXPERTS), sd)
    moe_b_expert = nrm((DEPTH, N_EXPERTS), 0.01)
    moe_w_gate_up = nrm((DEPTH, N_EXPERTS, D, 2 * EXP_HIDDEN), sd * DN_BETA)
    moe_w_down = nrm((DEPTH, N_EXPERTS, EXP_HIDDEN, D), EXP_HIDDEN ** -0.5 * DN_BETA)
    ln2_g = 1.0 + nrm((DEPTH, D), 0.02)
    ln2_b = nrm((DEPTH, D), 0.02)
    return {'x': x, 'hgrn_lb_logits': hgrn_lb_logits, 'ev_w_in': ev_w_in, 'ev_a_norm': ev_a_norm,
            'ev_s5_a_re': ev_s5_a_re, 'ev_s5_a_im': ev_s5_a_im, 'ev_s5_log_dt': ev_s5_log_dt,
            'ev_s5_b_re': ev_s5_b_re, 'ev_s5_b_im': ev_s5_b_im, 'ev_s5_c_re': ev_s5_c_re,
            'ev_s5_c_im': ev_s5_c_im, 'ev_s5_d': ev_s5_d, 'ev_s5_w_glu': ev_s5_w_glu,
            'ev_w_out': ev_w_out, 'od_w_in': od_w_in, 'od_sinks': od_sinks, 'od_w_out': od_w_out,
            'ln1_g': ln1_g, 'ln1_b': ln1_b, 'moe_w_group': moe_w_group, 'moe_b_group': moe_b_group,
            'moe_w_expert': moe_w_expert, 'moe_b_expert': moe_b_expert, 'moe_w_gate_up': moe_w_gate_up,
            'moe_w_down': moe_w_down, 'ln2_g': ln2_g, 'ln2_b': ln2_b}


def reference(x, hgrn_lb_logits, ev_w_in, ev_a_norm, ev_s5_a_re, ev_s5_a_im, ev_s5_log_dt,
              ev_s5_b_re, ev_s5_b_im, ev_s5_c_re, ev_s5_c_im, ev_s5_d, ev_s5_w_glu, ev_w_out,
              od_w_in, od_sinks, od_w_out, ln1_g, ln1_b, moe_w_group, moe_b_group, moe_w_expert,
              moe_b_expert, moe_w_gate_up, moe_w_down, ln2_g, ln2_b):
    lower_bounds = jnp.cumsum(jax.nn.softmax(hgrn_lb_logits.astype(F32), axis=0), axis=0)
    for layer in range(DEPTH):
        j = layer // 2
        if layer % 2 == 0:
            proj = jnp.einsum('bld,df->blf', x, ev_w_in[j])
            q_a, f_a, i_a, g_a, u_b = split_cols(proj, [A_KW, A_KW, A_VW, A_VW, B_WIDTH])
            y_a = hgrn2_mixer(q_a, f_a, i_a, g_a, lower_bounds[layer], ev_a_norm[j])
            y_b = s5_mixer(u_b, ev_s5_a_re[j], ev_s5_a_im[j], ev_s5_log_dt[j], ev_s5_b_re[j],
                           ev_s5_b_im[j], ev_s5_c_re[j], ev_s5_c_im[j], ev_s5_d[j], ev_s5_w_glu[j])
            mix = jnp.einsum('blf,fd->bld', jnp.concatenate([y_a, y_b], axis=-1), ev_w_out[j])
        else:
            proj = jnp.einsum('bld,df->blf', x, od_w_in[j])
            q_c, k_c, v_c, q_d, k_d, v_d = split_cols(proj, [C_QW, C_KVW, C_KVW, D_QW, D_KVW, D_KVW])
            y_c = sliding_window_attention(q_c, k_c, v_c, od_sinks[j])
            y_d = moba_attention(q_d, k_d, v_d)
            mix = jnp.einsum('blf,fd->bld', jnp.concatenate([y_c, y_d], axis=-1), od_w_out[j])
        x = layer_norm(DN_ALPHA * x + mix.astype(x.dtype), ln1_g[layer], ln1_b[layer])
        ffn = hier_moe(x, moe_w_group[layer], moe_b_group[layer], moe_w_expert[layer],
                       moe_b_expert[layer], moe_w_gate_up[layer], moe_w_down[layer])
        x = layer_norm(DN_ALPHA * x + ffn, ln2_g[layer], ln2_b[layer])
    return x
```

```python
import numpy as np
from contextlib import ExitStack
import concourse.bass as bass
import concourse.mybir as mybir
from concourse.bass_utils import run_bass_kernel_spmd

dt = mybir.dt
F32 = dt.float32
BF16 = dt.bfloat16
I32 = dt.int32
U32 = dt.uint32
AF = mybir.ActivationFunctionType
ALU = mybir.AluOpType
AX = mybir.AxisListType


class Buf:
    __slots__ = ("name", "lw", "rd", "excl")

    def __init__(self, name, excl=False):
        self.name = name
        self.lw = None
        self.rd = {}
        self.excl = excl


class Prog:
    ENG = ["pe", "dve", "act", "pool", "sp"]
    NDMA = 32

    def __init__(self, nc, same_engine_sync=True):
        self.nc = nc
        self.es = ExitStack()
        self.ops = {e: [] for e in self.ENG}
        self.cnt = {e: 0 for e in self.ENG}
        self.waited = {e: {} for e in self.ENG}
        self.dma_cnt = [0] * self.NDMA
        self.dma_rr = 0
        self.same = same_engine_sync
        self.sems = {}
        self.gen = 0
        self.semkey = {}
        for e in ["pe", "dve", "act", "pool"]:
            self.semkey[e] = e
            self.sems[e] = self.es.enter_context(nc.semaphore("s_" + e))
        for j in range(self.NDMA):
            self.sems["d%d" % j] = self.es.enter_context(nc.semaphore("s_d%d" % j))
        self.nbuf = 0
        self.scope = ExitStack()
        self.ncc = 0
        self.prefix = ""

    def sb(self, name, shape, dtype):
        return self.scope.enter_context(self.nc.sbuf_tensor("sb_" + self.prefix + name, list(shape), dtype))

    def ps(self, name, shape, dtype):
        return self.scope.enter_context(self.nc.psum_tensor("pp_" + self.prefix + name, list(shape), dtype))

    def debug_dump(self, name, ap, shape, dtype, reads):
        if not getattr(self, "debug", False):
            return
        d = self.nc.dram_tensor("dbg_" + name, list(shape), dtype, kind="ExternalOutput").ap()
        self.dma(d, ap, reads=reads)

    def barrier(self):
        targets = []
        for j in range(self.NDMA):
            if self.dma_cnt[j] > 0:
                targets.append(("d%d" % j, self.dma_cnt[j]))
        for e in ["pe", "dve", "act", "pool"]:
            if self.cnt[e] > 0:
                targets.append((self.semkey[e], self.cnt[e]))
        for k in self.sems:
            if k.startswith("cc"):
                targets.append((k, 1))
        for e in self.ENG:
            waits = []
            for k, v in targets:
                if k == self.semkey.get(e):
                    continue
                if self.waited[e].get(k, 0) >= v:
                    continue
                waits.append((k, v))
                self.waited[e][k] = v
            if waits:
                self.ops[e].append((waits, None, None, False))

    def end_phase(self):
        self.barrier()
        self.scope.close()
        self.scope = ExitStack()
        self.gen += 1
        for e in ["pe", "dve", "act", "pool"]:
            k = "%s@%d" % (e, self.gen)
            self.semkey[e] = k
            self.sems[k] = self.es.enter_context(self.nc.semaphore("s_%s_%d" % (e, self.gen)))
            self.cnt[e] = 0

    def collective(self, kind, alu, groups, ins, outs, reads=(), writes=()):
        k = "cc%d" % self.ncc
        self.ncc += 1
        self.sems[k] = self.es.enter_context(self.nc.semaphore("s_" + k))
        eng = "pool"
        waits = {}

        def need(kk, v):
            if v <= 0 or self.waited[eng].get(kk, 0) >= v:
                return
            if waits.get(kk, 0) < v:
                waits[kk] = v
        for b in reads:
            if b.lw is not None:
                need(*b.lw)
        for b in writes:
            if b.lw is not None:
                need(*b.lw)
            for kk, v in b.rd.items():
                need(kk, v)
        for kk, v in waits.items():
            self.waited[eng][kk] = v
        tok = (k, 1)
        self.ops[eng].append((list(waits.items()), lambda e: e.collective_compute(kind, alu, replica_groups=groups, ins=ins, outs=outs), tok, "cc"))
        for b in reads:
            if b.rd.get(tok[0], 0) < tok[1]:
                b.rd[tok[0]] = tok[1]
        for b in writes:
            b.lw = tok
            b.rd = {}
        return tok

    def buf(self, name=None, excl=False):
        self.nbuf += 1
        return Buf(name or ("b%d" % self.nbuf), excl)

    def pbuf(self, name=None):
        return self.buf(name, True)

    def op(self, eng, emit, reads=(), writes=(), dma=False):
        waits = {}
        ex = [b for b in reads if b.excl]
        if ex:
            writes = list(writes) + ex
            reads = [b for b in reads if not b.excl]

        def need(k, v):
            if v <= 0:
                return
            if k == self.semkey.get(eng) and (eng == "pe" or not self.same):
                return
            if self.waited[eng].get(k, 0) >= v:
                return
            if waits.get(k, 0) < v:
                waits[k] = v

        for b in reads:
            if b.lw is not None:
                need(*b.lw)
        for b in writes:
            if b.lw is not None:
                need(*b.lw)
            for k, v in b.rd.items():
                need(k, v)
        if dma:
            j = self.dma_rr
            self.dma_rr = (self.dma_rr + 1) % self.NDMA
            k = "d%d" % j
            need(k, self.dma_cnt[j])
            self.dma_cnt[j] += 16
            tok = (k, self.dma_cnt[j])
        else:
            self.cnt[eng] += 1
            tok = (self.semkey[eng], self.cnt[eng])
        for k, v in waits.items():
            self.waited[eng][k] = v
        self.ops[eng].append((list(waits.items()), emit, tok, dma))
        for b in reads:
            if b.rd.get(tok[0], 0) < tok[1]:
                b.rd[tok[0]] = tok[1]
        for b in writes:
            b.lw = tok
            b.rd = {}
        return tok

    def pe(self, emit, reads=(), writes=()):
        return self.op("pe", emit, reads, writes)

    def dve(self, emit, reads=(), writes=()):
        return self.op("dve", emit, reads, writes)

    def act(self, emit, reads=(), writes=()):
        return self.op("act", emit, reads, writes)

    def pool(self, emit, reads=(), writes=()):
        return self.op("pool", emit, reads, writes)

    def dma(self, out, in_, reads=(), writes=(), eng="sp", **kw):
        return self.op(eng, lambda e: e.dma_start(out=out, in_=in_, **kw), reads, writes, dma=True)

    def finish(self):
        waits = []
        for j in range(self.NDMA):
            if self.dma_cnt[j] > 0:
                waits.append(("d%d" % j, self.dma_cnt[j]))
        for e in ["pe", "dve", "act", "pool"]:
            if self.cnt[e] > 0:
                waits.append((self.semkey[e], self.cnt[e]))
        for k in self.sems:
            if k.startswith("cc"):
                waits.append((k, 1))
        self.ops["sp"].append((waits, None, None, False))
        nc = self.nc
        sems = self.sems
        ops = self.ops

        def run(name, e):
            for waits, emit, tok, dma in ops[name]:
                for k, v in waits:
                    e.wait_ge(sems[k], v)
                if emit is None:
                    continue
                ins = emit(e)
                ins.then_inc(sems[tok[0]], 16 if dma is True else 1)

        with nc.Block() as block:
            @block.tensor
            def _(e):
                run("pe", e)

            @block.vector
            def _(e):
                run("dve", e)

            @block.scalar
            def _(e):
                run("act", e)

            @block.gpsimd
            def _(e):
                run("pool", e)

            @block.sync
            def _(e):
                run("sp", e)
        self.scope.close()
        self.es.close()


DN_ALPHA = 4.0 ** 0.25
LN_EPS = 1e-5
BIG = 30000.0


def make_ident(p, dtype, name):
    ident = p.sb(name, [128, 128], dtype)
    b = p.buf(name)
    p.pool(lambda e: e.memset(ident[:], 0.0), writes=[b])
    p.pool(lambda e: e.affine_select(out=ident[:], in_=ident[:], compare_op=ALU.not_equal, fill=1.0,
                                     base=0, pattern=[[-1, 128]], channel_multiplier=1), reads=[b], writes=[b])
    return ident, b


def layer_norm_tm(p, src, dst, gam, bet, bsrc, bdst, bconst, tmp, btmp, eps, key):
    stats, mv, rstd = tmp
    p.dve(lambda e: e.bn_stats(out=stats[:, 0, :], in_=src[:, 0:512]), reads=[bsrc], writes=[btmp])
    p.dve(lambda e: e.bn_stats(out=stats[:, 1, :], in_=src[:, 512:1024]), reads=[bsrc], writes=[btmp])
    p.dve(lambda e: e.bn_aggr(out=mv[:], in_=stats[:].rearrange("p a b -> p (a b)")), reads=[btmp], writes=[btmp])
    p.act(lambda e: e.activation(out=rstd[:], in_=mv[:, 1:2], func=AF.Sqrt, bias=eps, scale=1.0), reads=[btmp], writes=[btmp])
    p.dve(lambda e: e.reciprocal(out=rstd[:], in_=rstd[:]), reads=[btmp], writes=[btmp])
    p.dve(lambda e: e.tensor_scalar(out=dst, in0=src, scalar1=mv[:, 0:1], scalar2=rstd[:, 0:1],
                                    op0=ALU.subtract, op1=ALU.mult), reads=[bsrc, btmp], writes=[bdst])
    p.dve(lambda e: e.tensor_tensor(out=dst, in0=dst, in1=gam, op=ALU.mult), reads=[bdst, bconst], writes=[bdst])
    p.dve(lambda e: e.tensor_tensor(out=dst, in0=dst, in1=bet, op=ALU.add), reads=[bdst, bconst], writes=[bdst])


def build_tail(NT, glu, TG=1024):
    nc = bass.Bass("TRN2", target_bir_lowering=False)
    D = 1024
    xres = nc.dram_tensor("xres", [NT, D], F32, kind="ExternalInput").ap()
    yT = nc.dram_tensor("yT", [D, NT], BF16, kind="ExternalInput").ap()
    xo = nc.dram_tensor("xo", [NT, D], F32, kind="ExternalOutput").ap()
    p = Prog(nc)

    def load_yt(res, g, yt, byt):
        p.dma(yt[:], yT[:, g * TG:(g + 1) * TG].rearrange("(c p) t -> p c t", p=128), writes=[byt])

    def load_xres(res, g, st, dst, bdst):
        t0 = g * TG + st * 128
        p.dma(dst, xres[t0:t0 + 128, :], writes=[bdst], eng="act")

    def store_out(res, g, st, o, bo):
        t0 = g * TG + st * 128
        p.dma(xo[t0:t0 + 128, :], o[:], reads=[bo])

    phase_tail(p, nc, "", NT, glu, load_yt, load_xres, store_out, TG)
    p.finish()
    return nc


def phase_tail(p, nc, pre, NT, glu, load_yt, load_xres, store_out, TG=1024):
    D = 1024
    NG = NT // TG
    NST = TG // 128
    if glu:
        w_glu = nc.dram_tensor(pre + "w_glu", [512, 1024], F32, kind="ExternalInput").ap()
    w_out = nc.dram_tensor(pre + "w_out", [D, D], F32, kind="ExternalInput").ap()
    lnp = nc.dram_tensor(pre + "lnp", [4, 128, D], F32, kind="ExternalInput").ap()
    w_r = nc.dram_tensor(pre + "w_r", [D, 20], F32, kind="ExternalInput").ap()
    b_r = nc.dram_tensor(pre + "b_r", [128, 20], F32, kind="ExternalInput").ap()
    w_gu = nc.dram_tensor(pre + "w_gu", [16, D, 512], F32, kind="ExternalInput").ap()
    w_dn = nc.dram_tensor(pre + "w_dn", [16, 256, D], F32, kind="ExternalInput").ap()
    identf, bidf = make_ident(p, F32, "identf")
    identb = p.sb("identb", [128, 128], BF16)
    bidb = p.buf()
    p.dve(lambda e: e.tensor_copy(out=identb[:], in_=identf[:]), reads=[bidf], writes=[bidb])
    wout = p.sb("wout", [128, 8, D], BF16)
    bwout = p.buf()
    p.dma(wout[:], w_out.rearrange("(c p) f -> p c f", p=128), writes=[bwout], eng="pool")
    if glu:
        wglu = p.sb("wglu", [128, 4, 1024], BF16)
        bwglu = p.buf()
        p.dma(wglu[:], w_glu.rearrange("(c p) f -> p c f", p=128), writes=[bwglu], eng="pool")
    lns = p.sb("lns", [128, 4, D], F32)
    bln = p.buf()
    p.dma(lns[:], lnp.rearrange("a p d -> p a d"), writes=[bln])
    wr = p.sb("wr", [128, 8, 20], BF16)
    bwr = p.buf()
    p.dma(wr[:], w_r.rearrange("(c p) f -> p c f", p=128), writes=[bwr], eng="pool")
    br_ = p.sb("br", [128, 20], F32)
    bbr = p.buf()
    p.dma(br_[:], b_r, writes=[bbr])

    yt = p.sb("yt", [128, 8, TG], BF16)
    byt = p.buf()
    if glu:
        yglu = p.sb("yglu", [128, 4, TG], BF16)
        byglu = [p.buf() for _ in range(4)]
        sig = p.sb("sig", [128, 512], F32)
        bsig = p.buf()
    acc = p.sb("acc", [128, NST, D], F32)
    bacc = [p.buf() for _ in range(NST)]
    x1T = p.sb("x1T", [128, 8, TG], BF16)
    bx1T = [p.buf() for _ in range(NST)]
    gates = p.sb("gates", [128, NST, 16], F32)
    bgates = [p.buf() for _ in range(NST)]
    gub = [p.sb("gub%d" % i, [128, 8, 512], BF16) for i in range(2)]
    bgub = [p.buf() for _ in range(2)]
    dnb = [p.sb("dnb%d" % i, [128, 2, D], BF16) for i in range(2)]
    bdnb = [p.buf() for _ in range(2)]
    sg = [p.sb("sg%d" % i, [128, 256], F32) for i in range(2)]
    bsg = [p.buf() for _ in range(2)]
    hh = [p.sb("hh%d" % i, [128, 256], BF16) for i in range(2)]
    bhh = [p.buf() for _ in range(2)]
    hT = [p.sb("hT%d" % i, [128, 2, 128], BF16) for i in range(2)]
    bhT = [p.buf() for _ in range(2)]
    stats = p.sb("stats", [128, 2, 6], F32)
    mv = p.sb("mv", [128, 2], F32)
    rstd = p.sb("rstd", [128, 1], F32)
    btmp = p.buf()
    rt = p.sb("rt", [128, 80], F32)
    brt = p.buf()
    top8 = p.sb("top8", [128, 8], F32)
    xout = [p.sb("xout%d" % i, [128, D], F32) for i in range(2)]
    bxout = [p.buf() for _ in range(2)]

    psA = [p.ps("psA%d" % i, [128, 512], F32) for i in range(2)]
    bpsA = [p.pbuf() for _ in range(2)]
    psT = [p.ps("psT%d" % i, [128, 8, 128], BF16) for i in range(2)]
    bpsT = [p.pbuf() for _ in range(2)]
    psY = [p.ps("psY%d" % i, [128, 1024], F32) for i in range(2)]
    bpsY = [p.pbuf() for _ in range(2)]

    res = dict(psA=psA, bpsA=bpsA, identf=identf, bidf=bidf, TG=TG, NST=NST)
    expert_steps = [(g, e) for g in range(NG) for e in range(16)]

    def load_expert(idx):
        g, e = expert_steps[idx]
        s = idx % 2
        p.dma(gub[s][:], w_gu[e].rearrange("(c p) f -> p c f", p=128), writes=[bgub[s]], eng="pool")
        p.dma(dnb[s][:], w_dn[e].rearrange("(c p) f -> p c f", p=128), writes=[bdnb[s]], eng="pool")

    load_expert(0)
    rot = 0
    for g in range(NG):
        t0 = g * TG
        load_yt(res, g, yt, byt)
        for st in range(NST):
            load_xres(res, g, st, acc[:, st, :], bacc[st])
        if glu:
            for half in range(TG // 512):
                ts = slice(half * 512, (half + 1) * 512)
                for f in range(4):
                    for which in range(2):
                        col0 = which * 512 + f * 128
                        for k in range(4):
                            p.pe(lambda e, which=which, col0=col0, k=k, ts=ts: e.matmul(
                                psA[which][:], lhsT=wglu[:, k, col0:col0 + 128], rhs=yt[:, 4 + k, ts],
                                start=(k == 0), stop=(k == 3)), reads=[bwglu, byt], writes=[bpsA[which]])
                    p.act(lambda e: e.activation(out=sig[:], in_=psA[1][:], func=AF.Sigmoid), reads=[bpsA[1]], writes=[bsig])
                    p.dve(lambda e, f=f, ts=ts: e.tensor_tensor(out=yglu[:, f, ts], in0=psA[0][:], in1=sig[:], op=ALU.mult),
                          reads=[bpsA[0], bsig], writes=[byglu[f]])
        for st in range(NST):
            tsl = slice(st * 128, (st + 1) * 128)
            py = psY[st % 2]
            bpy = bpsY[st % 2]
            for half in range(2):
                for c in range(8):
                    if glu and c >= 4:
                        lhs = yglu[:, c - 4, tsl]
                        rb = byglu[c - 4]
                    else:
                        lhs = yt[:, c, tsl]
                        rb = byt
                    p.pe(lambda e, lhs=lhs, c=c, half=half, py=py: e.matmul(
                        py[:, half * 512:(half + 1) * 512], lhsT=lhs, rhs=wout[:, c, half * 512:(half + 1) * 512],
                        start=(c == 0), stop=(c == 7)), reads=[rb, bwout], writes=[bpy])
            a = acc[:, st, :]
            p.dve(lambda e, a=a, py=py: e.scalar_tensor_tensor(out=a, in0=a, scalar=DN_ALPHA, in1=py[:],
                                                                 op0=ALU.mult, op1=ALU.add), reads=[bacc[st], bpy], writes=[bacc[st]])
            layer_norm_tm(p, a, a, lns[:, 0, :], lns[:, 1, :], bacc[st], bacc[st], bln, (stats, mv, rstd), btmp, LN_EPS, "ln1")
            for hf in range(2):
                pa = psA[hf]
                for c4 in range(4):
                    c = hf * 4 + c4
                    p.pe(lambda e, pa=pa, c4=c4, c=c, a=a: e.transpose(out=pa[:, c4 * 128:(c4 + 1) * 128], in_=a[:, c * 128:(c + 1) * 128],
                                                                       identity=identf[:]), reads=[bacc[st], bidf], writes=[bpsA[hf]])
                p.act(lambda e, pa=pa, hf=hf, tsl=tsl: e.copy(out=x1T[:, hf * 4:(hf + 1) * 4, tsl], in_=pa[:].rearrange("p (c t) -> p c t", c=4)),
                      reads=[bpsA[hf]], writes=[bx1T[st]])
            p.act(lambda e, a=a: e.activation(out=a, in_=a, func=AF.Copy, scale=DN_ALPHA), reads=[bacc[st]], writes=[bacc[st]])
            pr = psT[st % 2]
            pl = psY[(st + 1) % 2]
            bpl = bpsY[(st + 1) % 2]
            for c in range(8):
                p.pe(lambda e, c=c, pl=pl, tsl=tsl: e.matmul(pl[:, 0:20], lhsT=x1T[:, c, tsl], rhs=wr[:, c, :],
                                                             start=(c == 0), stop=(c == 7)), reads=[bx1T[st], bwr], writes=[bpl])
            lg = rt[:, 0:20]
            p.dve(lambda e, pl=pl: e.tensor_tensor(out=lg, in0=pl[:, 0:20], in1=br_[:], op=ALU.add), reads=[bpl, bbr], writes=[brt])
            gmax = rt[:, 20:21]
            ngmax = rt[:, 21:22]
            sume = rt[:, 22:23]
            gtop = rt[:, 23:24]
            eg = rt[:, 24:28]
            oh = rt[:, 28:32]
            em = rt[:, 32:48]
            dd = rt[:, 48:49]
            ex = rt[:, 49:50]
            w1 = rt[:, 50:51]
            w2 = rt[:, 51:52]
            t2 = rt[:, 56:72]
            R = [brt]
            p.dve(lambda e: e.tensor_reduce(out=gmax, in_=lg[:, 0:4], axis=AX.X, op=ALU.max), reads=R, writes=R)
            p.dve(lambda e: e.tensor_scalar(out=ngmax, in0=gmax, scalar1=-1.0, scalar2=None, op0=ALU.mult), reads=R, writes=R)
            p.act(lambda e: e.activation(out=eg, in_=lg[:, 0:4], func=AF.Exp, bias=ngmax, scale=1.0, accum_out=sume), reads=R, writes=R)
            p.dve(lambda e: e.reciprocal(out=gtop, in_=sume), reads=R, writes=R)
            p.dve(lambda e: e.tensor_scalar(out=oh, in0=lg[:, 0:4], scalar1=gmax, scalar2=BIG, op0=ALU.is_equal, op1=ALU.mult), reads=R, writes=R)
            p.dve(lambda e: e.tensor_scalar(out=oh, in0=oh, scalar1=-BIG, scalar2=None, op0=ALU.add), reads=R, writes=R)
            for gi in range(4):
                p.dve(lambda e, gi=gi: e.tensor_scalar(out=em[:, gi * 4:(gi + 1) * 4], in0=lg[:, 4 + gi * 4:8 + gi * 4],
                                                       scalar1=oh[:, gi:gi + 1], scalar2=None, op0=ALU.add), reads=R, writes=R)
            p.dve(lambda e: e.max(out=top8[:], in_=em), reads=R, writes=R)
            p.dve(lambda e: e.tensor_tensor(out=dd, in0=top8[:, 1:2], in1=top8[:, 0:1], op=ALU.subtract), reads=R, writes=R)
            p.act(lambda e: e.activation(out=ex, in_=dd, func=AF.Exp), reads=R, writes=R)
            p.dve(lambda e: e.tensor_scalar(out=w1, in0=ex, scalar1=1.0, scalar2=None, op0=ALU.add), reads=R, writes=R)
            p.dve(lambda e: e.reciprocal(out=w1, in_=w1), reads=R, writes=R)
            p.dve(lambda e: e.tensor_tensor(out=w2, in0=ex, in1=w1, op=ALU.mult), reads=R, writes=R)
            p.dve(lambda e: e.tensor_tensor(out=w1, in0=w1, in1=gtop, op=ALU.mult), reads=R, writes=R)
            p.dve(lambda e: e.tensor_tensor(out=w2, in0=w2, in1=gtop, op=ALU.mult), reads=R, writes=R)
            gt = gates[:, st, :]
            p.dve(lambda e, gt=gt: e.tensor_scalar(out=gt, in0=em, scalar1=top8[:, 0:1], scalar2=w1, op0=ALU.is_equal, op1=ALU.mult),
                  reads=R, writes=[bgates[st]])
            p.dve(lambda e: e.tensor_scalar(out=t2, in0=em, scalar1=top8[:, 1:2], scalar2=w2,
                                            op0=ALU.is_equal, op1=ALU.mult), reads=R, writes=R)
            p.dve(lambda e, gt=gt: e.tensor_tensor(out=gt, in0=gt, in1=t2, op=ALU.add), reads=R + [bgates[st]], writes=[bgates[st]])
        for e_i in range(16):
            idx = g * 16 + e_i
            s = idx % 2
            if idx + 1 < len(expert_steps):
                load_expert(idx + 1)
            for st in range(NST):
                tsl = slice(st * 128, (st + 1) * 128)
                r = rot % 2
                rot += 1
                pa = psA[r]
                for c in range(8):
                    p.pe(lambda e, pa=pa, c=c, tsl=tsl, s=s: e.matmul(pa[:], lhsT=x1T[:, c, tsl], rhs=gub[s][:, c, :],
                                                                      start=(c == 0), stop=(c == 7)), reads=[bx1T[st], bgub[s]], writes=[bpsA[r]])
                p.act(lambda e, pa=pa, r=r: e.activation(out=sg[r][:], in_=pa[:, 0:256], func=AF.Silu), reads=[bpsA[r]], writes=[bsg[r]])
                p.dve(lambda e, pa=pa, r=r, st=st, e_i=e_i: e.scalar_tensor_tensor(
                    out=hh[r][:], in0=pa[:, 256:512], scalar=gates[:, st, e_i:e_i + 1], in1=sg[r][:], op0=ALU.mult, op1=ALU.mult),
                    reads=[bpsA[r], bsg[r], bgates[st]], writes=[bhh[r]])
                for k in range(2):
                    p.pe(lambda e, r=r, k=k: e.transpose(out=psT[r][:, k, :], in_=hh[r][:, k * 128:(k + 1) * 128], identity=identb[:]),
                         reads=[bhh[r], bidb], writes=[bpsT[r]])
                p.act(lambda e, r=r: e.copy(out=hT[r][:], in_=psT[r][:, 0:2, :]), reads=[bpsT[r]], writes=[bhT[r]])
                py = psY[r]
                for half in range(2):
                    for k in range(2):
                        p.pe(lambda e, py=py, half=half, k=k, r=r, s=s: e.matmul(
                            py[:, half * 512:(half + 1) * 512], lhsT=hT[r][:, k, :], rhs=dnb[s][:, k, half * 512:(half + 1) * 512],
                            start=(k == 0), stop=(k == 1)), reads=[bhT[r], bdnb[s]], writes=[bpsY[r]])
                a = acc[:, st, :]
                p.dve(lambda e, a=a, py=py: e.tensor_tensor(out=a, in0=a, in1=py[:], op=ALU.add), reads=[bacc[st], bpsY[r]], writes=[bacc[st]])
        for st in range(NST):
            a = acc[:, st, :]
            o = xout[st % 2]
            layer_norm_tm(p, a, o[:], lns[:, 2, :], lns[:, 3, :], bacc[st], bxout[st % 2], bln, (stats, mv, rstd), btmp, LN_EPS, "ln2")
            store_out(res, g, st, o, bxout[st % 2])

import math

RMS_EPS = 1e-6
TWO_PI = 2.0 * math.pi


def s5_lambda(p, pre, shape, ar, ai, ldt, breads, T=None, extra=()):
    F = shape[1]
    if T is None:
        T = p.sb(pre + "_t", [128, 8, F], F32)
    Ti = p.sb(pre + "_ti", [128, F], I32)
    b = p.buf(pre)
    dtt, mag, th, t, kf, r, c1, s = [T[:, i, :] for i in range(8)]
    lre = p.sb(pre + "_lre", [128, F], F32)
    lim = p.sb(pre + "_lim", [128, F], F32)
    R = [b] + list(extra)
    p.act(lambda e: e.activation(out=dtt, in_=ldt, func=AF.Exp), reads=breads, writes=R)
    p.dve(lambda e: e.tensor_tensor(out=mag, in0=dtt, in1=ar, op=ALU.mult), reads=R + breads, writes=R)
    p.act(lambda e: e.activation(out=mag, in_=mag, func=AF.Exp), reads=R, writes=R)
    p.dve(lambda e: e.tensor_tensor(out=th, in0=dtt, in1=ai, op=ALU.mult), reads=R + breads, writes=R)
    for which, dst in ((0, lim), (1, lre)):
        p.dve(lambda e, which=which: e.tensor_scalar(out=t, in0=th, scalar1=1.0 / TWO_PI, scalar2=0.25 * which,
                                                     op0=ALU.mult, op1=ALU.add), reads=R, writes=R)
        p.dve(lambda e: e.tensor_copy(out=Ti[:], in_=t), reads=R, writes=R)
        p.dve(lambda e: e.tensor_copy(out=kf, in_=Ti[:]), reads=R, writes=R)
        p.dve(lambda e: e.tensor_tensor(out=r, in0=t, in1=kf, op=ALU.subtract), reads=R, writes=R)
        p.dve(lambda e: e.tensor_scalar(out=c1, in0=r, scalar1=0.5, scalar2=None, op0=ALU.is_gt), reads=R, writes=R)
        p.dve(lambda e: e.tensor_tensor(out=r, in0=r, in1=c1, op=ALU.subtract), reads=R, writes=R)
        p.dve(lambda e: e.tensor_scalar(out=c1, in0=r, scalar1=-0.5, scalar2=None, op0=ALU.is_lt), reads=R, writes=R)
        p.dve(lambda e: e.tensor_tensor(out=r, in0=r, in1=c1, op=ALU.add), reads=R, writes=R)
        p.act(lambda e: e.activation(out=s, in_=r, func=AF.Sin, scale=TWO_PI), reads=R, writes=R)
        p.dve(lambda e, dst=dst: e.tensor_tensor(out=dst[:], in0=s, in1=mag, op=ALU.mult), reads=R, writes=R)
    return lre, lim, b


def build_mixa(L, TS5=2048):
    nc = bass.Bass("TRN2", target_bir_lowering=False)
    yaT_d = nc.dram_tensor("yaT", [128, L], BF16, kind="ExternalOutput").ap()
    ysT_d = nc.dram_tensor("ysT", [128, L], BF16, kind="ExternalOutput").ap()
    p = Prog(nc)

    def emit_ya(res, ti, t, b):
        p.dma(yaT_d[:, ti * 512:(ti + 1) * 512], t[:], reads=[b])

    def emit_ys(res, ti, t, b):
        p.dma(ysT_d[:, ti * 512:(ti + 1) * 512], t[:], reads=[b])

    phase_mixa(p, nc, "", L, emit_ya, emit_ys, TS5)
    p.finish()
    return nc


def phase_mixa(p, nc, pre, L, emit_ya, emit_ys, TS5=2048):
    xT = nc.dram_tensor(pre + "xT", [1024, L], F32, kind="ExternalInput").ap()
    wA_d = nc.dram_tensor(pre + "wA", [1024, 640], F32, kind="ExternalInput").ap()
    lbl_d = nc.dram_tensor(pre + "lbl", [128, 3], F32, kind="ExternalInput").ap()
    ng_d = nc.dram_tensor(pre + "ng", [128, 128], F32, kind="ExternalInput").ap()
    sps_d = nc.dram_tensor(pre + "sps", [128, 3, 4], F32, kind="ExternalInput").ap()
    spw_d = nc.dram_tensor(pre + "spw", [128, 3, 512], F32, kind="ExternalInput").ap()
    bpad_d = nc.dram_tensor(pre + "bpad", [128, 2, 512], F32, kind="ExternalInput").ap()
    cpad_d = nc.dram_tensor(pre + "cpad", [128, 2, 512], F32, kind="ExternalInput").ap()
    dsk_d = nc.dram_tensor(pre + "dsk", [128, 1], F32, kind="ExternalInput").ap()
    tri_d = nc.dram_tensor(pre + "tri", [128, 128], F32, kind="ExternalInput").ap()
    m01_d = nc.dram_tensor(pre + "m01", [128, 512], F32, kind="ExternalInput").ap()
    res = {}
    identf, bidf = make_ident(p, F32, "identf")
    wA = p.sb("wA", [128, 8, 640], BF16)
    bwA = p.buf()
    p.dma(wA[:], wA_d.rearrange("(c p) f -> p c f", p=128), writes=[bwA], eng="pool")
    cst = p.sb("cst", [128, 3 + 128 + 12 + 1 + 128 + 512 + 8], F32)
    bc = p.buf("cst")
    lbl = cst[:, 0:3]
    ng = cst[:, 3:131]
    sps = cst[:, 131:143].rearrange("p (a b) -> p a b", a=3)
    dsk = cst[:, 143:144]
    tri = cst[:, 144:272]
    m01 = cst[:, 272:784]
    misc = cst[:, 784:792]
    p.dma(lbl, lbl_d, writes=[bc])
    p.dma(ng, ng_d, writes=[bc])
    p.dma(sps, sps_d, writes=[bc])
    p.dma(dsk, dsk_d, writes=[bc])
    p.dma(tri, tri_d, writes=[bc])
    p.dma(m01, m01_d, writes=[bc])
    spw = p.sb("spw", [128, 3, 512], F32)
    p.dma(spw[:], spw_d, writes=[bc])
    bpad = p.sb("bpad", [128, 2, 512], F32)
    p.dma(bpad[:], bpad_d, writes=[bc])
    cpad = p.sb("cpad", [128, 2, 512], F32)
    p.dma(cpad[:], cpad_d, writes=[bc])
    lbe = misc[:, 0:3]
    lbs = misc[:, 3:4]
    lb = misc[:, 4:5]
    oml = misc[:, 5:6]
    bm = p.buf("misc")
    p.act(lambda e: e.activation(out=lbe, in_=lbl, func=AF.Exp, accum_out=lbs), reads=[bc], writes=[bm])
    p.dve(lambda e: e.reciprocal(out=lbs, in_=lbs), reads=[bm], writes=[bm])
    p.dve(lambda e: e.tensor_tensor(out=lb, in0=lbe[:, 0:1], in1=lbs, op=ALU.mult), reads=[bm], writes=[bm])
    p.dve(lambda e: e.tensor_scalar(out=oml, in0=lb, scalar1=-1.0, scalar2=1.0, op0=ALU.mult, op1=ALU.add), reads=[bm], writes=[bm])

    dre = p.sb("dre", [128, 4, TS5], F32)
    dim_ = p.sb("dim", [128, 4, TS5], F32)
    bd = [p.buf("d%d" % i) for i in range(4)]
    assert TS5 >= 1024
    ls_re, ls_im, bls = s5_lambda(p, "ls", [128, 4], sps[:, 0, :], sps[:, 1, :], sps[:, 2, :], [bc])
    lw_re, lw_im, blw = s5_lambda(p, "lw", [128, 512], spw[:, 0, :], spw[:, 1, :], spw[:, 2, :], [bc],
                                  T=dre[:].rearrange("p a t -> p (a t)")[:, 0:4096].rearrange("p (a f) -> p a f", a=8), extra=bd)
    W = dim_[:].rearrange("p a t -> p (a t)")[:, 0:4096].rearrange("p (a f) -> p a f", a=8)
    bW = p.buf("wtmp")
    xr, den, t1, t2, fr, fi, o1, o2 = [W[:, i, :] for i in range(8)]
    arw, aiw = spw[:, 0, :], spw[:, 1, :]
    RW = [bW, blw, bc]
    p.dve(lambda e: e.tensor_scalar(out=xr, in0=lw_re[:], scalar1=-1.0, scalar2=None, op0=ALU.add), reads=RW, writes=[bW] + bd)
    p.dve(lambda e: e.tensor_tensor(out=den, in0=arw, in1=arw, op=ALU.mult), reads=RW, writes=[bW])
    p.dve(lambda e: e.tensor_tensor(out=t1, in0=aiw, in1=aiw, op=ALU.mult), reads=RW, writes=[bW])
    p.dve(lambda e: e.tensor_tensor(out=den, in0=den, in1=t1, op=ALU.add), reads=RW, writes=[bW])
    p.dve(lambda e: e.reciprocal(out=den, in_=den), reads=RW, writes=[bW])
    p.dve(lambda e: e.tensor_tensor(out=t1, in0=xr, in1=arw, op=ALU.mult), reads=RW, writes=[bW])
    p.dve(lambda e: e.tensor_tensor(out=t2, in0=lw_im[:], in1=aiw, op=ALU.mult), reads=RW, writes=[bW])
    p.dve(lambda e: e.tensor_tensor(out=fr, in0=t1, in1=t2, op=ALU.add), reads=RW, writes=[bW])
    p.dve(lambda e: e.tensor_tensor(out=fr, in0=fr, in1=den, op=ALU.mult), reads=RW, writes=[bW])
    p.dve(lambda e: e.tensor_tensor(out=t1, in0=lw_im[:], in1=arw, op=ALU.mult), reads=RW, writes=[bW])
    p.dve(lambda e: e.tensor_tensor(out=t2, in0=xr, in1=aiw, op=ALU.mult), reads=RW, writes=[bW])
    p.dve(lambda e: e.tensor_tensor(out=fi, in0=t1, in1=t2, op=ALU.subtract), reads=RW, writes=[bW])
    p.dve(lambda e: e.tensor_tensor(out=fi, in0=fi, in1=den, op=ALU.mult), reads=RW, writes=[bW])
    wB = p.sb("wB", [128, 2, 512], BF16)
    wC = p.sb("wC", [128, 2, 512], BF16)
    bwB = p.buf("wB")
    bre, bim = bpad[:, 0, :], bpad[:, 1, :]
    p.dve(lambda e: e.tensor_tensor(out=o1, in0=fr, in1=bre, op=ALU.mult), reads=RW, writes=[bW])
    p.dve(lambda e: e.tensor_tensor(out=o2, in0=fi, in1=bim, op=ALU.mult), reads=RW, writes=[bW])
    p.dve(lambda e: e.tensor_tensor(out=wB[:, 0, :], in0=o1, in1=o2, op=ALU.subtract), reads=RW, writes=[bwB])
    p.dve(lambda e: e.tensor_tensor(out=o1, in0=fr, in1=bim, op=ALU.mult), reads=RW, writes=[bW])
    p.dve(lambda e: e.tensor_tensor(out=o2, in0=fi, in1=bre, op=ALU.mult), reads=RW, writes=[bW])
    p.dve(lambda e: e.tensor_tensor(out=wB[:, 1, :], in0=o1, in1=o2, op=ALU.add), reads=RW, writes=[bwB])
    p.dve(lambda e: e.tensor_copy(out=wC[:, 0, :], in_=cpad[:, 0, :]), reads=[bc], writes=[bwB])
    p.dve(lambda e: e.tensor_scalar(out=wC[:, 1, :], in0=cpad[:, 1, :], scalar1=-1.0, scalar2=None, op0=ALU.mult), reads=[bc, bW], writes=[bwB] + bd)

    NLEV = 3
    lamp = p.sb("lamp", [128, NLEV, 3, 4], F32)
    ltmp = p.sb("ltmp", [128, 4, 4], F32)
    blam = p.buf("lam")
    RL = [blam, bls]
    p.dve(lambda e: e.tensor_copy(out=lamp[:, 0, 0, :], in_=ls_re[:]), reads=RL, writes=[blam])
    p.dve(lambda e: e.tensor_copy(out=lamp[:, 0, 1, :], in_=ls_im[:]), reads=RL, writes=[blam])
    for lev in range(1, NLEV):
        p.dve(lambda e, lev=lev: e.tensor_copy(out=lamp[:, lev, 0:2, :], in_=lamp[:, lev - 1, 0:2, :]), reads=RL, writes=[blam])
        for _ in range(4):
            a = lamp[:, lev, 0, :]
            b_ = lamp[:, lev, 1, :]
            p.dve(lambda e, a=a: e.tensor_tensor(out=ltmp[:, 0, :], in0=a, in1=a, op=ALU.mult), reads=RL, writes=[blam])
            p.dve(lambda e, b_=b_: e.tensor_tensor(out=ltmp[:, 1, :], in0=b_, in1=b_, op=ALU.mult), reads=RL, writes=[blam])
            p.dve(lambda e, a=a, b_=b_: e.tensor_tensor(out=ltmp[:, 2, :], in0=a, in1=b_, op=ALU.mult), reads=RL, writes=[blam])
            p.dve(lambda e, a=a: e.tensor_tensor(out=a, in0=ltmp[:, 0, :], in1=ltmp[:, 1, :], op=ALU.subtract), reads=RL, writes=[blam])
            p.dve(lambda e, b_=b_: e.tensor_scalar(out=b_, in0=ltmp[:, 2, :], scalar1=2.0, scalar2=None, op0=ALU.mult), reads=RL, writes=[blam])
    for lev in range(NLEV):
        p.dve(lambda e, lev=lev: e.tensor_scalar(out=lamp[:, lev, 2, :], in0=lamp[:, lev, 1, :], scalar1=-1.0, scalar2=None, op0=ALU.mult),
              reads=RL, writes=[blam])

    xt = [p.sb("xt%d" % i, [128, 8, 512], BF16) for i in range(2)]
    bxt = [p.buf() for _ in range(2)]
    H = p.sb("hg", [128, 9, 512], F32)
    bH = p.buf("hg")
    f_, lf, kk, bb, eb, enb, qq, kinv, ktT = [H[:, i, :] for i in range(9)]
    qdec = p.sb("qdec", [128, 512], BF16)
    kinvb = p.sb("kinvb", [128, 512], BF16)
    bqk = p.buf("qk")
    vb = p.sb("vb", [128, 4, 128], BF16)
    gn = p.sb("gn", [128, 4, 128], F32)
    bvg = [p.buf() for _ in range(4)]
    kt = p.sb("kt", [128, 4, 128], BF16)
    bkt = [p.buf() for _ in range(4)]
    attT = [p.sb("attT%d" % i, [128, 128], BF16) for i in range(2)]
    battT = [p.buf() for _ in range(2)]
    yasb = [p.sb("yasb%d" % i, [128, 4, 128], F32) for i in range(2)]
    yaTs = [p.sb("yaTs%d" % i, [128, 512], BF16) for i in range(2)]
    byaTs = [p.buf() for _ in range(2)]
    byasb = [p.buf() for _ in range(2)]
    S = p.sb("S", [128, 128], F32)
    bS = p.buf("S")
    Sb = [p.sb("Sb%d" % i, [128, 128], BF16) for i in range(2)]
    bSb = [p.buf() for _ in range(2)]
    osc = p.sb("osc", [128, 128], F32)
    om = p.sb("om", [128, 2], F32)
    bo = p.buf("o")
    p.dve(lambda e: e.memset(S[:], 0.0), writes=[bS])
    p.dve(lambda e: e.memset(Sb[1][:], 0.0), writes=[bSb[1]])
    NB1 = TS5 // 16
    assert NB1 % 16 == 0 or NB1 <= 16
    uT = p.sb("uT", [128, TS5], F32)
    buT = p.buf("uT")
    uTb = [p.sb("uTb%d" % i, [128, 512], BF16) for i in range(2)]
    buTb = [p.buf() for _ in range(2)]
    nb_levels = []
    n = TS5
    while n > 16:
        n //= 16
        nb_levels.append(n)
    Ebufs = []
    for li, nbl in enumerate(nb_levels):
        Ebufs.append((p.sb("Ere%d" % li, [128, 4, nbl + 1], F32), p.sb("Eim%d" % li, [128, 4, nbl + 1], F32)))
    carry = p.sb("carry", [128, 2, 4], F32)
    p.dve(lambda e: e.memset(carry[:], 0.0), writes=bd)
    stmp = p.sb("stmp", [128, 4, 2, max(NB1, 16)], F32)
    hb = [p.sb("hb%d" % i, [128, 2, 4, 512], BF16) for i in range(2)]
    bhb = [p.buf() for _ in range(2)]
    zs = p.sb("zs", [128, 512], F32)
    bzs = p.buf()
    ysb = [p.sb("ysb%d" % i, [128, 512], BF16) for i in range(2)]
    bysb = [p.buf() for _ in range(2)]

    psQ = p.ps("psQ", [128, 512], F32); bpsQ = p.pbuf()
    psF = p.ps("psF", [128, 512], F32); bpsF = p.pbuf()
    psU = p.ps("psU", [128, 512], F32); bpsU = p.pbuf()
    psVG = p.ps("psVG", [128, 2, 256], F32); _b = p.pbuf(); bpsVG = [_b, _b]
    psD1 = p.ps("psD", [128, 512], F32); psD = [psD1, psD1]; _b = p.pbuf(); bpsD = [_b, _b]
    psS = p.ps("psS", [128, 4, 128], F32); _b = p.pbuf(); bpsS = [_b, _b]
    psO = p.ps("psO", [128, 4, 128], F32); _b = p.pbuf(); bpsO = [_b, _b]
    psM = p.ps("psM", [128, 4, 128], F32); _b = p.pbuf(); bpsM = [_b] * 4

    def cstep(i, lev, dst_re, dst_im, prev_re, prev_im, n, add_re=None, add_im=None):
        ar = lamp[:, lev, 0, i:i + 1]
        ai = lamp[:, lev, 1, i:i + 1]
        nai = lamp[:, lev, 2, i:i + 1]
        if add_re is None:
            add_re, add_im = dst_re, dst_im
        ta = stmp[:, i, 0, 0:n]
        tb = stmp[:, i, 1, 0:n]
        R = [bd[i], blam]
        p.dve(lambda e: e.scalar_tensor_tensor(out=ta, in0=prev_im, scalar=nai, in1=add_re, op0=ALU.mult, op1=ALU.add), reads=R, writes=[bd[i]])
        p.dve(lambda e: e.scalar_tensor_tensor(out=tb, in0=prev_re, scalar=ai, in1=add_im, op0=ALU.mult, op1=ALU.add), reads=R, writes=[bd[i]])
        p.dve(lambda e: e.scalar_tensor_tensor(out=dst_re, in0=prev_re, scalar=ar, in1=ta, op0=ALU.mult, op1=ALU.add), reads=R, writes=[bd[i]])
        p.dve(lambda e: e.scalar_tensor_tensor(out=dst_im, in0=prev_im, scalar=ar, in1=tb, op0=ALU.mult, op1=ALU.add), reads=R, writes=[bd[i]])

    def cscan(lev, Xre, Xim, n, hin_re, hin_im):
        if n <= 16:
            for t in range(n):
                for i in range(4):
                    pr = hin_re(i) if t == 0 else Xre(i)[:, t - 1:t]
                    pi_ = hin_im(i) if t == 0 else Xim(i)[:, t - 1:t]
                    cstep(i, lev, Xre(i)[:, t:t + 1], Xim(i)[:, t:t + 1], pr, pi_, 1)
            return
        nb = n // 16
        Ere, Eim = Ebufs[lev]
        for i in range(4):
            p.dve(lambda e, i=i: e.tensor_copy(out=Ere[:, i, 0:1], in_=hin_re(i)), reads=[bd[i]], writes=[bd[i]])
            p.dve(lambda e, i=i: e.tensor_copy(out=Eim[:, i, 0:1], in_=hin_im(i)), reads=[bd[i]], writes=[bd[i]])
            p.dve(lambda e, i=i: e.tensor_copy(out=Ere[:, i, 1:nb + 1], in_=Xre(i)[:, 0:n:16]), reads=[bd[i]], writes=[bd[i]])
            p.dve(lambda e, i=i: e.tensor_copy(out=Eim[:, i, 1:nb + 1], in_=Xim(i)[:, 0:n:16]), reads=[bd[i]], writes=[bd[i]])
        for r in range(1, 16):
            for i in range(4):
                cstep(i, lev, Ere[:, i, 1:nb + 1], Eim[:, i, 1:nb + 1], Ere[:, i, 1:nb + 1], Eim[:, i, 1:nb + 1], nb,
                      add_re=Xre(i)[:, r:n:16], add_im=Xim(i)[:, r:n:16])
        cscan(lev + 1, lambda i: Ere[:, i, 1:nb + 1], lambda i: Eim[:, i, 1:nb + 1], nb,
              lambda i: Ere[:, i, 0:1], lambda i: Eim[:, i, 0:1])
        for r in range(16):
            for i in range(4):
                if r == 0:
                    pr, pi_ = Ere[:, i, 0:nb], Eim[:, i, 0:nb]
                else:
                    pr, pi_ = Xre(i)[:, r - 1:n:16], Xim(i)[:, r - 1:n:16]
                cstep(i, lev, Xre(i)[:, r:n:16], Xim(i)[:, r:n:16], pr, pi_, nb)

    NTILE = L // 512
    TPS = TS5 // 512

    def load_x(ti):
        s = ti % 2
        p.dma(xt[s][:], xT[:, ti * 512:(ti + 1) * 512].rearrange("(c p) t -> p c t", p=128), writes=[bxt[s]], eng="pool")

    load_x(0)
    chunk_idx = 0
    for ti in range(NTILE):
        s = ti % 2
        x_ = xt[s]
        if ti + 1 < NTILE:
            load_x(ti + 1)
        t0 = ti * 512
        tl = (ti % TPS) * 512
        for (ps_, bps_, c0) in ((psQ, bpsQ, 0), (psF, bpsF, 128), (psU, bpsU, 512)):
            for c in range(8):
                p.pe(lambda e, ps_=ps_, c=c, c0=c0, x_=x_: e.matmul(ps_[:], lhsT=wA[:, c, c0:c0 + 128], rhs=x_[:, c, :],
                                                                    start=(c == 0), stop=(c == 7)), reads=[bwA, bxt[s]], writes=[bps_])
        RH = [bH]
        p.act(lambda e: e.activation(out=f_, in_=psF[:], func=AF.Sigmoid), reads=[bpsF], writes=RH)
        p.dve(lambda e: e.tensor_scalar(out=f_, in0=f_, scalar1=oml, scalar2=lb, op0=ALU.mult, op1=ALU.add), reads=RH + [bm], writes=RH)
        p.act(lambda e: e.activation(out=lf, in_=f_, func=AF.Ln), reads=RH, writes=RH)
        p.dve(lambda e: e.tensor_scalar(out=kk, in0=f_, scalar1=-1.0, scalar2=1.0, op0=ALU.mult, op1=ALU.add), reads=RH, writes=RH)
        p.dve(lambda e: e.tensor_tensor_scan(out=bb, data0=m01, data1=lf, initial=0.0, op0=ALU.mult, op1=ALU.add), reads=RH + [bc], writes=RH)
        p.act(lambda e: e.activation(out=eb, in_=bb, func=AF.Exp), reads=RH, writes=RH)
        p.act(lambda e: e.activation(out=enb, in_=bb, func=AF.Exp, scale=-1.0), reads=RH, writes=RH)
        p.act(lambda e: e.activation(out=qq, in_=psQ[:], func=AF.Silu), reads=[bpsQ], writes=RH)
        p.dve(lambda e: e.tensor_tensor(out=qdec[:], in0=qq, in1=eb, op=ALU.mult), reads=RH, writes=[bqk])
        p.dve(lambda e: e.tensor_tensor(out=kinv, in0=kk, in1=enb, op=ALU.mult), reads=RH, writes=RH)
        p.act(lambda e: e.copy(out=kinvb[:], in_=kinv), reads=RH, writes=[bqk])
        eb3 = eb.rearrange("p (c s) -> p c s", s=64)
        p.dve(lambda e: e.tensor_tensor(out=ktT.rearrange("p (c s) -> p c s", s=64), in0=kinv.rearrange("p (c s) -> p c s", s=64),
                                        in1=eb3[:, :, 63:64].to_broadcast([128, 8, 64]), op=ALU.mult), reads=RH, writes=RH)
        sb5 = ti % 2
        p.act(lambda e, tl=tl: e.copy(out=uT[:, tl:tl + 512], in_=psU[:]), reads=[bpsU], writes=[buT])
        p.dve(lambda e, sb5=sb5: e.tensor_copy(out=uTb[sb5][:], in_=psU[:]), reads=[bpsU], writes=[buTb[sb5]])
        k = 0
        for i in range(4):
            for ri in range(2):
                pd = psD[k % 2]
                p.pe(lambda e, pd=pd, i=i, ri=ri, sb5=sb5: e.matmul(pd[:], lhsT=wB[:, ri, i * 128:(i + 1) * 128], rhs=uTb[sb5][:],
                                                                    start=True, stop=True), reads=[bwB, buTb[sb5]], writes=[bpsD[k % 2]])
                dst = (dre if ri == 0 else dim_)[:, i, tl:tl + 512]
                p.act(lambda e, pd=pd, dst=dst: e.copy(out=dst, in_=pd[:]), reads=[bpsD[k % 2]], writes=[bd[i]])
                k += 1
        for sub in range(4):
            tsl = slice(sub * 128, (sub + 1) * 128)
            h2 = sub % 2
            for c in range(8):
                p.pe(lambda e, c=c, tsl=tsl, h2=h2, x_=x_: e.matmul(psVG[:, h2, :], lhsT=x_[:, c, tsl], rhs=wA[:, c, 256:512],
                                                                    start=(c == 0), stop=(c == 7)), reads=[bwA, bxt[s]], writes=[bpsVG[h2]])
            p.act(lambda e, sub=sub, h2=h2: e.copy(out=vb[:, sub, :], in_=psVG[:, h2, 0:128]), reads=[bpsVG[h2]], writes=[bvg[sub]])
            p.act(lambda e, sub=sub, h2=h2: e.activation(out=gn[:, sub, :], in_=psVG[:, h2, 128:256], func=AF.Silu), reads=[bpsVG[h2]], writes=[bvg[sub]])
            p.dve(lambda e, sub=sub: e.tensor_tensor(out=gn[:, sub, :], in0=gn[:, sub, :], in1=ng, op=ALU.mult), reads=[bvg[sub], bc], writes=[bvg[sub]])
            mi = sub % 2
            p.pe(lambda e, mi=mi, tsl=tsl: e.transpose(out=psM[:, mi, :], in_=ktT[:, tsl], identity=identf[:]), reads=RH + [bidf], writes=[bpsM[mi]])
            p.act(lambda e, mi=mi, sub=sub: e.copy(out=kt[:, sub, :], in_=psM[:, mi, :]), reads=[bpsM[mi]], writes=[bkt[sub]])
        ys_ = yasb[ti % 2]
        for sub in range(4):
            tsl = slice(sub * 128, (sub + 1) * 128)
            ai_ = sub % 2
            mi = 2 + sub % 2
            p.pe(lambda e, mi=mi, tsl=tsl: e.matmul(psM[:, mi, :], lhsT=kinvb[:, tsl], rhs=qdec[:, tsl], start=True, stop=True),
                 reads=[bqk], writes=[bpsM[mi]])
            p.dve(lambda e, mi=mi, ai_=ai_: e.tensor_tensor(out=attT[ai_][:], in0=psM[:, mi, :], in1=tri, op=ALU.mult),
                  reads=[bpsM[mi], bc], writes=[battT[ai_]])
            oi = sub % 2
            p.pe(lambda e, oi=oi, ai_=ai_, sub=sub: e.matmul(psO[:, oi, :], lhsT=attT[ai_][:], rhs=vb[:, sub, :], start=True, stop=False),
                 reads=[battT[ai_], bvg[sub]], writes=[bpsO[oi]])
            for hc in range(2):
                rows = slice(hc * 64, (hc + 1) * 64)
                tch = slice(sub * 128 + hc * 64, sub * 128 + (hc + 1) * 64)
                sprev = (chunk_idx + 1) % 2
                snew = chunk_idx % 2
                p.pe(lambda e, oi=oi, rows=rows, tch=tch, sprev=sprev, hc=hc: e.matmul(
                    psO[rows, oi, :], lhsT=qdec[:, tch], rhs=Sb[sprev][:], start=False, stop=(hc == 1)),
                    reads=[bqk, bSb[sprev]], writes=[bpsO[oi]])
                di = chunk_idx % 2
                p.pe(lambda e, di=di, rows=rows, sub=sub: e.matmul(psS[:, di, :], lhsT=kt[rows, sub, :], rhs=vb[rows, sub, :], start=True, stop=True),
                     reads=[bkt[sub], bvg[sub]], writes=[bpsS[di]])
                cpos = sub * 128 + hc * 64 + 63
                p.dve(lambda e, di=di, cpos=cpos: e.scalar_tensor_tensor(out=S[:], in0=S[:], scalar=eb[:, cpos:cpos + 1], in1=psS[:, di, :],
                                                                         op0=ALU.mult, op1=ALU.add), reads=[bS, bpsS[di]] + RH, writes=[bS])
                p.act(lambda e, snew=snew: e.copy(out=Sb[snew][:], in_=S[:]), reads=[bS], writes=[bSb[snew]])
                chunk_idx += 1
            p.act(lambda e, oi=oi: e.activation(out=osc[:], in_=psO[:, oi, :], func=AF.Square, accum_out=om[:, 0:1]), reads=[bpsO[oi]], writes=[bo])
            p.act(lambda e: e.activation(out=om[:, 1:2], in_=om[:, 0:1], func=AF.Sqrt, scale=1.0 / 128.0, bias=RMS_EPS), reads=[bo], writes=[bo])
            p.dve(lambda e: e.reciprocal(out=om[:, 1:2], in_=om[:, 1:2]), reads=[bo], writes=[bo])
            p.dve(lambda e, oi=oi, sub=sub, ys_=ys_: e.scalar_tensor_tensor(out=ys_[:, sub, :], in0=psO[:, oi, :], scalar=om[:, 1:2], in1=gn[:, sub, :],
                                                                         op0=ALU.mult, op1=ALU.mult), reads=[bpsO[oi], bo, bvg[sub]], writes=[byasb[ti % 2]])
        for sub in range(4):
            p.pe(lambda e, sub=sub, ys_=ys_: e.transpose(out=psM[:, sub, :], in_=ys_[:, sub, :], identity=identf[:]),
                 reads=[byasb[ti % 2], bidf], writes=[bpsM[0]])
        yt_ = yaTs[ti % 2]
        p.act(lambda e, yt_=yt_: e.copy(out=yt_[:], in_=psM[:].rearrange("p a b -> p (a b)")), reads=[bpsM[0]], writes=[byaTs[ti % 2]])
        emit_ya(res, ti, yt_, byaTs[ti % 2])
        if (ti + 1) % TPS == 0:
            sup0 = (ti + 1 - TPS) * 512
            cscan(0, lambda i: dre[:, i, :], lambda i: dim_[:, i, :], TS5,
                  lambda i: carry[:, 0, i:i + 1], lambda i: carry[:, 1, i:i + 1])
            for i in range(4):
                p.dve(lambda e, i=i: e.tensor_copy(out=carry[:, 0, i:i + 1], in_=dre[:, i, TS5 - 1:TS5]), reads=[bd[i]], writes=[bd[i]])
                p.dve(lambda e, i=i: e.tensor_copy(out=carry[:, 1, i:i + 1], in_=dim_[:, i, TS5 - 1:TS5]), reads=[bd[i]], writes=[bd[i]])
            for ch in range(TPS):
                csl = slice(ch * 512, (ch + 1) * 512)
                hs = ch % 2
                for i in range(4):
                    p.act(lambda e, i=i, hs=hs, csl=csl: e.copy(out=hb[hs][:, 0, i, :], in_=dre[:, i, csl]), reads=[bd[i]], writes=[bhb[hs]])
                    p.act(lambda e, i=i, hs=hs, csl=csl: e.copy(out=hb[hs][:, 1, i, :], in_=dim_[:, i, csl]), reads=[bd[i]], writes=[bhb[hs]])
                k = 0
                for i in range(4):
                    for ri in range(2):
                        p.pe(lambda e, i=i, ri=ri, hs=hs, k=k: e.matmul(psQ[:], lhsT=wC[:, ri, i * 128:(i + 1) * 128], rhs=hb[hs][:, ri, i, :],
                                                                        start=(k == 0), stop=(k == 7)), reads=[bwB, bhb[hs]], writes=[bpsQ])
                        k += 1
                p.dve(lambda e, csl=csl: e.scalar_tensor_tensor(out=zs[:], in0=uT[:, csl], scalar=dsk, in1=psQ[:], op0=ALU.mult, op1=ALU.add),
                      reads=[buT, bc, bpsQ], writes=[bzs])
                p.act(lambda e, hs=hs: e.activation(out=ysb[hs][:], in_=zs[:], func=AF.Gelu_apprx_tanh), reads=[bzs], writes=[bysb[hs]])
                emit_ys(res, (sup0 // 512) + ch, ysb[hs], bysb[hs])


BIGM = 30000.0
DEBUG_MIXC = False


def build_mixc(L):
    nc = bass.Bass("TRN2", target_bir_lowering=False)
    xT = nc.dram_tensor("xT", [1024, L], F32, kind="ExternalInput").ap()
    ycT_d = nc.dram_tensor("ycT", [128, L], BF16, kind="ExternalOutput").ap()
    ydT_d = nc.dram_tensor("ydT", [128, L], BF16, kind="ExternalOutput").ap()
    p = Prog(nc)
    p.debug = DEBUG_MIXC

    def load_x(res, ti, dst, bdst):
        p.dma(dst[:], xT[:, ti * 512:(ti + 1) * 512].rearrange("(c p) t -> p c t", p=128), writes=[bdst], eng="pool")

    def emit_y(res, ti, which, t, b):
        d = ycT_d if which == 0 else ydT_d
        p.dma(d[:, ti * 512:(ti + 1) * 512], t[:], reads=[b])

    phase_mixc(p, nc, "", L, load_x, emit_y)
    p.finish()
    return nc


def phase_mixc(p, nc, pre, L, load_x_cb, emit_y):
    NBLK = L // 256
    wC_d = nc.dram_tensor(pre + "wC", [1024, 640], F32, kind="ExternalInput").ap()
    sink_d = nc.dram_tensor(pre + "sink", [128, 2], F32, kind="ExternalInput").ap()
    swm_d = nc.dram_tensor(pre + "swm", [128, 2, 256], F32, kind="ExternalInput").ap()
    cm_d = nc.dram_tensor(pre + "cm", [128, 2, 256], F32, kind="ExternalInput").ap()
    fut_d = nc.dram_tensor(pre + "fut", [128, 2, 128], F32, kind="ExternalInput").ap()
    res = {}
    identb_f, bidf = make_ident(p, F32, "identf")
    identb = p.sb("identb", [128, 128], BF16)
    bidb = p.buf()
    p.dve(lambda e: e.tensor_copy(out=identb[:], in_=identb_f[:]), reads=[bidf], writes=[bidb])
    wC = p.sb("wC", [128, 8, 640], BF16)
    bwC = p.buf()
    p.dma(wC[:], wC_d.rearrange("(c p) f -> p c f", p=128), writes=[bwC], eng="pool")
    cst = p.sb("cst", [128, 2 + 512 + 512 + 256], F32)
    bc = p.buf("cst")
    sink = cst[:, 0:2]
    swm = cst[:, 2:514].rearrange("p (a k) -> p a k", a=2)
    cm = cst[:, 514:1026].rearrange("p (a k) -> p a k", a=2)
    fut = cst[:, 1026:1282].rearrange("p (a k) -> p a k", a=2)
    p.dma(sink, sink_d, writes=[bc])
    p.dma(swm, swm_d, writes=[bc])
    p.dma(cm, cm_d, writes=[bc])
    p.dma(fut, fut_d, writes=[bc])
    ones = p.sb("ones", [128, 128], F32)
    bones = p.buf()
    p.dve(lambda e: e.memset(ones[:], 1.0), writes=[bones])

    qcT = p.sb("qcT", [128, 512], BF16); bqc = p.buf()
    qdT = p.sb("qdT", [128, 512], BF16); bqd = p.buf()
    kkc = p.sb("kkc", [128, L], BF16)
    kkd = p.sb("kkd", [128, L], BF16)
    vc = p.sb("vc", [128, L // 128, 64], BF16)
    vd = p.sb("vd", [128, L // 128, 64], BF16)
    bkv = p.buf("kv")
    bkvt = [p.buf("kv%d" % i) for i in range(L // 512)]
    kmT = p.sb("kmT", [128, 64], BF16)
    bkm = p.buf("km")
    p.dve(lambda e: e.memset(kmT[:], 0.0), writes=[bkm])
    kmx = p.sb("kmx", [128, 4], F32)
    p.dve(lambda e: e.memset(kmx[:], 0.0), writes=[bkm])
    xt = [p.sb("xt%d" % i, [128, 8, 512], BF16) for i in range(2)]
    bxt = [p.buf() for _ in range(2)]
    sq = p.sb("sq", [128, 512], F32); bsq = p.buf()
    qab = p.sb("qab", [128, 512], BF16)
    bqab = p.buf()
    kab = p.sb("kab", [128, 2], BF16)
    k8 = p.sb("k8", [128, 8], F32)
    sm = [p.sb("sm%d" % i, [128, 256], F32) for i in range(2)]; bsm = [p.buf() for _ in range(2)]
    Pb = [p.sb("Pb%d" % i, [128, 256], BF16) for i in range(2)]; bPb = [p.buf() for _ in range(2)]
    PT = [p.sb("PT%d" % i, [128, 2, 128], BF16) for i in range(2)]; bPT = [p.buf() for _ in range(2)]
    sc = p.sb("sc", [128, 16], F32); bsc = p.buf("sc")
    gm = p.sb("gm", [128, 64], F32)
    sbm = p.sb("sbm", [128, 64], F32)
    top8 = p.sb("top8", [128, 8], F32)
    lcols = p.sb("lcols", [128, 66], F32)
    bg = p.buf("gate")
    ycs = [p.sb("ycs%d" % i, [128, 4, 128], F32) for i in range(2)]; bycs = [p.buf() for _ in range(2)]
    yds = [p.sb("yds%d" % i, [128, 4, 128], F32) for i in range(2)]; byds = [p.buf() for _ in range(2)]
    yTs = [p.sb("yTs%d" % i, [128, 512], BF16) for i in range(4)]; byTs = [p.buf() for _ in range(4)]

    psP = [p.ps("psP%d" % i, [128, 512], F32) for i in range(2)]; bpsP = [p.pbuf() for _ in range(2)]
    psV = p.ps("psV", [128, 512], F32); bpsV = p.pbuf()
    psS = [p.ps("psS%d" % i, [128, 512], F32) for i in range(2)]; bpsS = [p.pbuf() for _ in range(2)]
    psT = p.ps("psT", [128, 8, 128], BF16); bpsT = p.pbuf()
    psO = [p.ps("psO%d" % i, [128, 512], F32) for i in range(2)]; bpsO = [p.pbuf() for _ in range(2)]

    NTILE = L // 512

    def load_x(ti):
        s = ti % 2
        load_x_cb(res, ti, xt[s], bxt[s])

    load_x(0)
    rot = 0
    for ti in range(NTILE):
        s = ti % 2
        x_ = xt[s]
        if ti + 1 < NTILE:
            load_x(ti + 1)
        t0 = ti * 512
        tsl = slice(t0, t0 + 512)
        dsts = [(qcT[:], bqc, 0.125), (kkc[:, tsl], bkvt[ti], 1.0), (qdT[:], bqd, 0.125), (kkd[:, tsl], bkvt[ti], 1.0)]
        for pi, (dst, bdst, scl) in enumerate(dsts):
            pp = psP[pi % 2]
            for c in range(8):
                p.pe(lambda e, pp=pp, c=c, pi=pi, x_=x_: e.matmul(pp[:], lhsT=wC[:, c, pi * 128:(pi + 1) * 128], rhs=x_[:, c, :],
                                                                  start=(c == 0), stop=(c == 7)), reads=[bwC, bxt[s]], writes=[bpsP[pi % 2]])
            p.act(lambda e, pp=pp, dst=dst, scl=scl: e.activation(out=dst, in_=pp[:], func=AF.Copy, scale=scl), reads=[bpsP[pi % 2]], writes=[bdst])
            if pi == 2:
                p.dve(lambda e, pp=pp: e.tensor_scalar(out=sq[:], in0=pp[:], scalar1=-1.0, scalar2=None, op0=ALU.mult), reads=[bpsP[pi % 2]], writes=[bsq])
                p.dve(lambda e, pp=pp: e.tensor_tensor(out=sq[:], in0=sq[:], in1=pp[:], op=ALU.max), reads=[bpsP[pi % 2], bsq], writes=[bsq])
                p.dve(lambda e: e.tensor_scalar(out=qab[:], in0=sq[:], scalar1=0.125, scalar2=None, op0=ALU.mult), reads=[bsq], writes=[bqab])
            if pi == 3:
                p.dve(lambda e, pp=pp: e.tensor_scalar(out=sq[:], in0=pp[:], scalar1=-1.0, scalar2=None, op0=ALU.mult), reads=[bpsP[pi % 2]], writes=[bsq])
                p.dve(lambda e, pp=pp: e.tensor_tensor(out=sq[:], in0=sq[:], in1=pp[:], op=ALU.max), reads=[bpsP[pi % 2], bsq], writes=[bsq])
                p.dve(lambda e: e.max(out=k8[:], in_=sq[:]), reads=[bsq], writes=[bkm])
                p.dve(lambda e: e.tensor_tensor(out=kmx[:, 0:1], in0=kmx[:, 0:1], in1=k8[:, 0:1], op=ALU.max), reads=[bkm], writes=[bkm])
                p.dve(lambda e: e.tensor_copy(out=kab[:, 0:1], in_=kmx[:, 0:1]), reads=[bkm], writes=[bkm])
                p.dve(lambda e: e.tensor_copy(out=kab[:, 1:2], in_=kmx[:, 0:1]), reads=[bkm], writes=[bkm])
        for sub in range(4):
            g = ti * 4 + sub
            for c in range(8):
                p.pe(lambda e, c=c, sub=sub, x_=x_: e.matmul(psV[:, 0:128], lhsT=x_[:, c, sub * 128:(sub + 1) * 128], rhs=wC[:, c, 512:640],
                                                             start=(c == 0), stop=(c == 7)), reads=[bwC, bxt[s]], writes=[bpsV])
            p.act(lambda e, g=g: e.copy(out=vc[:, g, :], in_=psV[:, 0:64]), reads=[bpsV], writes=[bkvt[ti]])
            p.act(lambda e, g=g: e.copy(out=vd[:, g, :], in_=psV[:, 64:128]), reads=[bpsV], writes=[bkvt[ti]])
        for hb in range(2):
            blk = ti * 2 + hb
            p.dve(lambda e, blk=blk: e.tensor_reduce(out=sq[:, 0:1], in_=kkd[:, blk * 256:(blk + 1) * 256], axis=AX.X, op=ALU.add),
                  reads=[bkvt[ti]], writes=[bsq])
            p.dve(lambda e, blk=blk: e.tensor_scalar(out=kmT[:, blk:blk + 1], in0=sq[:, 0:1], scalar1=1.0 / 256.0, scalar2=None, op0=ALU.mult),
                  reads=[bsq], writes=[bkm])
        yc_ = ycs[ti % 2]
        yd_ = yds[ti % 2]
        for sub in range(4):
            g = ti * 4 + sub
            qsl = slice(sub * 128, (sub + 1) * 128)
            n = g // 2
            a = g % 2
            for h in range(2):
                rows = slice(h * 64, (h + 1) * 64)
                r = rot % 2
                rot += 1
                ps_ = psS[r]
                if g == 0:
                    k0, nk, mk = 0, 128, swm[:, 1, 128:256]
                else:
                    k0, nk, mk = (g - 1) * 128, 256, swm[:, 1, :]
                p.pe(lambda e, ps_=ps_, rows=rows, qsl=qsl, k0=k0, nk=nk: e.matmul(ps_[:, 0:nk], lhsT=qcT[rows, qsl], rhs=kkc[rows, k0:k0 + nk],
                                                                                  start=True, stop=True),
                     reads=[bqc, bkvt[ti]] + ([bkvt[ti - 1]] if (sub == 0 and ti > 0) else []), writes=[bpsS[r]])
                smr = sm[r]
                p.dve(lambda e, ps_=ps_, smr=smr, nk=nk, mk=mk: e.tensor_tensor(out=smr[:, 0:nk], in0=ps_[:, 0:nk], in1=mk, op=ALU.add),
                      reads=[bpsS[r], bc], writes=[bsm[r]])
                R = [bsc]
                mx, nmx, rs, es, den = [sc[:, i:i + 1] for i in range(5)]
                p.dve(lambda e, smr=smr, nk=nk: e.tensor_reduce(out=mx, in_=smr[:, 0:nk], axis=AX.X, op=ALU.max), reads=[bsm[r]], writes=R)
                p.dve(lambda e, h=h: e.tensor_tensor(out=mx, in0=mx, in1=sink[:, h:h + 1], op=ALU.max), reads=R + [bc], writes=R)
                p.dve(lambda e: e.tensor_scalar(out=nmx, in0=mx, scalar1=-1.0, scalar2=None, op0=ALU.mult), reads=R, writes=R)
                pb = Pb[r]
                p.act(lambda e, pb=pb, smr=smr, nk=nk: e.activation(out=pb[:, 0:nk], in_=smr[:, 0:nk], func=AF.Exp, bias=nmx, scale=1.0, accum_out=rs),
                      reads=[bsm[r]] + R, writes=[bPb[r]] + R)
                p.act(lambda e, h=h: e.activation(out=es, in_=sink[:, h:h + 1], func=AF.Exp, bias=nmx, scale=1.0), reads=R + [bc], writes=R)
                p.dve(lambda e: e.tensor_tensor(out=den, in0=rs, in1=es, op=ALU.add), reads=R, writes=R)
                p.dve(lambda e: e.reciprocal(out=den, in_=den), reads=R, writes=R)
                nkc = nk // 128
                for kc in range(nkc):
                    p.pe(lambda e, pb=pb, kc=kc: e.transpose(out=psT[:, kc, :], in_=pb[:, kc * 128:(kc + 1) * 128], identity=identb[:]),
                         reads=[bPb[r], bidb], writes=[bpsT])
                pt = PT[r]
                p.dve(lambda e, pt=pt, nkc=nkc: e.tensor_copy(out=pt[:, 0:nkc, :], in_=psT[:, 0:nkc, :]), reads=[bpsT], writes=[bPT[r]])
                po = psO[0]
                for kc in range(nkc):
                    gk = (g - 1 + kc) if g > 0 else 0
                    p.pe(lambda e, po=po, pt=pt, kc=kc, gk=gk, nkc=nkc: e.matmul(po[:, 0:64], lhsT=pt[:, kc, :], rhs=vc[:, gk, :],
                                                                               start=(kc == 0), stop=(kc == nkc - 1)),
                         reads=[bPT[r], bkvt[ti]] + ([bkvt[ti - 1]] if (sub == 0 and ti > 0) else []), writes=[bpsO[0]])
                p.act(lambda e, po=po, sub=sub, h=h, yc_=yc_: e.activation(out=yc_[:, sub, h * 64:(h + 1) * 64], in_=po[:, 0:64], func=AF.Copy, scale=den),
                      reads=[bpsO[0]] + R, writes=[bycs[ti % 2]])

                G = [bg]
                qs2, mb, nmb, lsum = [sc[:, i:i + 1] for i in range(8, 12)]
                p.pe(lambda e, rows=rows, qsl=qsl: e.matmul(psV[:, 0:2], lhsT=qab[rows, qsl], rhs=kab[rows, 0:2], start=True, stop=True),
                     reads=[bqab, bkm], writes=[bpsV])
                p.dve(lambda e: e.tensor_copy(out=mb, in_=psV[:, 0:1]), reads=[bpsV], writes=G)
                p.dve(lambda e: e.tensor_scalar(out=nmb, in0=mb, scalar1=-1.0, scalar2=None, op0=ALU.mult), reads=G, writes=G)
                po = psO[1]
                first = True
                if n > 0:
                    p.pe(lambda e, rows=rows, qsl=qsl: e.matmul(psV[:, 64:128], lhsT=qdT[rows, qsl], rhs=kmT[rows, :], start=True, stop=True),
                         reads=[bqd, bkm], writes=[bpsV])
                    fsl = slice(64 - n, 128 - n)
                    p.dve(lambda e, fsl=fsl: e.tensor_tensor(out=gm[:], in0=psV[:, 64:128], in1=fut[:, 0, fsl], op=ALU.add), reads=[bpsV, bc], writes=G)
                    p.dve(lambda e: e.max(out=top8[:], in_=gm[:]), reads=G, writes=G)
                    p.dve(lambda e: e.tensor_scalar(out=sbm[:], in0=gm[:], scalar1=top8[:, 2:3], scalar2=BIGM, op0=ALU.is_ge, op1=ALU.mult),
                          reads=G, writes=G)
                    p.dve(lambda e, fsl=fsl: e.tensor_tensor(out=sbm[:], in0=sbm[:], in1=fut[:, 1, fsl], op=ALU.add), reads=G + [bc], writes=G)
                    p.dve(lambda e: e.tensor_scalar(out=sbm[:], in0=sbm[:], scalar1=mb, scalar2=None, op0=ALU.subtract), reads=G, writes=G)
                p.dve(lambda e: e.memset(lcols[:], 0.0), writes=G)
                for jb in range(n + 1):
                    own = (jb == n)
                    r = rot % 2
                    rot += 1
                    ps_ = psS[r]
                    kreads = [bkvt[jb // 2]]
                    p.pe(lambda e, ps_=ps_, rows=rows, qsl=qsl, jb=jb: e.matmul(ps_[:, 0:256], lhsT=qdT[rows, qsl], rhs=kkd[rows, jb * 256:(jb + 1) * 256],
                                                                              start=True, stop=True), reads=[bqd] + kreads, writes=[bpsS[r]])
                    pb = Pb[r]
                    if own:
                        smr = sm[r]
                        p.dve(lambda e, ps_=ps_, smr=smr, a=a: e.tensor_tensor(out=smr[:], in0=ps_[:, 0:256], in1=cm[:, a, :], op=ALU.add),
                              reads=[bpsS[r], bc], writes=[bsm[r]])
                        p.act(lambda e, pb=pb, smr=smr, jb=jb: e.activation(out=pb[:], in_=smr[:], func=AF.Exp, bias=nmb, scale=1.0,
                                                                            accum_out=lcols[:, 64:65]), reads=[bsm[r]] + G, writes=[bPb[r]] + G)
                    else:
                        p.act(lambda e, pb=pb, ps_=ps_, jb=jb: e.activation(out=pb[:], in_=ps_[:, 0:256], func=AF.Exp, bias=sbm[:, jb:jb + 1], scale=1.0,
                                                                            accum_out=lcols[:, jb:jb + 1]), reads=[bpsS[r]] + G, writes=[bPb[r]] + G)
                    for kc in range(2):
                        p.pe(lambda e, pb=pb, kc=kc: e.transpose(out=psT[:, kc, :], in_=pb[:, kc * 128:(kc + 1) * 128], identity=identb[:]),
                             reads=[bPb[r], bidb], writes=[bpsT])
                    pt = PT[r]
                    p.dve(lambda e, pt=pt: e.tensor_copy(out=pt[:], in_=psT[:, 0:2, :]), reads=[bpsT], writes=[bPT[r]])
                    for kc in range(2):
                        p.pe(lambda e, po=po, pt=pt, kc=kc, jb=jb, first=first, own=own: e.matmul(
                            po[:, 0:64], lhsT=pt[:, kc, :], rhs=vd[:, jb * 2 + kc, :], start=(first and kc == 0), stop=(own and kc == 1)),
                            reads=[bPT[r]] + kreads, writes=[bpsO[1]])
                    first = False
                if g == 0 and h == 0:
                    p.debug_dump("sc", sc[:], [128, 16], F32, G + R)
                    p.debug_dump("lcols", lcols[:], [128, 66], F32, G)
                    p.debug_dump("kmx", kmx[:], [128, 4], F32, [bkm])
                    p.debug_dump("Pb", Pb[r][:], [128, 256], BF16, [bPb[r]])
                    p.debug_dump("kmT", kmT[:], [128, 64], BF16, [bkm])
                    p.debug_dump("qab", qab[:], [128, 512], BF16, [bqab])
                p.dve(lambda e: e.tensor_reduce(out=lsum, in_=lcols[:, 0:65], axis=AX.X, op=ALU.add), reads=G, writes=G)
                p.dve(lambda e: e.reciprocal(out=lsum, in_=lsum), reads=G, writes=G)
                p.act(lambda e, po=po, sub=sub, h=h, yd_=yd_: e.activation(out=yd_[:, sub, h * 64:(h + 1) * 64], in_=po[:, 0:64], func=AF.Copy, scale=lsum),
                      reads=[bpsO[1]] + G, writes=[byds[ti % 2]])
        for which, (src, bsrc) in enumerate(((yc_, bycs[ti % 2]), (yd_, byds[ti % 2]))):
            pp = psP[which]
            for sub in range(4):
                p.pe(lambda e, sub=sub, src=src, pp=pp: e.transpose(out=pp[:, sub * 128:(sub + 1) * 128], in_=src[:, sub, :], identity=identb_f[:]),
                     reads=[bsrc, bidf], writes=[bpsP[which]])
            k = (ti % 2) * 2 + which
            p.act(lambda e, k=k, pp=pp: e.copy(out=yTs[k][:], in_=pp[:]), reads=[bpsP[which]], writes=[byTs[k]])
            emit_y(res, ti, which, yTs[k], byTs[k])


GROUPS = [[0, 1, 2, 3], [4, 5, 6, 7]]


def build_fused(L, debug=False):
    SEG = L // 4
    NCH = SEG // 512
    TG = min(1024, SEG)
    CPG = TG // 512
    nc = bass.Bass("TRN2", target_bir_lowering=False)
    msk_d = nc.dram_tensor("rankmask", [128, 4], F32, kind="ExternalInput").ap()
    xres_d = nc.dram_tensor("xres", [SEG, 1024], F32, kind="ExternalInput").ap()
    xo_d = nc.dram_tensor("xo", [SEG, 1024], F32, kind="ExternalOutput").ap()
    exin = [nc.dram_tensor("exin%d" % i, [NCH, 4, 4, 256, 512], BF16).ap() for i in range(2)]
    exout = [nc.dram_tensor("exout%d" % i, [NCH, 4, 256, 512], BF16).ap() for i in range(2)]
    agin = nc.dram_tensor("agin", [NCH, 1024, 512], BF16).ap()
    agout = nc.dram_tensor("agout", [NCH, 4, 1024, 512], BF16).ap()
    x1loc = nc.dram_tensor("x1loc", [SEG, 1024], F32).ap()

    p = Prog(nc)
    p.debug = debug
    bexin = [[p.buf() for _ in range(NCH)] for _ in range(2)]
    bexout = [[p.buf() for _ in range(NCH)] for _ in range(2)]
    bagin = [p.buf() for _ in range(NCH)]
    bagout = [p.buf() for _ in range(NCH)]
    bx1loc = p.buf()

    def make_exchange_writer(xi):
        st = {}

        def setup():
            st["msk"] = p.sb("msk%d" % xi, [128, 4], F32)
            st["bmsk"] = p.buf()
            p.dma(st["msk"][:], msk_d, writes=[st["bmsk"]])
            st["tmp"] = [p.sb("extmp%d_%d" % (xi, i), [128, 4, 512], BF16) for i in range(2)]
            st["btmp"] = [p.buf() for _ in range(2)]
            st["k"] = 0

        def emit(ti, f0, t, b):
            if "msk" not in st:
                setup()
            k = st["k"] % 2
            st["k"] += 1
            tmp, btmp = st["tmp"][k], st["btmp"][k]
            for q in range(4):
                p.pool(lambda e, q=q, tmp=tmp: e.tensor_scalar(out=tmp[:, q, :], in0=t[:], scalar1=st["msk"][:, q:q + 1], scalar2=0.0,
                                                               op0=ALU.mult, op1=ALU.add), reads=[b, st["bmsk"]], writes=[btmp])
            d, c = ti // NCH, ti % NCH
            p.dma(exin[xi][c, d, :, f0:f0 + 128, :].rearrange("q f t -> f q t"), tmp[:], reads=[btmp], writes=[bexin[xi][c]])

        def finish():
            for c in range(NCH):
                p.collective("ReduceScatter", ALU.add, GROUPS,
                             ins=[exin[xi][c].rearrange("d q f t -> (d q f) t")], outs=[exout[xi][c].rearrange("q f t -> (q f) t")],
                             reads=[bexin[xi][c]], writes=[bexout[xi][c]])
        return emit, finish

    def make_load_yt(xi):
        def load_yt(res, g, yt, byt):
            for cc in range(CPG):
                c = g * CPG + cc
                for half in range(2):
                    p.dma(yt[:, half * 4:(half + 1) * 4, cc * 512:(cc + 1) * 512],
                          exout[xi][c, :, half * 128:(half + 1) * 128, :].rearrange("q f t -> f q t"),
                          reads=[bexout[xi][c]], writes=[byt], eng=("sp" if half == 0 else "act"))
        return load_yt

    p.prefix = "A_"
    emitA, finA = make_exchange_writer(0)
    phase_mixa(p, nc, "a_", L, lambda res, ti, t, b: emitA(ti, 0, t, b), lambda res, ti, t, b: emitA(ti, 128, t, b))
    finA()
    p.end_phase()

    p.prefix = "T0_"
    st0 = {}

    def load_xres0(res, g, st, dst, bdst):
        t0 = g * TG + st * 128
        p.dma(dst, xres_d[t0:t0 + 128, :], writes=[bdst], eng="act")

    def store_out0(res, g, st, o, bo):
        if "xT" not in st0:
            st0["xT"] = [p.sb("x1Ts%d" % i, [128, 8, 128], BF16) for i in range(2)]
            st0["bxT"] = [p.buf() for _ in range(2)]
            st0["k"] = 0
        t0 = g * TG + st * 128
        p.dma(x1loc[t0:t0 + 128, :], o[:], reads=[bo], writes=[bx1loc])
        k = st0["k"] % 2
        st0["k"] += 1
        xT, bxT = st0["xT"][k], st0["bxT"][k]
        psA, bpsA, identf, bidf = res["psA"], res["bpsA"], res["identf"], res["bidf"]
        for hf in range(2):
            pa = psA[hf]
            for c4 in range(4):
                c = hf * 4 + c4
                p.pe(lambda e, pa=pa, c4=c4, c=c: e.transpose(out=pa[:, c4 * 128:(c4 + 1) * 128], in_=o[:, c * 128:(c + 1) * 128], identity=identf[:]),
                     reads=[bo, bidf], writes=[bpsA[hf]])
            p.act(lambda e, pa=pa, hf=hf, xT=xT: e.copy(out=xT[:, hf * 4:(hf + 1) * 4, :], in_=pa[:].rearrange("p (c t) -> p c t", c=4)),
                  reads=[bpsA[hf]], writes=[bxT])
        chunk, toff = t0 // 512, t0 % 512
        p.dma(agin[chunk].rearrange("(dc p) t -> p dc t", p=128)[:, :, toff:toff + 128], xT[:], reads=[bxT], writes=[bagin[chunk]])
        if toff == 384:
            p.collective("AllGather", ALU.bypass, GROUPS, ins=[agin[chunk]], outs=[agout[chunk].rearrange("r d t -> (r d) t")],
                         reads=[bagin[chunk]], writes=[bagout[chunk]])

    phase_tail(p, nc, "t0_", SEG, True, make_load_yt(0), load_xres0, store_out0, TG)
    p.end_phase()

    p.prefix = "C_"
    emitC, finC = make_exchange_writer(1)

    def load_xc(res, ti, dst, bdst):
        r, c = ti // NCH, ti % NCH
        p.dma(dst[:], agout[c, r].rearrange("(dc p) t -> p dc t", p=128), reads=[bagout[c]], writes=[bdst])

    phase_mixc(p, nc, "c_", L, load_xc, lambda res, ti, which, t, b: emitC(ti, which * 128, t, b))
    finC()
    p.end_phase()

    p.prefix = "T1_"

    def load_xres1(res, g, st, dst, bdst):
        t0 = g * TG + st * 128
        p.dma(dst, x1loc[t0:t0 + 128, :], reads=[bx1loc], writes=[bdst], eng="act")

    def store_out1(res, g, st, o, bo):
        t0 = g * TG + st * 128
        p.dma(xo_d[t0:t0 + 128, :], o[:], reads=[bo])

    phase_tail(p, nc, "t1_", SEG, False, make_load_yt(1), load_xres1, store_out1, TG)
    if debug:
        for name, src, bufs in (("dbg_exout0", exout[0], bexout[0]), ("dbg_exout1", exout[1], bexout[1]), ("dbg_agout", agout, bagout),
                                ("dbg_exin0", exin[0], bexin[0])):
            d = nc.dram_tensor(name, list(src.shape), BF16, kind="ExternalOutput").ap()
            for c in range(NCH):
                p.dma(d[c], src[c], reads=[bufs[c]])
        d = nc.dram_tensor("dbg_x1loc", [SEG, 1024], F32, kind="ExternalOutput").ap()
        p.dma(d, x1loc, reads=[bx1loc])
    p.finish()
    return nc

import ml_dtypes

_BF = ml_dtypes.bfloat16
_PROGS = {}


def _prog(key, fn):
    if key not in _PROGS:
        _PROGS[key] = fn()
    return _PROGS[key]


def mixa_inputs(inp, j, xT):
    W = inp['ev_w_in'][0]
    sl = slice(128 * j, 128 * (j + 1))
    wA = np.concatenate([W[:, 0:512][:, sl], W[:, 512:1024][:, sl], W[:, 1024:1536][:, sl], W[:, 1536:2048][:, sl], W[:, 2048:2560][:, sl]], axis=1)
    lbl = np.ascontiguousarray(inp['hgrn_lb_logits'][:, sl].T)
    ng = np.ascontiguousarray(np.broadcast_to(inp['ev_a_norm'][0, sl][None], (128, 128)))
    G0 = 8 * j
    are, aim, ldt = inp['ev_s5_a_re'][0], inp['ev_s5_a_im'][0], inp['ev_s5_log_dt'][0]
    sps = np.zeros((128, 3, 4), np.float32)
    spw = np.zeros((128, 3, 4, 128), np.float32)
    bpad = np.zeros((128, 2, 4, 128), np.float32)
    cpad = np.zeros((128, 2, 4, 128), np.float32)
    for i in range(4):
        for gl in range(2):
            g = G0 + 2 * i + gl
            glc = 2 * i + gl
            sps[gl * 64:(gl + 1) * 64, 0, i] = are[g]
            sps[gl * 64:(gl + 1) * 64, 1, i] = aim[g]
            sps[gl * 64:(gl + 1) * 64, 2, i] = ldt[g]
            spw[:, 0, i, gl * 64:(gl + 1) * 64] = are[g][None]
            spw[:, 1, i, gl * 64:(gl + 1) * 64] = aim[g][None]
            spw[:, 2, i, gl * 64:(gl + 1) * 64] = ldt[g]
            bpad[glc * 16:(glc + 1) * 16, 0, i, gl * 64:(gl + 1) * 64] = inp['ev_s5_b_re'][0, g].T
            bpad[glc * 16:(glc + 1) * 16, 1, i, gl * 64:(gl + 1) * 64] = inp['ev_s5_b_im'][0, g].T
            cpad[gl * 64:(gl + 1) * 64, 0, i, glc * 16:(glc + 1) * 16] = inp['ev_s5_c_re'][0, g].T
            cpad[gl * 64:(gl + 1) * 64, 1, i, glc * 16:(glc + 1) * 16] = inp['ev_s5_c_im'][0, g].T
    dsk = np.ascontiguousarray(inp['ev_s5_d'][0, sl][:, None])
    s_ = np.arange(128)[:, None]
    c_ = np.arange(128)[None, :]
    tri = ((s_ // 64 == c_ // 64) & (s_ <= c_)).astype(np.float32)
    m01 = np.broadcast_to((np.arange(512) % 64 != 0).astype(np.float32)[None], (128, 512))
    return {"xT": xT, "wA": np.ascontiguousarray(wA), "lbl": lbl, "ng": ng, "sps": sps, "spw": spw.reshape(128, 3, 512),
            "bpad": bpad.reshape(128, 2, 512), "cpad": cpad.reshape(128, 2, 512), "dsk": dsk, "tri": tri, "m01": np.ascontiguousarray(m01)}


def mixc_inputs(inp, j, xT):
    W = inp['od_w_in'][0]
    kv = j // 2
    qc = W[:, 128 * j:128 * (j + 1)]
    kc = W[:, 512 + 64 * kv:512 + 64 * (kv + 1)]
    vc = W[:, 640 + 64 * kv:640 + 64 * (kv + 1)]
    qd = W[:, 768 + 128 * j:768 + 128 * (j + 1)]
    kd = W[:, 1280 + 64 * kv:1280 + 64 * (kv + 1)]
    vd = W[:, 1408 + 64 * kv:1408 + 64 * (kv + 1)]
    wC = np.concatenate([qc, kc, kc, qd, kd, kd, vc, vd], axis=1)
    sink = np.ascontiguousarray(np.broadcast_to(inp['od_sinks'][0, 2 * j:2 * j + 2][None], (128, 2)))
    q = np.arange(128)[:, None]
    k = np.arange(256)[None, :]
    band = np.where(((k < 128) & (k > q)) | ((k >= 128) & (k - 128 <= q)), 0.0, -BIGM).astype(np.float32)
    swm = np.stack([band, band], axis=1)
    cm = np.stack([np.where(k <= q + 128 * a, 0.0, -BIGM) for a in range(2)], axis=1).astype(np.float32)
    i = np.arange(128)[None, :]
    f0 = np.where(i < 64, 0.0, -BIGM) * np.ones((128, 1))
    fut = np.stack([f0, f0 - BIGM], axis=1).astype(np.float32)
    return {"xT": xT, "wC": np.ascontiguousarray(wC), "sink": sink, "swm": np.ascontiguousarray(swm), "cm": np.ascontiguousarray(cm),
            "fut": np.ascontiguousarray(fut)}


def tail_inputs(inp, layer, xres, yT, w_out, w_glu):
    lnp = np.ascontiguousarray(np.stack([np.broadcast_to(inp[k][layer], (128, 1024)) for k in ['ln1_g', 'ln1_b', 'ln2_g', 'ln2_b']]).astype(np.float32))
    w_r = np.ascontiguousarray(np.concatenate([inp['moe_w_group'][layer], inp['moe_w_expert'][layer]], axis=1))
    b_r = np.ascontiguousarray(np.broadcast_to(np.concatenate([inp['moe_b_group'][layer], inp['moe_b_expert'][layer]])[None], (128, 20)).astype(np.float32))
    m = {"xres": xres, "yT": yT, "w_out": w_out, "lnp": lnp, "w_r": w_r, "b_r": b_r,
         "w_gu": inp['moe_w_gate_up'][layer], "w_dn": inp['moe_w_down'][layer]}
    if w_glu is not None:
        m["w_glu"] = w_glu
    return m


def fused_inputs(inp, x, xTb, b, r, SEG):
    m = {}
    for k, v in mixa_inputs(inp, r, xTb).items():
        m["a_" + k] = v
    for k, v in mixc_inputs(inp, r, None).items():
        if k != "xT":
            m["c_" + k] = v
    for k, v in tail_inputs(inp, 0, None, None, inp['ev_w_out'][0], inp['ev_s5_w_glu'][0]).items():
        if k not in ("xres", "yT"):
            m["t0_" + k] = v
    for k, v in tail_inputs(inp, 1, None, None, inp['od_w_out'][0], None).items():
        if k not in ("xres", "yT"):
            m["t1_" + k] = v
    msk = np.zeros((128, 4), np.float32)
    msk[:, r] = 1.0
    m["rankmask"] = msk
    m["xres"] = np.ascontiguousarray(x[b, r * SEG:(r + 1) * SEG])
    return m


def kernel(**inputs):
    inp = {k: np.ascontiguousarray(np.asarray(v)) for k, v in inputs.items()}
    x = inp['x']
    B, L, D = x.shape
    SEG = L // 4
    cores = list(range(8))
    xT = [np.ascontiguousarray(x[b].T) for b in range(B)]
    nc = _prog(("F", L), lambda: build_fused(L))
    maps = [fused_inputs(inp, x, xT[c // 4], c // 4, c % 4, SEG) for c in cores]
    res = run_bass_kernel_spmd(nc, maps, core_ids=cores).results
    out = np.stack([np.concatenate([np.asarray(res[b * 4 + s]["xo"]) for s in range(4)], axis=0) for b in range(B)])
    return out.astype(np.float32)
```

```python
import numpy as np
from contextlib import ExitStack
import concourse.bass as bass
import concourse.mybir as mybir
from concourse.bass_utils import run_bass_kernel_spmd

dt = mybir.dt
F32 = dt.float32
BF16 = dt.bfloat16
I32 = dt.int32
U32 = dt.uint32
AF = mybir.ActivationFunctionType
ALU = mybir.AluOpType
AX = mybir.AxisListType


class Buf:
    __slots__ = ("name", "lw", "rd", "excl")

    def __init__(self, name, excl=False):
        self.name = name
        self.lw = None
        self.rd = {}
        self.excl = excl


class Prog:
    ENG = ["pe", "dve", "act", "pool", "sp"]
    NDMA = 32

    def __init__(self, nc, same_engine_sync=True):
        self.nc = nc
        self.es = ExitStack()
        self.ops = {e: [] for e in self.ENG}
        self.cnt = {e: 0 for e in self.ENG}
        self.waited = {e: {} for e in self.ENG}
        self.dma_cnt = [0] * self.NDMA
        self.dma_rr = 0
        self.same = same_engine_sync
        self.sems = {}
        self.gen = 0
        self.semkey = {}
        for e in ["pe", "dve", "act", "pool"]:
            self.semkey[e] = e
            self.sems[e] = self.es.enter_context(nc.semaphore("s_" + e))
        for j in range(self.NDMA):
            self.sems["d%d" % j] = self.es.enter_context(nc.semaphore("s_d%d" % j))
        self.nbuf = 0
        self.scope = ExitStack()
        self.ncc = 0
        self.prefix = ""

    def sb(self, name, shape, dtype):
        return self.scope.enter_context(self.nc.sbuf_tensor("sb_" + self.prefix + name, list(shape), dtype))

    def ps(self, name, shape, dtype):
        return self.scope.enter_context(self.nc.psum_tensor("pp_" + self.prefix + name, list(shape), dtype))

    def debug_dump(self, name, ap, shape, dtype, reads):
        if not getattr(self, "debug", False):
            return
        d = self.nc.dram_tensor("dbg_" + name, list(shape), dtype, kind="ExternalOutput").ap()
        self.dma(d, ap, reads=reads)

    def barrier(self):
        targets = []
        for j in range(self.NDMA):
            if self.dma_cnt[j] > 0:
                targets.append(("d%d" % j, self.dma_cnt[j]))
        for e in ["pe", "dve", "act", "pool"]:
            if self.cnt[e] > 0:
                targets.append((self.semkey[e], self.cnt[e]))
        for k in self.sems:
            if k.startswith("cc"):
                targets.append((k, 1))
        for e in self.ENG:
            waits = []
            for k, v in targets:
                if k == self.semkey.get(e):
                    continue
                if self.waited[e].get(k, 0) >= v:
                    continue
                waits.append((k, v))
                self.waited[e][k] = v
            if waits:
                self.ops[e].append((waits, None, None, False))

    def end_phase(self):
        self.barrier()
        self.scope.close()
        self.scope = ExitStack()
        self.gen += 1
        for e in ["pe", "dve", "act", "pool"]:
            k = "%s@%d" % (e, self.gen)
            self.semkey[e] = k
            self.sems[k] = self.es.enter_context(self.nc.semaphore("s_%s_%d" % (e, self.gen)))
            self.cnt[e] = 0

    def collective(self, kind, alu, groups, ins, outs, reads=(), writes=()):
        k = "cc%d" % self.ncc
        self.ncc += 1
        self.sems[k] = self.es.enter_context(self.nc.semaphore("s_" + k))
        eng = "pool"
        waits = {}

        def need(kk, v):
            if v <= 0 or self.waited[eng].get(kk, 0) >= v:
                return
            if waits.get(kk, 0) < v:
                waits[kk] = v
        for b in reads:
            if b.lw is not None:
                need(*b.lw)
        for b in writes:
            if b.lw is not None:
                need(*b.lw)
            for kk, v in b.rd.items():
                need(kk, v)
        for kk, v in waits.items():
            self.waited[eng][kk] = v
        tok = (k, 1)
        self.ops[eng].append((list(waits.items()), lambda e: e.collective_compute(kind, alu, replica_groups=groups, ins=ins, outs=outs), tok, "cc"))
        for b in reads:
            if b.rd.get(tok[0], 0) < tok[1]:
                b.rd[tok[0]] = tok[1]
        for b in writes:
            b.lw = tok
            b.rd = {}
        return tok

    def buf(self, name=None, excl=False):
        self.nbuf += 1
        return Buf(name or ("b%d" % self.nbuf), excl)

    def pbuf(self, name=None):
        return self.buf(name, True)

    def op(self, eng, emit, reads=(), writes=(), dma=False):
        waits = {}
        ex = [b for b in reads if b.excl]
        if ex:
            writes = list(writes) + ex
            reads = [b for b in reads if not b.excl]

        def need(k, v):
            if v <= 0:
                return
            if k == self.semkey.get(eng) and (eng == "pe" or not self.same):
                return
            if self.waited[eng].get(k, 0) >= v:
                return
            if waits.get(k, 0) < v:
                waits[k] = v

        for b in reads:
            if b.lw is not None:
                need(*b.lw)
        for b in writes:
            if b.lw is not None:
                need(*b.lw)
            for k, v in b.rd.items():
                need(k, v)
        if dma:
            j = self.dma_rr
            self.dma_rr = (self.dma_rr + 1) % self.NDMA
            k = "d%d" % j
            need(k, self.dma_cnt[j])
            self.dma_cnt[j] += 16
            tok = (k, self.dma_cnt[j])
        else:
            self.cnt[eng] += 1
            tok = (self.semkey[eng], self.cnt[eng])
        for k, v in waits.items():
            self.waited[eng][k] = v
        self.ops[eng].append((list(waits.items()), emit, tok, dma))
        for b in reads:
            if b.rd.get(tok[0], 0) < tok[1]:
                b.rd[tok[0]] = tok[1]
        for b in writes:
            b.lw = tok
            b.rd = {}
        return tok

    def pe(self, emit, reads=(), writes=()):
        return self.op("pe", emit, reads, writes)

    def dve(self, emit, reads=(), writes=()):
        return self.op("dve", emit, reads, writes)

    def act(self, emit, reads=(), writes=()):
        return self.op("act", emit, reads, writes)

    def pool(self, emit, reads=(), writes=()):
        return self.op("pool", emit, reads, writes)

    def dma(self, out, in_, reads=(), writes=(), eng="sp", **kw):
        return self.op(eng, lambda e: e.dma_start(out=out, in_=in_, **kw), reads, writes, dma=True)

    def finish(self):
        waits = []
        for j in range(self.NDMA):
            if self.dma_cnt[j] > 0:
                waits.append(("d%d" % j, self.dma_cnt[j]))
        for e in ["pe", "dve", "act", "pool"]:
            if self.cnt[e] > 0:
                waits.append((self.semkey[e], self.cnt[e]))
        for k in self.sems:
            if k.startswith("cc"):
                waits.append((k, 1))
        self.ops["sp"].append((waits, None, None, False))
        nc = self.nc
        sems = self.sems
        ops = self.ops

        def run(name, e):
            for waits, emit, tok, dma in ops[name]:
                for k, v in waits:
                    e.wait_ge(sems[k], v)
                if emit is None:
                    continue
                ins = emit(e)
                ins.then_inc(sems[tok[0]], 16 if dma is True else 1)

        with nc.Block() as block:
            @block.tensor
            def _(e):
                run("pe", e)

            @block.vector
            def _(e):
                run("dve", e)

            @block.scalar
            def _(e):
                run("act", e)

            @block.gpsimd
            def _(e):
                run("pool", e)

            @block.sync
            def _(e):
                run("sp", e)
        self.scope.close()
        self.es.close()


DN_ALPHA = 4.0 ** 0.25
LN_EPS = 1e-5
BIG = 30000.0


def make_ident(p, dtype, name):
    ident = p.sb(name, [128, 128], dtype)
    b = p.buf(name)
    p.pool(lambda e: e.memset(ident[:], 0.0), writes=[b])
    p.pool(lambda e: e.affine_select(out=ident[:], in_=ident[:], compare_op=ALU.not_equal, fill=1.0,
                                     base=0, pattern=[[-1, 128]], channel_multiplier=1), reads=[b], writes=[b])
    return ident, b


def layer_norm_tm(p, src, dst, gam, bet, bsrc, bdst, bconst, tmp, btmp, eps, key):
    stats, mv, rstd = tmp
    p.dve(lambda e: e.bn_stats(out=stats[:, 0, :], in_=src[:, 0:512]), reads=[bsrc], writes=[btmp])
    p.dve(lambda e: e.bn_stats(out=stats[:, 1, :], in_=src[:, 512:1024]), reads=[bsrc], writes=[btmp])
    p.dve(lambda e: e.bn_aggr(out=mv[:], in_=stats[:].rearrange("p a b -> p (a b)")), reads=[btmp], writes=[btmp])
    p.act(lambda e: e.activation(out=rstd[:], in_=mv[:, 1:2], func=AF.Sqrt, bias=eps, scale=1.0), reads=[btmp], writes=[btmp])
    p.dve(lambda e: e.reciprocal(out=rstd[:], in_=rstd[:]), reads=[btmp], writes=[btmp])
    p.dve(lambda e: e.tensor_scalar(out=dst, in0=src, scalar1=mv[:, 0:1], scalar2=rstd[:, 0:1],
                                    op0=ALU.subtract, op1=ALU.mult), reads=[bsrc, btmp], writes=[bdst])
    p.dve(lambda e: e.tensor_tensor(out=dst, in0=dst, in1=gam, op=ALU.mult), reads=[bdst, bconst], writes=[bdst])
    p.dve(lambda e: e.tensor_tensor(out=dst, in0=dst, in1=bet, op=ALU.add), reads=[bdst, bconst], writes=[bdst])


def build_tail(NT, glu, TG=1024):
    nc = bass.Bass("TRN2", target_bir_lowering=False)
    D = 1024
    xres = nc.dram_tensor("xres", [NT, D], F32, kind="ExternalInput").ap()
    yT = nc.dram_tensor("yT", [D, NT], BF16, kind="ExternalInput").ap()
    xo = nc.dram_tensor("xo", [NT, D], F32, kind="ExternalOutput").ap()
    p = Prog(nc)

    def load_yt(res, g, yt, byt):
        p.dma(yt[:], yT[:, g * TG:(g + 1) * TG].rearrange("(c p) t -> p c t", p=128), writes=[byt])

    def load_xres(res, g, st, dst, bdst):
        t0 = g * TG + st * 128
        p.dma(dst, xres[t0:t0 + 128, :], writes=[bdst], eng="act")

    def store_out(res, g, st, o, bo):
        t0 = g * TG + st * 128
        p.dma(xo[t0:t0 + 128, :], o[:], reads=[bo])

    phase_tail(p, nc, "", NT, glu, load_yt, load_xres, store_out, TG)
    p.finish()
    return nc


def phase_tail(p, nc, pre, NT, glu, load_yt, load_xres, store_out, TG=1024):
    D = 1024
    NG = NT // TG
    NST = TG // 128
    if glu:
        w_glu = nc.dram_tensor(pre + "w_glu", [512, 1024], F32, kind="ExternalInput").ap()
    w_out = nc.dram_tensor(pre + "w_out", [D, D], F32, kind="ExternalInput").ap()
    lnp = nc.dram_tensor(pre + "lnp", [4, 128, D], F32, kind="ExternalInput").ap()
    w_r = nc.dram_tensor(pre + "w_r", [D, 20], F32, kind="ExternalInput").ap()
    b_r = nc.dram_tensor(pre + "b_r", [128, 20], F32, kind="ExternalInput").ap()
    w_gu = nc.dram_tensor(pre + "w_gu", [16, D, 512], F32, kind="ExternalInput").ap()
    w_dn = nc.dram_tensor(pre + "w_dn", [16, 256, D], F32, kind="ExternalInput").ap()
    identf, bidf = make_ident(p, F32, "identf")
    identb = p.sb("identb", [128, 128], BF16)
    bidb = p.buf()
    p.dve(lambda e: e.tensor_copy(out=identb[:], in_=identf[:]), reads=[bidf], writes=[bidb])
    wout = p.sb("wout", [128, 8, D], BF16)
    bwout = p.buf()
    p.dma(wout[:], w_out.rearrange("(c p) f -> p c f", p=128), writes=[bwout], eng="pool")
    if glu:
        wglu = p.sb("wglu", [128, 4, 1024], BF16)
        bwglu = p.buf()
        p.dma(wglu[:], w_glu.rearrange("(c p) f -> p c f", p=128), writes=[bwglu], eng="pool")
    lns = p.sb("lns", [128, 4, D], F32)
    bln = p.buf()
    p.dma(lns[:], lnp.rearrange("a p d -> p a d"), writes=[bln])
    wr = p.sb("wr", [128, 8, 20], BF16)
    bwr = p.buf()
    p.dma(wr[:], w_r.rearrange("(c p) f -> p c f", p=128), writes=[bwr], eng="pool")
    br_ = p.sb("br", [128, 20], F32)
    bbr = p.buf()
    p.dma(br_[:], b_r, writes=[bbr])

    yt = p.sb("yt", [128, 8, TG], BF16)
    byt = p.buf()
    if glu:
        yglu = p.sb("yglu", [128, 4, TG], BF16)
        byglu = [p.buf() for _ in range(4)]
        sig = p.sb("sig", [128, 512], F32)
        bsig = p.buf()
    acc = p.sb("acc", [128, NST, D], F32)
    bacc = [p.buf() for _ in range(NST)]
    x1T = p.sb("x1T", [128, 8, TG], BF16)
    bx1T = [p.buf() for _ in range(NST)]
    gates = p.sb("gates", [128, NST, 16], F32)
    bgates = [p.buf() for _ in range(NST)]
    gub = [p.sb("gub%d" % i, [128, 8, 512], BF16) for i in range(2)]
    bgub = [p.buf() for _ in range(2)]
    dnb = [p.sb("dnb%d" % i, [128, 2, D], BF16) for i in range(2)]
    bdnb = [p.buf() for _ in range(2)]
    sg = [p.sb("sg%d" % i, [128, 256], F32) for i in range(2)]
    bsg = [p.buf() for _ in range(2)]
    hh = [p.sb("hh%d" % i, [128, 256], BF16) for i in range(2)]
    bhh = [p.buf() for _ in range(2)]
    hT = [p.sb("hT%d" % i, [128, 2, 128], BF16) for i in range(2)]
    bhT = [p.buf() for _ in range(2)]
    stats = p.sb("stats", [128, 2, 6], F32)
    mv = p.sb("mv", [128, 2], F32)
    rstd = p.sb("rstd", [128, 1], F32)
    btmp = p.buf()
    rt = p.sb("rt", [128, 80], F32)
    brt = p.buf()
    top8 = p.sb("top8", [128, 8], F32)
    xout = [p.sb("xout%d" % i, [128, D], F32) for i in range(2)]
    bxout = [p.buf() for _ in range(2)]

    psA = [p.ps("psA%d" % i, [128, 512], F32) for i in range(2)]
    bpsA = [p.pbuf() for _ in range(2)]
    psT = [p.ps("psT%d" % i, [128, 8, 128], BF16) for i in range(2)]
    bpsT = [p.pbuf() for _ in range(2)]
    psY = [p.ps("psY%d" % i, [128, 1024], F32) for i in range(2)]
    bpsY = [p.pbuf() for _ in range(2)]

    res = dict(psA=psA, bpsA=bpsA, identf=identf, bidf=bidf, TG=TG, NST=NST)
    expert_steps = [(g, e) for g in range(NG) for e in range(16)]

    def load_expert(idx):
        g, e = expert_steps[idx]
        s = idx % 2
        p.dma(gub[s][:], w_gu[e].rearrange("(c p) f -> p c f", p=128), writes=[bgub[s]], eng="pool")
        p.dma(dnb[s][:], w_dn[e].rearrange("(c p) f -> p c f", p=128), writes=[bdnb[s]], eng="pool")

    load_expert(0)
    rot = 0
    for g in range(NG):
        t0 = g * TG
        load_yt(res, g, yt, byt)
        for st in range(NST):
            load_xres(res, g, st, acc[:, st, :], bacc[st])
        if glu:
            for half in range(TG // 512):
                ts = slice(half * 512, (half + 1) * 512)
                for f in range(4):
                    for which in range(2):
                        col0 = which * 512 + f * 128
                        for k in range(4):
                            p.pe(lambda e, which=which, col0=col0, k=k, ts=ts: e.matmul(
                                psA[which][:], lhsT=wglu[:, k, col0:col0 + 128], rhs=yt[:, 4 + k, ts],
                                start=(k == 0), stop=(k == 3)), reads=[bwglu, byt], writes=[bpsA[which]])
                    p.act(lambda e: e.activation(out=sig[:], in_=psA[1][:], func=AF.Sigmoid), reads=[bpsA[1]], writes=[bsig])
                    p.dve(lambda e, f=f, ts=ts: e.tensor_tensor(out=yglu[:, f, ts], in0=psA[0][:], in1=sig[:], op=ALU.mult),
                          reads=[bpsA[0], bsig], writes=[byglu[f]])
        for st in range(NST):
            tsl = slice(st * 128, (st + 1) * 128)
            py = psY[st % 2]
            bpy = bpsY[st % 2]
            for half in range(2):
                for c in range(8):
                    if glu and c >= 4:
                        lhs = yglu[:, c - 4, tsl]
                        rb = byglu[c - 4]
                    else:
                        lhs = yt[:, c, tsl]
                        rb = byt
                    p.pe(lambda e, lhs=lhs, c=c, half=half, py=py: e.matmul(
                        py[:, half * 512:(half + 1) * 512], lhsT=lhs, rhs=wout[:, c, half * 512:(half + 1) * 512],
                        start=(c == 0), stop=(c == 7)), reads=[rb, bwout], writes=[bpy])
            a = acc[:, st, :]
            p.dve(lambda e, a=a, py=py: e.scalar_tensor_tensor(out=a, in0=a, scalar=DN_ALPHA, in1=py[:],
                                                                 op0=ALU.mult, op1=ALU.add), reads=[bacc[st], bpy], writes=[bacc[st]])
            layer_norm_tm(p, a, a, lns[:, 0, :], lns[:, 1, :], bacc[st], bacc[st], bln, (stats, mv, rstd), btmp, LN_EPS, "ln1")
            for hf in range(2):
                pa = psA[hf]
                for c4 in range(4):
                    c = hf * 4 + c4
                    p.pe(lambda e, pa=pa, c4=c4, c=c, a=a: e.transpose(out=pa[:, c4 * 128:(c4 + 1) * 128], in_=a[:, c * 128:(c + 1) * 128],
                                                                       identity=identf[:]), reads=[bacc[st], bidf], writes=[bpsA[hf]])
                p.act(lambda e, pa=pa, hf=hf, tsl=tsl: e.copy(out=x1T[:, hf * 4:(hf + 1) * 4, tsl], in_=pa[:].rearrange("p (c t) -> p c t", c=4)),
                      reads=[bpsA[hf]], writes=[bx1T[st]])
            p.act(lambda e, a=a: e.activation(out=a, in_=a, func=AF.Copy, scale=DN_ALPHA), reads=[bacc[st]], writes=[bacc[st]])
            pr = psT[st % 2]
            pl = psY[(st + 1) % 2]
            bpl = bpsY[(st + 1) % 2]
            for c in range(8):
                p.pe(lambda e, c=c, pl=pl, tsl=tsl: e.matmul(pl[:, 0:20], lhsT=x1T[:, c, tsl], rhs=wr[:, c, :],
                                                             start=(c == 0), stop=(c == 7)), reads=[bx1T[st], bwr], writes=[bpl])
            lg = rt[:, 0:20]
            p.dve(lambda e, pl=pl: e.tensor_tensor(out=lg, in0=pl[:, 0:20], in1=br_[:], op=ALU.add), reads=[bpl, bbr], writes=[brt])
            gmax = rt[:, 20:21]
            ngmax = rt[:, 21:22]
            sume = rt[:, 22:23]
            gtop = rt[:, 23:24]
            eg = rt[:, 24:28]
            oh = rt[:, 28:32]
            em = rt[:, 32:48]
            dd = rt[:, 48:49]
            ex = rt[:, 49:50]
            w1 = rt[:, 50:51]
            w2 = rt[:, 51:52]
            t2 = rt[:, 56:72]
            R = [brt]
            p.dve(lambda e: e.tensor_reduce(out=gmax, in_=lg[:, 0:4], axis=AX.X, op=ALU.max), reads=R, writes=R)
            p.dve(lambda e: e.tensor_scalar(out=ngmax, in0=gmax, scalar1=-1.0, scalar2=None, op0=ALU.mult), reads=R, writes=R)
            p.act(lambda e: e.activation(out=eg, in_=lg[:, 0:4], func=AF.Exp, bias=ngmax, scale=1.0, accum_out=sume), reads=R, writes=R)
            p.dve(lambda e: e.reciprocal(out=gtop, in_=sume), reads=R, writes=R)
            p.dve(lambda e: e.tensor_scalar(out=oh, in0=lg[:, 0:4], scalar1=gmax, scalar2=BIG, op0=ALU.is_equal, op1=ALU.mult), reads=R, writes=R)
            p.dve(lambda e: e.tensor_scalar(out=oh, in0=oh, scalar1=-BIG, scalar2=None, op0=ALU.add), reads=R, writes=R)
            for gi in range(4):
                p.dve(lambda e, gi=gi: e.tensor_scalar(out=em[:, gi * 4:(gi + 1) * 4], in0=lg[:, 4 + gi * 4:8 + gi * 4],
                                                       scalar1=oh[:, gi:gi + 1], scalar2=None, op0=ALU.add), reads=R, writes=R)
            p.dve(lambda e: e.max(out=top8[:], in_=em), reads=R, writes=R)
            p.dve(lambda e: e.tensor_tensor(out=dd, in0=top8[:, 1:2], in1=top8[:, 0:1], op=ALU.subtract), reads=R, writes=R)
            p.act(lambda e: e.activation(out=ex, in_=dd, func=AF.Exp), reads=R, writes=R)
            p.dve(lambda e: e.tensor_scalar(out=w1, in0=ex, scalar1=1.0, scalar2=None, op0=ALU.add), reads=R, writes=R)
            p.dve(lambda e: e.reciprocal(out=w1, in_=w1), reads=R, writes=R)
            p.dve(lambda e: e.tensor_tensor(out=w2, in0=ex, in1=w1, op=ALU.mult), reads=R, writes=R)
            p.dve(lambda e: e.tensor_tensor(out=w1, in0=w1, in1=gtop, op=ALU.mult), reads=R, writes=R)
            p.dve(lambda e: e.tensor_tensor(out=w2, in0=w2, in1=gtop, op=ALU.mult), reads=R, writes=R)
            gt = gates[:, st, :]
            p.dve(lambda e, gt=gt: e.tensor_scalar(out=gt, in0=em, scalar1=top8[:, 0:1], scalar2=w1, op0=ALU.is_equal, op1=ALU.mult),
                  reads=R, writes=[bgates[st]])
            p.dve(lambda e: e.tensor_scalar(out=t2, in0=em, scalar1=top8[:, 1:2], scalar2=w2,
                                            op0=ALU.is_equal, op1=ALU.mult), reads=R, writes=R)
            p.dve(lambda e, gt=gt: e.tensor_tensor(out=gt, in0=gt, in1=t2, op=ALU.add), reads=R + [bgates[st]], writes=[bgates[st]])
        for e_i in range(16):
            idx = g * 16 + e_i
            s = idx % 2
            if idx + 1 < len(expert_steps):
                load_expert(idx + 1)
            for st in range(NST):
                tsl = slice(st * 128, (st + 1) * 128)
                r = rot % 2
                rot += 1
                pa = psA[r]
                for c in range(8):
                    p.pe(lambda e, pa=pa, c=c, tsl=tsl, s=s: e.matmul(pa[:], lhsT=x1T[:, c, tsl], rhs=gub[s][:, c, :],
                                                                      start=(c == 0), stop=(c == 7)), reads=[bx1T[st], bgub[s]], writes=[bpsA[r]])
                p.act(lambda e, pa=pa, r=r: e.activation(out=sg[r][:], in_=pa[:, 0:256], func=AF.Silu), reads=[bpsA[r]], writes=[bsg[r]])
                p.dve(lambda e, pa=pa, r=r, st=st, e_i=e_i: e.scalar_tensor_tensor(
                    out=hh[r][:], in0=pa[:, 256:512], scalar=gates[:, st, e_i:e_i + 1], in1=sg[r][:], op0=ALU.mult, op1=ALU.mult),
                    reads=[bpsA[r], bsg[r], bgates[st]], writes=[bhh[r]])
                for k in range(2):
                    p.pe(lambda e, r=r, k=k: e.transpose(out=psT[r][:, k, :], in_=hh[r][:, k * 128:(k + 1) * 128], identity=identb[:]),
                         reads=[bhh[r], bidb], writes=[bpsT[r]])
                p.act(lambda e, r=r: e.copy(out=hT[r][:], in_=psT[r][:, 0:2, :]), reads=[bpsT[r]], writes=[bhT[r]])
                py = psY[r]
                for half in range(2):
                    for k in range(2):
                        p.pe(lambda e, py=py, half=half, k=k, r=r, s=s: e.matmul(
                            py[:, half * 512:(half + 1) * 512], lhsT=hT[r][:, k, :], rhs=dnb[s][:, k, half * 512:(half + 1) * 512],
                            start=(k == 0), stop=(k == 1)), reads=[bhT[r], bdnb[s]], writes=[bpsY[r]])
                a = acc[:, st, :]
                p.dve(lambda e, a=a, py=py: e.tensor_tensor(out=a, in0=a, in1=py[:], op=ALU.add), reads=[bacc[st], bpsY[r]], writes=[bacc[st]])
        for st in range(NST):
            a = acc[:, st, :]
            o = xout[st % 2]
            layer_norm_tm(p, a, o[:], lns[:, 2, :], lns[:, 3, :], bacc[st], bxout[st % 2], bln, (stats, mv, rstd), btmp, LN_EPS, "ln2")
            store_out(res, g, st, o, bxout[st % 2])

import math

RMS_EPS = 1e-6
TWO_PI = 2.0 * math.pi


def s5_lambda(p, pre, shape, ar, ai, ldt, breads, T=None, extra=()):
    F = shape[1]
    if T is None:
        T = p.sb(pre + "_t", [128, 8, F], F32)
    Ti = p.sb(pre + "_ti", [128, F], I32)
    b = p.buf(pre)
    dtt, mag, th, t, kf, r, c1, s = [T[:, i, :] for i in range(8)]
    lre = p.sb(pre + "_lre", [128, F], F32)
    lim = p.sb(pre + "_lim", [128, F], F32)
    R = [b] + list(extra)
    p.act(lambda e: e.activation(out=dtt, in_=ldt, func=AF.Exp), reads=breads, writes=R)
    p.dve(lambda e: e.tensor_tensor(out=mag, in0=dtt, in1=ar, op=ALU.mult), reads=R + breads, writes=R)
    p.act(lambda e: e.activation(out=mag, in_=mag, func=AF.Exp), reads=R, writes=R)
    p.dve(lambda e: e.tensor_tensor(out=th, in0=dtt, in1=ai, op=ALU.mult), reads=R + breads, writes=R)
    for which, dst in ((0, lim), (1, lre)):
        p.dve(lambda e, which=which: e.tensor_scalar(out=t, in0=th, scalar1=1.0 / TWO_PI, scalar2=0.25 * which,
                                                     op0=ALU.mult, op1=ALU.add), reads=R, writes=R)
        p.dve(lambda e: e.tensor_copy(out=Ti[:], in_=t), reads=R, writes=R)
        p.dve(lambda e: e.tensor_copy(out=kf, in_=Ti[:]), reads=R, writes=R)
        p.dve(lambda e: e.tensor_tensor(out=r, in0=t, in1=kf, op=ALU.subtract), reads=R, writes=R)
        p.dve(lambda e: e.tensor_scalar(out=c1, in0=r, scalar1=0.5, scalar2=None, op0=ALU.is_gt), reads=R, writes=R)
        p.dve(lambda e: e.tensor_tensor(out=r, in0=r, in1=c1, op=ALU.subtract), reads=R, writes=R)
        p.dve(lambda e: e.tensor_scalar(out=c1, in0=r, scalar1=-0.5, scalar2=None, op0=ALU.is_lt), reads=R, writes=R)
        p.dve(lambda e: e.tensor_tensor(out=r, in0=r, in1=c1, op=ALU.add), reads=R, writes=R)
        p.act(lambda e: e.activation(out=s, in_=r, func=AF.Sin, scale=TWO_PI), reads=R, writes=R)
        p.dve(lambda e, dst=dst: e.tensor_tensor(out=dst[:], in0=s, in1=mag, op=ALU.mult), reads=R, writes=R)
    return lre, lim, b


def build_mixa(L, TS5=2048):
    nc = bass.Bass("TRN2", target_bir_lowering=False)
    yaT_d = nc.dram_tensor("yaT", [128, L], BF16, kind="ExternalOutput").ap()
    ysT_d = nc.dram_tensor("ysT", [128, L], BF16, kind="ExternalOutput").ap()
    p = Prog(nc)

    def emit_ya(res, ti, t, b):
        p.dma(yaT_d[:, ti * 512:(ti + 1) * 512], t[:], reads=[b])

    def emit_ys(res, ti, t, b):
        p.dma(ysT_d[:, ti * 512:(ti + 1) * 512], t[:], reads=[b])

    phase_mixa(p, nc, "", L, emit_ya, emit_ys, TS5)
    p.finish()
    return nc


def phase_mixa(p, nc, pre, L, emit_ya, emit_ys, TS5=2048):
    xT = nc.dram_tensor(pre + "xT", [1024, L], F32, kind="ExternalInput").ap()
    wA_d = nc.dram_tensor(pre + "wA", [1024, 640], F32, kind="ExternalInput").ap()
    lbl_d = nc.dram_tensor(pre + "lbl", [128, 3], F32, kind="ExternalInput").ap()
    ng_d = nc.dram_tensor(pre + "ng", [128, 128], F32, kind="ExternalInput").ap()
    sps_d = nc.dram_tensor(pre + "sps", [128, 3, 4], F32, kind="ExternalInput").ap()
    spw_d = nc.dram_tensor(pre + "spw", [128, 3, 512], F32, kind="ExternalInput").ap()
    bpad_d = nc.dram_tensor(pre + "bpad", [128, 2, 512], F32, kind="ExternalInput").ap()
    cpad_d = nc.dram_tensor(pre + "cpad", [128, 2, 512], F32, kind="ExternalInput").ap()
    dsk_d = nc.dram_tensor(pre + "dsk", [128, 1], F32, kind="ExternalInput").ap()
    tri_d = nc.dram_tensor(pre + "tri", [128, 128], F32, kind="ExternalInput").ap()
    m01_d = nc.dram_tensor(pre + "m01", [128, 512], F32, kind="ExternalInput").ap()
    res = {}
    identf, bidf = make_ident(p, F32, "identf")
    wA = p.sb("wA", [128, 8, 640], BF16)
    bwA = p.buf()
    p.dma(wA[:], wA_d.rearrange("(c p) f -> p c f", p=128), writes=[bwA], eng="pool")
    cst = p.sb("cst", [128, 3 + 128 + 12 + 1 + 128 + 512 + 8], F32)
    bc = p.buf("cst")
    lbl = cst[:, 0:3]
    ng = cst[:, 3:131]
    sps = cst[:, 131:143].rearrange("p (a b) -> p a b", a=3)
    dsk = cst[:, 143:144]
    tri = cst[:, 144:272]
    m01 = cst[:, 272:784]
    misc = cst[:, 784:792]
    p.dma(lbl, lbl_d, writes=[bc])
    p.dma(ng, ng_d, writes=[bc])
    p.dma(sps, sps_d, writes=[bc])
    p.dma(dsk, dsk_d, writes=[bc])
    p.dma(tri, tri_d, writes=[bc])
    p.dma(m01, m01_d, writes=[bc])
    spw = p.sb("spw", [128, 3, 512], F32)
    p.dma(spw[:], spw_d, writes=[bc])
    bpad = p.sb("bpad", [128, 2, 512], F32)
    p.dma(bpad[:], bpad_d, writes=[bc])
    cpad = p.sb("cpad", [128, 2, 512], F32)
    p.dma(cpad[:], cpad_d, writes=[bc])
    lbe = misc[:, 0:3]
    lbs = misc[:, 3:4]
    lb = misc[:, 4:5]
    oml = misc[:, 5:6]
    bm = p.buf("misc")
    p.act(lambda e: e.activation(out=lbe, in_=lbl, func=AF.Exp, accum_out=lbs), reads=[bc], writes=[bm])
    p.dve(lambda e: e.reciprocal(out=lbs, in_=lbs), reads=[bm], writes=[bm])
    p.dve(lambda e: e.tensor_tensor(out=lb, in0=lbe[:, 0:1], in1=lbs, op=ALU.mult), reads=[bm], writes=[bm])
    p.dve(lambda e: e.tensor_scalar(out=oml, in0=lb, scalar1=-1.0, scalar2=1.0, op0=ALU.mult, op1=ALU.add), reads=[bm], writes=[bm])

    dre = p.sb("dre", [128, 4, TS5], F32)
    dim_ = p.sb("dim", [128, 4, TS5], F32)
    bd = [p.buf("d%d" % i) for i in range(4)]
    assert TS5 >= 1024
    ls_re, ls_im, bls = s5_lambda(p, "ls", [128, 4], sps[:, 0, :], sps[:, 1, :], sps[:, 2, :], [bc])
    lw_re, lw_im, blw = s5_lambda(p, "lw", [128, 512], spw[:, 0, :], spw[:, 1, :], spw[:, 2, :], [bc],
                                  T=dre[:].rearrange("p a t -> p (a t)")[:, 0:4096].rearrange("p (a f) -> p a f", a=8), extra=bd)
    W = dim_[:].rearrange("p a t -> p (a t)")[:, 0:4096].rearrange("p (a f) -> p a f", a=8)
    bW = p.buf("wtmp")
    xr, den, t1, t2, fr, fi, o1, o2 = [W[:, i, :] for i in range(8)]
    arw, aiw = spw[:, 0, :], spw[:, 1, :]
    RW = [bW, blw, bc]
    p.dve(lambda e: e.tensor_scalar(out=xr, in0=lw_re[:], scalar1=-1.0, scalar2=None, op0=ALU.add), reads=RW, writes=[bW] + bd)
    p.dve(lambda e: e.tensor_tensor(out=den, in0=arw, in1=arw, op=ALU.mult), reads=RW, writes=[bW])
    p.dve(lambda e: e.tensor_tensor(out=t1, in0=aiw, in1=aiw, op=ALU.mult), reads=RW, writes=[bW])
    p.dve(lambda e: e.tensor_tensor(out=den, in0=den, in1=t1, op=ALU.add), reads=RW, writes=[bW])
    p.dve(lambda e: e.reciprocal(out=den, in_=den), reads=RW, writes=[bW])
    p.dve(lambda e: e.tensor_tensor(out=t1, in0=xr, in1=arw, op=ALU.mult), reads=RW, writes=[bW])
    p.dve(lambda e: e.tensor_tensor(out=t2, in0=lw_im[:], in1=aiw, op=ALU.mult), reads=RW, writes=[bW])
    p.dve(lambda e: e.tensor_tensor(out=fr, in0=t1, in1=t2, op=ALU.add), reads=RW, writes=[bW])
    p.dve(lambda e: e.tensor_tensor(out=fr, in0=fr, in1=den, op=ALU.mult), reads=RW, writes=[bW])
    p.dve(lambda e: e.tensor_tensor(out=t1, in0=lw_im[:], in1=arw, op=ALU.mult), reads=RW, writes=[bW])
    p.dve(lambda e: e.tensor_tensor(out=t2, in0=xr, in1=aiw, op=ALU.mult), reads=RW, writes=[bW])
    p.dve(lambda e: e.tensor_tensor(out=fi, in0=t1, in1=t2, op=ALU.subtract), reads=RW, writes=[bW])
    p.dve(lambda e: e.tensor_tensor(out=fi, in0=fi, in1=den, op=ALU.mult), reads=RW, writes=[bW])
    wB = p.sb("wB", [128, 2, 512], BF16)
    wC = p.sb("wC", [128, 2, 512], BF16)
    bwB = p.buf("wB")
    bre, bim = bpad[:, 0, :], bpad[:, 1, :]
    p.dve(lambda e: e.tensor_tensor(out=o1, in0=fr, in1=bre, op=ALU.mult), reads=RW, writes=[bW])
    p.dve(lambda e: e.tensor_tensor(out=o2, in0=fi, in1=bim, op=ALU.mult), reads=RW, writes=[bW])
    p.dve(lambda e: e.tensor_tensor(out=wB[:, 0, :], in0=o1, in1=o2, op=ALU.subtract), reads=RW, writes=[bwB])
    p.dve(lambda e: e.tensor_tensor(out=o1, in0=fr, in1=bim, op=ALU.mult), reads=RW, writes=[bW])
    p.dve(lambda e: e.tensor_tensor(out=o2, in0=fi, in1=bre, op=ALU.mult), reads=RW, writes=[bW])
    p.dve(lambda e: e.tensor_tensor(out=wB[:, 1, :], in0=o1, in1=o2, op=ALU.add), reads=RW, writes=[bwB])
    p.dve(lambda e: e.tensor_copy(out=wC[:, 0, :], in_=cpad[:, 0, :]), reads=[bc], writes=[bwB])
    p.dve(lambda e: e.tensor_scalar(out=wC[:, 1, :], in0=cpad[:, 1, :], scalar1=-1.0, scalar2=None, op0=ALU.mult), reads=[bc, bW], writes=[bwB] + bd)

    NLEV = 3
    lamp = p.sb("lamp", [128, NLEV, 3, 4], F32)
    ltmp = p.sb("ltmp", [128, 4, 4], F32)
    blam = p.buf("lam")
    RL = [blam, bls]
    p.dve(lambda e: e.tensor_copy(out=lamp[:, 0, 0, :], in_=ls_re[:]), reads=RL, writes=[blam])
    p.dve(lambda e: e.tensor_copy(out=lamp[:, 0, 1, :], in_=ls_im[:]), reads=RL, writes=[blam])
    for lev in range(1, NLEV):
        p.dve(lambda e, lev=lev: e.tensor_copy(out=lamp[:, lev, 0:2, :], in_=lamp[:, lev - 1, 0:2, :]), reads=RL, writes=[blam])
        for _ in range(4):
            a = lamp[:, lev, 0, :]
            b_ = lamp[:, lev, 1, :]
            p.dve(lambda e, a=a: e.tensor_tensor(out=ltmp[:, 0, :], in0=a, in1=a, op=ALU.mult), reads=RL, writes=[blam])
            p.dve(lambda e, b_=b_: e.tensor_tensor(out=ltmp[:, 1, :], in0=b_, in1=b_, op=ALU.mult), reads=RL, writes=[blam])
            p.dve(lambda e, a=a, b_=b_: e.tensor_tensor(out=ltmp[:, 2, :], in0=a, in1=b_, op=ALU.mult), reads=RL, writes=[blam])
            p.dve(lambda e, a=a: e.tensor_tensor(out=a, in0=ltmp[:, 0, :], in1=ltmp[:, 1, :], op=ALU.subtract), reads=RL, writes=[blam])
            p.dve(lambda e, b_=b_: e.tensor_scalar(out=b_, in0=ltmp[:, 2, :], scalar1=2.0, scalar2=None, op0=ALU.mult), reads=RL, writes=[blam])
    for lev in range(NLEV):
        p.dve(lambda e, lev=lev: e.tensor_scalar(out=lamp[:, lev, 2, :], in0=lamp[:, lev, 1, :], scalar1=-1.0, scalar2=None, op0=ALU.mult),
              reads=RL, writes=[blam])

    xt = [p.sb("xt%d" % i, [128, 8, 512], BF16) for i in range(2)]
    bxt = [p.buf() for _ in range(2)]
    H = p.sb("hg", [128, 9, 512], F32)
    bH = p.buf("hg")
    f_, lf, kk, bb, eb, enb, qq, kinv, ktT = [H[:, i, :] for i in range(9)]
    qdec = p.sb("qdec", [128, 512], BF16)
    kinvb = p.sb("kinvb", [128, 512], BF16)
    bqk = p.buf("qk")
    vb = p.sb("vb", [128, 4, 128], BF16)
    gn = p.sb("gn", [128, 4, 128], F32)
    bvg = [p.buf() for _ in range(4)]
    kt = p.sb("kt", [128, 4, 128], BF16)
    bkt = [p.buf() for _ in range(4)]
    attT = [p.sb("attT%d" % i, [128, 128], BF16) for i in range(2)]
    battT = [p.buf() for _ in range(2)]
    yasb = [p.sb("yasb%d" % i, [128, 4, 128], F32) for i in range(2)]
    yaTs = [p.sb("yaTs%d" % i, [128, 512], BF16) for i in range(2)]
    byaTs = [p.buf() for _ in range(2)]
    byasb = [p.buf() for _ in range(2)]
    S = p.sb("S", [128, 128], F32)
    bS = p.buf("S")
    Sb = [p.sb("Sb%d" % i, [128, 128], BF16) for i in range(2)]
    bSb = [p.buf() for _ in range(2)]
    osc = p.sb("osc", [128, 128], F32)
    om = p.sb("om", [128, 2], F32)
    bo = p.buf("o")
    p.dve(lambda e: e.memset(S[:], 0.0), writes=[bS])
    p.dve(lambda e: e.memset(Sb[1][:], 0.0), writes=[bSb[1]])
    NB1 = TS5 // 16
    assert NB1 % 16 == 0 or NB1 <= 16
    uT = p.sb("uT", [128, TS5], F32)
    buT = p.buf("uT")
    uTb = [p.sb("uTb%d" % i, [128, 512], BF16) for i in range(2)]
    buTb = [p.buf() for _ in range(2)]
    nb_levels = []
    n = TS5
    while n > 16:
        n //= 16
        nb_levels.append(n)
    Ebufs = []
    for li, nbl in enumerate(nb_levels):
        Ebufs.append((p.sb("Ere%d" % li, [128, 4, nbl + 1], F32), p.sb("Eim%d" % li, [128, 4, nbl + 1], F32)))
    carry = p.sb("carry", [128, 2, 4], F32)
    p.dve(lambda e: e.memset(carry[:], 0.0), writes=bd)
    stmp = p.sb("stmp", [128, 4, 2, max(NB1, 16)], F32)
    hb = [p.sb("hb%d" % i, [128, 2, 4, 512], BF16) for i in range(2)]
    bhb = [p.buf() for _ in range(2)]
    zs = p.sb("zs", [128, 512], F32)
    bzs = p.buf()
    ysb = [p.sb("ysb%d" % i, [128, 512], BF16) for i in range(2)]
    bysb = [p.buf() for _ in range(2)]

    psQ = p.ps("psQ", [128, 512], F32); bpsQ = p.pbuf()
    psF = p.ps("psF", [128, 512], F32); bpsF = p.pbuf()
    psU = p.ps("psU", [128, 512], F32); bpsU = p.pbuf()
    psVG = p.ps("psVG", [128, 2, 256], F32); _b = p.pbuf(); bpsVG = [_b, _b]
    psD1 = p.ps("psD", [128, 512], F32); psD = [psD1, psD1]; _b = p.pbuf(); bpsD = [_b, _b]
    psS = p.ps("psS", [128, 4, 128], F32); _b = p.pbuf(); bpsS = [_b, _b]
    psO = p.ps("psO", [128, 4, 128], F32); _b = p.pbuf(); bpsO = [_b, _b]
    psM = p.ps("psM", [128, 4, 128], F32); _b = p.pbuf(); bpsM = [_b] * 4

    def cstep(i, lev, dst_re, dst_im, prev_re, prev_im, n, add_re=None, add_im=None):
        ar = lamp[:, lev, 0, i:i + 1]
        ai = lamp[:, lev, 1, i:i + 1]
        nai = lamp[:, lev, 2, i:i + 1]
        if add_re is None:
            add_re, add_im = dst_re, dst_im
        ta = stmp[:, i, 0, 0:n]
        tb = stmp[:, i, 1, 0:n]
        R = [bd[i], blam]
        p.dve(lambda e: e.scalar_tensor_tensor(out=ta, in0=prev_im, scalar=nai, in1=add_re, op0=ALU.mult, op1=ALU.add), reads=R, writes=[bd[i]])
        p.dve(lambda e: e.scalar_tensor_tensor(out=tb, in0=prev_re, scalar=ai, in1=add_im, op0=ALU.mult, op1=ALU.add), reads=R, writes=[bd[i]])
        p.dve(lambda e: e.scalar_tensor_tensor(out=dst_re, in0=prev_re, scalar=ar, in1=ta, op0=ALU.mult, op1=ALU.add), reads=R, writes=[bd[i]])
        p.dve(lambda e: e.scalar_tensor_tensor(out=dst_im, in0=prev_im, scalar=ar, in1=tb, op0=ALU.mult, op1=ALU.add), reads=R, writes=[bd[i]])

    def cscan(lev, Xre, Xim, n, hin_re, hin_im):
        if n <= 16:
            for t in range(n):
                for i in range(4):
                    pr = hin_re(i) if t == 0 else Xre(i)[:, t - 1:t]
                    pi_ = hin_im(i) if t == 0 else Xim(i)[:, t - 1:t]
                    cstep(i, lev, Xre(i)[:, t:t + 1], Xim(i)[:, t:t + 1], pr, pi_, 1)
            return
        nb = n // 16
        Ere, Eim = Ebufs[lev]
        for i in range(4):
            p.dve(lambda e, i=i: e.tensor_copy(out=Ere[:, i, 0:1], in_=hin_re(i)), reads=[bd[i]], writes=[bd[i]])
            p.dve(lambda e, i=i: e.tensor_copy(out=Eim[:, i, 0:1], in_=hin_im(i)), reads=[bd[i]], writes=[bd[i]])
            p.dve(lambda e, i=i: e.tensor_copy(out=Ere[:, i, 1:nb + 1], in_=Xre(i)[:, 0:n:16]), reads=[bd[i]], writes=[bd[i]])
            p.dve(lambda e, i=i: e.tensor_copy(out=Eim[:, i, 1:nb + 1], in_=Xim(i)[:, 0:n:16]), reads=[bd[i]], writes=[bd[i]])
        for r in range(1, 16):
            for i in range(4):
                cstep(i, lev, Ere[:, i, 1:nb + 1], Eim[:, i, 1:nb + 1], Ere[:, i, 1:nb + 1], Eim[:, i, 1:nb + 1], nb,
                      add_re=Xre(i)[:, r:n:16], add_im=Xim(i)[:, r:n:16])
        cscan(lev + 1, lambda i: Ere[:, i, 1:nb + 1], lambda i: Eim[:, i, 1:nb + 1], nb,
              lambda i: Ere[:, i, 0:1], lambda i: Eim[:, i, 0:1])
        for r in range(16):
            for i in range(4):
                if r == 0:
                    pr, pi_ = Ere[:, i, 0:nb], Eim[:, i, 0:nb]
                else:
                    pr, pi_ = Xre(i)[:, r - 1:n:16], Xim(i)[:, r - 1:n:16]
                cstep(i, lev, Xre(i)[:, r:n:16], Xim(i)[:, r:n:16], pr, pi_, nb)

    NTILE = L // 512
    TPS = TS5 // 512

    def load_x(ti):
        s = ti % 2
        p.dma(xt[s][:], xT[:, ti * 512:(ti + 1) * 512].rearrange("(c p) t -> p c t", p=128), writes=[bxt[s]], eng="pool")

    load_x(0)
    chunk_idx = 0
    for ti in range(NTILE):
        s = ti % 2
        x_ = xt[s]
        if ti + 1 < NTILE:
            load_x(ti + 1)
        t0 = ti * 512
        tl = (ti % TPS) * 512
        for (ps_, bps_, c0) in ((psQ, bpsQ, 0), (psF, bpsF, 128), (psU, bpsU, 512)):
            for c in range(8):
                p.pe(lambda e, ps_=ps_, c=c, c0=c0, x_=x_: e.matmul(ps_[:], lhsT=wA[:, c, c0:c0 + 128], rhs=x_[:, c, :],
                                                                    start=(c == 0), stop=(c == 7)), reads=[bwA, bxt[s]], writes=[bps_])
        RH = [bH]
        p.act(lambda e: e.activation(out=f_, in_=psF[:], func=AF.Sigmoid), reads=[bpsF], writes=RH)
        p.dve(lambda e: e.tensor_scalar(out=f_, in0=f_, scalar1=oml, scalar2=lb, op0=ALU.mult, op1=ALU.add), reads=RH + [bm], writes=RH)
        p.act(lambda e: e.activation(out=lf, in_=f_, func=AF.Ln), reads=RH, writes=RH)
        p.dve(lambda e: e.tensor_scalar(out=kk, in0=f_, scalar1=-1.0, scalar2=1.0, op0=ALU.mult, op1=ALU.add), reads=RH, writes=RH)
        p.dve(lambda e: e.tensor_tensor_scan(out=bb, data0=m01, data1=lf, initial=0.0, op0=ALU.mult, op1=ALU.add), reads=RH + [bc], writes=RH)
        p.act(lambda e: e.activation(out=eb, in_=bb, func=AF.Exp), reads=RH, writes=RH)
        p.act(lambda e: e.activation(out=enb, in_=bb, func=AF.Exp, scale=-1.0), reads=RH, writes=RH)
        p.act(lambda e: e.activation(out=qq, in_=psQ[:], func=AF.Silu), reads=[bpsQ], writes=RH)
        p.dve(lambda e: e.tensor_tensor(out=qdec[:], in0=qq, in1=eb, op=ALU.mult), reads=RH, writes=[bqk])
        p.dve(lambda e: e.tensor_tensor(out=kinv, in0=kk, in1=enb, op=ALU.mult), reads=RH, writes=RH)
        p.act(lambda e: e.copy(out=kinvb[:], in_=kinv), reads=RH, writes=[bqk])
        eb3 = eb.rearrange("p (c s) -> p c s", s=64)
        p.dve(lambda e: e.tensor_tensor(out=ktT.rearrange("p (c s) -> p c s", s=64), in0=kinv.rearrange("p (c s) -> p c s", s=64),
                                        in1=eb3[:, :, 63:64].to_broadcast([128, 8, 64]), op=ALU.mult), reads=RH, writes=RH)
        sb5 = ti % 2
        p.act(lambda e, tl=tl: e.copy(out=uT[:, tl:tl + 512], in_=psU[:]), reads=[bpsU], writes=[buT])
        p.dve(lambda e, sb5=sb5: e.tensor_copy(out=uTb[sb5][:], in_=psU[:]), reads=[bpsU], writes=[buTb[sb5]])
        k = 0
        for i in range(4):
            for ri in range(2):
                pd = psD[k % 2]
                p.pe(lambda e, pd=pd, i=i, ri=ri, sb5=sb5: e.matmul(pd[:], lhsT=wB[:, ri, i * 128:(i + 1) * 128], rhs=uTb[sb5][:],
                                                                    start=True, stop=True), reads=[bwB, buTb[sb5]], writes=[bpsD[k % 2]])
                dst = (dre if ri == 0 else dim_)[:, i, tl:tl + 512]
                p.act(lambda e, pd=pd, dst=dst: e.copy(out=dst, in_=pd[:]), reads=[bpsD[k % 2]], writes=[bd[i]])
                k += 1
        for sub in range(4):
            tsl = slice(sub * 128, (sub + 1) * 128)
            h2 = sub % 2
            for c in range(8):
                p.pe(lambda e, c=c, tsl=tsl, h2=h2, x_=x_: e.matmul(psVG[:, h2, :], lhsT=x_[:, c, tsl], rhs=wA[:, c, 256:512],
                                                                    start=(c == 0), stop=(c == 7)), reads=[bwA, bxt[s]], writes=[bpsVG[h2]])
            p.act(lambda e, sub=sub, h2=h2: e.copy(out=vb[:, sub, :], in_=psVG[:, h2, 0:128]), reads=[bpsVG[h2]], writes=[bvg[sub]])
            p.act(lambda e, sub=sub, h2=h2: e.activation(out=gn[:, sub, :], in_=psVG[:, h2, 128:256], func=AF.Silu), reads=[bpsVG[h2]], writes=[bvg[sub]])
            p.dve(lambda e, sub=sub: e.tensor_tensor(out=gn[:, sub, :], in0=gn[:, sub, :], in1=ng, op=ALU.mult), reads=[bvg[sub], bc], writes=[bvg[sub]])
            mi = sub % 2
            p.pe(lambda e, mi=mi, tsl=tsl: e.transpose(out=psM[:, mi, :], in_=ktT[:, tsl], identity=identf[:]), reads=RH + [bidf], writes=[bpsM[mi]])
            p.act(lambda e, mi=mi, sub=sub: e.copy(out=kt[:, sub, :], in_=psM[:, mi, :]), reads=[bpsM[mi]], writes=[bkt[sub]])
        ys_ = yasb[ti % 2]
        for sub in range(4):
            tsl = slice(sub * 128, (sub + 1) * 128)
            ai_ = sub % 2
            mi = 2 + sub % 2
            p.pe(lambda e, mi=mi, tsl=tsl: e.matmul(psM[:, mi, :], lhsT=kinvb[:, tsl], rhs=qdec[:, tsl], start=True, stop=True),
                 reads=[bqk], writes=[bpsM[mi]])
            p.dve(lambda e, mi=mi, ai_=ai_: e.tensor_tensor(out=attT[ai_][:], in0=psM[:, mi, :], in1=tri, op=ALU.mult),
                  reads=[bpsM[mi], bc], writes=[battT[ai_]])
            oi = sub % 2
            p.pe(lambda e, oi=oi, ai_=ai_, sub=sub: e.matmul(psO[:, oi, :], lhsT=attT[ai_][:], rhs=vb[:, sub, :], start=True, stop=False),
                 reads=[battT[ai_], bvg[sub]], writes=[bpsO[oi]])
            for hc in range(2):
                rows = slice(hc * 64, (hc + 1) * 64)
                tch = slice(sub * 128 + hc * 64, sub * 128 + (hc + 1) * 64)
                sprev = (chunk_idx + 1) % 2
                snew = chunk_idx % 2
                p.pe(lambda e, oi=oi, rows=rows, tch=tch, sprev=sprev, hc=hc: e.matmul(
                    psO[rows, oi, :], lhsT=qdec[:, tch], rhs=Sb[sprev][:], start=False, stop=(hc == 1)),
                    reads=[bqk, bSb[sprev]], writes=[bpsO[oi]])
                di = chunk_idx % 2
                p.pe(lambda e, di=di, rows=rows, sub=sub: e.matmul(psS[:, di, :], lhsT=kt[rows, sub, :], rhs=vb[rows, sub, :], start=True, stop=True),
                     reads=[bkt[sub], bvg[sub]], writes=[bpsS[di]])
                cpos = sub * 128 + hc * 64 + 63
                p.dve(lambda e, di=di, cpos=cpos: e.scalar_tensor_tensor(out=S[:], in0=S[:], scalar=eb[:, cpos:cpos + 1], in1=psS[:, di, :],
                                                                         op0=ALU.mult, op1=ALU.add), reads=[bS, bpsS[di]] + RH, writes=[bS])
                p.act(lambda e, snew=snew: e.copy(out=Sb[snew][:], in_=S[:]), reads=[bS], writes=[bSb[snew]])
                chunk_idx += 1
            p.act(lambda e, oi=oi: e.activation(out=osc[:], in_=psO[:, oi, :], func=AF.Square, accum_out=om[:, 0:1]), reads=[bpsO[oi]], writes=[bo])
            p.act(lambda e: e.activation(out=om[:, 1:2], in_=om[:, 0:1], func=AF.Sqrt, scale=1.0 / 128.0, bias=RMS_EPS), reads=[bo], writes=[bo])
            p.dve(lambda e: e.reciprocal(out=om[:, 1:2], in_=om[:, 1:2]), reads=[bo], writes=[bo])
            p.dve(lambda e, oi=oi, sub=sub, ys_=ys_: e.scalar_tensor_tensor(out=ys_[:, sub, :], in0=psO[:, oi, :], scalar=om[:, 1:2], in1=gn[:, sub, :],
                                                                         op0=ALU.mult, op1=ALU.mult), reads=[bpsO[oi], bo, bvg[sub]], writes=[byasb[ti % 2]])
        for sub in range(4):
            p.pe(lambda e, sub=sub, ys_=ys_: e.transpose(out=psM[:, sub, :], in_=ys_[:, sub, :], identity=identf[:]),
                 reads=[byasb[ti % 2], bidf], writes=[bpsM[0]])
        yt_ = yaTs[ti % 2]
        p.act(lambda e, yt_=yt_: e.copy(out=yt_[:], in_=psM[:].rearrange("p a b -> p (a b)")), reads=[bpsM[0]], writes=[byaTs[ti % 2]])
        emit_ya(res, ti, yt_, byaTs[ti % 2])
        if (ti + 1) % TPS == 0:
            sup0 = (ti + 1 - TPS) * 512
            cscan(0, lambda i: dre[:, i, :], lambda i: dim_[:, i, :], TS5,
                  lambda i: carry[:, 0, i:i + 1], lambda i: carry[:, 1, i:i + 1])
            for i in range(4):
                p.dve(lambda e, i=i: e.tensor_copy(out=carry[:, 0, i:i + 1], in_=dre[:, i, TS5 - 1:TS5]), reads=[bd[i]], writes=[bd[i]])
                p.dve(lambda e, i=i: e.tensor_copy(out=carry[:, 1, i:i + 1], in_=dim_[:, i, TS5 - 1:TS5]), reads=[bd[i]], writes=[bd[i]])
            for ch in range(TPS):
                csl = slice(ch * 512, (ch + 1) * 512)
                hs = ch % 2
                for i in range(4):
                    p.act(lambda e, i=i, hs=hs, csl=csl: e.copy(out=hb[hs][:, 0, i, :], in_=dre[:, i, csl]), reads=[bd[i]], writes=[bhb[hs]])
                    p.act(lambda e, i=i, hs=hs, csl=csl: e.copy(out=hb[hs][:, 1, i, :], in_=dim_[:, i, csl]), reads=[bd[i]], writes=[bhb[hs]])
                k = 0
                for i in range(4):
                    for ri in range(2):
                        p.pe(lambda e, i=i, ri=ri, hs=hs, k=k: e.matmul(psQ[:], lhsT=wC[:, ri, i * 128:(i + 1) * 128], rhs=hb[hs][:, ri, i, :],
                                                                        start=(k == 0), stop=(k == 7)), reads=[bwB, bhb[hs]], writes=[bpsQ])
                        k += 1
                p.dve(lambda e, csl=csl: e.scalar_tensor_tensor(out=zs[:], in0=uT[:, csl], scalar=dsk, in1=psQ[:], op0=ALU.mult, op1=ALU.add),
                      reads=[buT, bc, bpsQ], writes=[bzs])
                p.act(lambda e, hs=hs: e.activation(out=ysb[hs][:], in_=zs[:], func=AF.Gelu_apprx_tanh), reads=[bzs], writes=[bysb[hs]])
                emit_ys(res, (sup0 // 512) + ch, ysb[hs], bysb[hs])


BIGM = 30000.0
DEBUG_MIXC = False


def build_mixc(L):
    nc = bass.Bass("TRN2", target_bir_lowering=False)
    xT = nc.dram_tensor("xT", [1024, L], F32, kind="ExternalInput").ap()
    ycT_d = nc.dram_tensor("ycT", [128, L], BF16, kind="ExternalOutput").ap()
    ydT_d = nc.dram_tensor("ydT", [128, L], BF16, kind="ExternalOutput").ap()
    p = Prog(nc)
    p.debug = DEBUG_MIXC

    def load_x(res, ti, dst, bdst):
        p.dma(dst[:], xT[:, ti * 512:(ti + 1) * 512].rearrange("(c p) t -> p c t", p=128), writes=[bdst], eng="pool")

    def emit_y(res, ti, which, t, b):
        d = ycT_d if which == 0 else ydT_d
        p.dma(d[:, ti * 512:(ti + 1) * 512], t[:], reads=[b])

    phase_mixc(p, nc, "", L, load_x, emit_y)
    p.finish()
    return nc


def phase_mixc(p, nc, pre, L, load_x_cb, emit_y):
    NBLK = L // 256
    wC_d = nc.dram_tensor(pre + "wC", [1024, 640], F32, kind="ExternalInput").ap()
    sink_d = nc.dram_tensor(pre + "sink", [128, 2], F32, kind="ExternalInput").ap()
    swm_d = nc.dram_tensor(pre + "swm", [128, 2, 256], F32, kind="ExternalInput").ap()
    cm_d = nc.dram_tensor(pre + "cm", [128, 2, 256], F32, kind="ExternalInput").ap()
    fut_d = nc.dram_tensor(pre + "fut", [128, 2, 128], F32, kind="ExternalInput").ap()
    res = {}
    identb_f, bidf = make_ident(p, F32, "identf")
    identb = p.sb("identb", [128, 128], BF16)
    bidb = p.buf()
    p.dve(lambda e: e.tensor_copy(out=identb[:], in_=identb_f[:]), reads=[bidf], writes=[bidb])
    wC = p.sb("wC", [128, 8, 640], BF16)
    bwC = p.buf()
    p.dma(wC[:], wC_d.rearrange("(c p) f -> p c f", p=128), writes=[bwC], eng="pool")
    cst = p.sb("cst", [128, 2 + 512 + 512 + 256], F32)
    bc = p.buf("cst")
    sink = cst[:, 0:2]
    swm = cst[:, 2:514].rearrange("p (a k) -> p a k", a=2)
    cm = cst[:, 514:1026].rearrange("p (a k) -> p a k", a=2)
    fut = cst[:, 1026:1282].rearrange("p (a k) -> p a k", a=2)
    p.dma(sink, sink_d, writes=[bc])
    p.dma(swm, swm_d, writes=[bc])
    p.dma(cm, cm_d, writes=[bc])
    p.dma(fut, fut_d, writes=[bc])
    ones = p.sb("ones", [128, 128], F32)
    bones = p.buf()
    p.dve(lambda e: e.memset(ones[:], 1.0), writes=[bones])

    qcT = p.sb("qcT", [128, 512], BF16); bqc = p.buf()
    qdT = p.sb("qdT", [128, 512], BF16); bqd = p.buf()
    kkc = p.sb("kkc", [128, L], BF16)
    kkd = p.sb("kkd", [128, L], BF16)
    vc = p.sb("vc", [128, L // 128, 64], BF16)
    vd = p.sb("vd", [128, L // 128, 64], BF16)
    bkv = p.buf("kv")
    bkvt = [p.buf("kv%d" % i) for i in range(L // 512)]
    kmT = p.sb("kmT", [128, 64], BF16)
    bkm = p.buf("km")
    p.dve(lambda e: e.memset(kmT[:], 0.0), writes=[bkm])
    kmx = p.sb("kmx", [128, 4], F32)
    p.dve(lambda e: e.memset(kmx[:], 0.0), writes=[bkm])
    xt = [p.sb("xt%d" % i, [128, 8, 512], BF16) for i in range(2)]
    bxt = [p.buf() for _ in range(2)]
    sq = p.sb("sq", [128, 512], F32); bsq = p.buf()
    qab = p.sb("qab", [128, 512], BF16)
    bqab = p.buf()
    kab = p.sb("kab", [128, 2], BF16)
    k8 = p.sb("k8", [128, 8], F32)
    sm = [p.sb("sm%d" % i, [128, 256], F32) for i in range(2)]; bsm = [p.buf() for _ in range(2)]
    Pb = [p.sb("Pb%d" % i, [128, 256], BF16) for i in range(4)]; bPb = [p.buf() for _ in range(4)]
    PT = [p.sb("PT%d" % i, [128, 2, 128], BF16) for i in range(4)]; bPT = [p.buf() for _ in range(4)]
    scS = [p.sb("scS%d" % i, [128, 8], F32) for i in range(6)]; bscS = [p.buf() for _ in range(6)]
    gset = []
    for i in range(5):
        gset.append(dict(sc=p.sb("scg%d" % i, [128, 8], F32), gm=p.sb("gm%d" % i, [128, 64], F32), sbm=p.sb("sbm%d" % i, [128, 64], F32),
                         top8=p.sb("top8_%d" % i, [128, 8], F32), lcols=p.sb("lcols%d" % i, [128, 66], F32), b=p.buf("gate%d" % i)))
    ycs = [p.sb("ycs%d" % i, [128, 4, 128], F32) for i in range(2)]; bycs = [p.buf() for _ in range(2)]
    yds = [p.sb("yds%d" % i, [128, 4, 128], F32) for i in range(2)]; byds = [p.buf() for _ in range(2)]
    yTs = [p.sb("yTs%d" % i, [128, 512], BF16) for i in range(4)]; byTs = [p.buf() for _ in range(4)]

    _psP = p.ps("psP", [128, 512], F32); _bpsP = p.pbuf(); psP = [_psP, _psP]; bpsP = [_bpsP, _bpsP]
    psV = p.ps("psV", [128, 512], F32); bpsV = p.pbuf()
    psS = [p.ps("psS%d" % i, [128, 512], F32) for i in range(2)]; bpsS = [p.pbuf() for _ in range(2)]
    psT = [p.ps("psT%d" % i, [128, 8, 128], BF16) for i in range(2)]; bpsT = [p.pbuf() for _ in range(2)]
    psO = [p.ps("psO%d" % i, [128, 512], F32) for i in range(2)]; bpsO = [p.pbuf() for _ in range(2)]

    NTILE = L // 512

    def load_x(ti):
        s = ti % 2
        load_x_cb(res, ti, xt[s], bxt[s])

    load_x(0)
    rot = 0
    kbase = 0
    for ti in range(NTILE):
        s = ti % 2
        x_ = xt[s]
        if ti + 1 < NTILE:
            load_x(ti + 1)
        t0 = ti * 512
        tsl = slice(t0, t0 + 512)
        dsts = [(qcT[:], bqc, 0.125), (kkc[:, tsl], bkvt[ti], 1.0), (qdT[:], bqd, 0.125), (kkd[:, tsl], bkvt[ti], 1.0)]
        for pi, (dst, bdst, scl) in enumerate(dsts):
            pp = psP[pi % 2]
            for c in range(8):
                p.pe(lambda e, pp=pp, c=c, pi=pi, x_=x_: e.matmul(pp[:], lhsT=wC[:, c, pi * 128:(pi + 1) * 128], rhs=x_[:, c, :],
                                                                  start=(c == 0), stop=(c == 7)), reads=[bwC, bxt[s]], writes=[bpsP[pi % 2]])
            p.act(lambda e, pp=pp, dst=dst, scl=scl: e.activation(out=dst, in_=pp[:], func=AF.Copy, scale=scl), reads=[bpsP[pi % 2]], writes=[bdst])
            if pi == 2:
                p.dve(lambda e, pp=pp: e.tensor_scalar(out=sq[:], in0=pp[:], scalar1=-1.0, scalar2=None, op0=ALU.mult), reads=[bpsP[pi % 2]], writes=[bsq])
                p.dve(lambda e, pp=pp: e.tensor_tensor(out=sq[:], in0=sq[:], in1=pp[:], op=ALU.max), reads=[bpsP[pi % 2], bsq], writes=[bsq])
                p.dve(lambda e: e.tensor_scalar(out=qab[:], in0=sq[:], scalar1=0.125, scalar2=None, op0=ALU.mult), reads=[bsq], writes=[bqab])
            if pi == 3:
                p.dve(lambda e, pp=pp: e.tensor_scalar(out=sq[:], in0=pp[:], scalar1=-1.0, scalar2=None, op0=ALU.mult), reads=[bpsP[pi % 2]], writes=[bsq])
                p.dve(lambda e, pp=pp: e.tensor_tensor(out=sq[:], in0=sq[:], in1=pp[:], op=ALU.max), reads=[bpsP[pi % 2], bsq], writes=[bsq])
                p.dve(lambda e: e.max(out=k8[:], in_=sq[:]), reads=[bsq], writes=[bkm])
                p.dve(lambda e: e.tensor_tensor(out=kmx[:, 0:1], in0=kmx[:, 0:1], in1=k8[:, 0:1], op=ALU.max), reads=[bkm], writes=[bkm])
                p.dve(lambda e: e.tensor_copy(out=kab[:, 0:1], in_=kmx[:, 0:1]), reads=[bkm], writes=[bkm])
                p.dve(lambda e: e.tensor_copy(out=kab[:, 1:2], in_=kmx[:, 0:1]), reads=[bkm], writes=[bkm])
        for sub in range(4):
            g = ti * 4 + sub
            for c in range(8):
                p.pe(lambda e, c=c, sub=sub, x_=x_: e.matmul(psV[:, 0:128], lhsT=x_[:, c, sub * 128:(sub + 1) * 128], rhs=wC[:, c, 512:640],
                                                             start=(c == 0), stop=(c == 7)), reads=[bwC, bxt[s]], writes=[bpsV])
            p.act(lambda e, g=g: e.copy(out=vc[:, g, :], in_=psV[:, 0:64]), reads=[bpsV], writes=[bkvt[ti]])
            p.act(lambda e, g=g: e.copy(out=vd[:, g, :], in_=psV[:, 64:128]), reads=[bpsV], writes=[bkvt[ti]])
        for hb in range(2):
            blk = ti * 2 + hb
            p.dve(lambda e, blk=blk: e.tensor_reduce(out=sq[:, 0:1], in_=kkd[:, blk * 256:(blk + 1) * 256], axis=AX.X, op=ALU.add),
                  reads=[bkvt[ti]], writes=[bsq])
            p.dve(lambda e, blk=blk: e.tensor_scalar(out=kmT[:, blk:blk + 1], in0=sq[:, 0:1], scalar1=1.0 / 256.0, scalar2=None, op0=ALU.mult),
                  reads=[bsq], writes=[bkm])
        yc_ = ycs[ti % 2]
        yd_ = yds[ti % 2]
        prev_kv = [bkvt[ti - 1]] if ti > 0 else []

        steps = []
        fins = {}

        def swa_step(g, sub, h, yc_=yc_):
            rows = slice(h * 64, (h + 1) * 64)
            qsl = slice(sub * 128, (sub + 1) * 128)
            if g == 0:
                k0, nk, mk = 0, 128, swm[:, 1, 128:256]
            else:
                k0, nk, mk = (g - 1) * 128, 256, swm[:, 1, :]
            nkc = nk // 128
            kvr = [bkvt[ti]] + (prev_kv if sub == 0 else [])

            def A(k):
                r2, r3 = k % 2, k % 4
                ps_, smr, pb = psS[r2], sm[r2], Pb[r3]
                scs, R = scS[k % 6], [bscS[k % 6]]
                mx, nmx, rs, es, den = [scs[:, i:i + 1] for i in range(5)]
                p.pe(lambda e: e.matmul(ps_[:, 0:nk], lhsT=qcT[rows, qsl], rhs=kkc[rows, k0:k0 + nk], start=True, stop=True),
                     reads=[bqc] + kvr, writes=[bpsS[r2]])
                p.dve(lambda e: e.tensor_tensor(out=smr[:, 0:nk], in0=ps_[:, 0:nk], in1=mk, op=ALU.add), reads=[bpsS[r2], bc], writes=[bsm[r2]])
                p.dve(lambda e: e.tensor_reduce(out=mx, in_=smr[:, 0:nk], axis=AX.X, op=ALU.max), reads=[bsm[r2]], writes=R)
                p.dve(lambda e: e.tensor_tensor(out=mx, in0=mx, in1=sink[:, h:h + 1], op=ALU.max), reads=R + [bc], writes=R)
                p.dve(lambda e: e.tensor_scalar(out=nmx, in0=mx, scalar1=-1.0, scalar2=None, op0=ALU.mult), reads=R, writes=R)
                p.act(lambda e: e.activation(out=pb[:, 0:nk], in_=smr[:, 0:nk], func=AF.Exp, bias=nmx, scale=1.0, accum_out=rs),
                      reads=[bsm[r2]] + R, writes=[bPb[r3]] + R)
                p.act(lambda e: e.activation(out=es, in_=sink[:, h:h + 1], func=AF.Exp, bias=nmx, scale=1.0), reads=R + [bc], writes=R)
                p.dve(lambda e: e.tensor_tensor(out=den, in0=rs, in1=es, op=ALU.add), reads=R, writes=R)
                p.dve(lambda e: e.reciprocal(out=den, in_=den), reads=R, writes=R)

            def B(k):
                r2, r3 = k % 2, k % 4
                pb, pt = Pb[r3], PT[r3]
                for kc in range(nkc):
                    p.pe(lambda e, kc=kc: e.transpose(out=psT[r2][:, kc, :], in_=pb[:, kc * 128:(kc + 1) * 128], identity=identb[:]),
                         reads=[bPb[r3], bidb], writes=[bpsT[r2]])
                p.dve(lambda e: e.tensor_copy(out=pt[:, 0:nkc, :], in_=psT[r2][:, 0:nkc, :]), reads=[bpsT[r2]], writes=[bPT[r3]])

            def C(k):
                r2, r3 = k % 4, k % 6
                pt = PT[r2]
                den = scS[r3][:, 4:5]
                for kc in range(nkc):
                    gk = (g - 1 + kc) if g > 0 else 0
                    p.pe(lambda e, kc=kc, gk=gk: e.matmul(psV[:, 256:320], lhsT=pt[:, kc, :], rhs=vc[:, gk, :], start=(kc == 0), stop=(kc == nkc - 1)),
                         reads=[bPT[r2]] + kvr, writes=[bpsV])
                p.act(lambda e: e.activation(out=yc_[:, sub, h * 64:(h + 1) * 64], in_=psV[:, 256:320], func=AF.Copy, scale=den),
                      reads=[bpsV, bscS[r3]], writes=[bycs[ti % 2]])
            return dict(A=A, B=B, C=C)

        def moba_prologue(g, sub, h, gi):
            rows = slice(h * 64, (h + 1) * 64)
            qsl = slice(sub * 128, (sub + 1) * 128)
            n = g // 2
            gs = gset[gi]
            G = [gs["b"]]
            mb, nmb = gs["sc"][:, 0:1], gs["sc"][:, 1:2]
            p.pe(lambda e: e.matmul(psV[:, 0:2], lhsT=qab[rows, qsl], rhs=kab[rows, 0:2], start=True, stop=True), reads=[bqab, bkm], writes=[bpsV])
            p.dve(lambda e: e.tensor_copy(out=mb, in_=psV[:, 0:1]), reads=[bpsV], writes=G)
            p.dve(lambda e: e.tensor_scalar(out=nmb, in0=mb, scalar1=-1.0, scalar2=None, op0=ALU.mult), reads=G, writes=G)
            if n > 0:
                gm_, sbm_, top8_ = gs["gm"], gs["sbm"], gs["top8"]
                p.pe(lambda e: e.matmul(psV[:, 64:128], lhsT=qdT[rows, qsl], rhs=kmT[rows, :], start=True, stop=True), reads=[bqd, bkm], writes=[bpsV])
                fsl = slice(64 - n, 128 - n)
                p.dve(lambda e: e.tensor_tensor(out=gm_[:], in0=psV[:, 64:128], in1=fut[:, 0, fsl], op=ALU.add), reads=[bpsV, bc], writes=G)
                p.dve(lambda e: e.max(out=top8_[:], in_=gm_[:]), reads=G, writes=G)
                p.dve(lambda e: e.tensor_scalar(out=sbm_[:], in0=gm_[:], scalar1=top8_[:, 2:3], scalar2=BIGM, op0=ALU.is_ge, op1=ALU.mult), reads=G, writes=G)
                p.dve(lambda e: e.tensor_tensor(out=sbm_[:], in0=sbm_[:], in1=fut[:, 1, fsl], op=ALU.add), reads=G + [bc], writes=G)
                p.dve(lambda e: e.tensor_scalar(out=sbm_[:], in0=sbm_[:], scalar1=mb, scalar2=None, op0=ALU.subtract), reads=G, writes=G)
            p.dve(lambda e: e.memset(gs["lcols"][:], 0.0), writes=G)

        def moba_step(g, sub, h, gi, jb, oi, yd_=yd_):
            rows = slice(h * 64, (h + 1) * 64)
            qsl = slice(sub * 128, (sub + 1) * 128)
            n, a = g // 2, g % 2
            own = (jb == n)
            gs = gset[gi]
            G = [gs["b"]]
            kreads = [bkvt[jb // 2]]
            po = psO[oi]

            def A(k):
                r2, r3 = k % 2, k % 4
                ps_, pb = psS[r2], Pb[r3]
                p.pe(lambda e: e.matmul(ps_[:, 0:256], lhsT=qdT[rows, qsl], rhs=kkd[rows, jb * 256:(jb + 1) * 256], start=True, stop=True),
                     reads=[bqd] + kreads, writes=[bpsS[r2]])
                if own:
                    smr = sm[r2]
                    p.dve(lambda e: e.tensor_tensor(out=smr[:], in0=ps_[:, 0:256], in1=cm[:, a, :], op=ALU.add), reads=[bpsS[r2], bc], writes=[bsm[r2]])
                    p.act(lambda e: e.activation(out=pb[:], in_=smr[:], func=AF.Exp, bias=gs["sc"][:, 1:2], scale=1.0, accum_out=gs["lcols"][:, 64:65]),
                          reads=[bsm[r2]] + G, writes=[bPb[r3]] + G)
                else:
                    p.act(lambda e: e.activation(out=pb[:], in_=ps_[:, 0:256], func=AF.Exp, bias=gs["sbm"][:, jb:jb + 1], scale=1.0,
                                                 accum_out=gs["lcols"][:, jb:jb + 1]), reads=[bpsS[r2]] + G, writes=[bPb[r3]] + G)

            def B(k):
                r2, r3 = k % 2, k % 4
                pb, pt = Pb[r3], PT[r3]
                for kc in range(2):
                    p.pe(lambda e, kc=kc: e.transpose(out=psT[r2][:, kc, :], in_=pb[:, kc * 128:(kc + 1) * 128], identity=identb[:]),
                         reads=[bPb[r3], bidb], writes=[bpsT[r2]])
                p.dve(lambda e: e.tensor_copy(out=pt[:], in_=psT[r2][:, 0:2, :]), reads=[bpsT[r2]], writes=[bPT[r3]])

            def C(k):
                r2 = k % 4
                pt = PT[r2]
                for kc in range(2):
                    p.pe(lambda e, kc=kc: e.matmul(po[:, 0:64], lhsT=pt[:, kc, :], rhs=vd[:, jb * 2 + kc, :], start=(jb == 0 and kc == 0), stop=(own and kc == 1)),
                         reads=[bPT[r2]] + kreads, writes=[bpsO[oi]])
                if own:
                    lsum = gs["sc"][:, 2:3]
                    p.dve(lambda e: e.tensor_reduce(out=lsum, in_=gs["lcols"][:, 0:65], axis=AX.X, op=ALU.add), reads=G, writes=G)
                    p.dve(lambda e: e.reciprocal(out=lsum, in_=lsum), reads=G, writes=G)
                    p.act(lambda e: e.activation(out=yd_[:, sub, h * 64:(h + 1) * 64], in_=po[:, 0:64], func=AF.Copy, scale=lsum),
                          reads=[bpsO[oi]] + G, writes=[byds[ti % 2]])
            return dict(A=A, B=B, C=C)

        units = [(ti * 4 + sub, sub, h) for sub in range(4) for h in range(2)]
        seq = []
        for ui, (g, sub, h) in enumerate(units):
            gi = (ti * 8 + ui) % 5
            if ui == 0:
                seq.append(("pro", lambda g=g, sub=sub, h=h, gi=gi: moba_prologue(g, sub, h, gi)))
            if ui + 1 < len(units):
                g2, sub2, h2 = units[ui + 1]
                seq.append(("pro", lambda g2=g2, sub2=sub2, h2=h2, gi=gi: moba_prologue(g2, sub2, h2, (gi + 1) % 5)))
            seq.append(("step", swa_step(g, sub, h)))
            for jb in range(g // 2 + 1):
                seq.append(("step", moba_step(g, sub, h, gi, jb, (ti * 8 + ui) % 2)))
        D = 2
        stp = [x[1] for x in seq if x[0] == "step"]
        nst = len(stp)
        k = 0
        for kind, x in seq:
            if kind == "pro":
                x()
                continue
            x["A"](kbase + k)
            if k - D >= 0:
                stp[k - D]["B"](kbase + k - D)
            if k - 2 * D >= 0:
                stp[k - 2 * D]["C"](kbase + k - 2 * D)
            k += 1
        for k in range(nst, nst + 2 * D):
            if 0 <= k - D < nst:
                stp[k - D]["B"](kbase + k - D)
            if 0 <= k - 2 * D < nst:
                stp[k - 2 * D]["C"](kbase + k - 2 * D)
        kbase += nst
        for which, (src, bsrc) in enumerate(((yc_, bycs[ti % 2]), (yd_, byds[ti % 2]))):
            pp = psP[which]
            for sub in range(4):
                p.pe(lambda e, sub=sub, src=src, pp=pp: e.transpose(out=pp[:, sub * 128:(sub + 1) * 128], in_=src[:, sub, :], identity=identb_f[:]),
                     reads=[bsrc, bidf], writes=[bpsP[which]])
            k = (ti % 2) * 2 + which
            p.act(lambda e, k=k, pp=pp: e.copy(out=yTs[k][:], in_=pp[:]), reads=[bpsP[which]], writes=[byTs[k]])
            emit_y(res, ti, which, yTs[k], byTs[k])


GROUPS = [[0, 1, 2, 3], [4, 5, 6, 7]]


def build_fused(L, debug=False):
    SEG = L // 4
    NCH = SEG // 512
    TG = min(1024, SEG)
    CPG = TG // 512
    nc = bass.Bass("TRN2", target_bir_lowering=False)
    msk_d = nc.dram_tensor("rankmask", [128, 4], F32, kind="ExternalInput").ap()
    xres_d = nc.dram_tensor("xres", [SEG, 1024], F32, kind="ExternalInput").ap()
    xo_d = nc.dram_tensor("xo", [SEG, 1024], F32, kind="ExternalOutput").ap()
    exin = [nc.dram_tensor("exin%d" % i, [NCH, 4, 4, 256, 512], BF16).ap() for i in range(2)]
    exout = [nc.dram_tensor("exout%d" % i, [NCH, 4, 256, 512], BF16).ap() for i in range(2)]
    agin = nc.dram_tensor("agin", [NCH, 1024, 512], BF16).ap()
    agout = nc.dram_tensor("agout", [NCH, 4, 1024, 512], BF16).ap()
    x1loc = nc.dram_tensor("x1loc", [SEG, 1024], F32).ap()

    p = Prog(nc)
    p.debug = debug
    bexin = [[p.buf() for _ in range(NCH)] for _ in range(2)]
    bexout = [[p.buf() for _ in range(NCH)] for _ in range(2)]
    bagin = [p.buf() for _ in range(NCH)]
    bagout = [p.buf() for _ in range(NCH)]
    bx1loc = p.buf()

    def make_exchange_writer(xi):
        st = {}

        def setup():
            st["msk"] = p.sb("msk%d" % xi, [128, 4], F32)
            st["bmsk"] = p.buf()
            p.dma(st["msk"][:], msk_d, writes=[st["bmsk"]])
            st["tmp"] = [p.sb("extmp%d_%d" % (xi, i), [128, 4, 512], BF16) for i in range(2)]
            st["btmp"] = [p.buf() for _ in range(2)]
            st["k"] = 0

        def emit(ti, f0, t, b):
            if "msk" not in st:
                setup()
            k = st["k"] % 2
            st["k"] += 1
            tmp, btmp = st["tmp"][k], st["btmp"][k]
            for q in range(4):
                p.pool(lambda e, q=q, tmp=tmp: e.tensor_scalar(out=tmp[:, q, :], in0=t[:], scalar1=st["msk"][:, q:q + 1], scalar2=0.0,
                                                               op0=ALU.mult, op1=ALU.add), reads=[b, st["bmsk"]], writes=[btmp])
            d, c = ti // NCH, ti % NCH
            p.dma(exin[xi][c, d, :, f0:f0 + 128, :].rearrange("q f t -> f q t"), tmp[:], reads=[btmp], writes=[bexin[xi][c]])

        def finish():
            for c in range(NCH):
                p.collective("ReduceScatter", ALU.add, GROUPS,
                             ins=[exin[xi][c].rearrange("d q f t -> (d q f) t")], outs=[exout[xi][c].rearrange("q f t -> (q f) t")],
                             reads=[bexin[xi][c]], writes=[bexout[xi][c]])
        return emit, finish

    def make_load_yt(xi):
        def load_yt(res, g, yt, byt):
            for cc in range(CPG):
                c = g * CPG + cc
                for half in range(2):
                    p.dma(yt[:, half * 4:(half + 1) * 4, cc * 512:(cc + 1) * 512],
                          exout[xi][c, :, half * 128:(half + 1) * 128, :].rearrange("q f t -> f q t"),
                          reads=[bexout[xi][c]], writes=[byt], eng=("sp" if half == 0 else "act"))
        return load_yt

    p.prefix = "A_"
    emitA, finA = make_exchange_writer(0)
    phase_mixa(p, nc, "a_", L, lambda res, ti, t, b: emitA(ti, 0, t, b), lambda res, ti, t, b: emitA(ti, 128, t, b))
    finA()
    p.end_phase()

    p.prefix = "T0_"
    st0 = {}

    def load_xres0(res, g, st, dst, bdst):
        t0 = g * TG + st * 128
        p.dma(dst, xres_d[t0:t0 + 128, :], writes=[bdst], eng="act")

    def store_out0(res, g, st, o, bo):
        if "xT" not in st0:
            st0["xT"] = [p.sb("x1Ts%d" % i, [128, 8, 128], BF16) for i in range(2)]
            st0["bxT"] = [p.buf() for _ in range(2)]
            st0["k"] = 0
        t0 = g * TG + st * 128
        p.dma(x1loc[t0:t0 + 128, :], o[:], reads=[bo], writes=[bx1loc])
        k = st0["k"] % 2
        st0["k"] += 1
        xT, bxT = st0["xT"][k], st0["bxT"][k]
        psA, bpsA, identf, bidf = res["psA"], res["bpsA"], res["identf"], res["bidf"]
        for hf in range(2):
            pa = psA[hf]
            for c4 in range(4):
                c = hf * 4 + c4
                p.pe(lambda e, pa=pa, c4=c4, c=c: e.transpose(out=pa[:, c4 * 128:(c4 + 1) * 128], in_=o[:, c * 128:(c + 1) * 128], identity=identf[:]),
                     reads=[bo, bidf], writes=[bpsA[hf]])
            p.act(lambda e, pa=pa, hf=hf, xT=xT: e.copy(out=xT[:, hf * 4:(hf + 1) * 4, :], in_=pa[:].rearrange("p (c t) -> p c t", c=4)),
                  reads=[bpsA[hf]], writes=[bxT])
        chunk, toff = t0 // 512, t0 % 512
        p.dma(agin[chunk].rearrange("(dc p) t -> p dc t", p=128)[:, :, toff:toff + 128], xT[:], reads=[bxT], writes=[bagin[chunk]])
        if toff == 384:
            p.collective("AllGather", ALU.bypass, GROUPS, ins=[agin[chunk]], outs=[agout[chunk].rearrange("r d t -> (r d) t")],
                         reads=[bagin[chunk]], writes=[bagout[chunk]])

    phase_tail(p, nc, "t0_", SEG, True, make_load_yt(0), load_xres0, store_out0, TG)
    p.end_phase()

    p.prefix = "C_"
    emitC, finC = make_exchange_writer(1)

    def load_xc(res, ti, dst, bdst):
        r, c = ti // NCH, ti % NCH
        p.dma(dst[:], agout[c, r].rearrange("(dc p) t -> p dc t", p=128), reads=[bagout[c]], writes=[bdst])

    phase_mixc(p, nc, "c_", L, load_xc, lambda res, ti, which, t, b: emitC(ti, which * 128, t, b))
    finC()
    p.end_phase()

    p.prefix = "T1_"

    def load_xres1(res, g, st, dst, bdst):
        t0 = g * TG + st * 128
        p.dma(dst, x1loc[t0:t0 + 128, :], reads=[bx1loc], writes=[bdst], eng="act")

    def store_out1(res, g, st, o, bo):
        t0 = g * TG + st * 128
        p.dma(xo_d[t0:t0 + 128, :], o[:], reads=[bo])

    phase_tail(p, nc, "t1_", SEG, False, make_load_yt(1), load_xres1, store_out1, TG)
    if debug:
        for name, src, bufs in (("dbg_exout0", exout[0], bexout[0]), ("dbg_exout1", exout[1], bexout[1]), ("dbg_agout", agout, bagout),
                                ("dbg_exin0", exin[0], bexin[0])):
            d = nc.dram_tensor(name, list(src.shape), BF16, kind="ExternalOutput").ap()
            for c in range(NCH):
                p.dma(d[c], src[c], reads=[bufs[c]])
        d = nc.dram_tensor("dbg_x1loc", [SEG, 1024], F32, kind="ExternalOutput").ap()
        p.dma(d, x1loc, reads=[bx1loc])
    p.finish()
    return nc

import ml_dtypes

_BF = ml_dtypes.bfloat16
_PROGS = {}


def _prog(key, fn):
    if key not in _PROGS:
        _PROGS[key] = fn()
    return _PROGS[key]


def mixa_inputs(inp, j, xT):
    W = inp['ev_w_in'][0]
    sl = slice(128 * j, 128 * (j + 1))
    wA = np.concatenate([W[:, 0:512][:, sl], W[:, 512:1024][:, sl], W[:, 1024:1536][:, sl], W[:, 1536:2048][:, sl], W[:, 2048:2560][:, sl]], axis=1)
    lbl = np.ascontiguousarray(inp['hgrn_lb_logits'][:, sl].T)
    ng = np.ascontiguousarray(np.broadcast_to(inp['ev_a_norm'][0, sl][None], (128, 128)))
    G0 = 8 * j
    are, aim, ldt = inp['ev_s5_a_re'][0], inp['ev_s5_a_im'][0], inp['ev_s5_log_dt'][0]
    sps = np.zeros((128, 3, 4), np.float32)
    spw = np.zeros((128, 3, 4, 128), np.float32)
    bpad = np.zeros((128, 2, 4, 128), np.float32)
    cpad = np.zeros((128, 2, 4, 128), np.float32)
    for i in range(4):
        for gl in range(2):
            g = G0 + 2 * i + gl
            glc = 2 * i + gl
            sps[gl * 64:(gl + 1) * 64, 0, i] = are[g]
            sps[gl * 64:(gl + 1) * 64, 1, i] = aim[g]
            sps[gl * 64:(gl + 1) * 64, 2, i] = ldt[g]
            spw[:, 0, i, gl * 64:(gl + 1) * 64] = are[g][None]
            spw[:, 1, i, gl * 64:(gl + 1) * 64] = aim[g][None]
            spw[:, 2, i, gl * 64:(gl + 1) * 64] = ldt[g]
            bpad[glc * 16:(glc + 1) * 16, 0, i, gl * 64:(gl + 1) * 64] = inp['ev_s5_b_re'][0, g].T
            bpad[glc * 16:(glc + 1) * 16, 1, i, gl * 64:(gl + 1) * 64] = inp['ev_s5_b_im'][0, g].T
            cpad[gl * 64:(gl + 1) * 64, 0, i, glc * 16:(glc + 1) * 16] = inp['ev_s5_c_re'][0, g].T
            cpad[gl * 64:(gl + 1) * 64, 1, i, glc * 16:(glc + 1) * 16] = inp['ev_s5_c_im'][0, g].T
    dsk = np.ascontiguousarray(inp['ev_s5_d'][0, sl][:, None])
    s_ = np.arange(128)[:, None]
    c_ = np.arange(128)[None, :]
    tri = ((s_ // 64 == c_ // 64) & (s_ <= c_)).astype(np.float32)
    m01 = np.broadcast_to((np.arange(512) % 64 != 0).astype(np.float32)[None], (128, 512))
    return {"xT": xT, "wA": np.ascontiguousarray(wA), "lbl": lbl, "ng": ng, "sps": sps, "spw": spw.reshape(128, 3, 512),
            "bpad": bpad.reshape(128, 2, 512), "cpad": cpad.reshape(128, 2, 512), "dsk": dsk, "tri": tri, "m01": np.ascontiguousarray(m01)}


def mixc_inputs(inp, j, xT):
    W = inp['od_w_in'][0]
    kv = j // 2
    qc = W[:, 128 * j:128 * (j + 1)]
    kc = W[:, 512 + 64 * kv:512 + 64 * (kv + 1)]
    vc = W[:, 640 + 64 * kv:640 + 64 * (kv + 1)]
    qd = W[:, 768 + 128 * j:768 + 128 * (j + 1)]
    kd = W[:, 1280 + 64 * kv:1280 + 64 * (kv + 1)]
    vd = W[:, 1408 + 64 * kv:1408 + 64 * (kv + 1)]
    wC = np.concatenate([qc, kc, kc, qd, kd, kd, vc, vd], axis=1)
    sink = np.ascontiguousarray(np.broadcast_to(inp['od_sinks'][0, 2 * j:2 * j + 2][None], (128, 2)))
    q = np.arange(128)[:, None]
    k = np.arange(256)[None, :]
    band = np.where(((k < 128) & (k > q)) | ((k >= 128) & (k - 128 <= q)), 0.0, -BIGM).astype(np.float32)
    swm = np.stack([band, band], axis=1)
    cm = np.stack([np.where(k <= q + 128 * a, 0.0, -BIGM) for a in range(2)], axis=1).astype(np.float32)
    i = np.arange(128)[None, :]
    f0 = np.where(i < 64, 0.0, -BIGM) * np.ones((128, 1))
    fut = np.stack([f0, f0 - BIGM], axis=1).astype(np.float32)
    return {"xT": xT, "wC": np.ascontiguousarray(wC), "sink": sink, "swm": np.ascontiguousarray(swm), "cm": np.ascontiguousarray(cm),
            "fut": np.ascontiguousarray(fut)}


def tail_inputs(inp, layer, xres, yT, w_out, w_glu):
    lnp = np.ascontiguousarray(np.stack([np.broadcast_to(inp[k][layer], (128, 1024)) for k in ['ln1_g', 'ln1_b', 'ln2_g', 'ln2_b']]).astype(np.float32))
    w_r = np.ascontiguousarray(np.concatenate([inp['moe_w_group'][layer], inp['moe_w_expert'][layer]], axis=1))
    b_r = np.ascontiguousarray(np.broadcast_to(np.concatenate([inp['moe_b_group'][layer], inp['moe_b_expert'][layer]])[None], (128, 20)).astype(np.float32))
    m = {"xres": xres, "yT": yT, "w_out": w_out, "lnp": lnp, "w_r": w_r, "b_r": b_r,
         "w_gu": inp['moe_w_gate_up'][layer], "w_dn": inp['moe_w_down'][layer]}
    if w_glu is not None:
        m["w_glu"] = w_glu
    return m


def fused_inputs(inp, x, xTb, b, r, SEG):
    m = {}
    for k, v in mixa_inputs(inp, r, xTb).items():
        m["a_" + k] = v
    for k, v in mixc_inputs(inp, r, None).items():
        if k != "xT":
            m["c_" + k] = v
    for k, v in tail_inputs(inp, 0, None, None, inp['ev_w_out'][0], inp['ev_s5_w_glu'][0]).items():
        if k not in ("xres", "yT"):
            m["t0_" + k] = v
    for k, v in tail_inputs(inp, 1, None, None, inp['od_w_out'][0], None).items():
        if k not in ("xres", "yT"):
            m["t1_" + k] = v
    msk = np.zeros((128, 4), np.float32)
    msk[:, r] = 1.0
    m["rankmask"] = msk
    m["xres"] = np.ascontiguousarray(x[b, r * SEG:(r + 1) * SEG])
    return m


def kernel(**inputs):
    inp = {k: np.ascontiguousarray(np.asarray(v)) for k, v in inputs.items()}
    x = inp['x']
    B, L, D = x.shape
    SEG = L // 4
    cores = list(range(8))
    xT = [np.ascontiguousarray(x[b].T) for b in range(B)]
    nc = _prog(("F", L), lambda: build_fused(L))
    maps = [fused_inputs(inp, x, xT[c // 4], c // 4, c % 4, SEG) for c in cores]
    res = run_bass_kernel_spmd(nc, maps, core_ids=cores).results
    out = np.stack([np.concatenate([np.asarray(res[b * 4 + s]["xo"]) for s in range(4)], axis=0) for b in range(B)])
    return out.astype(np.float32)
```

```python
import numpy as np
from contextlib import ExitStack
import concourse.bass as bass
import concourse.mybir as mybir
from concourse.bass_utils import run_bass_kernel_spmd

dt = mybir.dt
F32 = dt.float32
BF16 = dt.bfloat16
I32 = dt.int32
U32 = dt.uint32
AF = mybir.ActivationFunctionType
ALU = mybir.AluOpType
AX = mybir.AxisListType


class Buf:
    __slots__ = ("name", "lw", "rd", "excl")

    def __init__(self, name, excl=False):
        self.name = name
        self.lw = None
        self.rd = {}
        self.excl = excl


class Prog:
    ENG = ["pe", "dve", "act", "pool", "sp"]
    NDMA = 32

    def __init__(self, nc, same_engine_sync=True):
        self.nc = nc
        self.es = ExitStack()
        self.ops = {e: [] for e in self.ENG}
        self.cnt = {e: 0 for e in self.ENG}
        self.waited = {e: {} for e in self.ENG}
        self.dma_cnt = [0] * self.NDMA
        self.dma_rr = 0
        self.same = same_engine_sync
        self.sems = {}
        self.gen = 0
        self.semkey = {}
        for e in ["pe", "dve", "act", "pool"]:
            self.semkey[e] = e
            self.sems[e] = self.es.enter_context(nc.semaphore("s_" + e))
        for j in range(self.NDMA):
            self.sems["d%d" % j] = self.es.enter_context(nc.semaphore("s_d%d" % j))
        self.nbuf = 0
        self.scope = ExitStack()
        self.ncc = 0
        self.prefix = ""

    def sb(self, name, shape, dtype):
        return self.scope.enter_context(self.nc.sbuf_tensor("sb_" + self.prefix + name, list(shape), dtype))

    def ps(self, name, shape, dtype):
        return self.scope.enter_context(self.nc.psum_tensor("pp_" + self.prefix + name, list(shape), dtype))

    def debug_dump(self, name, ap, shape, dtype, reads):
        if not getattr(self, "debug", False):
            return
        d = self.nc.dram_tensor("dbg_" + name, list(shape), dtype, kind="ExternalOutput").ap()
        self.dma(d, ap, reads=reads)

    def barrier(self):
        targets = []
        for j in range(self.NDMA):
            if self.dma_cnt[j] > 0:
                targets.append(("d%d" % j, self.dma_cnt[j]))
        for e in ["pe", "dve", "act", "pool"]:
            if self.cnt[e] > 0:
                targets.append((self.semkey[e], self.cnt[e]))
        for k in self.sems:
            if k.startswith("cc"):
                targets.append((k, 1))
        for e in self.ENG:
            waits = []
            for k, v in targets:
                if k == self.semkey.get(e):
                    continue
                if self.waited[e].get(k, 0) >= v:
                    continue
                waits.append((k, v))
                self.waited[e][k] = v
            if waits:
                self.ops[e].append((waits, None, None, False))

    def end_phase(self):
        self.barrier()
        self.scope.close()
        self.scope = ExitStack()
        self.gen += 1
        for e in ["pe", "dve", "act", "pool"]:
            k = "%s@%d" % (e, self.gen)
            self.semkey[e] = k
            self.sems[k] = self.es.enter_context(self.nc.semaphore("s_%s_%d" % (e, self.gen)))
            self.cnt[e] = 0

    def collective(self, kind, alu, groups, ins, outs, reads=(), writes=()):
        k = "cc%d" % self.ncc
        self.ncc += 1
        self.sems[k] = self.es.enter_context(self.nc.semaphore("s_" + k))
        eng = "pool"
        waits = {}

        def need(kk, v):
            if v <= 0 or self.waited[eng].get(kk, 0) >= v:
                return
            if waits.get(kk, 0) < v:
                waits[kk] = v
        for b in reads:
            if b.lw is not None:
                need(*b.lw)
        for b in writes:
            if b.lw is not None:
                need(*b.lw)
            for kk, v in b.rd.items():
                need(kk, v)
        for kk, v in waits.items():
            self.waited[eng][kk] = v
        tok = (k, 1)
        self.ops[eng].append((list(waits.items()), lambda e: e.collective_compute(kind, alu, replica_groups=groups, ins=ins, outs=outs), tok, "cc"))
        for b in reads:
            if b.rd.get(tok[0], 0) < tok[1]:
                b.rd[tok[0]] = tok[1]
        for b in writes:
            b.lw = tok
            b.rd = {}
        return tok

    def buf(self, name=None, excl=False):
        self.nbuf += 1
        return Buf(name or ("b%d" % self.nbuf), excl)

    def pbuf(self, name=None):
        return self.buf(name, True)

    def op(self, eng, emit, reads=(), writes=(), dma=False):
        waits = {}
        ex = [b for b in reads if b.excl]
        if ex:
            writes = list(writes) + ex
            reads = [b for b in reads if not b.excl]

        def need(k, v):
            if v <= 0:
                return
            if k == self.semkey.get(eng) and (eng == "pe" or not self.same):
                return
            if self.waited[eng].get(k, 0) >= v:
                return
            if waits.get(k, 0) < v:
                waits[k] = v

        for b in reads:
            if b.lw is not None:
                need(*b.lw)
        for b in writes:
            if b.lw is not None:
                need(*b.lw)
            for k, v in b.rd.items():
                need(k, v)
        if dma:
            j = self.dma_rr
            self.dma_rr = (self.dma_rr + 1) % self.NDMA
            k = "d%d" % j
            need(k, self.dma_cnt[j])
            self.dma_cnt[j] += 16
            tok = (k, self.dma_cnt[j])
        else:
            self.cnt[eng] += 1
            tok = (self.semkey[eng], self.cnt[eng])
        for k, v in waits.items():
            self.waited[eng][k] = v
        self.ops[eng].append((list(waits.items()), emit, tok, dma))
        for b in reads:
            if b.rd.get(tok[0], 0) < tok[1]:
                b.rd[tok[0]] = tok[1]
        for b in writes:
            b.lw = tok
            b.rd = {}
        return tok

    def pe(self, emit, reads=(), writes=()):
        return self.op("pe", emit, reads, writes)

    def dve(self, emit, reads=(), writes=()):
        return self.op("dve", emit, reads, writes)

    def act(self, emit, reads=(), writes=()):
        return self.op("act", emit, reads, writes)

    def pool(self, emit, reads=(), writes=()):
        return self.op("pool", emit, reads, writes)

    def dma(self, out, in_, reads=(), writes=(), eng="sp", **kw):
        return self.op(eng, lambda e: e.dma_start(out=out, in_=in_, **kw), reads, writes, dma=True)

    def finish(self):
        waits = []
        for j in range(self.NDMA):
            if self.dma_cnt[j] > 0:
                waits.append(("d%d" % j, self.dma_cnt[j]))
        for e in ["pe", "dve", "act", "pool"]:
            if self.cnt[e] > 0:
                waits.append((self.semkey[e], self.cnt[e]))
        for k in self.sems:
            if k.startswith("cc"):
                waits.append((k, 1))
        self.ops["sp"].append((waits, None, None, False))
        nc = self.nc
        sems = self.sems
        ops = self.ops

        def run(name, e):
            for waits, emit, tok, dma in ops[name]:
                for k, v in waits:
                    e.wait_ge(sems[k], v)
                if emit is None:
                    continue
                ins = emit(e)
                ins.then_inc(sems[tok[0]], 16 if dma is True else 1)

        with nc.Block() as block:
            @block.tensor
            def _(e):
                run("pe", e)

            @block.vector
            def _(e):
                run("dve", e)

            @block.scalar
            def _(e):
                run("act", e)

            @block.gpsimd
            def _(e):
                run("pool", e)

            @block.sync
            def _(e):
                run("sp", e)
        self.scope.close()
        self.es.close()


DN_ALPHA = 4.0 ** 0.25
LN_EPS = 1e-5
BIG = 30000.0


def make_ident(p, dtype, name):
    ident = p.sb(name, [128, 128], dtype)
    b = p.buf(name)
    p.pool(lambda e: e.memset(ident[:], 0.0), writes=[b])
    p.pool(lambda e: e.affine_select(out=ident[:], in_=ident[:], compare_op=ALU.not_equal, fill=1.0,
                                     base=0, pattern=[[-1, 128]], channel_multiplier=1), reads=[b], writes=[b])
    return ident, b


def layer_norm_tm(p, src, dst, gam, bet, bsrc, bdst, bconst, tmp, btmp, eps, key):
    stats, mv, rstd = tmp
    p.dve(lambda e: e.bn_stats(out=stats[:, 0, :], in_=src[:, 0:512]), reads=[bsrc], writes=[btmp])
    p.dve(lambda e: e.bn_stats(out=stats[:, 1, :], in_=src[:, 512:1024]), reads=[bsrc], writes=[btmp])
    p.dve(lambda e: e.bn_aggr(out=mv[:], in_=stats[:].rearrange("p a b -> p (a b)")), reads=[btmp], writes=[btmp])
    p.act(lambda e: e.activation(out=rstd[:], in_=mv[:, 1:2], func=AF.Sqrt, bias=eps, scale=1.0), reads=[btmp], writes=[btmp])
    p.dve(lambda e: e.reciprocal(out=rstd[:], in_=rstd[:]), reads=[btmp], writes=[btmp])
    p.dve(lambda e: e.tensor_scalar(out=dst, in0=src, scalar1=mv[:, 0:1], scalar2=rstd[:, 0:1],
                                    op0=ALU.subtract, op1=ALU.mult), reads=[bsrc, btmp], writes=[bdst])
    p.dve(lambda e: e.tensor_tensor(out=dst, in0=dst, in1=gam, op=ALU.mult), reads=[bdst, bconst], writes=[bdst])
    p.dve(lambda e: e.tensor_tensor(out=dst, in0=dst, in1=bet, op=ALU.add), reads=[bdst, bconst], writes=[bdst])


def build_tail(NT, glu, TG=1024):
    nc = bass.Bass("TRN2", target_bir_lowering=False)
    D = 1024
    xres = nc.dram_tensor("xres", [NT, D], F32, kind="ExternalInput").ap()
    yT = nc.dram_tensor("yT", [D, NT], BF16, kind="ExternalInput").ap()
    xo = nc.dram_tensor("xo", [NT, D], F32, kind="ExternalOutput").ap()
    p = Prog(nc)

    def load_yt(res, g, yt, byt):
        p.dma(yt[:], yT[:, g * TG:(g + 1) * TG].rearrange("(c p) t -> p c t", p=128), writes=[byt])

    def load_xres(res, g, st, dst, bdst):
        t0 = g * TG + st * 128
        p.dma(dst, xres[t0:t0 + 128, :], writes=[bdst], eng="act")

    def store_out(res, g, st, o, bo):
        t0 = g * TG + st * 128
        p.dma(xo[t0:t0 + 128, :], o[:], reads=[bo])

    phase_tail(p, nc, "", NT, glu, load_yt, load_xres, store_out, TG)
    p.finish()
    return nc


def phase_tail(p, nc, pre, NT, glu, load_yt, load_xres, store_out, TG=1024):
    D = 1024
    NG = NT // TG
    NST = TG // 128
    if glu:
        w_glu = nc.dram_tensor(pre + "w_glu", [512, 1024], F32, kind="ExternalInput").ap()
    w_out = nc.dram_tensor(pre + "w_out", [D, D], F32, kind="ExternalInput").ap()
    lnp = nc.dram_tensor(pre + "lnp", [4, 128, D], F32, kind="ExternalInput").ap()
    w_r = nc.dram_tensor(pre + "w_r", [D, 20], F32, kind="ExternalInput").ap()
    b_r = nc.dram_tensor(pre + "b_r", [128, 20], F32, kind="ExternalInput").ap()
    w_gu = nc.dram_tensor(pre + "w_gu", [16, D, 512], F32, kind="ExternalInput").ap()
    w_dn = nc.dram_tensor(pre + "w_dn", [16, 256, D], F32, kind="ExternalInput").ap()
    identf, bidf = make_ident(p, F32, "identf")
    identb = p.sb("identb", [128, 128], BF16)
    bidb = p.buf()
    p.dve(lambda e: e.tensor_copy(out=identb[:], in_=identf[:]), reads=[bidf], writes=[bidb])
    wout = p.sb("wout", [128, 8, D], BF16)
    bwout = p.buf()
    p.dma(wout[:], w_out.rearrange("(c p) f -> p c f", p=128), writes=[bwout], eng="pool")
    if glu:
        wglu = p.sb("wglu", [128, 4, 1024], BF16)
        bwglu = p.buf()
        p.dma(wglu[:], w_glu.rearrange("(c p) f -> p c f", p=128), writes=[bwglu], eng="pool")
    lns = p.sb("lns", [128, 4, D], F32)
    bln = p.buf()
    p.dma(lns[:], lnp.rearrange("a p d -> p a d"), writes=[bln])
    wr = p.sb("wr", [128, 8, 20], BF16)
    bwr = p.buf()
    p.dma(wr[:], w_r.rearrange("(c p) f -> p c f", p=128), writes=[bwr], eng="pool")
    br_ = p.sb("br", [128, 20], F32)
    bbr = p.buf()
    p.dma(br_[:], b_r, writes=[bbr])

    yt = p.sb("yt", [128, 8, TG], BF16)
    byt = p.buf()
    if glu:
        yglu = p.sb("yglu", [128, 4, TG], BF16)
        byglu = [p.buf() for _ in range(4)]
        sig = p.sb("sig", [128, 512], F32)
        bsig = p.buf()
    acc = p.sb("acc", [128, NST, D], F32)
    bacc = [p.buf() for _ in range(NST)]
    x1T = p.sb("x1T", [128, 8, TG], BF16)
    bx1T = [p.buf() for _ in range(NST)]
    gates = p.sb("gates", [128, NST, 16], F32)
    bgates = [p.buf() for _ in range(NST)]
    gub = [p.sb("gub%d" % i, [128, 8, 512], BF16) for i in range(2)]
    bgub = [p.buf() for _ in range(2)]
    dnb = [p.sb("dnb%d" % i, [128, 2, D], BF16) for i in range(2)]
    bdnb = [p.buf() for _ in range(2)]
    sg = [p.sb("sg%d" % i, [128, 256], F32) for i in range(2)]
    bsg = [p.buf() for _ in range(2)]
    hh = [p.sb("hh%d" % i, [128, 256], BF16) for i in range(3)]
    bhh = [p.buf() for _ in range(3)]
    hT = [p.sb("hT%d" % i, [128, 2, 128], BF16) for i in range(3)]
    bhT = [p.buf() for _ in range(3)]
    stats = p.sb("stats", [128, 2, 6], F32)
    mv = p.sb("mv", [128, 2], F32)
    rstd = p.sb("rstd", [128, 1], F32)
    btmp = p.buf()
    rt = p.sb("rt", [128, 80], F32)
    brt = p.buf()
    top8 = p.sb("top8", [128, 8], F32)
    xout = [p.sb("xout%d" % i, [128, D], F32) for i in range(2)]
    bxout = [p.buf() for _ in range(2)]

    psA = [p.ps("psA%d" % i, [128, 512], F32) for i in range(2)]
    bpsA = [p.pbuf() for _ in range(2)]
    psT = [p.ps("psT%d" % i, [128, 8, 128], BF16) for i in range(2)]
    bpsT = [p.pbuf() for _ in range(2)]
    psY = [p.ps("psY%d" % i, [128, 1024], F32) for i in range(2)]
    bpsY = [p.pbuf() for _ in range(2)]

    res = dict(psA=psA, bpsA=bpsA, identf=identf, bidf=bidf, TG=TG, NST=NST)
    expert_steps = [(g, e) for g in range(NG) for e in range(16)]

    def load_expert(idx):
        g, e = expert_steps[idx]
        s = idx % 2
        p.dma(gub[s][:], w_gu[e].rearrange("(c p) f -> p c f", p=128), writes=[bgub[s]], eng="pool")
        p.dma(dnb[s][:], w_dn[e].rearrange("(c p) f -> p c f", p=128), writes=[bdnb[s]], eng="pool")

    load_expert(0)
    rot = 0
    mk0 = 0
    for g in range(NG):
        t0 = g * TG
        load_yt(res, g, yt, byt)
        for st in range(NST):
            load_xres(res, g, st, acc[:, st, :], bacc[st])
        if glu:
            for half in range(TG // 512):
                ts = slice(half * 512, (half + 1) * 512)
                for f in range(4):
                    for which in range(2):
                        col0 = which * 512 + f * 128
                        for k in range(4):
                            p.pe(lambda e, which=which, col0=col0, k=k, ts=ts: e.matmul(
                                psA[which][:], lhsT=wglu[:, k, col0:col0 + 128], rhs=yt[:, 4 + k, ts],
                                start=(k == 0), stop=(k == 3)), reads=[bwglu, byt], writes=[bpsA[which]])
                    p.act(lambda e: e.activation(out=sig[:], in_=psA[1][:], func=AF.Sigmoid), reads=[bpsA[1]], writes=[bsig])
                    p.dve(lambda e, f=f, ts=ts: e.tensor_tensor(out=yglu[:, f, ts], in0=psA[0][:], in1=sig[:], op=ALU.mult),
                          reads=[bpsA[0], bsig], writes=[byglu[f]])
        for st in range(NST):
            tsl = slice(st * 128, (st + 1) * 128)
            py = psY[st % 2]
            bpy = bpsY[st % 2]
            for half in range(2):
                for c in range(8):
                    if glu and c >= 4:
                        lhs = yglu[:, c - 4, tsl]
                        rb = byglu[c - 4]
                    else:
                        lhs = yt[:, c, tsl]
                        rb = byt
                    p.pe(lambda e, lhs=lhs, c=c, half=half, py=py: e.matmul(
                        py[:, half * 512:(half + 1) * 512], lhsT=lhs, rhs=wout[:, c, half * 512:(half + 1) * 512],
                        start=(c == 0), stop=(c == 7)), reads=[rb, bwout], writes=[bpy])
            a = acc[:, st, :]
            p.dve(lambda e, a=a, py=py: e.scalar_tensor_tensor(out=a, in0=a, scalar=DN_ALPHA, in1=py[:],
                                                                 op0=ALU.mult, op1=ALU.add), reads=[bacc[st], bpy], writes=[bacc[st]])
            layer_norm_tm(p, a, a, lns[:, 0, :], lns[:, 1, :], bacc[st], bacc[st], bln, (stats, mv, rstd), btmp, LN_EPS, "ln1")
            for hf in range(2):
                pa = psA[hf]
                for c4 in range(4):
                    c = hf * 4 + c4
                    p.pe(lambda e, pa=pa, c4=c4, c=c, a=a: e.transpose(out=pa[:, c4 * 128:(c4 + 1) * 128], in_=a[:, c * 128:(c + 1) * 128],
                                                                       identity=identf[:]), reads=[bacc[st], bidf], writes=[bpsA[hf]])
                p.act(lambda e, pa=pa, hf=hf, tsl=tsl: e.copy(out=x1T[:, hf * 4:(hf + 1) * 4, tsl], in_=pa[:].rearrange("p (c t) -> p c t", c=4)),
                      reads=[bpsA[hf]], writes=[bx1T[st]])
            p.act(lambda e, a=a: e.activation(out=a, in_=a, func=AF.Copy, scale=DN_ALPHA), reads=[bacc[st]], writes=[bacc[st]])
            pr = psT[st % 2]
            pl = psY[(st + 1) % 2]
            bpl = bpsY[(st + 1) % 2]
            for c in range(8):
                p.pe(lambda e, c=c, pl=pl, tsl=tsl: e.matmul(pl[:, 0:20], lhsT=x1T[:, c, tsl], rhs=wr[:, c, :],
                                                             start=(c == 0), stop=(c == 7)), reads=[bx1T[st], bwr], writes=[bpl])
            lg = rt[:, 0:20]
            p.dve(lambda e, pl=pl: e.tensor_tensor(out=lg, in0=pl[:, 0:20], in1=br_[:], op=ALU.add), reads=[bpl, bbr], writes=[brt])
            gmax = rt[:, 20:21]
            ngmax = rt[:, 21:22]
            sume = rt[:, 22:23]
            gtop = rt[:, 23:24]
            eg = rt[:, 24:28]
            oh = rt[:, 28:32]
            em = rt[:, 32:48]
            dd = rt[:, 48:49]
            ex = rt[:, 49:50]
            w1 = rt[:, 50:51]
            w2 = rt[:, 51:52]
            t2 = rt[:, 56:72]
            R = [brt]
            p.dve(lambda e: e.tensor_reduce(out=gmax, in_=lg[:, 0:4], axis=AX.X, op=ALU.max), reads=R, writes=R)
            p.dve(lambda e: e.tensor_scalar(out=ngmax, in0=gmax, scalar1=-1.0, scalar2=None, op0=ALU.mult), reads=R, writes=R)
            p.act(lambda e: e.activation(out=eg, in_=lg[:, 0:4], func=AF.Exp, bias=ngmax, scale=1.0, accum_out=sume), reads=R, writes=R)
            p.dve(lambda e: e.reciprocal(out=gtop, in_=sume), reads=R, writes=R)
            p.dve(lambda e: e.tensor_scalar(out=oh, in0=lg[:, 0:4], scalar1=gmax, scalar2=BIG, op0=ALU.is_equal, op1=ALU.mult), reads=R, writes=R)
            p.dve(lambda e: e.tensor_scalar(out=oh, in0=oh, scalar1=-BIG, scalar2=None, op0=ALU.add), reads=R, writes=R)
            for gi in range(4):
                p.dve(lambda e, gi=gi: e.tensor_scalar(out=em[:, gi * 4:(gi + 1) * 4], in0=lg[:, 4 + gi * 4:8 + gi * 4],
                                                       scalar1=oh[:, gi:gi + 1], scalar2=None, op0=ALU.add), reads=R, writes=R)
            p.dve(lambda e: e.max(out=top8[:], in_=em), reads=R, writes=R)
            p.dve(lambda e: e.tensor_tensor(out=dd, in0=top8[:, 1:2], in1=top8[:, 0:1], op=ALU.subtract), reads=R, writes=R)
            p.act(lambda e: e.activation(out=ex, in_=dd, func=AF.Exp), reads=R, writes=R)
            p.dve(lambda e: e.tensor_scalar(out=w1, in0=ex, scalar1=1.0, scalar2=None, op0=ALU.add), reads=R, writes=R)
            p.dve(lambda e: e.reciprocal(out=w1, in_=w1), reads=R, writes=R)
            p.dve(lambda e: e.tensor_tensor(out=w2, in0=ex, in1=w1, op=ALU.mult), reads=R, writes=R)
            p.dve(lambda e: e.tensor_tensor(out=w1, in0=w1, in1=gtop, op=ALU.mult), reads=R, writes=R)
            p.dve(lambda e: e.tensor_tensor(out=w2, in0=w2, in1=gtop, op=ALU.mult), reads=R, writes=R)
            gt = gates[:, st, :]
            p.dve(lambda e, gt=gt: e.tensor_scalar(out=gt, in0=em, scalar1=top8[:, 0:1], scalar2=w1, op0=ALU.is_equal, op1=ALU.mult),
                  reads=R, writes=[bgates[st]])
            p.dve(lambda e: e.tensor_scalar(out=t2, in0=em, scalar1=top8[:, 1:2], scalar2=w2,
                                            op0=ALU.is_equal, op1=ALU.mult), reads=R, writes=R)
            p.dve(lambda e, gt=gt: e.tensor_tensor(out=gt, in0=gt, in1=t2, op=ALU.add), reads=R + [bgates[st]], writes=[bgates[st]])
        def moe_step(e_i, st, s):
            tsl = slice(st * 128, (st + 1) * 128)

            def A(k):
                r, r3 = k % 2, k % 3
                pa = psA[r]
                for c in range(8):
                    p.pe(lambda e, c=c: e.matmul(pa[:], lhsT=x1T[:, c, tsl], rhs=gub[s][:, c, :], start=(c == 0), stop=(c == 7)),
                         reads=[bx1T[st], bgub[s]], writes=[bpsA[r]])
                p.act(lambda e: e.activation(out=sg[r][:], in_=pa[:, 0:256], func=AF.Silu), reads=[bpsA[r]], writes=[bsg[r]])
                p.dve(lambda e: e.scalar_tensor_tensor(out=hh[r3][:], in0=pa[:, 256:512], scalar=gates[:, st, e_i:e_i + 1], in1=sg[r][:],
                                                       op0=ALU.mult, op1=ALU.mult), reads=[bpsA[r], bsg[r], bgates[st]], writes=[bhh[r3]])

            def B(k):
                r, r3 = k % 2, k % 3
                for kk in range(2):
                    p.pe(lambda e, kk=kk: e.transpose(out=psT[r][:, kk, :], in_=hh[r3][:, kk * 128:(kk + 1) * 128], identity=identb[:]),
                         reads=[bhh[r3], bidb], writes=[bpsT[r]])
                p.act(lambda e: e.copy(out=hT[r3][:], in_=psT[r][:, 0:2, :]), reads=[bpsT[r]], writes=[bhT[r3]])

            def C(k):
                r, r3 = k % 2, k % 3
                py = psY[r]
                for half in range(2):
                    for kk in range(2):
                        p.pe(lambda e, half=half, kk=kk: e.matmul(py[:, half * 512:(half + 1) * 512], lhsT=hT[r3][:, kk, :],
                                                                 rhs=dnb[s][:, kk, half * 512:(half + 1) * 512], start=(kk == 0), stop=(kk == 1)),
                             reads=[bhT[r3], bdnb[s]], writes=[bpsY[r]])
                a = acc[:, st, :]
                p.dve(lambda e: e.tensor_tensor(out=a, in0=a, in1=py[:], op=ALU.add), reads=[bacc[st], bpsY[r]], writes=[bacc[st]])
            return dict(A=A, B=B, C=C)

        msteps = []
        for e_i in range(16):
            idx = g * 16 + e_i
            for st in range(NST):
                msteps.append((idx, st, moe_step(e_i, st, idx % 2)))
        nst_ = len(msteps)
        ld_at = min(2, NST - 1)
        for k in range(nst_ + 2):
            if k < nst_:
                idx, st, sd = msteps[k]
                if st == ld_at and idx + 1 < len(expert_steps):
                    load_expert(idx + 1)
                sd["A"](mk0 + k)
            if 0 <= k - 1 < nst_:
                msteps[k - 1][2]["B"](mk0 + k - 1)
            if 0 <= k - 2 < nst_:
                msteps[k - 2][2]["C"](mk0 + k - 2)
        mk0 += nst_
        for st in range(NST):
            a = acc[:, st, :]
            o = xout[st % 2]
            layer_norm_tm(p, a, o[:], lns[:, 2, :], lns[:, 3, :], bacc[st], bxout[st % 2], bln, (stats, mv, rstd), btmp, LN_EPS, "ln2")
            store_out(res, g, st, o, bxout[st % 2])

import math

RMS_EPS = 1e-6
TWO_PI = 2.0 * math.pi


def s5_lambda(p, pre, shape, ar, ai, ldt, breads, T=None, extra=()):
    F = shape[1]
    if T is None:
        T = p.sb(pre + "_t", [128, 8, F], F32)
    Ti = p.sb(pre + "_ti", [128, F], I32)
    b = p.buf(pre)
    dtt, mag, th, t, kf, r, c1, s = [T[:, i, :] for i in range(8)]
    lre = p.sb(pre + "_lre", [128, F], F32)
    lim = p.sb(pre + "_lim", [128, F], F32)
    R = [b] + list(extra)
    p.act(lambda e: e.activation(out=dtt, in_=ldt, func=AF.Exp), reads=breads, writes=R)
    p.dve(lambda e: e.tensor_tensor(out=mag, in0=dtt, in1=ar, op=ALU.mult), reads=R + breads, writes=R)
    p.act(lambda e: e.activation(out=mag, in_=mag, func=AF.Exp), reads=R, writes=R)
    p.dve(lambda e: e.tensor_tensor(out=th, in0=dtt, in1=ai, op=ALU.mult), reads=R + breads, writes=R)
    for which, dst in ((0, lim), (1, lre)):
        p.dve(lambda e, which=which: e.tensor_scalar(out=t, in0=th, scalar1=1.0 / TWO_PI, scalar2=0.25 * which,
                                                     op0=ALU.mult, op1=ALU.add), reads=R, writes=R)
        p.dve(lambda e: e.tensor_copy(out=Ti[:], in_=t), reads=R, writes=R)
        p.dve(lambda e: e.tensor_copy(out=kf, in_=Ti[:]), reads=R, writes=R)
        p.dve(lambda e: e.tensor_tensor(out=r, in0=t, in1=kf, op=ALU.subtract), reads=R, writes=R)
        p.dve(lambda e: e.tensor_scalar(out=c1, in0=r, scalar1=0.5, scalar2=None, op0=ALU.is_gt), reads=R, writes=R)
        p.dve(lambda e: e.tensor_tensor(out=r, in0=r, in1=c1, op=ALU.subtract), reads=R, writes=R)
        p.dve(lambda e: e.tensor_scalar(out=c1, in0=r, scalar1=-0.5, scalar2=None, op0=ALU.is_lt), reads=R, writes=R)
        p.dve(lambda e: e.tensor_tensor(out=r, in0=r, in1=c1, op=ALU.add), reads=R, writes=R)
        p.act(lambda e: e.activation(out=s, in_=r, func=AF.Sin, scale=TWO_PI), reads=R, writes=R)
        p.dve(lambda e, dst=dst: e.tensor_tensor(out=dst[:], in0=s, in1=mag, op=ALU.mult), reads=R, writes=R)
    return lre, lim, b


def build_mixa(L, TS5=2048):
    nc = bass.Bass("TRN2", target_bir_lowering=False)
    yaT_d = nc.dram_tensor("yaT", [128, L], BF16, kind="ExternalOutput").ap()
    ysT_d = nc.dram_tensor("ysT", [128, L], BF16, kind="ExternalOutput").ap()
    p = Prog(nc)

    def emit_ya(res, ti, t, b):
        p.dma(yaT_d[:, ti * 512:(ti + 1) * 512], t[:], reads=[b])

    def emit_ys(res, ti, t, b):
        p.dma(ysT_d[:, ti * 512:(ti + 1) * 512], t[:], reads=[b])

    phase_mixa(p, nc, "", L, emit_ya, emit_ys, TS5)
    p.finish()
    return nc


def phase_mixa(p, nc, pre, L, emit_ya, emit_ys, TS5=2048):
    xT = nc.dram_tensor(pre + "xT", [1024, L], F32, kind="ExternalInput").ap()
    wA_d = nc.dram_tensor(pre + "wA", [1024, 640], F32, kind="ExternalInput").ap()
    lbl_d = nc.dram_tensor(pre + "lbl", [128, 3], F32, kind="ExternalInput").ap()
    ng_d = nc.dram_tensor(pre + "ng", [128, 128], F32, kind="ExternalInput").ap()
    sps_d = nc.dram_tensor(pre + "sps", [128, 3, 4], F32, kind="ExternalInput").ap()
    spw_d = nc.dram_tensor(pre + "spw", [128, 3, 512], F32, kind="ExternalInput").ap()
    bpad_d = nc.dram_tensor(pre + "bpad", [128, 2, 512], F32, kind="ExternalInput").ap()
    cpad_d = nc.dram_tensor(pre + "cpad", [128, 2, 512], F32, kind="ExternalInput").ap()
    dsk_d = nc.dram_tensor(pre + "dsk", [128, 1], F32, kind="ExternalInput").ap()
    tri_d = nc.dram_tensor(pre + "tri", [128, 128], F32, kind="ExternalInput").ap()
    m01_d = nc.dram_tensor(pre + "m01", [128, 512], F32, kind="ExternalInput").ap()
    res = {}
    identf, bidf = make_ident(p, F32, "identf")
    wA = p.sb("wA", [128, 8, 640], BF16)
    bwA = p.buf()
    p.dma(wA[:], wA_d.rearrange("(c p) f -> p c f", p=128), writes=[bwA], eng="pool")
    cst = p.sb("cst", [128, 3 + 128 + 12 + 1 + 128 + 512 + 8], F32)
    bc = p.buf("cst")
    lbl = cst[:, 0:3]
    ng = cst[:, 3:131]
    sps = cst[:, 131:143].rearrange("p (a b) -> p a b", a=3)
    dsk = cst[:, 143:144]
    tri = cst[:, 144:272]
    m01 = cst[:, 272:784]
    misc = cst[:, 784:792]
    p.dma(lbl, lbl_d, writes=[bc])
    p.dma(ng, ng_d, writes=[bc])
    p.dma(sps, sps_d, writes=[bc])
    p.dma(dsk, dsk_d, writes=[bc])
    p.dma(tri, tri_d, writes=[bc])
    p.dma(m01, m01_d, writes=[bc])
    spw = p.sb("spw", [128, 3, 512], F32)
    p.dma(spw[:], spw_d, writes=[bc])
    bpad = p.sb("bpad", [128, 2, 512], F32)
    p.dma(bpad[:], bpad_d, writes=[bc])
    cpad = p.sb("cpad", [128, 2, 512], F32)
    p.dma(cpad[:], cpad_d, writes=[bc])
    lbe = misc[:, 0:3]
    lbs = misc[:, 3:4]
    lb = misc[:, 4:5]
    oml = misc[:, 5:6]
    bm = p.buf("misc")
    p.act(lambda e: e.activation(out=lbe, in_=lbl, func=AF.Exp, accum_out=lbs), reads=[bc], writes=[bm])
    p.dve(lambda e: e.reciprocal(out=lbs, in_=lbs), reads=[bm], writes=[bm])
    p.dve(lambda e: e.tensor_tensor(out=lb, in0=lbe[:, 0:1], in1=lbs, op=ALU.mult), reads=[bm], writes=[bm])
    p.dve(lambda e: e.tensor_scalar(out=oml, in0=lb, scalar1=-1.0, scalar2=1.0, op0=ALU.mult, op1=ALU.add), reads=[bm], writes=[bm])

    dre = p.sb("dre", [128, 4, TS5], F32)
    dim_ = p.sb("dim", [128, 4, TS5], F32)
    bd = [p.buf("d%d" % i) for i in range(4)]
    assert TS5 >= 1024
    ls_re, ls_im, bls = s5_lambda(p, "ls", [128, 4], sps[:, 0, :], sps[:, 1, :], sps[:, 2, :], [bc])
    lw_re, lw_im, blw = s5_lambda(p, "lw", [128, 512], spw[:, 0, :], spw[:, 1, :], spw[:, 2, :], [bc],
                                  T=dre[:].rearrange("p a t -> p (a t)")[:, 0:4096].rearrange("p (a f) -> p a f", a=8), extra=bd)
    W = dim_[:].rearrange("p a t -> p (a t)")[:, 0:4096].rearrange("p (a f) -> p a f", a=8)
    bW = p.buf("wtmp")
    xr, den, t1, t2, fr, fi, o1, o2 = [W[:, i, :] for i in range(8)]
    arw, aiw = spw[:, 0, :], spw[:, 1, :]
    RW = [bW, blw, bc]
    p.dve(lambda e: e.tensor_scalar(out=xr, in0=lw_re[:], scalar1=-1.0, scalar2=None, op0=ALU.add), reads=RW, writes=[bW] + bd)
    p.dve(lambda e: e.tensor_tensor(out=den, in0=arw, in1=arw, op=ALU.mult), reads=RW, writes=[bW])
    p.dve(lambda e: e.tensor_tensor(out=t1, in0=aiw, in1=aiw, op=ALU.mult), reads=RW, writes=[bW])
    p.dve(lambda e: e.tensor_tensor(out=den, in0=den, in1=t1, op=ALU.add), reads=RW, writes=[bW])
    p.dve(lambda e: e.reciprocal(out=den, in_=den), reads=RW, writes=[bW])
    p.dve(lambda e: e.tensor_tensor(out=t1, in0=xr, in1=arw, op=ALU.mult), reads=RW, writes=[bW])
    p.dve(lambda e: e.tensor_tensor(out=t2, in0=lw_im[:], in1=aiw, op=ALU.mult), reads=RW, writes=[bW])
    p.dve(lambda e: e.tensor_tensor(out=fr, in0=t1, in1=t2, op=ALU.add), reads=RW, writes=[bW])
    p.dve(lambda e: e.tensor_tensor(out=fr, in0=fr, in1=den, op=ALU.mult), reads=RW, writes=[bW])
    p.dve(lambda e: e.tensor_tensor(out=t1, in0=lw_im[:], in1=arw, op=ALU.mult), reads=RW, writes=[bW])
    p.dve(lambda e: e.tensor_tensor(out=t2, in0=xr, in1=aiw, op=ALU.mult), reads=RW, writes=[bW])
    p.dve(lambda e: e.tensor_tensor(out=fi, in0=t1, in1=t2, op=ALU.subtract), reads=RW, writes=[bW])
    p.dve(lambda e: e.tensor_tensor(out=fi, in0=fi, in1=den, op=ALU.mult), reads=RW, writes=[bW])
    wB = p.sb("wB", [128, 2, 512], BF16)
    wC = p.sb("wC", [128, 2, 512], BF16)
    bwB = p.buf("wB")
    bre, bim = bpad[:, 0, :], bpad[:, 1, :]
    p.dve(lambda e: e.tensor_tensor(out=o1, in0=fr, in1=bre, op=ALU.mult), reads=RW, writes=[bW])
    p.dve(lambda e: e.tensor_tensor(out=o2, in0=fi, in1=bim, op=ALU.mult), reads=RW, writes=[bW])
    p.dve(lambda e: e.tensor_tensor(out=wB[:, 0, :], in0=o1, in1=o2, op=ALU.subtract), reads=RW, writes=[bwB])
    p.dve(lambda e: e.tensor_tensor(out=o1, in0=fr, in1=bim, op=ALU.mult), reads=RW, writes=[bW])
    p.dve(lambda e: e.tensor_tensor(out=o2, in0=fi, in1=bre, op=ALU.mult), reads=RW, writes=[bW])
    p.dve(lambda e: e.tensor_tensor(out=wB[:, 1, :], in0=o1, in1=o2, op=ALU.add), reads=RW, writes=[bwB])
    p.dve(lambda e: e.tensor_copy(out=wC[:, 0, :], in_=cpad[:, 0, :]), reads=[bc], writes=[bwB])
    p.dve(lambda e: e.tensor_scalar(out=wC[:, 1, :], in0=cpad[:, 1, :], scalar1=-1.0, scalar2=None, op0=ALU.mult), reads=[bc, bW], writes=[bwB] + bd)

    NLEV = 3
    lamp = p.sb("lamp", [128, NLEV, 3, 4], F32)
    ltmp = p.sb("ltmp", [128, 4, 4], F32)
    blam = p.buf("lam")
    RL = [blam, bls]
    p.dve(lambda e: e.tensor_copy(out=lamp[:, 0, 0, :], in_=ls_re[:]), reads=RL, writes=[blam])
    p.dve(lambda e: e.tensor_copy(out=lamp[:, 0, 1, :], in_=ls_im[:]), reads=RL, writes=[blam])
    for lev in range(1, NLEV):
        p.dve(lambda e, lev=lev: e.tensor_copy(out=lamp[:, lev, 0:2, :], in_=lamp[:, lev - 1, 0:2, :]), reads=RL, writes=[blam])
        for _ in range(4):
            a = lamp[:, lev, 0, :]
            b_ = lamp[:, lev, 1, :]
            p.dve(lambda e, a=a: e.tensor_tensor(out=ltmp[:, 0, :], in0=a, in1=a, op=ALU.mult), reads=RL, writes=[blam])
            p.dve(lambda e, b_=b_: e.tensor_tensor(out=ltmp[:, 1, :], in0=b_, in1=b_, op=ALU.mult), reads=RL, writes=[blam])
            p.dve(lambda e, a=a, b_=b_: e.tensor_tensor(out=ltmp[:, 2, :], in0=a, in1=b_, op=ALU.mult), reads=RL, writes=[blam])
            p.dve(lambda e, a=a: e.tensor_tensor(out=a, in0=ltmp[:, 0, :], in1=ltmp[:, 1, :], op=ALU.subtract), reads=RL, writes=[blam])
            p.dve(lambda e, b_=b_: e.tensor_scalar(out=b_, in0=ltmp[:, 2, :], scalar1=2.0, scalar2=None, op0=ALU.mult), reads=RL, writes=[blam])
    for lev in range(NLEV):
        p.dve(lambda e, lev=lev: e.tensor_scalar(out=lamp[:, lev, 2, :], in0=lamp[:, lev, 1, :], scalar1=-1.0, scalar2=None, op0=ALU.mult),
              reads=RL, writes=[blam])

    xt = [p.sb("xt%d" % i, [128, 8, 512], BF16) for i in range(2)]
    bxt = [p.buf() for _ in range(2)]
    H = p.sb("hg", [128, 9, 512], F32)
    bH = p.buf("hg")
    f_, lf, kk, bb, eb, enb, qq, kinv, ktT = [H[:, i, :] for i in range(9)]
    qdec = p.sb("qdec", [128, 512], BF16)
    kinvb = p.sb("kinvb", [128, 512], BF16)
    bqk = p.buf("qk")
    vb = p.sb("vb", [128, 4, 128], BF16)
    gn = p.sb("gn", [128, 4, 128], F32)
    bvg = [p.buf() for _ in range(4)]
    kt = p.sb("kt", [128, 4, 128], BF16)
    bkt = [p.buf() for _ in range(4)]
    attT = [p.sb("attT%d" % i, [128, 128], BF16) for i in range(2)]
    battT = [p.buf() for _ in range(2)]
    yasb = [p.sb("yasb%d" % i, [128, 4, 128], F32) for i in range(2)]
    yaTs = [p.sb("yaTs%d" % i, [128, 512], BF16) for i in range(2)]
    byaTs = [p.buf() for _ in range(2)]
    byasb = [p.buf() for _ in range(2)]
    S = p.sb("S", [128, 128], F32)
    bS = p.buf("S")
    Sb = [p.sb("Sb%d" % i, [128, 128], BF16) for i in range(2)]
    bSb = [p.buf() for _ in range(2)]
    osc = p.sb("osc", [128, 128], F32)
    om = p.sb("om", [128, 2], F32)
    bo = p.buf("o")
    p.dve(lambda e: e.memset(S[:], 0.0), writes=[bS])
    p.dve(lambda e: e.memset(Sb[1][:], 0.0), writes=[bSb[1]])
    NB1 = TS5 // 16
    assert NB1 % 16 == 0 or NB1 <= 16
    uT = p.sb("uT", [128, TS5], F32)
    buT = p.buf("uT")
    uTb = [p.sb("uTb%d" % i, [128, 512], BF16) for i in range(2)]
    buTb = [p.buf() for _ in range(2)]
    nb_levels = []
    n = TS5
    while n > 16:
        n //= 16
        nb_levels.append(n)
    Ebufs = []
    for li, nbl in enumerate(nb_levels):
        Ebufs.append((p.sb("Ere%d" % li, [128, 4, nbl + 1], F32), p.sb("Eim%d" % li, [128, 4, nbl + 1], F32)))
    carry = p.sb("carry", [128, 2, 4], F32)
    p.dve(lambda e: e.memset(carry[:], 0.0), writes=bd)
    stmp = p.sb("stmp", [128, 4, 2, max(NB1, 16)], F32)
    hb = [p.sb("hb%d" % i, [128, 2, 4, 512], BF16) for i in range(2)]
    bhb = [p.buf() for _ in range(2)]
    zs = p.sb("zs", [128, 512], F32)
    bzs = p.buf()
    ysb = [p.sb("ysb%d" % i, [128, 512], BF16) for i in range(2)]
    bysb = [p.buf() for _ in range(2)]

    psQ = p.ps("psQ", [128, 512], F32); bpsQ = p.pbuf()
    psF = p.ps("psF", [128, 512], F32); bpsF = p.pbuf()
    psU = p.ps("psU", [128, 512], F32); bpsU = p.pbuf()
    psVG = p.ps("psVG", [128, 2, 256], F32); _b = p.pbuf(); bpsVG = [_b, _b]
    psD1 = p.ps("psD", [128, 512], F32); psD = [psD1, psD1]; _b = p.pbuf(); bpsD = [_b, _b]
    psS = p.ps("psS", [128, 4, 128], F32); _b = p.pbuf(); bpsS = [_b, _b]
    psO = p.ps("psO", [128, 4, 128], F32); _b = p.pbuf(); bpsO = [_b, _b]
    psM = p.ps("psM", [128, 4, 128], F32); _b = p.pbuf(); bpsM = [_b] * 4

    def cstep(i, lev, dst_re, dst_im, prev_re, prev_im, n, add_re=None, add_im=None):
        ar = lamp[:, lev, 0, i:i + 1]
        ai = lamp[:, lev, 1, i:i + 1]
        nai = lamp[:, lev, 2, i:i + 1]
        if add_re is None:
            add_re, add_im = dst_re, dst_im
        ta = stmp[:, i, 0, 0:n]
        tb = stmp[:, i, 1, 0:n]
        R = [bd[i], blam]
        p.dve(lambda e: e.scalar_tensor_tensor(out=ta, in0=prev_im, scalar=nai, in1=add_re, op0=ALU.mult, op1=ALU.add), reads=R, writes=[bd[i]])
        p.dve(lambda e: e.scalar_tensor_tensor(out=tb, in0=prev_re, scalar=ai, in1=add_im, op0=ALU.mult, op1=ALU.add), reads=R, writes=[bd[i]])
        p.dve(lambda e: e.scalar_tensor_tensor(out=dst_re, in0=prev_re, scalar=ar, in1=ta, op0=ALU.mult, op1=ALU.add), reads=R, writes=[bd[i]])
        p.dve(lambda e: e.scalar_tensor_tensor(out=dst_im, in0=prev_im, scalar=ar, in1=tb, op0=ALU.mult, op1=ALU.add), reads=R, writes=[bd[i]])

    def cscan(lev, Xre, Xim, n, hin_re, hin_im):
        if n <= 16:
            for t in range(n):
                for i in range(4):
                    pr = hin_re(i) if t == 0 else Xre(i)[:, t - 1:t]
                    pi_ = hin_im(i) if t == 0 else Xim(i)[:, t - 1:t]
                    cstep(i, lev, Xre(i)[:, t:t + 1], Xim(i)[:, t:t + 1], pr, pi_, 1)
            return
        nb = n // 16
        Ere, Eim = Ebufs[lev]
        for i in range(4):
            p.dve(lambda e, i=i: e.tensor_copy(out=Ere[:, i, 0:1], in_=hin_re(i)), reads=[bd[i]], writes=[bd[i]])
            p.dve(lambda e, i=i: e.tensor_copy(out=Eim[:, i, 0:1], in_=hin_im(i)), reads=[bd[i]], writes=[bd[i]])
            p.dve(lambda e, i=i: e.tensor_copy(out=Ere[:, i, 1:nb + 1], in_=Xre(i)[:, 0:n:16]), reads=[bd[i]], writes=[bd[i]])
            p.dve(lambda e, i=i: e.tensor_copy(out=Eim[:, i, 1:nb + 1], in_=Xim(i)[:, 0:n:16]), reads=[bd[i]], writes=[bd[i]])
        for r in range(1, 16):
            for i in range(4):
                cstep(i, lev, Ere[:, i, 1:nb + 1], Eim[:, i, 1:nb + 1], Ere[:, i, 1:nb + 1], Eim[:, i, 1:nb + 1], nb,
                      add_re=Xre(i)[:, r:n:16], add_im=Xim(i)[:, r:n:16])
        cscan(lev + 1, lambda i: Ere[:, i, 1:nb + 1], lambda i: Eim[:, i, 1:nb + 1], nb,
              lambda i: Ere[:, i, 0:1], lambda i: Eim[:, i, 0:1])
        for r in range(16):
            for i in range(4):
                if r == 0:
                    pr, pi_ = Ere[:, i, 0:nb], Eim[:, i, 0:nb]
                else:
                    pr, pi_ = Xre(i)[:, r - 1:n:16], Xim(i)[:, r - 1:n:16]
                cstep(i, lev, Xre(i)[:, r:n:16], Xim(i)[:, r:n:16], pr, pi_, nb)

    NTILE = L // 512
    TPS = TS5 // 512

    def load_x(ti):
        s = ti % 2
        p.dma(xt[s][:], xT[:, ti * 512:(ti + 1) * 512].rearrange("(c p) t -> p c t", p=128), writes=[bxt[s]], eng="pool")

    load_x(0)
    chunk_idx = 0
    for ti in range(NTILE):
        s = ti % 2
        x_ = xt[s]
        if ti + 1 < NTILE:
            load_x(ti + 1)
        t0 = ti * 512
        tl = (ti % TPS) * 512
        for (ps_, bps_, c0) in ((psQ, bpsQ, 0), (psF, bpsF, 128), (psU, bpsU, 512)):
            for c in range(8):
                p.pe(lambda e, ps_=ps_, c=c, c0=c0, x_=x_: e.matmul(ps_[:], lhsT=wA[:, c, c0:c0 + 128], rhs=x_[:, c, :],
                                                                    start=(c == 0), stop=(c == 7)), reads=[bwA, bxt[s]], writes=[bps_])
        RH = [bH]
        p.act(lambda e: e.activation(out=f_, in_=psF[:], func=AF.Sigmoid), reads=[bpsF], writes=RH)
        p.dve(lambda e: e.tensor_scalar(out=f_, in0=f_, scalar1=oml, scalar2=lb, op0=ALU.mult, op1=ALU.add), reads=RH + [bm], writes=RH)
        p.act(lambda e: e.activation(out=lf, in_=f_, func=AF.Ln), reads=RH, writes=RH)
        p.dve(lambda e: e.tensor_scalar(out=kk, in0=f_, scalar1=-1.0, scalar2=1.0, op0=ALU.mult, op1=ALU.add), reads=RH, writes=RH)
        p.dve(lambda e: e.tensor_tensor_scan(out=bb, data0=m01, data1=lf, initial=0.0, op0=ALU.mult, op1=ALU.add), reads=RH + [bc], writes=RH)
        p.act(lambda e: e.activation(out=eb, in_=bb, func=AF.Exp), reads=RH, writes=RH)
        p.act(lambda e: e.activation(out=enb, in_=bb, func=AF.Exp, scale=-1.0), reads=RH, writes=RH)
        p.act(lambda e: e.activation(out=qq, in_=psQ[:], func=AF.Silu), reads=[bpsQ], writes=RH)
        p.dve(lambda e: e.tensor_tensor(out=qdec[:], in0=qq, in1=eb, op=ALU.mult), reads=RH, writes=[bqk])
        p.dve(lambda e: e.tensor_tensor(out=kinv, in0=kk, in1=enb, op=ALU.mult), reads=RH, writes=RH)
        p.act(lambda e: e.copy(out=kinvb[:], in_=kinv), reads=RH, writes=[bqk])
        eb3 = eb.rearrange("p (c s) -> p c s", s=64)
        p.dve(lambda e: e.tensor_tensor(out=ktT.rearrange("p (c s) -> p c s", s=64), in0=kinv.rearrange("p (c s) -> p c s", s=64),
                                        in1=eb3[:, :, 63:64].to_broadcast([128, 8, 64]), op=ALU.mult), reads=RH, writes=RH)
        sb5 = ti % 2
        p.act(lambda e, tl=tl: e.copy(out=uT[:, tl:tl + 512], in_=psU[:]), reads=[bpsU], writes=[buT])
        p.dve(lambda e, sb5=sb5: e.tensor_copy(out=uTb[sb5][:], in_=psU[:]), reads=[bpsU], writes=[buTb[sb5]])
        k = 0
        for i in range(4):
            for ri in range(2):
                pd = psD[k % 2]
                p.pe(lambda e, pd=pd, i=i, ri=ri, sb5=sb5: e.matmul(pd[:], lhsT=wB[:, ri, i * 128:(i + 1) * 128], rhs=uTb[sb5][:],
                                                                    start=True, stop=True), reads=[bwB, buTb[sb5]], writes=[bpsD[k % 2]])
                dst = (dre if ri == 0 else dim_)[:, i, tl:tl + 512]
                p.act(lambda e, pd=pd, dst=dst: e.copy(out=dst, in_=pd[:]), reads=[bpsD[k % 2]], writes=[bd[i]])
                k += 1
        for sub in range(4):
            tsl = slice(sub * 128, (sub + 1) * 128)
            h2 = sub % 2
            for c in range(8):
                p.pe(lambda e, c=c, tsl=tsl, h2=h2, x_=x_: e.matmul(psVG[:, h2, :], lhsT=x_[:, c, tsl], rhs=wA[:, c, 256:512],
                                                                    start=(c == 0), stop=(c == 7)), reads=[bwA, bxt[s]], writes=[bpsVG[h2]])
            p.act(lambda e, sub=sub, h2=h2: e.copy(out=vb[:, sub, :], in_=psVG[:, h2, 0:128]), reads=[bpsVG[h2]], writes=[bvg[sub]])
            p.act(lambda e, sub=sub, h2=h2: e.activation(out=gn[:, sub, :], in_=psVG[:, h2, 128:256], func=AF.Silu), reads=[bpsVG[h2]], writes=[bvg[sub]])
            p.dve(lambda e, sub=sub: e.tensor_tensor(out=gn[:, sub, :], in0=gn[:, sub, :], in1=ng, op=ALU.mult), reads=[bvg[sub], bc], writes=[bvg[sub]])
            mi = sub % 2
            p.pe(lambda e, mi=mi, tsl=tsl: e.transpose(out=psM[:, mi, :], in_=ktT[:, tsl], identity=identf[:]), reads=RH + [bidf], writes=[bpsM[mi]])
            p.act(lambda e, mi=mi, sub=sub: e.copy(out=kt[:, sub, :], in_=psM[:, mi, :]), reads=[bpsM[mi]], writes=[bkt[sub]])
        ys_ = yasb[ti % 2]
        for sub in range(4):
            tsl = slice(sub * 128, (sub + 1) * 128)
            ai_ = sub % 2
            mi = 2 + sub % 2
            p.pe(lambda e, mi=mi, tsl=tsl: e.matmul(psM[:, mi, :], lhsT=kinvb[:, tsl], rhs=qdec[:, tsl], start=True, stop=True),
                 reads=[bqk], writes=[bpsM[mi]])
            p.dve(lambda e, mi=mi, ai_=ai_: e.tensor_tensor(out=attT[ai_][:], in0=psM[:, mi, :], in1=tri, op=ALU.mult),
                  reads=[bpsM[mi], bc], writes=[battT[ai_]])
            oi = sub % 2
            p.pe(lambda e, oi=oi, ai_=ai_, sub=sub: e.matmul(psO[:, oi, :], lhsT=attT[ai_][:], rhs=vb[:, sub, :], start=True, stop=False),
                 reads=[battT[ai_], bvg[sub]], writes=[bpsO[oi]])
            for hc in range(2):
                rows = slice(hc * 64, (hc + 1) * 64)
                tch = slice(sub * 128 + hc * 64, sub * 128 + (hc + 1) * 64)
                sprev = (chunk_idx + 1) % 2
                snew = chunk_idx % 2
                p.pe(lambda e, oi=oi, rows=rows, tch=tch, sprev=sprev, hc=hc: e.matmul(
                    psO[rows, oi, :], lhsT=qdec[:, tch], rhs=Sb[sprev][:], start=False, stop=(hc == 1)),
                    reads=[bqk, bSb[sprev]], writes=[bpsO[oi]])
                di = chunk_idx % 2
                p.pe(lambda e, di=di, rows=rows, sub=sub: e.matmul(psS[:, di, :], lhsT=kt[rows, sub, :], rhs=vb[rows, sub, :], start=True, stop=True),
                     reads=[bkt[sub], bvg[sub]], writes=[bpsS[di]])
                cpos = sub * 128 + hc * 64 + 63
                p.dve(lambda e, di=di, cpos=cpos: e.scalar_tensor_tensor(out=S[:], in0=S[:], scalar=eb[:, cpos:cpos + 1], in1=psS[:, di, :],
                                                                         op0=ALU.mult, op1=ALU.add), reads=[bS, bpsS[di]] + RH, writes=[bS])
                p.act(lambda e, snew=snew: e.copy(out=Sb[snew][:], in_=S[:]), reads=[bS], writes=[bSb[snew]])
                chunk_idx += 1
            p.act(lambda e, oi=oi: e.activation(out=osc[:], in_=psO[:, oi, :], func=AF.Square, accum_out=om[:, 0:1]), reads=[bpsO[oi]], writes=[bo])
            p.act(lambda e: e.activation(out=om[:, 1:2], in_=om[:, 0:1], func=AF.Sqrt, scale=1.0 / 128.0, bias=RMS_EPS), reads=[bo], writes=[bo])
            p.dve(lambda e: e.reciprocal(out=om[:, 1:2], in_=om[:, 1:2]), reads=[bo], writes=[bo])
            p.dve(lambda e, oi=oi, sub=sub, ys_=ys_: e.scalar_tensor_tensor(out=ys_[:, sub, :], in0=psO[:, oi, :], scalar=om[:, 1:2], in1=gn[:, sub, :],
                                                                         op0=ALU.mult, op1=ALU.mult), reads=[bpsO[oi], bo, bvg[sub]], writes=[byasb[ti % 2]])
        for sub in range(4):
            p.pe(lambda e, sub=sub, ys_=ys_: e.transpose(out=psM[:, sub, :], in_=ys_[:, sub, :], identity=identf[:]),
                 reads=[byasb[ti % 2], bidf], writes=[bpsM[0]])
        yt_ = yaTs[ti % 2]
        p.act(lambda e, yt_=yt_: e.copy(out=yt_[:], in_=psM[:].rearrange("p a b -> p (a b)")), reads=[bpsM[0]], writes=[byaTs[ti % 2]])
        emit_ya(res, ti, yt_, byaTs[ti % 2])
        if (ti + 1) % TPS == 0:
            sup0 = (ti + 1 - TPS) * 512
            cscan(0, lambda i: dre[:, i, :], lambda i: dim_[:, i, :], TS5,
                  lambda i: carry[:, 0, i:i + 1], lambda i: carry[:, 1, i:i + 1])
            for i in range(4):
                p.dve(lambda e, i=i: e.tensor_copy(out=carry[:, 0, i:i + 1], in_=dre[:, i, TS5 - 1:TS5]), reads=[bd[i]], writes=[bd[i]])
                p.dve(lambda e, i=i: e.tensor_copy(out=carry[:, 1, i:i + 1], in_=dim_[:, i, TS5 - 1:TS5]), reads=[bd[i]], writes=[bd[i]])
            for ch in range(TPS):
                csl = slice(ch * 512, (ch + 1) * 512)
                hs = ch % 2
                for i in range(4):
                    p.act(lambda e, i=i, hs=hs, csl=csl: e.copy(out=hb[hs][:, 0, i, :], in_=dre[:, i, csl]), reads=[bd[i]], writes=[bhb[hs]])
                    p.act(lambda e, i=i, hs=hs, csl=csl: e.copy(out=hb[hs][:, 1, i, :], in_=dim_[:, i, csl]), reads=[bd[i]], writes=[bhb[hs]])
                k = 0
                for i in range(4):
                    for ri in range(2):
                        p.pe(lambda e, i=i, ri=ri, hs=hs, k=k: e.matmul(psQ[:], lhsT=wC[:, ri, i * 128:(i + 1) * 128], rhs=hb[hs][:, ri, i, :],
                                                                        start=(k == 0), stop=(k == 7)), reads=[bwB, bhb[hs]], writes=[bpsQ])
                        k += 1
                p.dve(lambda e, csl=csl: e.scalar_tensor_tensor(out=zs[:], in0=uT[:, csl], scalar=dsk, in1=psQ[:], op0=ALU.mult, op1=ALU.add),
                      reads=[buT, bc, bpsQ], writes=[bzs])
                p.act(lambda e, hs=hs: e.activation(out=ysb[hs][:], in_=zs[:], func=AF.Gelu_apprx_tanh), reads=[bzs], writes=[bysb[hs]])
                emit_ys(res, (sup0 // 512) + ch, ysb[hs], bysb[hs])


BIGM = 30000.0
DEBUG_MIXC = False


def build_mixc(L):
    nc = bass.Bass("TRN2", target_bir_lowering=False)
    xT = nc.dram_tensor("xT", [1024, L], F32, kind="ExternalInput").ap()
    ycT_d = nc.dram_tensor("ycT", [128, L], BF16, kind="ExternalOutput").ap()
    ydT_d = nc.dram_tensor("ydT", [128, L], BF16, kind="ExternalOutput").ap()
    p = Prog(nc)
    p.debug = DEBUG_MIXC

    def load_x(res, ti, dst, bdst):
        p.dma(dst[:], xT[:, ti * 512:(ti + 1) * 512].rearrange("(c p) t -> p c t", p=128), writes=[bdst], eng="pool")

    def emit_y(res, ti, which, t, b):
        d = ycT_d if which == 0 else ydT_d
        p.dma(d[:, ti * 512:(ti + 1) * 512], t[:], reads=[b])

    phase_mixc(p, nc, "", L, load_x, emit_y)
    p.finish()
    return nc


def phase_mixc(p, nc, pre, L, load_x_cb, emit_y):
    NBLK = L // 256
    wC_d = nc.dram_tensor(pre + "wC", [1024, 640], F32, kind="ExternalInput").ap()
    sink_d = nc.dram_tensor(pre + "sink", [128, 2], F32, kind="ExternalInput").ap()
    swm_d = nc.dram_tensor(pre + "swm", [128, 2, 256], F32, kind="ExternalInput").ap()
    cm_d = nc.dram_tensor(pre + "cm", [128, 2, 256], F32, kind="ExternalInput").ap()
    fut_d = nc.dram_tensor(pre + "fut", [128, 2, 128], F32, kind="ExternalInput").ap()
    res = {}
    identb_f, bidf = make_ident(p, F32, "identf")
    identb = p.sb("identb", [128, 128], BF16)
    bidb = p.buf()
    p.dve(lambda e: e.tensor_copy(out=identb[:], in_=identb_f[:]), reads=[bidf], writes=[bidb])
    wC = p.sb("wC", [128, 8, 640], BF16)
    bwC = p.buf()
    p.dma(wC[:], wC_d.rearrange("(c p) f -> p c f", p=128), writes=[bwC], eng="pool")
    cst = p.sb("cst", [128, 2 + 512 + 512 + 256], F32)
    bc = p.buf("cst")
    sink = cst[:, 0:2]
    swm = cst[:, 2:514].rearrange("p (a k) -> p a k", a=2)
    cm = cst[:, 514:1026].rearrange("p (a k) -> p a k", a=2)
    fut = cst[:, 1026:1282].rearrange("p (a k) -> p a k", a=2)
    p.dma(sink, sink_d, writes=[bc])
    p.dma(swm, swm_d, writes=[bc])
    p.dma(cm, cm_d, writes=[bc])
    p.dma(fut, fut_d, writes=[bc])
    ones = p.sb("ones", [128, 128], F32)
    bones = p.buf()
    p.dve(lambda e: e.memset(ones[:], 1.0), writes=[bones])

    qcT = p.sb("qcT", [128, 512], BF16); bqc = p.buf()
    qdT = p.sb("qdT", [128, 512], BF16); bqd = p.buf()
    kkc = p.sb("kkc", [128, L], BF16)
    kkd = p.sb("kkd", [128, L], BF16)
    vc = p.sb("vc", [128, L // 128, 64], BF16)
    vd = p.sb("vd", [128, L // 128, 64], BF16)
    bkv = p.buf("kv")
    bkvt = [p.buf("kv%d" % i) for i in range(L // 512)]
    kmT = p.sb("kmT", [128, 64], BF16)
    bkm = p.buf("km")
    p.dve(lambda e: e.memset(kmT[:], 0.0), writes=[bkm])
    kmx = p.sb("kmx", [128, 4], F32)
    p.dve(lambda e: e.memset(kmx[:], 0.0), writes=[bkm])
    xt = [p.sb("xt%d" % i, [128, 8, 512], BF16) for i in range(2)]
    bxt = [p.buf() for _ in range(2)]
    sq = p.sb("sq", [128, 512], F32); bsq = p.buf()
    qab = p.sb("qab", [128, 512], BF16)
    bqab = p.buf()
    kab = p.sb("kab", [128, 2], BF16)
    k8 = p.sb("k8", [128, 8], F32)
    sm = [p.sb("sm%d" % i, [128, 256], F32) for i in range(2)]; bsm = [p.buf() for _ in range(2)]
    Pb = [p.sb("Pb%d" % i, [128, 256], BF16) for i in range(4)]; bPb = [p.buf() for _ in range(4)]
    PT = [p.sb("PT%d" % i, [128, 2, 128], BF16) for i in range(4)]; bPT = [p.buf() for _ in range(4)]
    scS = [p.sb("scS%d" % i, [128, 8], F32) for i in range(6)]; bscS = [p.buf() for _ in range(6)]
    gset = []
    for i in range(5):
        gset.append(dict(sc=p.sb("scg%d" % i, [128, 8], F32), gm=p.sb("gm%d" % i, [128, 64], F32), sbm=p.sb("sbm%d" % i, [128, 64], F32),
                         top8=p.sb("top8_%d" % i, [128, 8], F32), lcols=p.sb("lcols%d" % i, [128, 66], F32), b=p.buf("gate%d" % i)))
    ycs = [p.sb("ycs%d" % i, [128, 4, 128], F32) for i in range(2)]; bycs = [p.buf() for _ in range(2)]
    yds = [p.sb("yds%d" % i, [128, 4, 128], F32) for i in range(2)]; byds = [p.buf() for _ in range(2)]
    yTs = [p.sb("yTs%d" % i, [128, 512], BF16) for i in range(4)]; byTs = [p.buf() for _ in range(4)]

    _psP = p.ps("psP", [128, 512], F32); _bpsP = p.pbuf(); psP = [_psP, _psP]; bpsP = [_bpsP, _bpsP]
    psV = p.ps("psV", [128, 512], F32); bpsV = p.pbuf()
    psS = [p.ps("psS%d" % i, [128, 512], F32) for i in range(2)]; bpsS = [p.pbuf() for _ in range(2)]
    psT = [p.ps("psT%d" % i, [128, 8, 128], BF16) for i in range(2)]; bpsT = [p.pbuf() for _ in range(2)]
    psO = [p.ps("psO%d" % i, [128, 512], F32) for i in range(2)]; bpsO = [p.pbuf() for _ in range(2)]

    NTILE = L // 512

    def load_x(ti):
        s = ti % 2
        load_x_cb(res, ti, xt[s], bxt[s])

    load_x(0)
    rot = 0
    kbase = 0
    for ti in range(NTILE):
        s = ti % 2
        x_ = xt[s]
        if ti + 1 < NTILE:
            load_x(ti + 1)
        t0 = ti * 512
        tsl = slice(t0, t0 + 512)
        dsts = [(qcT[:], bqc, 0.125), (kkc[:, tsl], bkvt[ti], 1.0), (qdT[:], bqd, 0.125), (kkd[:, tsl], bkvt[ti], 1.0)]
        for pi, (dst, bdst, scl) in enumerate(dsts):
            pp = psP[pi % 2]
            for c in range(8):
                p.pe(lambda e, pp=pp, c=c, pi=pi, x_=x_: e.matmul(pp[:], lhsT=wC[:, c, pi * 128:(pi + 1) * 128], rhs=x_[:, c, :],
                                                                  start=(c == 0), stop=(c == 7)), reads=[bwC, bxt[s]], writes=[bpsP[pi % 2]])
            p.act(lambda e, pp=pp, dst=dst, scl=scl: e.activation(out=dst, in_=pp[:], func=AF.Copy, scale=scl), reads=[bpsP[pi % 2]], writes=[bdst])
            if pi == 2:
                p.dve(lambda e, pp=pp: e.tensor_scalar(out=sq[:], in0=pp[:], scalar1=-1.0, scalar2=None, op0=ALU.mult), reads=[bpsP[pi % 2]], writes=[bsq])
                p.dve(lambda e, pp=pp: e.tensor_tensor(out=sq[:], in0=sq[:], in1=pp[:], op=ALU.max), reads=[bpsP[pi % 2], bsq], writes=[bsq])
                p.dve(lambda e: e.tensor_scalar(out=qab[:], in0=sq[:], scalar1=0.125, scalar2=None, op0=ALU.mult), reads=[bsq], writes=[bqab])
            if pi == 3:
                p.dve(lambda e, pp=pp: e.tensor_scalar(out=sq[:], in0=pp[:], scalar1=-1.0, scalar2=None, op0=ALU.mult), reads=[bpsP[pi % 2]], writes=[bsq])
                p.dve(lambda e, pp=pp: e.tensor_tensor(out=sq[:], in0=sq[:], in1=pp[:], op=ALU.max), reads=[bpsP[pi % 2], bsq], writes=[bsq])
                p.dve(lambda e: e.max(out=k8[:], in_=sq[:]), reads=[bsq], writes=[bkm])
                p.dve(lambda e: e.tensor_tensor(out=kmx[:, 0:1], in0=kmx[:, 0:1], in1=k8[:, 0:1], op=ALU.max), reads=[bkm], writes=[bkm])
                p.dve(lambda e: e.tensor_copy(out=kab[:, 0:1], in_=kmx[:, 0:1]), reads=[bkm], writes=[bkm])
                p.dve(lambda e: e.tensor_copy(out=kab[:, 1:2], in_=kmx[:, 0:1]), reads=[bkm], writes=[bkm])
        for sub in range(4):
            g = ti * 4 + sub
            for c in range(8):
                p.pe(lambda e, c=c, sub=sub, x_=x_: e.matmul(psV[:, 0:128], lhsT=x_[:, c, sub * 128:(sub + 1) * 128], rhs=wC[:, c, 512:640],
                                                             start=(c == 0), stop=(c == 7)), reads=[bwC, bxt[s]], writes=[bpsV])
            p.act(lambda e, g=g: e.copy(out=vc[:, g, :], in_=psV[:, 0:64]), reads=[bpsV], writes=[bkvt[ti]])
            p.act(lambda e, g=g: e.copy(out=vd[:, g, :], in_=psV[:, 64:128]), reads=[bpsV], writes=[bkvt[ti]])
        for hb in range(2):
            blk = ti * 2 + hb
            p.dve(lambda e, blk=blk: e.tensor_reduce(out=sq[:, 0:1], in_=kkd[:, blk * 256:(blk + 1) * 256], axis=AX.X, op=ALU.add),
                  reads=[bkvt[ti]], writes=[bsq])
            p.dve(lambda e, blk=blk: e.tensor_scalar(out=kmT[:, blk:blk + 1], in0=sq[:, 0:1], scalar1=1.0 / 256.0, scalar2=None, op0=ALU.mult),
                  reads=[bsq], writes=[bkm])
        yc_ = ycs[ti % 2]
        yd_ = yds[ti % 2]
        prev_kv = [bkvt[ti - 1]] if ti > 0 else []

        steps = []
        fins = {}

        def swa_step(g, sub, h, yc_=yc_):
            rows = slice(h * 64, (h + 1) * 64)
            qsl = slice(sub * 128, (sub + 1) * 128)
            if g == 0:
                k0, nk, mk = 0, 128, swm[:, 1, 128:256]
            else:
                k0, nk, mk = (g - 1) * 128, 256, swm[:, 1, :]
            nkc = nk // 128
            kvr = [bkvt[ti]] + (prev_kv if sub == 0 else [])

            def A(k):
                r2, r3 = k % 2, k % 4
                ps_, smr, pb = psS[r2], sm[r2], Pb[r3]
                scs, R = scS[k % 6], [bscS[k % 6]]
                mx, nmx, rs, es, den = [scs[:, i:i + 1] for i in range(5)]
                p.pe(lambda e: e.matmul(ps_[:, 0:nk], lhsT=qcT[rows, qsl], rhs=kkc[rows, k0:k0 + nk], start=True, stop=True),
                     reads=[bqc] + kvr, writes=[bpsS[r2]])
                p.dve(lambda e: e.tensor_tensor(out=smr[:, 0:nk], in0=ps_[:, 0:nk], in1=mk, op=ALU.add), reads=[bpsS[r2], bc], writes=[bsm[r2]])
                p.dve(lambda e: e.tensor_reduce(out=mx, in_=smr[:, 0:nk], axis=AX.X, op=ALU.max), reads=[bsm[r2]], writes=R)
                p.dve(lambda e: e.tensor_tensor(out=mx, in0=mx, in1=sink[:, h:h + 1], op=ALU.max), reads=R + [bc], writes=R)
                p.dve(lambda e: e.tensor_scalar(out=nmx, in0=mx, scalar1=-1.0, scalar2=None, op0=ALU.mult), reads=R, writes=R)
                p.act(lambda e: e.activation(out=pb[:, 0:nk], in_=smr[:, 0:nk], func=AF.Exp, bias=nmx, scale=1.0, accum_out=rs),
                      reads=[bsm[r2]] + R, writes=[bPb[r3]] + R)
                p.act(lambda e: e.activation(out=es, in_=sink[:, h:h + 1], func=AF.Exp, bias=nmx, scale=1.0), reads=R + [bc], writes=R)
                p.dve(lambda e: e.tensor_tensor(out=den, in0=rs, in1=es, op=ALU.add), reads=R, writes=R)
                p.dve(lambda e: e.reciprocal(out=den, in_=den), reads=R, writes=R)

            def B(k):
                r2, r3 = k % 2, k % 4
                pb, pt = Pb[r3], PT[r3]
                for kc in range(nkc):
                    p.pe(lambda e, kc=kc: e.transpose(out=psT[r2][:, kc, :], in_=pb[:, kc * 128:(kc + 1) * 128], identity=identb[:]),
                         reads=[bPb[r3], bidb], writes=[bpsT[r2]])
                p.dve(lambda e: e.tensor_copy(out=pt[:, 0:nkc, :], in_=psT[r2][:, 0:nkc, :]), reads=[bpsT[r2]], writes=[bPT[r3]])

            def C(k):
                r2, r3 = k % 4, k % 6
                pt = PT[r2]
                den = scS[r3][:, 4:5]
                for kc in range(nkc):
                    gk = (g - 1 + kc) if g > 0 else 0
                    p.pe(lambda e, kc=kc, gk=gk: e.matmul(psV[:, 256:320], lhsT=pt[:, kc, :], rhs=vc[:, gk, :], start=(kc == 0), stop=(kc == nkc - 1)),
                         reads=[bPT[r2]] + kvr, writes=[bpsV])
                p.act(lambda e: e.activation(out=yc_[:, sub, h * 64:(h + 1) * 64], in_=psV[:, 256:320], func=AF.Copy, scale=den),
                      reads=[bpsV, bscS[r3]], writes=[bycs[ti % 2]])
            return dict(A=A, B=B, C=C)

        def moba_prologue(g, sub, h, gi):
            rows = slice(h * 64, (h + 1) * 64)
            qsl = slice(sub * 128, (sub + 1) * 128)
            n = g // 2
            gs = gset[gi]
            G = [gs["b"]]
            mb, nmb = gs["sc"][:, 0:1], gs["sc"][:, 1:2]
            p.pe(lambda e: e.matmul(psV[:, 0:2], lhsT=qab[rows, qsl], rhs=kab[rows, 0:2], start=True, stop=True), reads=[bqab, bkm], writes=[bpsV])
            p.dve(lambda e: e.tensor_copy(out=mb, in_=psV[:, 0:1]), reads=[bpsV], writes=G)
            p.dve(lambda e: e.tensor_scalar(out=nmb, in0=mb, scalar1=-1.0, scalar2=None, op0=ALU.mult), reads=G, writes=G)
            if n > 0:
                gm_, sbm_, top8_ = gs["gm"], gs["sbm"], gs["top8"]
                p.pe(lambda e: e.matmul(psV[:, 64:128], lhsT=qdT[rows, qsl], rhs=kmT[rows, :], start=True, stop=True), reads=[bqd, bkm], writes=[bpsV])
                fsl = slice(64 - n, 128 - n)
                p.dve(lambda e: e.tensor_tensor(out=gm_[:], in0=psV[:, 64:128], in1=fut[:, 0, fsl], op=ALU.add), reads=[bpsV, bc], writes=G)
                p.dve(lambda e: e.max(out=top8_[:], in_=gm_[:]), reads=G, writes=G)
                p.dve(lambda e: e.tensor_scalar(out=sbm_[:], in0=gm_[:], scalar1=top8_[:, 2:3], scalar2=BIGM, op0=ALU.is_ge, op1=ALU.mult), reads=G, writes=G)
                p.dve(lambda e: e.tensor_tensor(out=sbm_[:], in0=sbm_[:], in1=fut[:, 1, fsl], op=ALU.add), reads=G + [bc], writes=G)
                p.dve(lambda e: e.tensor_scalar(out=sbm_[:], in0=sbm_[:], scalar1=mb, scalar2=None, op0=ALU.subtract), reads=G, writes=G)
            p.dve(lambda e: e.memset(gs["lcols"][:], 0.0), writes=G)

        def moba_step(g, sub, h, gi, jb, oi, yd_=yd_):
            rows = slice(h * 64, (h + 1) * 64)
            qsl = slice(sub * 128, (sub + 1) * 128)
            n, a = g // 2, g % 2
            own = (jb == n)
            gs = gset[gi]
            G = [gs["b"]]
            kreads = [bkvt[jb // 2]]
            po = psO[oi]

            def A(k):
                r2, r3 = k % 2, k % 4
                ps_, pb = psS[r2], Pb[r3]
                p.pe(lambda e: e.matmul(ps_[:, 0:256], lhsT=qdT[rows, qsl], rhs=kkd[rows, jb * 256:(jb + 1) * 256], start=True, stop=True),
                     reads=[bqd] + kreads, writes=[bpsS[r2]])
                if own:
                    smr = sm[r2]
                    p.dve(lambda e: e.tensor_tensor(out=smr[:], in0=ps_[:, 0:256], in1=cm[:, a, :], op=ALU.add), reads=[bpsS[r2], bc], writes=[bsm[r2]])
                    p.act(lambda e: e.activation(out=pb[:], in_=smr[:], func=AF.Exp, bias=gs["sc"][:, 1:2], scale=1.0, accum_out=gs["lcols"][:, 64:65]),
                          reads=[bsm[r2]] + G, writes=[bPb[r3]] + G)
                else:
                    p.act(lambda e: e.activation(out=pb[:], in_=ps_[:, 0:256], func=AF.Exp, bias=gs["sbm"][:, jb:jb + 1], scale=1.0,
                                                 accum_out=gs["lcols"][:, jb:jb + 1]), reads=[bpsS[r2]] + G, writes=[bPb[r3]] + G)

            def B(k):
                r2, r3 = k % 2, k % 4
                pb, pt = Pb[r3], PT[r3]
                for kc in range(2):
                    p.pe(lambda e, kc=kc: e.transpose(out=psT[r2][:, kc, :], in_=pb[:, kc * 128:(kc + 1) * 128], identity=identb[:]),
                         reads=[bPb[r3], bidb], writes=[bpsT[r2]])
                p.dve(lambda e: e.tensor_copy(out=pt[:], in_=psT[r2][:, 0:2, :]), reads=[bpsT[r2]], writes=[bPT[r3]])

            def C(k):
                r2 = k % 4
                pt = PT[r2]
                for kc in range(2):
                    p.pe(lambda e, kc=kc: e.matmul(po[:, 0:64], lhsT=pt[:, kc, :], rhs=vd[:, jb * 2 + kc, :], start=(jb == 0 and kc == 0), stop=(own and kc == 1)),
                         reads=[bPT[r2]] + kreads, writes=[bpsO[oi]])
                if own:
                    lsum = gs["sc"][:, 2:3]
                    p.dve(lambda e: e.tensor_reduce(out=lsum, in_=gs["lcols"][:, 0:65], axis=AX.X, op=ALU.add), reads=G, writes=G)
                    p.dve(lambda e: e.reciprocal(out=lsum, in_=lsum), reads=G, writes=G)
                    p.act(lambda e: e.activation(out=yd_[:, sub, h * 64:(h + 1) * 64], in_=po[:, 0:64], func=AF.Copy, scale=lsum),
                          reads=[bpsO[oi]] + G, writes=[byds[ti % 2]])
            return dict(A=A, B=B, C=C)

        units = [(ti * 4 + sub, sub, h) for sub in range(4) for h in range(2)]
        seq = []
        for ui, (g, sub, h) in enumerate(units):
            gi = (ti * 8 + ui) % 5
            if ui == 0:
                seq.append(("pro", lambda g=g, sub=sub, h=h, gi=gi: moba_prologue(g, sub, h, gi)))
            if ui + 1 < len(units):
                g2, sub2, h2 = units[ui + 1]
                seq.append(("pro", lambda g2=g2, sub2=sub2, h2=h2, gi=gi: moba_prologue(g2, sub2, h2, (gi + 1) % 5)))
            seq.append(("step", swa_step(g, sub, h)))
            for jb in range(g // 2 + 1):
                seq.append(("step", moba_step(g, sub, h, gi, jb, (ti * 8 + ui) % 2)))
        D = 2
        stp = [x[1] for x in seq if x[0] == "step"]
        nst = len(stp)
        k = 0
        for kind, x in seq:
            if kind == "pro":
                x()
                continue
            x["A"](kbase + k)
            if k - D >= 0:
                stp[k - D]["B"](kbase + k - D)
            if k - 2 * D >= 0:
                stp[k - 2 * D]["C"](kbase + k - 2 * D)
            k += 1
        for k in range(nst, nst + 2 * D):
            if 0 <= k - D < nst:
                stp[k - D]["B"](kbase + k - D)
            if 0 <= k - 2 * D < nst:
                stp[k - 2 * D]["C"](kbase + k - 2 * D)
        kbase += nst
        for which, (src, bsrc) in enumerate(((yc_, bycs[ti % 2]), (yd_, byds[ti % 2]))):
            pp = psP[which]
            for sub in range(4):
                p.pe(lambda e, sub=sub, src=src, pp=pp: e.transpose(out=pp[:, sub * 128:(sub + 1) * 128], in_=src[:, sub, :], identity=identb_f[:]),
                     reads=[bsrc, bidf], writes=[bpsP[which]])
            k = (ti % 2) * 2 + which
            p.act(lambda e, k=k, pp=pp: e.copy(out=yTs[k][:], in_=pp[:]), reads=[bpsP[which]], writes=[byTs[k]])
            emit_y(res, ti, which, yTs[k], byTs[k])


GROUPS = [[0, 1, 2, 3], [4, 5, 6, 7]]


def build_fused(L, debug=False):
    SEG = L // 4
    NCH = SEG // 512
    TG = min(1024, SEG)
    CPG = TG // 512
    nc = bass.Bass("TRN2", target_bir_lowering=False)
    msk_d = nc.dram_tensor("rankmask", [128, 4], F32, kind="ExternalInput").ap()
    xres_d = nc.dram_tensor("xres", [SEG, 1024], F32, kind="ExternalInput").ap()
    xo_d = nc.dram_tensor("xo", [SEG, 1024], F32, kind="ExternalOutput").ap()
    exin = [nc.dram_tensor("exin%d" % i, [NCH, 4, 4, 256, 512], BF16).ap() for i in range(2)]
    exout = [nc.dram_tensor("exout%d" % i, [NCH, 4, 256, 512], BF16).ap() for i in range(2)]
    agin = nc.dram_tensor("agin", [NCH, 1024, 512], BF16).ap()
    agout = nc.dram_tensor("agout", [NCH, 4, 1024, 512], BF16).ap()
    x1loc = nc.dram_tensor("x1loc", [SEG, 1024], F32).ap()

    p = Prog(nc)
    p.debug = debug
    bexin = [[p.buf() for _ in range(NCH)] for _ in range(2)]
    bexout = [[p.buf() for _ in range(NCH)] for _ in range(2)]
    bagin = [p.buf() for _ in range(NCH)]
    bagout = [p.buf() for _ in range(NCH)]
    bx1loc = p.buf()

    def make_exchange_writer(xi):
        st = {}

        def setup():
            st["msk"] = p.sb("msk%d" % xi, [128, 4], F32)
            st["bmsk"] = p.buf()
            p.dma(st["msk"][:], msk_d, writes=[st["bmsk"]])
            st["tmp"] = [p.sb("extmp%d_%d" % (xi, i), [128, 4, 512], BF16) for i in range(2)]
            st["btmp"] = [p.buf() for _ in range(2)]
            st["k"] = 0

        def emit(ti, f0, t, b):
            if "msk" not in st:
                setup()
            k = st["k"] % 2
            st["k"] += 1
            tmp, btmp = st["tmp"][k], st["btmp"][k]
            for q in range(4):
                p.pool(lambda e, q=q, tmp=tmp: e.tensor_scalar(out=tmp[:, q, :], in0=t[:], scalar1=st["msk"][:, q:q + 1], scalar2=0.0,
                                                               op0=ALU.mult, op1=ALU.add), reads=[b, st["bmsk"]], writes=[btmp])
            d, c = ti // NCH, ti % NCH
            p.dma(exin[xi][c, d, :, f0:f0 + 128, :].rearrange("q f t -> f q t"), tmp[:], reads=[btmp], writes=[bexin[xi][c]])

        def finish():
            for c in range(NCH):
                p.collective("ReduceScatter", ALU.add, GROUPS,
                             ins=[exin[xi][c].rearrange("d q f t -> (d q f) t")], outs=[exout[xi][c].rearrange("q f t -> (q f) t")],
                             reads=[bexin[xi][c]], writes=[bexout[xi][c]])
        return emit, finish

    def make_load_yt(xi):
        def load_yt(res, g, yt, byt):
            for cc in range(CPG):
                c = g * CPG + cc
                for half in range(2):
                    p.dma(yt[:, half * 4:(half + 1) * 4, cc * 512:(cc + 1) * 512],
                          exout[xi][c, :, half * 128:(half + 1) * 128, :].rearrange("q f t -> f q t"),
                          reads=[bexout[xi][c]], writes=[byt], eng=("sp" if half == 0 else "act"))
        return load_yt

    p.prefix = "A_"
    emitA, finA = make_exchange_writer(0)
    phase_mixa(p, nc, "a_", L, lambda res, ti, t, b: emitA(ti, 0, t, b), lambda res, ti, t, b: emitA(ti, 128, t, b))
    finA()
    p.end_phase()

    p.prefix = "T0_"
    st0 = {}

    def load_xres0(res, g, st, dst, bdst):
        t0 = g * TG + st * 128
        p.dma(dst, xres_d[t0:t0 + 128, :], writes=[bdst], eng="act")

    def store_out0(res, g, st, o, bo):
        if "xT" not in st0:
            st0["xT"] = [p.sb("x1Ts%d" % i, [128, 8, 128], BF16) for i in range(2)]
            st0["bxT"] = [p.buf() for _ in range(2)]
            st0["k"] = 0
        t0 = g * TG + st * 128
        p.dma(x1loc[t0:t0 + 128, :], o[:], reads=[bo], writes=[bx1loc])
        k = st0["k"] % 2
        st0["k"] += 1
        xT, bxT = st0["xT"][k], st0["bxT"][k]
        psA, bpsA, identf, bidf = res["psA"], res["bpsA"], res["identf"], res["bidf"]
        for hf in range(2):
            pa = psA[hf]
            for c4 in range(4):
                c = hf * 4 + c4
                p.pe(lambda e, pa=pa, c4=c4, c=c: e.transpose(out=pa[:, c4 * 128:(c4 + 1) * 128], in_=o[:, c * 128:(c + 1) * 128], identity=identf[:]),
                     reads=[bo, bidf], writes=[bpsA[hf]])
            p.act(lambda e, pa=pa, hf=hf, xT=xT: e.copy(out=xT[:, hf * 4:(hf + 1) * 4, :], in_=pa[:].rearrange("p (c t) -> p c t", c=4)),
                  reads=[bpsA[hf]], writes=[bxT])
        chunk, toff = t0 // 512, t0 % 512
        p.dma(agin[chunk].rearrange("(dc p) t -> p dc t", p=128)[:, :, toff:toff + 128], xT[:], reads=[bxT], writes=[bagin[chunk]])
        if toff == 384:
            p.collective("AllGather", ALU.bypass, GROUPS, ins=[agin[chunk]], outs=[agout[chunk].rearrange("r d t -> (r d) t")],
                         reads=[bagin[chunk]], writes=[bagout[chunk]])

    phase_tail(p, nc, "t0_", SEG, True, make_load_yt(0), load_xres0, store_out0, TG)
    p.end_phase()

    p.prefix = "C_"
    emitC, finC = make_exchange_writer(1)

    def load_xc(res, ti, dst, bdst):
        r, c = ti // NCH, ti % NCH
        p.dma(dst[:], agout[c, r].rearrange("(dc p) t -> p dc t", p=128), reads=[bagout[c]], writes=[bdst])

    phase_mixc(p, nc, "c_", L, load_xc, lambda res, ti, which, t, b: emitC(ti, which * 128, t, b))
    finC()
    p.end_phase()

    p.prefix = "T1_"

    def load_xres1(res, g, st, dst, bdst):
        t0 = g * TG + st * 128
        p.dma(dst, x1loc[t0:t0 + 128, :], reads=[bx1loc], writes=[bdst], eng="act")

    def store_out1(res, g, st, o, bo):
        t0 = g * TG + st * 128
        p.dma(xo_d[t0:t0 + 128, :], o[:], reads=[bo])

    phase_tail(p, nc, "t1_", SEG, False, make_load_yt(1), load_xres1, store_out1, TG)
    if debug:
        for name, src, bufs in (("dbg_exout0", exout[0], bexout[0]), ("dbg_exout1", exout[1], bexout[1]), ("dbg_agout", agout, bagout),
                                ("dbg_exin0", exin[0], bexin[0])):
            d = nc.dram_tensor(name, list(src.shape), BF16, kind="ExternalOutput").ap()
            for c in range(NCH):
                p.dma(d[c], src[c], reads=[bufs[c]])
        d = nc.dram_tensor("dbg_x1loc", [SEG, 1024], F32, kind="ExternalOutput").ap()
        p.dma(d, x1loc, reads=[bx1loc])
    p.finish()
    return nc

import ml_dtypes

_BF = ml_dtypes.bfloat16
_PROGS = {}


def _prog(key, fn):
    if key not in _PROGS:
        _PROGS[key] = fn()
    return _PROGS[key]


def mixa_inputs(inp, j, xT):
    W = inp['ev_w_in'][0]
    sl = slice(128 * j, 128 * (j + 1))
    wA = np.concatenate([W[:, 0:512][:, sl], W[:, 512:1024][:, sl], W[:, 1024:1536][:, sl], W[:, 1536:2048][:, sl], W[:, 2048:2560][:, sl]], axis=1)
    lbl = np.ascontiguousarray(inp['hgrn_lb_logits'][:, sl].T)
    ng = np.ascontiguousarray(np.broadcast_to(inp['ev_a_norm'][0, sl][None], (128, 128)))
    G0 = 8 * j
    are, aim, ldt = inp['ev_s5_a_re'][0], inp['ev_s5_a_im'][0], inp['ev_s5_log_dt'][0]
    sps = np.zeros((128, 3, 4), np.float32)
    spw = np.zeros((128, 3, 4, 128), np.float32)
    bpad = np.zeros((128, 2, 4, 128), np.float32)
    cpad = np.zeros((128, 2, 4, 128), np.float32)
    for i in range(4):
        for gl in range(2):
            g = G0 + 2 * i + gl
            glc = 2 * i + gl
            sps[gl * 64:(gl + 1) * 64, 0, i] = are[g]
            sps[gl * 64:(gl + 1) * 64, 1, i] = aim[g]
            sps[gl * 64:(gl + 1) * 64, 2, i] = ldt[g]
            spw[:, 0, i, gl * 64:(gl + 1) * 64] = are[g][None]
            spw[:, 1, i, gl * 64:(gl + 1) * 64] = aim[g][None]
            spw[:, 2, i, gl * 64:(gl + 1) * 64] = ldt[g]
            bpad[glc * 16:(glc + 1) * 16, 0, i, gl * 64:(gl + 1) * 64] = inp['ev_s5_b_re'][0, g].T
            bpad[glc * 16:(glc + 1) * 16, 1, i, gl * 64:(gl + 1) * 64] = inp['ev_s5_b_im'][0, g].T
            cpad[gl * 64:(gl + 1) * 64, 0, i, glc * 16:(glc + 1) * 16] = inp['ev_s5_c_re'][0, g].T
            cpad[gl * 64:(gl + 1) * 64, 1, i, glc * 16:(glc + 1) * 16] = inp['ev_s5_c_im'][0, g].T
    dsk = np.ascontiguousarray(inp['ev_s5_d'][0, sl][:, None])
    s_ = np.arange(128)[:, None]
    c_ = np.arange(128)[None, :]
    tri = ((s_ // 64 == c_ // 64) & (s_ <= c_)).astype(np.float32)
    m01 = np.broadcast_to((np.arange(512) % 64 != 0).astype(np.float32)[None], (128, 512))
    return {"xT": xT, "wA": np.ascontiguousarray(wA), "lbl": lbl, "ng": ng, "sps": sps, "spw": spw.reshape(128, 3, 512),
            "bpad": bpad.reshape(128, 2, 512), "cpad": cpad.reshape(128, 2, 512), "dsk": dsk, "tri": tri, "m01": np.ascontiguousarray(m01)}


def mixc_inputs(inp, j, xT):
    W = inp['od_w_in'][0]
    kv = j // 2
    qc = W[:, 128 * j:128 * (j + 1)]
    kc = W[:, 512 + 64 * kv:512 + 64 * (kv + 1)]
    vc = W[:, 640 + 64 * kv:640 + 64 * (kv + 1)]
    qd = W[:, 768 + 128 * j:768 + 128 * (j + 1)]
    kd = W[:, 1280 + 64 * kv:1280 + 64 * (kv + 1)]
    vd = W[:, 1408 + 64 * kv:1408 + 64 * (kv + 1)]
    wC = np.concatenate([qc, kc, kc, qd, kd, kd, vc, vd], axis=1)
    sink = np.ascontiguousarray(np.broadcast_to(inp['od_sinks'][0, 2 * j:2 * j + 2][None], (128, 2)))
    q = np.arange(128)[:, None]
    k = np.arange(256)[None, :]
    band = np.where(((k < 128) & (k > q)) | ((k >= 128) & (k - 128 <= q)), 0.0, -BIGM).astype(np.float32)
    swm = np.stack([band, band], axis=1)
    cm = np.stack([np.where(k <= q + 128 * a, 0.0, -BIGM) for a in range(2)], axis=1).astype(np.float32)
    i = np.arange(128)[None, :]
    f0 = np.where(i < 64, 0.0, -BIGM) * np.ones((128, 1))
    fut = np.stack([f0, f0 - BIGM], axis=1).astype(np.float32)
    return {"xT": xT, "wC": np.ascontiguousarray(wC), "sink": sink, "swm": np.ascontiguousarray(swm), "cm": np.ascontiguousarray(cm),
            "fut": np.ascontiguousarray(fut)}


def tail_inputs(inp, layer, xres, yT, w_out, w_glu):
    lnp = np.ascontiguousarray(np.stack([np.broadcast_to(inp[k][layer], (128, 1024)) for k in ['ln1_g', 'ln1_b', 'ln2_g', 'ln2_b']]).astype(np.float32))
    w_r = np.ascontiguousarray(np.concatenate([inp['moe_w_group'][layer], inp['moe_w_expert'][layer]], axis=1))
    b_r = np.ascontiguousarray(np.broadcast_to(np.concatenate([inp['moe_b_group'][layer], inp['moe_b_expert'][layer]])[None], (128, 20)).astype(np.float32))
    m = {"xres": xres, "yT": yT, "w_out": w_out, "lnp": lnp, "w_r": w_r, "b_r": b_r,
         "w_gu": inp['moe_w_gate_up'][layer], "w_dn": inp['moe_w_down'][layer]}
    if w_glu is not None:
        m["w_glu"] = w_glu
    return m


def fused_inputs(inp, x, xTb, b, r, SEG):
    m = {}
    for k, v in mixa_inputs(inp, r, xTb).items():
        m["a_" + k] = v
    for k, v in mixc_inputs(inp, r, None).items():
        if k != "xT":
            m["c_" + k] = v
    for k, v in tail_inputs(inp, 0, None, None, inp['ev_w_out'][0], inp['ev_s5_w_glu'][0]).items():
        if k not in ("xres", "yT"):
            m["t0_" + k] = v
    for k, v in tail_inputs(inp, 1, None, None, inp['od_w_out'][0], None).items():
        if k not in ("xres", "yT"):
            m["t1_" + k] = v
    msk = np.zeros((128, 4), np.float32)
    msk[:, r] = 1.0
    m["rankmask"] = msk
    m["xres"] = np.ascontiguousarray(x[b, r * SEG:(r + 1) * SEG])
    return m


def kernel(**inputs):
    inp = {k: np.ascontiguousarray(np.asarray(v)) for k, v in inputs.items()}
    x = inp['x']
    B, L, D = x.shape
    SEG = L // 4
    cores = list(range(8))
    xT = [np.ascontiguousarray(x[b].T) for b in range(B)]
    nc = _prog(("F", L), lambda: build_fused(L))
    maps = [fused_inputs(inp, x, xT[c // 4], c // 4, c % 4, SEG) for c in cores]
    res = run_bass_kernel_spmd(nc, maps, core_ids=cores).results
    out = np.stack([np.concatenate([np.asarray(res[b * 4 + s]["xo"]) for s in range(4)], axis=0) for b in range(B)])
    return out.astype(np.float32)
```

```python
import numpy as np
from contextlib import ExitStack
import concourse.bass as bass
import concourse.mybir as mybir
from concourse.bass_utils import run_bass_kernel_spmd

dt = mybir.dt
F32 = dt.float32
BF16 = dt.bfloat16
I32 = dt.int32
U32 = dt.uint32
AF = mybir.ActivationFunctionType
ALU = mybir.AluOpType
AX = mybir.AxisListType


class Buf:
    __slots__ = ("name", "lw", "rd", "excl")

    def __init__(self, name, excl=False):
        self.name = name
        self.lw = None
        self.rd = {}
        self.excl = excl


class Prog:
    ENG = ["pe", "dve", "act", "pool", "sp"]
    NDMA = 32

    def __init__(self, nc, same_engine_sync=True):
        self.nc = nc
        self.es = ExitStack()
        self.ops = {e: [] for e in self.ENG}
        self.cnt = {e: 0 for e in self.ENG}
        self.waited = {e: {} for e in self.ENG}
        self.dma_cnt = [0] * self.NDMA
        self.dma_rr = 0
        self.same = same_engine_sync
        self.sems = {}
        self.gen = 0
        self.semkey = {}
        for e in ["pe", "dve", "act", "pool"]:
            self.semkey[e] = e
            self.sems[e] = self.es.enter_context(nc.semaphore("s_" + e))
        for j in range(self.NDMA):
            self.sems["d%d" % j] = self.es.enter_context(nc.semaphore("s_d%d" % j))
        self.nbuf = 0
        self.scope = ExitStack()
        self.ncc = 0
        self.prefix = ""

    def sb(self, name, shape, dtype):
        return self.scope.enter_context(self.nc.sbuf_tensor("sb_" + self.prefix + name, list(shape), dtype))

    def ps(self, name, shape, dtype):
        return self.scope.enter_context(self.nc.psum_tensor("pp_" + self.prefix + name, list(shape), dtype))

    def debug_dump(self, name, ap, shape, dtype, reads):
        if not getattr(self, "debug", False):
            return
        d = self.nc.dram_tensor("dbg_" + name, list(shape), dtype, kind="ExternalOutput").ap()
        self.dma(d, ap, reads=reads)

    def barrier(self):
        targets = []
        for j in range(self.NDMA):
            if self.dma_cnt[j] > 0:
                targets.append(("d%d" % j, self.dma_cnt[j]))
        for e in ["pe", "dve", "act", "pool"]:
            if self.cnt[e] > 0:
                targets.append((self.semkey[e], self.cnt[e]))
        for k in self.sems:
            if k.startswith("cc"):
                targets.append((k, 1))
        for e in self.ENG:
            waits = []
            for k, v in targets:
                if k == self.semkey.get(e):
                    continue
                if self.waited[e].get(k, 0) >= v:
                    continue
                waits.append((k, v))
                self.waited[e][k] = v
            if waits:
                self.ops[e].append((waits, None, None, False))

    def end_phase(self):
        self.barrier()
        self.scope.close()
        self.scope = ExitStack()
        self.gen += 1
        for e in ["pe", "dve", "act", "pool"]:
            k = "%s@%d" % (e, self.gen)
            self.semkey[e] = k
            self.sems[k] = self.es.enter_context(self.nc.semaphore("s_%s_%d" % (e, self.gen)))
            self.cnt[e] = 0

    def collective(self, kind, alu, groups, ins, outs, reads=(), writes=()):
        k = "cc%d" % self.ncc
        self.ncc += 1
        self.sems[k] = self.es.enter_context(self.nc.semaphore("s_" + k))
        eng = "pool"
        waits = {}

        def need(kk, v):
            if v <= 0 or self.waited[eng].get(kk, 0) >= v:
                return
            if waits.get(kk, 0) < v:
                waits[kk] = v
        for b in reads:
            if b.lw is not None:
                need(*b.lw)
        for b in writes:
            if b.lw is not None:
                need(*b.lw)
            for kk, v in b.rd.items():
                need(kk, v)
        for kk, v in waits.items():
            self.waited[eng][kk] = v
        tok = (k, 1)
        self.ops[eng].append((list(waits.items()), lambda e: e.collective_compute(kind, alu, replica_groups=groups, ins=ins, outs=outs), tok, "cc"))
        for b in reads:
            if b.rd.get(tok[0], 0) < tok[1]:
                b.rd[tok[0]] = tok[1]
        for b in writes:
            b.lw = tok
            b.rd = {}
        return tok

    def buf(self, name=None, excl=False):
        self.nbuf += 1
        return Buf(name or ("b%d" % self.nbuf), excl)

    def pbuf(self, name=None):
        return self.buf(name, True)

    def op(self, eng, emit, reads=(), writes=(), dma=False):
        waits = {}
        ex = [b for b in reads if b.excl]
        if ex:
            writes = list(writes) + ex
            reads = [b for b in reads if not b.excl]

        def need(k, v):
            if v <= 0:
                return
            if k == self.semkey.get(eng) and (eng == "pe" or not self.same):
                return
            if self.waited[eng].get(k, 0) >= v:
                return
            if waits.get(k, 0) < v:
                waits[k] = v

        for b in reads:
            if b.lw is not None:
                need(*b.lw)
        for b in writes:
            if b.lw is not None:
                need(*b.lw)
            for k, v in b.rd.items():
                need(k, v)
        if dma:
            j = self.dma_rr
            self.dma_rr = (self.dma_rr + 1) % self.NDMA
            k = "d%d" % j
            need(k, self.dma_cnt[j])
            self.dma_cnt[j] += 16
            tok = (k, self.dma_cnt[j])
        else:
            self.cnt[eng] += 1
            tok = (self.semkey[eng], self.cnt[eng])
        for k, v in waits.items():
            self.waited[eng][k] = v
        self.ops[eng].append((list(waits.items()), emit, tok, dma))
        for b in reads:
            if b.rd.get(tok[0], 0) < tok[1]:
                b.rd[tok[0]] = tok[1]
        for b in writes:
            b.lw = tok
            b.rd = {}
        return tok

    def pe(self, emit, reads=(), writes=()):
        return self.op("pe", emit, reads, writes)

    def dve(self, emit, reads=(), writes=()):
        return self.op("dve", emit, reads, writes)

    def act(self, emit, reads=(), writes=()):
        return self.op("act", emit, reads, writes)

    def pool(self, emit, reads=(), writes=()):
        return self.op("pool", emit, reads, writes)

    def dma(self, out, in_, reads=(), writes=(), eng="sp", **kw):
        return self.op(eng, lambda e: e.dma_start(out=out, in_=in_, **kw), reads, writes, dma=True)

    def finish(self):
        waits = []
        for j in range(self.NDMA):
            if self.dma_cnt[j] > 0:
                waits.append(("d%d" % j, self.dma_cnt[j]))
        for e in ["pe", "dve", "act", "pool"]:
            if self.cnt[e] > 0:
                waits.append((self.semkey[e], self.cnt[e]))
        for k in self.sems:
            if k.startswith("cc"):
                waits.append((k, 1))
        self.ops["sp"].append((waits, None, None, False))
        nc = self.nc
        sems = self.sems
        ops = self.ops

        def run(name, e):
            for waits, emit, tok, dma in ops[name]:
                for k, v in waits:
                    e.wait_ge(sems[k], v)
                if emit is None:
                    continue
                ins = emit(e)
                ins.then_inc(sems[tok[0]], 16 if dma is True else 1)

        with nc.Block() as block:
            @block.tensor
            def _(e):
                run("pe", e)

            @block.vector
            def _(e):
                run("dve", e)

            @block.scalar
            def _(e):
                run("act", e)

            @block.gpsimd
            def _(e):
                run("pool", e)

            @block.sync
            def _(e):
                run("sp", e)
        self.scope.close()
        self.es.close()


DN_ALPHA = 4.0 ** 0.25
LN_EPS = 1e-5
BIG = 30000.0


def make_ident(p, dtype, name):
    ident = p.sb(name, [128, 128], dtype)
    b = p.buf(name)
    p.pool(lambda e: e.memset(ident[:], 0.0), writes=[b])
    p.pool(lambda e: e.affine_select(out=ident[:], in_=ident[:], compare_op=ALU.not_equal, fill=1.0,
                                     base=0, pattern=[[-1, 128]], channel_multiplier=1), reads=[b], writes=[b])
    return ident, b


def layer_norm_tm(p, src, dst, gam, bet, bsrc, bdst, bconst, tmp, btmp, eps, key):
    stats, mv, rstd = tmp
    p.dve(lambda e: e.bn_stats(out=stats[:, 0, :], in_=src[:, 0:512]), reads=[bsrc], writes=[btmp])
    p.dve(lambda e: e.bn_stats(out=stats[:, 1, :], in_=src[:, 512:1024]), reads=[bsrc], writes=[btmp])
    p.dve(lambda e: e.bn_aggr(out=mv[:], in_=stats[:].rearrange("p a b -> p (a b)")), reads=[btmp], writes=[btmp])
    p.act(lambda e: e.activation(out=rstd[:], in_=mv[:, 1:2], func=AF.Sqrt, bias=eps, scale=1.0), reads=[btmp], writes=[btmp])
    p.dve(lambda e: e.reciprocal(out=rstd[:], in_=rstd[:]), reads=[btmp], writes=[btmp])
    p.dve(lambda e: e.tensor_scalar(out=dst, in0=src, scalar1=mv[:, 0:1], scalar2=rstd[:, 0:1],
                                    op0=ALU.subtract, op1=ALU.mult), reads=[bsrc, btmp], writes=[bdst])
    p.dve(lambda e: e.tensor_tensor(out=dst, in0=dst, in1=gam, op=ALU.mult), reads=[bdst, bconst], writes=[bdst])
    p.dve(lambda e: e.tensor_tensor(out=dst, in0=dst, in1=bet, op=ALU.add), reads=[bdst, bconst], writes=[bdst])


def build_tail(NT, glu, TG=1024):
    nc = bass.Bass("TRN2", target_bir_lowering=False)
    D = 1024
    xres = nc.dram_tensor("xres", [NT, D], F32, kind="ExternalInput").ap()
    yT = nc.dram_tensor("yT", [D, NT], BF16, kind="ExternalInput").ap()
    xo = nc.dram_tensor("xo", [NT, D], F32, kind="ExternalOutput").ap()
    p = Prog(nc)

    def load_yt(res, g, yt, byt):
        p.dma(yt[:], yT[:, g * TG:(g + 1) * TG].rearrange("(c p) t -> p c t", p=128), writes=[byt])

    def load_xres(res, g, st, dst, bdst):
        t0 = g * TG + st * 128
        p.dma(dst, xres[t0:t0 + 128, :], writes=[bdst], eng="act")

    def store_out(res, g, st, o, bo):
        t0 = g * TG + st * 128
        p.dma(xo[t0:t0 + 128, :], o[:], reads=[bo])

    phase_tail(p, nc, "", NT, glu, load_yt, load_xres, store_out, TG)
    p.finish()
    return nc


def phase_tail(p, nc, pre, NT, glu, load_yt, load_xres, store_out, TG=1024):
    D = 1024
    NG = NT // TG
    NST = TG // 128
    if glu:
        w_glu = nc.dram_tensor(pre + "w_glu", [512, 1024], F32, kind="ExternalInput").ap()
    w_out = nc.dram_tensor(pre + "w_out", [D, D], F32, kind="ExternalInput").ap()
    lnp = nc.dram_tensor(pre + "lnp", [4, 128, D], F32, kind="ExternalInput").ap()
    w_r = nc.dram_tensor(pre + "w_r", [D, 20], F32, kind="ExternalInput").ap()
    b_r = nc.dram_tensor(pre + "b_r", [128, 20], F32, kind="ExternalInput").ap()
    w_gu = nc.dram_tensor(pre + "w_gu", [16, D, 512], F32, kind="ExternalInput").ap()
    w_dn = nc.dram_tensor(pre + "w_dn", [16, 256, D], F32, kind="ExternalInput").ap()
    identf, bidf = make_ident(p, F32, "identf")
    identb = p.sb("identb", [128, 128], BF16)
    bidb = p.buf()
    p.dve(lambda e: e.tensor_copy(out=identb[:], in_=identf[:]), reads=[bidf], writes=[bidb])
    wout = p.sb("wout", [128, 8, D], BF16)
    bwout = p.buf()
    p.dma(wout[:], w_out.rearrange("(c p) f -> p c f", p=128), writes=[bwout], eng="pool")
    if glu:
        wglu = p.sb("wglu", [128, 4, 1024], BF16)
        bwglu = p.buf()
        p.dma(wglu[:], w_glu.rearrange("(c p) f -> p c f", p=128), writes=[bwglu], eng="pool")
    lns = p.sb("lns", [128, 4, D], F32)
    bln = p.buf()
    p.dma(lns[:], lnp.rearrange("a p d -> p a d"), writes=[bln])
    wr = p.sb("wr", [128, 8, 20], BF16)
    bwr = p.buf()
    p.dma(wr[:], w_r.rearrange("(c p) f -> p c f", p=128), writes=[bwr], eng="pool")
    br_ = p.sb("br", [128, 20], F32)
    bbr = p.buf()
    p.dma(br_[:], b_r, writes=[bbr])

    yt = p.sb("yt", [128, 8, TG], BF16)
    byt = p.buf()
    if glu:
        yglu = p.sb("yglu", [128, 4, TG], BF16)
        byglu = [p.buf() for _ in range(4)]
        sig = p.sb("sig", [128, 512], F32)
        bsig = p.buf()
    acc = p.sb("acc", [128, NST, D], F32)
    bacc = [p.buf() for _ in range(NST)]
    x1T = p.sb("x1T", [128, 8, TG], BF16)
    bx1T = [p.buf() for _ in range(NST)]
    gates = p.sb("gates", [128, NST, 16], F32)
    bgates = [p.buf() for _ in range(NST)]
    gub = [p.sb("gub%d" % i, [128, 8, 512], BF16) for i in range(2)]
    bgub = [p.buf() for _ in range(2)]
    dnb = [p.sb("dnb%d" % i, [128, 2, D], BF16) for i in range(2)]
    bdnb = [p.buf() for _ in range(2)]
    sg = [p.sb("sg%d" % i, [128, 256], F32) for i in range(2)]
    bsg = [p.buf() for _ in range(2)]
    hh = [p.sb("hh%d" % i, [128, 256], BF16) for i in range(3)]
    bhh = [p.buf() for _ in range(3)]
    hT = [p.sb("hT%d" % i, [128, 2, 128], BF16) for i in range(3)]
    bhT = [p.buf() for _ in range(3)]
    stats = p.sb("stats", [128, 2, 6], F32)
    mv = p.sb("mv", [128, 2], F32)
    rstd = p.sb("rstd", [128, 1], F32)
    btmp = p.buf()
    rt = p.sb("rt", [128, 80], F32)
    brt = p.buf()
    top8 = p.sb("top8", [128, 8], F32)
    xout = [p.sb("xout%d" % i, [128, D], F32) for i in range(2)]
    bxout = [p.buf() for _ in range(2)]

    psA = [p.ps("psA%d" % i, [128, 512], F32) for i in range(2)]
    bpsA = [p.pbuf() for _ in range(2)]
    psT = [p.ps("psT%d" % i, [128, 8, 128], BF16) for i in range(2)]
    bpsT = [p.pbuf() for _ in range(2)]
    psY = [p.ps("psY%d" % i, [128, 1024], F32) for i in range(2)]
    bpsY = [p.pbuf() for _ in range(2)]

    res = dict(psA=psA, bpsA=bpsA, identf=identf, bidf=bidf, TG=TG, NST=NST)
    expert_steps = [(g, e) for g in range(NG) for e in range(16)]

    def load_expert(idx):
        g, e = expert_steps[idx]
        s = idx % 2
        p.dma(gub[s][:], w_gu[e].rearrange("(c p) f -> p c f", p=128), writes=[bgub[s]], eng="pool")
        p.dma(dnb[s][:], w_dn[e].rearrange("(c p) f -> p c f", p=128), writes=[bdnb[s]], eng="pool")

    load_expert(0)
    rot = 0
    mk0 = 0
    for g in range(NG):
        t0 = g * TG
        load_yt(res, g, yt, byt)
        for st in range(NST):
            load_xres(res, g, st, acc[:, st, :], bacc[st])
        if glu:
            for half in range(TG // 512):
                ts = slice(half * 512, (half + 1) * 512)
                for f in range(4):
                    for which in range(2):
                        col0 = which * 512 + f * 128
                        for k in range(4):
                            p.pe(lambda e, which=which, col0=col0, k=k, ts=ts: e.matmul(
                                psA[which][:], lhsT=wglu[:, k, col0:col0 + 128], rhs=yt[:, 4 + k, ts],
                                start=(k == 0), stop=(k == 3)), reads=[bwglu, byt], writes=[bpsA[which]])
                    p.act(lambda e: e.activation(out=sig[:], in_=psA[1][:], func=AF.Sigmoid), reads=[bpsA[1]], writes=[bsig])
                    p.dve(lambda e, f=f, ts=ts: e.tensor_tensor(out=yglu[:, f, ts], in0=psA[0][:], in1=sig[:], op=ALU.mult),
                          reads=[bpsA[0], bsig], writes=[byglu[f]])
        for st in range(NST):
            tsl = slice(st * 128, (st + 1) * 128)
            py = psY[st % 2]
            bpy = bpsY[st % 2]
            for half in range(2):
                for c in range(8):
                    if glu and c >= 4:
                        lhs = yglu[:, c - 4, tsl]
                        rb = byglu[c - 4]
                    else:
                        lhs = yt[:, c, tsl]
                        rb = byt
                    p.pe(lambda e, lhs=lhs, c=c, half=half, py=py: e.matmul(
                        py[:, half * 512:(half + 1) * 512], lhsT=lhs, rhs=wout[:, c, half * 512:(half + 1) * 512],
                        start=(c == 0), stop=(c == 7)), reads=[rb, bwout], writes=[bpy])
            a = acc[:, st, :]
            p.dve(lambda e, a=a, py=py: e.scalar_tensor_tensor(out=a, in0=a, scalar=DN_ALPHA, in1=py[:],
                                                                 op0=ALU.mult, op1=ALU.add), reads=[bacc[st], bpy], writes=[bacc[st]])
            layer_norm_tm(p, a, a, lns[:, 0, :], lns[:, 1, :], bacc[st], bacc[st], bln, (stats, mv, rstd), btmp, LN_EPS, "ln1")
            for hf in range(2):
                pa = psA[hf]
                for c4 in range(4):
                    c = hf * 4 + c4
                    p.pe(lambda e, pa=pa, c4=c4, c=c, a=a: e.transpose(out=pa[:, c4 * 128:(c4 + 1) * 128], in_=a[:, c * 128:(c + 1) * 128],
                                                                       identity=identf[:]), reads=[bacc[st], bidf], writes=[bpsA[hf]])
                p.act(lambda e, pa=pa, hf=hf, tsl=tsl: e.copy(out=x1T[:, hf * 4:(hf + 1) * 4, tsl], in_=pa[:].rearrange("p (c t) -> p c t", c=4)),
                      reads=[bpsA[hf]], writes=[bx1T[st]])
            p.act(lambda e, a=a: e.activation(out=a, in_=a, func=AF.Copy, scale=DN_ALPHA), reads=[bacc[st]], writes=[bacc[st]])
            pr = psT[st % 2]
            pl = psY[(st + 1) % 2]
            bpl = bpsY[(st + 1) % 2]
            for c in range(8):
                p.pe(lambda e, c=c, pl=pl, tsl=tsl: e.matmul(pl[:, 0:20], lhsT=x1T[:, c, tsl], rhs=wr[:, c, :],
                                                             start=(c == 0), stop=(c == 7)), reads=[bx1T[st], bwr], writes=[bpl])
            lg = rt[:, 0:20]
            p.dve(lambda e, pl=pl: e.tensor_tensor(out=lg, in0=pl[:, 0:20], in1=br_[:], op=ALU.add), reads=[bpl, bbr], writes=[brt])
            gmax = rt[:, 20:21]
            ngmax = rt[:, 21:22]
            sume = rt[:, 22:23]
            gtop = rt[:, 23:24]
            eg = rt[:, 24:28]
            oh = rt[:, 28:32]
            em = rt[:, 32:48]
            dd = rt[:, 48:49]
            ex = rt[:, 49:50]
            w1 = rt[:, 50:51]
            w2 = rt[:, 51:52]
            t2 = rt[:, 56:72]
            R = [brt]
            p.dve(lambda e: e.tensor_reduce(out=gmax, in_=lg[:, 0:4], axis=AX.X, op=ALU.max), reads=R, writes=R)
            p.dve(lambda e: e.tensor_scalar(out=ngmax, in0=gmax, scalar1=-1.0, scalar2=None, op0=ALU.mult), reads=R, writes=R)
            p.act(lambda e: e.activation(out=eg, in_=lg[:, 0:4], func=AF.Exp, bias=ngmax, scale=1.0, accum_out=sume), reads=R, writes=R)
            p.dve(lambda e: e.reciprocal(out=gtop, in_=sume), reads=R, writes=R)
            p.dve(lambda e: e.tensor_scalar(out=oh, in0=lg[:, 0:4], scalar1=gmax, scalar2=BIG, op0=ALU.is_equal, op1=ALU.mult), reads=R, writes=R)
            p.dve(lambda e: e.tensor_scalar(out=oh, in0=oh, scalar1=-BIG, scalar2=None, op0=ALU.add), reads=R, writes=R)
            for gi in range(4):
                p.dve(lambda e, gi=gi: e.tensor_scalar(out=em[:, gi * 4:(gi + 1) * 4], in0=lg[:, 4 + gi * 4:8 + gi * 4],
                                                       scalar1=oh[:, gi:gi + 1], scalar2=None, op0=ALU.add), reads=R, writes=R)
            p.dve(lambda e: e.max(out=top8[:], in_=em), reads=R, writes=R)
            p.dve(lambda e: e.tensor_tensor(out=dd, in0=top8[:, 1:2], in1=top8[:, 0:1], op=ALU.subtract), reads=R, writes=R)
            p.act(lambda e: e.activation(out=ex, in_=dd, func=AF.Exp), reads=R, writes=R)
            p.dve(lambda e: e.tensor_scalar(out=w1, in0=ex, scalar1=1.0, scalar2=None, op0=ALU.add), reads=R, writes=R)
            p.dve(lambda e: e.reciprocal(out=w1, in_=w1), reads=R, writes=R)
            p.dve(lambda e: e.tensor_tensor(out=w2, in0=ex, in1=w1, op=ALU.mult), reads=R, writes=R)
            p.dve(lambda e: e.tensor_tensor(out=w1, in0=w1, in1=gtop, op=ALU.mult), reads=R, writes=R)
            p.dve(lambda e: e.tensor_tensor(out=w2, in0=w2, in1=gtop, op=ALU.mult), reads=R, writes=R)
            gt = gates[:, st, :]
            p.dve(lambda e, gt=gt: e.tensor_scalar(out=gt, in0=em, scalar1=top8[:, 0:1], scalar2=w1, op0=ALU.is_equal, op1=ALU.mult),
                  reads=R, writes=[bgates[st]])
            p.dve(lambda e: e.tensor_scalar(out=t2, in0=em, scalar1=top8[:, 1:2], scalar2=w2,
                                            op0=ALU.is_equal, op1=ALU.mult), reads=R, writes=R)
            p.dve(lambda e, gt=gt: e.tensor_tensor(out=gt, in0=gt, in1=t2, op=ALU.add), reads=R + [bgates[st]], writes=[bgates[st]])
        def moe_step(e_i, st, s):
            tsl = slice(st * 128, (st + 1) * 128)

            def A(k):
                r, r3 = k % 2, k % 3
                pa = psA[r]
                for c in range(8):
                    p.pe(lambda e, c=c: e.matmul(pa[:], lhsT=x1T[:, c, tsl], rhs=gub[s][:, c, :], start=(c == 0), stop=(c == 7)),
                         reads=[bx1T[st], bgub[s]], writes=[bpsA[r]])
                p.act(lambda e: e.activation(out=sg[r][:], in_=pa[:, 0:256], func=AF.Silu), reads=[bpsA[r]], writes=[bsg[r]])
                p.dve(lambda e: e.scalar_tensor_tensor(out=hh[r3][:], in0=pa[:, 256:512], scalar=gates[:, st, e_i:e_i + 1], in1=sg[r][:],
                                                       op0=ALU.mult, op1=ALU.mult), reads=[bpsA[r], bsg[r], bgates[st]], writes=[bhh[r3]])

            def B(k):
                r, r3 = k % 2, k % 3
                for kk in range(2):
                    p.pe(lambda e, kk=kk: e.transpose(out=psT[r][:, kk, :], in_=hh[r3][:, kk * 128:(kk + 1) * 128], identity=identb[:]),
                         reads=[bhh[r3], bidb], writes=[bpsT[r]])
                p.act(lambda e: e.copy(out=hT[r3][:], in_=psT[r][:, 0:2, :]), reads=[bpsT[r]], writes=[bhT[r3]])

            def C(k):
                r, r3 = k % 2, k % 3
                py = psY[r]
                for half in range(2):
                    for kk in range(2):
                        p.pe(lambda e, half=half, kk=kk: e.matmul(py[:, half * 512:(half + 1) * 512], lhsT=hT[r3][:, kk, :],
                                                                 rhs=dnb[s][:, kk, half * 512:(half + 1) * 512], start=(kk == 0), stop=(kk == 1)),
                             reads=[bhT[r3], bdnb[s]], writes=[bpsY[r]])
                a = acc[:, st, :]
                p.dve(lambda e: e.tensor_tensor(out=a, in0=a, in1=py[:], op=ALU.add), reads=[bacc[st], bpsY[r]], writes=[bacc[st]])
            return dict(A=A, B=B, C=C)

        msteps = []
        for e_i in range(16):
            idx = g * 16 + e_i
            for st in range(NST):
                msteps.append((idx, st, moe_step(e_i, st, idx % 2)))
        nst_ = len(msteps)
        ld_at = min(2, NST - 1)
        for k in range(nst_ + 2):
            if k < nst_:
                idx, st, sd = msteps[k]
                if st == ld_at and idx + 1 < len(expert_steps):
                    load_expert(idx + 1)
                sd["A"](mk0 + k)
            if 0 <= k - 1 < nst_:
                msteps[k - 1][2]["B"](mk0 + k - 1)
            if 0 <= k - 2 < nst_:
                msteps[k - 2][2]["C"](mk0 + k - 2)
        mk0 += nst_
        for st in range(NST):
            a = acc[:, st, :]
            o = xout[st % 2]
            layer_norm_tm(p, a, o[:], lns[:, 2, :], lns[:, 3, :], bacc[st], bxout[st % 2], bln, (stats, mv, rstd), btmp, LN_EPS, "ln2")
            store_out(res, g, st, o, bxout[st % 2])

import math

RMS_EPS = 1e-6
TWO_PI = 2.0 * math.pi


def s5_lambda(p, pre, shape, ar, ai, ldt, breads, T=None, extra=()):
    F = shape[1]
    if T is None:
        T = p.sb(pre + "_t", [128, 8, F], F32)
    Ti = p.sb(pre + "_ti", [128, F], I32)
    b = p.buf(pre)
    dtt, mag, th, t, kf, r, c1, s = [T[:, i, :] for i in range(8)]
    lre = p.sb(pre + "_lre", [128, F], F32)
    lim = p.sb(pre + "_lim", [128, F], F32)
    R = [b] + list(extra)
    p.act(lambda e: e.activation(out=dtt, in_=ldt, func=AF.Exp), reads=breads, writes=R)
    p.dve(lambda e: e.tensor_tensor(out=mag, in0=dtt, in1=ar, op=ALU.mult), reads=R + breads, writes=R)
    p.act(lambda e: e.activation(out=mag, in_=mag, func=AF.Exp), reads=R, writes=R)
    p.dve(lambda e: e.tensor_tensor(out=th, in0=dtt, in1=ai, op=ALU.mult), reads=R + breads, writes=R)
    for which, dst in ((0, lim), (1, lre)):
        p.dve(lambda e, which=which: e.tensor_scalar(out=t, in0=th, scalar1=1.0 / TWO_PI, scalar2=0.25 * which,
                                                     op0=ALU.mult, op1=ALU.add), reads=R, writes=R)
        p.dve(lambda e: e.tensor_copy(out=Ti[:], in_=t), reads=R, writes=R)
        p.dve(lambda e: e.tensor_copy(out=kf, in_=Ti[:]), reads=R, writes=R)
        p.dve(lambda e: e.tensor_tensor(out=r, in0=t, in1=kf, op=ALU.subtract), reads=R, writes=R)
        p.dve(lambda e: e.tensor_scalar(out=c1, in0=r, scalar1=0.5, scalar2=None, op0=ALU.is_gt), reads=R, writes=R)
        p.dve(lambda e: e.tensor_tensor(out=r, in0=r, in1=c1, op=ALU.subtract), reads=R, writes=R)
        p.dve(lambda e: e.tensor_scalar(out=c1, in0=r, scalar1=-0.5, scalar2=None, op0=ALU.is_lt), reads=R, writes=R)
        p.dve(lambda e: e.tensor_tensor(out=r, in0=r, in1=c1, op=ALU.add), reads=R, writes=R)
        p.act(lambda e: e.activation(out=s, in_=r, func=AF.Sin, scale=TWO_PI), reads=R, writes=R)
        p.dve(lambda e, dst=dst: e.tensor_tensor(out=dst[:], in0=s, in1=mag, op=ALU.mult), reads=R, writes=R)
    return lre, lim, b


def build_mixa(L, TS5=2048):
    nc = bass.Bass("TRN2", target_bir_lowering=False)
    yaT_d = nc.dram_tensor("yaT", [128, L], BF16, kind="ExternalOutput").ap()
    ysT_d = nc.dram_tensor("ysT", [128, L], BF16, kind="ExternalOutput").ap()
    p = Prog(nc)

    def emit_ya(res, ti, t, b):
        p.dma(yaT_d[:, ti * 512:(ti + 1) * 512], t[:], reads=[b])

    def emit_ys(res, ti, t, b):
        p.dma(ysT_d[:, ti * 512:(ti + 1) * 512], t[:], reads=[b])

    phase_mixa(p, nc, "", L, emit_ya, emit_ys, TS5)
    p.finish()
    return nc


def phase_mixa(p, nc, pre, L, emit_ya, emit_ys, TS5=2048):
    xT = nc.dram_tensor(pre + "xT", [1024, L], F32, kind="ExternalInput").ap()
    wA_d = nc.dram_tensor(pre + "wA", [1024, 640], F32, kind="ExternalInput").ap()
    lbl_d = nc.dram_tensor(pre + "lbl", [128, 3], F32, kind="ExternalInput").ap()
    ng_d = nc.dram_tensor(pre + "ng", [128, 128], F32, kind="ExternalInput").ap()
    sps_d = nc.dram_tensor(pre + "sps", [128, 3, 4], F32, kind="ExternalInput").ap()
    spw_d = nc.dram_tensor(pre + "spw", [128, 3, 512], F32, kind="ExternalInput").ap()
    bpad_d = nc.dram_tensor(pre + "bpad", [128, 2, 512], F32, kind="ExternalInput").ap()
    cpad_d = nc.dram_tensor(pre + "cpad", [128, 2, 512], F32, kind="ExternalInput").ap()
    dsk_d = nc.dram_tensor(pre + "dsk", [128, 1], F32, kind="ExternalInput").ap()
    tri_d = nc.dram_tensor(pre + "tri", [128, 128], F32, kind="ExternalInput").ap()
    m01_d = nc.dram_tensor(pre + "m01", [128, 512], F32, kind="ExternalInput").ap()
    res = {}
    identf, bidf = make_ident(p, F32, "identf")
    wA = p.sb("wA", [128, 8, 640], BF16)
    bwA = p.buf()
    p.dma(wA[:], wA_d.rearrange("(c p) f -> p c f", p=128), writes=[bwA], eng="pool")
    cst = p.sb("cst", [128, 3 + 128 + 12 + 1 + 128 + 512 + 8], F32)
    bc = p.buf("cst")
    lbl = cst[:, 0:3]
    ng = cst[:, 3:131]
    sps = cst[:, 131:143].rearrange("p (a b) -> p a b", a=3)
    dsk = cst[:, 143:144]
    tri = cst[:, 144:272]
    m01 = cst[:, 272:784]
    misc = cst[:, 784:792]
    p.dma(lbl, lbl_d, writes=[bc])
    p.dma(ng, ng_d, writes=[bc])
    p.dma(sps, sps_d, writes=[bc])
    p.dma(dsk, dsk_d, writes=[bc])
    p.dma(tri, tri_d, writes=[bc])
    p.dma(m01, m01_d, writes=[bc])
    spw = p.sb("spw", [128, 3, 512], F32)
    p.dma(spw[:], spw_d, writes=[bc])
    bpad = p.sb("bpad", [128, 2, 512], F32)
    p.dma(bpad[:], bpad_d, writes=[bc])
    cpad = p.sb("cpad", [128, 2, 512], F32)
    p.dma(cpad[:], cpad_d, writes=[bc])
    lbe = misc[:, 0:3]
    lbs = misc[:, 3:4]
    lb = misc[:, 4:5]
    oml = misc[:, 5:6]
    bm = p.buf("misc")
    p.act(lambda e: e.activation(out=lbe, in_=lbl, func=AF.Exp, accum_out=lbs), reads=[bc], writes=[bm])
    p.dve(lambda e: e.reciprocal(out=lbs, in_=lbs), reads=[bm], writes=[bm])
    p.dve(lambda e: e.tensor_tensor(out=lb, in0=lbe[:, 0:1], in1=lbs, op=ALU.mult), reads=[bm], writes=[bm])
    p.dve(lambda e: e.tensor_scalar(out=oml, in0=lb, scalar1=-1.0, scalar2=1.0, op0=ALU.mult, op1=ALU.add), reads=[bm], writes=[bm])

    dre = p.sb("dre", [128, 4, TS5], F32)
    dim_ = p.sb("dim", [128, 4, TS5], F32)
    bd = [p.buf("d%d" % i) for i in range(4)]
    assert TS5 >= 1024
    ls_re, ls_im, bls = s5_lambda(p, "ls", [128, 4], sps[:, 0, :], sps[:, 1, :], sps[:, 2, :], [bc])
    lw_re, lw_im, blw = s5_lambda(p, "lw", [128, 512], spw[:, 0, :], spw[:, 1, :], spw[:, 2, :], [bc],
                                  T=dre[:].rearrange("p a t -> p (a t)")[:, 0:4096].rearrange("p (a f) -> p a f", a=8), extra=bd)
    W = dim_[:].rearrange("p a t -> p (a t)")[:, 0:4096].rearrange("p (a f) -> p a f", a=8)
    bW = p.buf("wtmp")
    xr, den, t1, t2, fr, fi, o1, o2 = [W[:, i, :] for i in range(8)]
    arw, aiw = spw[:, 0, :], spw[:, 1, :]
    RW = [bW, blw, bc]
    p.dve(lambda e: e.tensor_scalar(out=xr, in0=lw_re[:], scalar1=-1.0, scalar2=None, op0=ALU.add), reads=RW, writes=[bW] + bd)
    p.dve(lambda e: e.tensor_tensor(out=den, in0=arw, in1=arw, op=ALU.mult), reads=RW, writes=[bW])
    p.dve(lambda e: e.tensor_tensor(out=t1, in0=aiw, in1=aiw, op=ALU.mult), reads=RW, writes=[bW])
    p.dve(lambda e: e.tensor_tensor(out=den, in0=den, in1=t1, op=ALU.add), reads=RW, writes=[bW])
    p.dve(lambda e: e.reciprocal(out=den, in_=den), reads=RW, writes=[bW])
    p.dve(lambda e: e.tensor_tensor(out=t1, in0=xr, in1=arw, op=ALU.mult), reads=RW, writes=[bW])
    p.dve(lambda e: e.tensor_tensor(out=t2, in0=lw_im[:], in1=aiw, op=ALU.mult), reads=RW, writes=[bW])
    p.dve(lambda e: e.tensor_tensor(out=fr, in0=t1, in1=t2, op=ALU.add), reads=RW, writes=[bW])
    p.dve(lambda e: e.tensor_tensor(out=fr, in0=fr, in1=den, op=ALU.mult), reads=RW, writes=[bW])
    p.dve(lambda e: e.tensor_tensor(out=t1, in0=lw_im[:], in1=arw, op=ALU.mult), reads=RW, writes=[bW])
    p.dve(lambda e: e.tensor_tensor(out=t2, in0=xr, in1=aiw, op=ALU.mult), reads=RW, writes=[bW])
    p.dve(lambda e: e.tensor_tensor(out=fi, in0=t1, in1=t2, op=ALU.subtract), reads=RW, writes=[bW])
    p.dve(lambda e: e.tensor_tensor(out=fi, in0=fi, in1=den, op=ALU.mult), reads=RW, writes=[bW])
    wB = p.sb("wB", [128, 2, 512], BF16)
    wC = p.sb("wC", [128, 2, 512], BF16)
    bwB = p.buf("wB")
    bre, bim = bpad[:, 0, :], bpad[:, 1, :]
    p.dve(lambda e: e.tensor_tensor(out=o1, in0=fr, in1=bre, op=ALU.mult), reads=RW, writes=[bW])
    p.dve(lambda e: e.tensor_tensor(out=o2, in0=fi, in1=bim, op=ALU.mult), reads=RW, writes=[bW])
    p.dve(lambda e: e.tensor_tensor(out=wB[:, 0, :], in0=o1, in1=o2, op=ALU.subtract), reads=RW, writes=[bwB])
    p.dve(lambda e: e.tensor_tensor(out=o1, in0=fr, in1=bim, op=ALU.mult), reads=RW, writes=[bW])
    p.dve(lambda e: e.tensor_tensor(out=o2, in0=fi, in1=bre, op=ALU.mult), reads=RW, writes=[bW])
    p.dve(lambda e: e.tensor_tensor(out=wB[:, 1, :], in0=o1, in1=o2, op=ALU.add), reads=RW, writes=[bwB])
    p.dve(lambda e: e.tensor_copy(out=wC[:, 0, :], in_=cpad[:, 0, :]), reads=[bc], writes=[bwB])
    p.dve(lambda e: e.tensor_scalar(out=wC[:, 1, :], in0=cpad[:, 1, :], scalar1=-1.0, scalar2=None, op0=ALU.mult), reads=[bc, bW], writes=[bwB] + bd)

    NLEV = 3
    lamp = p.sb("lamp", [128, NLEV, 3, 4], F32)
    ltmp = p.sb("ltmp", [128, 4, 4], F32)
    blam = p.buf("lam")
    RL = [blam, bls]
    p.dve(lambda e: e.tensor_copy(out=lamp[:, 0, 0, :], in_=ls_re[:]), reads=RL, writes=[blam])
    p.dve(lambda e: e.tensor_copy(out=lamp[:, 0, 1, :], in_=ls_im[:]), reads=RL, writes=[blam])
    for lev in range(1, NLEV):
        p.dve(lambda e, lev=lev: e.tensor_copy(out=lamp[:, lev, 0:2, :], in_=lamp[:, lev - 1, 0:2, :]), reads=RL, writes=[blam])
        for _ in range(4):
            a = lamp[:, lev, 0, :]
            b_ = lamp[:, lev, 1, :]
            p.dve(lambda e, a=a: e.tensor_tensor(out=ltmp[:, 0, :], in0=a, in1=a, op=ALU.mult), reads=RL, writes=[blam])
            p.dve(lambda e, b_=b_: e.tensor_tensor(out=ltmp[:, 1, :], in0=b_, in1=b_, op=ALU.mult), reads=RL, writes=[blam])
            p.dve(lambda e, a=a, b_=b_: e.tensor_tensor(out=ltmp[:, 2, :], in0=a, in1=b_, op=ALU.mult), reads=RL, writes=[blam])
            p.dve(lambda e, a=a: e.tensor_tensor(out=a, in0=ltmp[:, 0, :], in1=ltmp[:, 1, :], op=ALU.subtract), reads=RL, writes=[blam])
            p.dve(lambda e, b_=b_: e.tensor_scalar(out=b_, in0=ltmp[:, 2, :], scalar1=2.0, scalar2=None, op0=ALU.mult), reads=RL, writes=[blam])
    for lev in range(NLEV):
        p.dve(lambda e, lev=lev: e.tensor_scalar(out=lamp[:, lev, 2, :], in0=lamp[:, lev, 1, :], scalar1=-1.0, scalar2=None, op0=ALU.mult),
              reads=RL, writes=[blam])

    xt = [p.sb("xt%d" % i, [128, 8, 512], BF16) for i in range(2)]
    bxt = [p.buf() for _ in range(2)]
    H = p.sb("hg", [128, 9, 512], F32)
    bH = p.buf("hg")
    f_, lf, kk, bb, eb, enb, qq, kinv, ktT = [H[:, i, :] for i in range(9)]
    qdec = p.sb("qdec", [128, 512], BF16)
    kinvb = p.sb("kinvb", [128, 512], BF16)
    bqk = p.buf("qk")
    vb = p.sb("vb", [128, 4, 128], BF16)
    gn = p.sb("gn", [128, 4, 128], F32)
    bvg = [p.buf() for _ in range(4)]
    kt = p.sb("kt", [128, 4, 128], BF16)
    bkt = [p.buf() for _ in range(4)]
    attT = [p.sb("attT%d" % i, [128, 128], BF16) for i in range(2)]
    battT = [p.buf() for _ in range(2)]
    yasb = [p.sb("yasb%d" % i, [128, 4, 128], F32) for i in range(2)]
    yaTs = [p.sb("yaTs%d" % i, [128, 512], BF16) for i in range(2)]
    byaTs = [p.buf() for _ in range(2)]
    byasb = [p.buf() for _ in range(2)]
    S = p.sb("S", [128, 128], F32)
    bS = p.buf("S")
    Sb = [p.sb("Sb%d" % i, [128, 128], BF16) for i in range(2)]
    bSb = [p.buf() for _ in range(2)]
    osc = p.sb("osc", [128, 128], F32)
    om = p.sb("om", [128, 2], F32)
    bo = p.buf("o")
    p.dve(lambda e: e.memset(S[:], 0.0), writes=[bS])
    p.dve(lambda e: e.memset(Sb[1][:], 0.0), writes=[bSb[1]])
    NB1 = TS5 // 16
    assert NB1 % 16 == 0 or NB1 <= 16
    uT = p.sb("uT", [128, TS5], F32)
    buT = p.buf("uT")
    uTb = [p.sb("uTb%d" % i, [128, 512], BF16) for i in range(2)]
    buTb = [p.buf() for _ in range(2)]
    nb_levels = []
    n = TS5
    while n > 16:
        n //= 16
        nb_levels.append(n)
    Ebufs = []
    for li, nbl in enumerate(nb_levels):
        Ebufs.append((p.sb("Ere%d" % li, [128, 4, nbl + 1], F32), p.sb("Eim%d" % li, [128, 4, nbl + 1], F32)))
    carry = p.sb("carry", [128, 2, 4], F32)
    p.dve(lambda e: e.memset(carry[:], 0.0), writes=bd)
    stmp = p.sb("stmp", [128, 4, 2, max(NB1, 16)], F32)
    hb = [p.sb("hb%d" % i, [128, 2, 4, 512], BF16) for i in range(2)]
    bhb = [p.buf() for _ in range(2)]
    zs = p.sb("zs", [128, 512], F32)
    bzs = p.buf()
    ysb = [p.sb("ysb%d" % i, [128, 512], BF16) for i in range(2)]
    bysb = [p.buf() for _ in range(2)]

    psQ = p.ps("psQ", [128, 512], F32); bpsQ = p.pbuf()
    psF = p.ps("psF", [128, 512], F32); bpsF = p.pbuf()
    psU = p.ps("psU", [128, 512], F32); bpsU = p.pbuf()
    psVG = p.ps("psVG", [128, 2, 256], F32); _b = p.pbuf(); bpsVG = [_b, _b]
    psD1 = p.ps("psD", [128, 512], F32); psD = [psD1, psD1]; _b = p.pbuf(); bpsD = [_b, _b]
    psS = p.ps("psS", [128, 4, 128], F32); _b = p.pbuf(); bpsS = [_b, _b]
    psO = p.ps("psO", [128, 4, 128], F32); _b = p.pbuf(); bpsO = [_b, _b]
    psM = p.ps("psM", [128, 4, 128], F32); _b = p.pbuf(); bpsM = [_b] * 4

    def cstep(i, lev, dst_re, dst_im, prev_re, prev_im, n, add_re=None, add_im=None):
        ar = lamp[:, lev, 0, i:i + 1]
        ai = lamp[:, lev, 1, i:i + 1]
        nai = lamp[:, lev, 2, i:i + 1]
        if add_re is None:
            add_re, add_im = dst_re, dst_im
        ta = stmp[:, i, 0, 0:n]
        tb = stmp[:, i, 1, 0:n]
        R = [bd[i], blam]
        p.dve(lambda e: e.scalar_tensor_tensor(out=ta, in0=prev_im, scalar=nai, in1=add_re, op0=ALU.mult, op1=ALU.add), reads=R, writes=[bd[i]])
        p.dve(lambda e: e.scalar_tensor_tensor(out=tb, in0=prev_re, scalar=ai, in1=add_im, op0=ALU.mult, op1=ALU.add), reads=R, writes=[bd[i]])
        p.dve(lambda e: e.scalar_tensor_tensor(out=dst_re, in0=prev_re, scalar=ar, in1=ta, op0=ALU.mult, op1=ALU.add), reads=R, writes=[bd[i]])
        p.dve(lambda e: e.scalar_tensor_tensor(out=dst_im, in0=prev_im, scalar=ar, in1=tb, op0=ALU.mult, op1=ALU.add), reads=R, writes=[bd[i]])

    def cscan(lev, Xre, Xim, n, hin_re, hin_im):
        if n <= 16:
            for t in range(n):
                for i in range(4):
                    pr = hin_re(i) if t == 0 else Xre(i)[:, t - 1:t]
                    pi_ = hin_im(i) if t == 0 else Xim(i)[:, t - 1:t]
                    cstep(i, lev, Xre(i)[:, t:t + 1], Xim(i)[:, t:t + 1], pr, pi_, 1)
            return
        nb = n // 16
        Ere, Eim = Ebufs[lev]
        for i in range(4):
            p.dve(lambda e, i=i: e.tensor_copy(out=Ere[:, i, 0:1], in_=hin_re(i)), reads=[bd[i]], writes=[bd[i]])
            p.dve(lambda e, i=i: e.tensor_copy(out=Eim[:, i, 0:1], in_=hin_im(i)), reads=[bd[i]], writes=[bd[i]])
            p.dve(lambda e, i=i: e.tensor_copy(out=Ere[:, i, 1:nb + 1], in_=Xre(i)[:, 0:n:16]), reads=[bd[i]], writes=[bd[i]])
            p.dve(lambda e, i=i: e.tensor_copy(out=Eim[:, i, 1:nb + 1], in_=Xim(i)[:, 0:n:16]), reads=[bd[i]], writes=[bd[i]])
        for r in range(1, 16):
            for i in range(4):
                cstep(i, lev, Ere[:, i, 1:nb + 1], Eim[:, i, 1:nb + 1], Ere[:, i, 1:nb + 1], Eim[:, i, 1:nb + 1], nb,
                      add_re=Xre(i)[:, r:n:16], add_im=Xim(i)[:, r:n:16])
        cscan(lev + 1, lambda i: Ere[:, i, 1:nb + 1], lambda i: Eim[:, i, 1:nb + 1], nb,
              lambda i: Ere[:, i, 0:1], lambda i: Eim[:, i, 0:1])
        for r in range(16):
            for i in range(4):
                if r == 0:
                    pr, pi_ = Ere[:, i, 0:nb], Eim[:, i, 0:nb]
                else:
                    pr, pi_ = Xre(i)[:, r - 1:n:16], Xim(i)[:, r - 1:n:16]
                cstep(i, lev, Xre(i)[:, r:n:16], Xim(i)[:, r:n:16], pr, pi_, nb)

    NTILE = L // 512
    TPS = TS5 // 512

    def load_x(ti):
        s = ti % 2
        p.dma(xt[s][:], xT[:, ti * 512:(ti + 1) * 512].rearrange("(c p) t -> p c t", p=128), writes=[bxt[s]], eng="pool")

    load_x(0)
    chunk_idx = 0
    for ti in range(NTILE):
        s = ti % 2
        x_ = xt[s]
        if ti + 1 < NTILE:
            load_x(ti + 1)
        t0 = ti * 512
        tl = (ti % TPS) * 512
        for (ps_, bps_, c0) in ((psQ, bpsQ, 0), (psF, bpsF, 128), (psU, bpsU, 512)):
            for c in range(8):
                p.pe(lambda e, ps_=ps_, c=c, c0=c0, x_=x_: e.matmul(ps_[:], lhsT=wA[:, c, c0:c0 + 128], rhs=x_[:, c, :],
                                                                    start=(c == 0), stop=(c == 7)), reads=[bwA, bxt[s]], writes=[bps_])
        RH = [bH]
        p.act(lambda e: e.activation(out=f_, in_=psF[:], func=AF.Sigmoid), reads=[bpsF], writes=RH)
        p.dve(lambda e: e.tensor_scalar(out=f_, in0=f_, scalar1=oml, scalar2=lb, op0=ALU.mult, op1=ALU.add), reads=RH + [bm], writes=RH)
        p.act(lambda e: e.activation(out=lf, in_=f_, func=AF.Ln), reads=RH, writes=RH)
        p.dve(lambda e: e.tensor_scalar(out=kk, in0=f_, scalar1=-1.0, scalar2=1.0, op0=ALU.mult, op1=ALU.add), reads=RH, writes=RH)
        p.dve(lambda e: e.tensor_tensor_scan(out=bb, data0=m01, data1=lf, initial=0.0, op0=ALU.mult, op1=ALU.add), reads=RH + [bc], writes=RH)
        p.act(lambda e: e.activation(out=eb, in_=bb, func=AF.Exp), reads=RH, writes=RH)
        p.act(lambda e: e.activation(out=enb, in_=bb, func=AF.Exp, scale=-1.0), reads=RH, writes=RH)
        p.act(lambda e: e.activation(out=qq, in_=psQ[:], func=AF.Silu), reads=[bpsQ], writes=RH)
        p.dve(lambda e: e.tensor_tensor(out=qdec[:], in0=qq, in1=eb, op=ALU.mult), reads=RH, writes=[bqk])
        p.dve(lambda e: e.tensor_tensor(out=kinv, in0=kk, in1=enb, op=ALU.mult), reads=RH, writes=RH)
        p.act(lambda e: e.copy(out=kinvb[:], in_=kinv), reads=RH, writes=[bqk])
        eb3 = eb.rearrange("p (c s) -> p c s", s=64)
        p.dve(lambda e: e.tensor_tensor(out=ktT.rearrange("p (c s) -> p c s", s=64), in0=kinv.rearrange("p (c s) -> p c s", s=64),
                                        in1=eb3[:, :, 63:64].to_broadcast([128, 8, 64]), op=ALU.mult), reads=RH, writes=RH)
        sb5 = ti % 2
        p.act(lambda e, tl=tl: e.copy(out=uT[:, tl:tl + 512], in_=psU[:]), reads=[bpsU], writes=[buT])
        p.dve(lambda e, sb5=sb5: e.tensor_copy(out=uTb[sb5][:], in_=psU[:]), reads=[bpsU], writes=[buTb[sb5]])
        k = 0
        for i in range(4):
            for ri in range(2):
                pd = psD[k % 2]
                p.pe(lambda e, pd=pd, i=i, ri=ri, sb5=sb5: e.matmul(pd[:], lhsT=wB[:, ri, i * 128:(i + 1) * 128], rhs=uTb[sb5][:],
                                                                    start=True, stop=True), reads=[bwB, buTb[sb5]], writes=[bpsD[k % 2]])
                dst = (dre if ri == 0 else dim_)[:, i, tl:tl + 512]
                p.act(lambda e, pd=pd, dst=dst: e.copy(out=dst, in_=pd[:]), reads=[bpsD[k % 2]], writes=[bd[i]])
                k += 1
        for sub in range(4):
            tsl = slice(sub * 128, (sub + 1) * 128)
            h2 = sub % 2
            for c in range(8):
                p.pe(lambda e, c=c, tsl=tsl, h2=h2, x_=x_: e.matmul(psVG[:, h2, :], lhsT=x_[:, c, tsl], rhs=wA[:, c, 256:512],
                                                                    start=(c == 0), stop=(c == 7)), reads=[bwA, bxt[s]], writes=[bpsVG[h2]])
            p.act(lambda e, sub=sub, h2=h2: e.copy(out=vb[:, sub, :], in_=psVG[:, h2, 0:128]), reads=[bpsVG[h2]], writes=[bvg[sub]])
            p.act(lambda e, sub=sub, h2=h2: e.activation(out=gn[:, sub, :], in_=psVG[:, h2, 128:256], func=AF.Silu), reads=[bpsVG[h2]], writes=[bvg[sub]])
            p.dve(lambda e, sub=sub: e.tensor_tensor(out=gn[:, sub, :], in0=gn[:, sub, :], in1=ng, op=ALU.mult), reads=[bvg[sub], bc], writes=[bvg[sub]])
            mi = sub % 2
            p.pe(lambda e, mi=mi, tsl=tsl: e.transpose(out=psM[:, mi, :], in_=ktT[:, tsl], identity=identf[:]), reads=RH + [bidf], writes=[bpsM[mi]])
            p.act(lambda e, mi=mi, sub=sub: e.copy(out=kt[:, sub, :], in_=psM[:, mi, :]), reads=[bpsM[mi]], writes=[bkt[sub]])
        ys_ = yasb[ti % 2]
        for sub in range(4):
            tsl = slice(sub * 128, (sub + 1) * 128)
            ai_ = sub % 2
            mi = 2 + sub % 2
            p.pe(lambda e, mi=mi, tsl=tsl: e.matmul(psM[:, mi, :], lhsT=kinvb[:, tsl], rhs=qdec[:, tsl], start=True, stop=True),
                 reads=[bqk], writes=[bpsM[mi]])
            p.dve(lambda e, mi=mi, ai_=ai_: e.tensor_tensor(out=attT[ai_][:], in0=psM[:, mi, :], in1=tri, op=ALU.mult),
                  reads=[bpsM[mi], bc], writes=[battT[ai_]])
            oi = sub % 2
            p.pe(lambda e, oi=oi, ai_=ai_, sub=sub: e.matmul(psO[:, oi, :], lhsT=attT[ai_][:], rhs=vb[:, sub, :], start=True, stop=False),
                 reads=[battT[ai_], bvg[sub]], writes=[bpsO[oi]])
            for hc in range(2):
                rows = slice(hc * 64, (hc + 1) * 64)
                tch = slice(sub * 128 + hc * 64, sub * 128 + (hc + 1) * 64)
                sprev = (chunk_idx + 1) % 2
                snew = chunk_idx % 2
                p.pe(lambda e, oi=oi, rows=rows, tch=tch, sprev=sprev, hc=hc: e.matmul(
                    psO[rows, oi, :], lhsT=qdec[:, tch], rhs=Sb[sprev][:], start=False, stop=(hc == 1)),
                    reads=[bqk, bSb[sprev]], writes=[bpsO[oi]])
                di = chunk_idx % 2
                p.pe(lambda e, di=di, rows=rows, sub=sub: e.matmul(psS[:, di, :], lhsT=kt[rows, sub, :], rhs=vb[rows, sub, :], start=True, stop=True),
                     reads=[bkt[sub], bvg[sub]], writes=[bpsS[di]])
                cpos = sub * 128 + hc * 64 + 63
                p.dve(lambda e, di=di, cpos=cpos: e.scalar_tensor_tensor(out=S[:], in0=S[:], scalar=eb[:, cpos:cpos + 1], in1=psS[:, di, :],
                                                                         op0=ALU.mult, op1=ALU.add), reads=[bS, bpsS[di]] + RH, writes=[bS])
                p.act(lambda e, snew=snew: e.copy(out=Sb[snew][:], in_=S[:]), reads=[bS], writes=[bSb[snew]])
                chunk_idx += 1
            p.act(lambda e, oi=oi: e.activation(out=osc[:], in_=psO[:, oi, :], func=AF.Square, accum_out=om[:, 0:1]), reads=[bpsO[oi]], writes=[bo])
            p.act(lambda e: e.activation(out=om[:, 1:2], in_=om[:, 0:1], func=AF.Sqrt, scale=1.0 / 128.0, bias=RMS_EPS), reads=[bo], writes=[bo])
            p.dve(lambda e: e.reciprocal(out=om[:, 1:2], in_=om[:, 1:2]), reads=[bo], writes=[bo])
            p.dve(lambda e, oi=oi, sub=sub, ys_=ys_: e.scalar_tensor_tensor(out=ys_[:, sub, :], in0=psO[:, oi, :], scalar=om[:, 1:2], in1=gn[:, sub, :],
                                                                         op0=ALU.mult, op1=ALU.mult), reads=[bpsO[oi], bo, bvg[sub]], writes=[byasb[ti % 2]])
        for sub in range(4):
            p.pe(lambda e, sub=sub, ys_=ys_: e.transpose(out=psM[:, sub, :], in_=ys_[:, sub, :], identity=identf[:]),
                 reads=[byasb[ti % 2], bidf], writes=[bpsM[0]])
        yt_ = yaTs[ti % 2]
        p.act(lambda e, yt_=yt_: e.copy(out=yt_[:], in_=psM[:].rearrange("p a b -> p (a b)")), reads=[bpsM[0]], writes=[byaTs[ti % 2]])
        emit_ya(res, ti, yt_, byaTs[ti % 2])
        if (ti + 1) % TPS == 0:
            sup0 = (ti + 1 - TPS) * 512
            cscan(0, lambda i: dre[:, i, :], lambda i: dim_[:, i, :], TS5,
                  lambda i: carry[:, 0, i:i + 1], lambda i: carry[:, 1, i:i + 1])
            for i in range(4):
                p.dve(lambda e, i=i: e.tensor_copy(out=carry[:, 0, i:i + 1], in_=dre[:, i, TS5 - 1:TS5]), reads=[bd[i]], writes=[bd[i]])
                p.dve(lambda e, i=i: e.tensor_copy(out=carry[:, 1, i:i + 1], in_=dim_[:, i, TS5 - 1:TS5]), reads=[bd[i]], writes=[bd[i]])
            for ch in range(TPS):
                csl = slice(ch * 512, (ch + 1) * 512)
                hs = ch % 2
                for i in range(4):
                    p.act(lambda e, i=i, hs=hs, csl=csl: e.copy(out=hb[hs][:, 0, i, :], in_=dre[:, i, csl]), reads=[bd[i]], writes=[bhb[hs]])
                    p.act(lambda e, i=i, hs=hs, csl=csl: e.copy(out=hb[hs][:, 1, i, :], in_=dim_[:, i, csl]), reads=[bd[i]], writes=[bhb[hs]])
                k = 0
                for i in range(4):
                    for ri in range(2):
                        p.pe(lambda e, i=i, ri=ri, hs=hs, k=k: e.matmul(psQ[:], lhsT=wC[:, ri, i * 128:(i + 1) * 128], rhs=hb[hs][:, ri, i, :],
                                                                        start=(k == 0), stop=(k == 7)), reads=[bwB, bhb[hs]], writes=[bpsQ])
                        k += 1
                p.dve(lambda e, csl=csl: e.scalar_tensor_tensor(out=zs[:], in0=uT[:, csl], scalar=dsk, in1=psQ[:], op0=ALU.mult, op1=ALU.add),
                      reads=[buT, bc, bpsQ], writes=[bzs])
                p.act(lambda e, hs=hs: e.activation(out=ysb[hs][:], in_=zs[:], func=AF.Gelu_apprx_tanh), reads=[bzs], writes=[bysb[hs]])
                emit_ys(res, (sup0 // 512) + ch, ysb[hs], bysb[hs])


BIGM = 30000.0
DEBUG_MIXC = False


def build_mixc(L):
    nc = bass.Bass("TRN2", target_bir_lowering=False)
    xT = nc.dram_tensor("xT", [1024, L], F32, kind="ExternalInput").ap()
    ycT_d = nc.dram_tensor("ycT", [128, L], BF16, kind="ExternalOutput").ap()
    ydT_d = nc.dram_tensor("ydT", [128, L], BF16, kind="ExternalOutput").ap()
    p = Prog(nc)
    p.debug = DEBUG_MIXC

    def load_x(res, ti, dst, bdst):
        p.dma(dst[:], xT[:, ti * 512:(ti + 1) * 512].rearrange("(c p) t -> p c t", p=128), writes=[bdst], eng="pool")

    def emit_y(res, ti, which, t, b):
        d = ycT_d if which == 0 else ydT_d
        p.dma(d[:, ti * 512:(ti + 1) * 512], t[:], reads=[b])

    phase_mixc(p, nc, "", L, load_x, emit_y)
    p.finish()
    return nc


def phase_mixc(p, nc, pre, L, load_x_cb, emit_y):
    NBLK = L // 256
    wC_d = nc.dram_tensor(pre + "wC", [1024, 768], F32, kind="ExternalInput").ap()
    sink_d = nc.dram_tensor(pre + "sink", [128, 2], F32, kind="ExternalInput").ap()
    swm_d = nc.dram_tensor(pre + "swm", [128, 2, 256], F32, kind="ExternalInput").ap()
    cm_d = nc.dram_tensor(pre + "cm", [128, 2, 256], F32, kind="ExternalInput").ap()
    fut_d = nc.dram_tensor(pre + "fut", [128, 2, 128], F32, kind="ExternalInput").ap()
    res = {}
    identb_f, bidf = make_ident(p, F32, "identf")
    identb = p.sb("identb", [128, 128], BF16)
    bidb = p.buf()
    p.dve(lambda e: e.tensor_copy(out=identb[:], in_=identb_f[:]), reads=[bidf], writes=[bidb])
    wC = p.sb("wC", [128, 8, 768], BF16)
    bwC = p.buf()
    p.dma(wC[:], wC_d.rearrange("(c p) f -> p c f", p=128), writes=[bwC], eng="pool")
    cst = p.sb("cst", [128, 2 + 512 + 512 + 256], F32)
    bc = p.buf("cst")
    sink = cst[:, 0:2]
    swm = cst[:, 2:514].rearrange("p (a k) -> p a k", a=2)
    cm = cst[:, 514:1026].rearrange("p (a k) -> p a k", a=2)
    fut = cst[:, 1026:1282].rearrange("p (a k) -> p a k", a=2)
    p.dma(sink, sink_d, writes=[bc])
    p.dma(swm, swm_d, writes=[bc])
    p.dma(cm, cm_d, writes=[bc])
    p.dma(fut, fut_d, writes=[bc])
    ones = p.sb("ones", [128, 128], F32)
    bones = p.buf()
    p.dve(lambda e: e.memset(ones[:], 1.0), writes=[bones])

    qcT = p.sb("qcT", [128, 512], BF16); bqc = p.buf()
    qdT = p.sb("qdT", [128, 512], BF16); bqd = p.buf()
    kkc = p.sb("kkc", [128, L], BF16)
    kkd = p.sb("kkd", [128, 2, L], BF16)
    vc = p.sb("vc", [128, L // 128, 64], BF16)
    vd = p.sb("vd", [128, L // 128, 66], BF16)
    bkv = p.buf("kv")
    bkvt = [p.buf("kv%d" % i) for i in range(L // 512)]
    kmT = p.sb("kmT", [128, 2, 64], BF16)
    bkm = p.buf("km")
    p.dve(lambda e: e.memset(kmT[:], 0.0), writes=[bkm])
    kmx = p.sb("kmx", [128, 4], F32)
    p.dve(lambda e: e.memset(kmx[:], 0.0), writes=[bkm])
    p.dve(lambda e: e.memset(vd[:, :, 64:66], 1.0), writes=[bkvt[0]])
    xt = [p.sb("xt%d" % i, [128, 8, 512], BF16) for i in range(2)]
    bxt = [p.buf() for _ in range(2)]
    sq = p.sb("sq", [128, 512], F32); bsq = p.buf()
    qab = p.sb("qab", [128, 512], BF16)
    bqab = p.buf()
    kab = p.sb("kab", [128, 2, 2], BF16)
    k8 = p.sb("k8", [128, 8], F32)
    sm = [p.sb("sm%d" % i, [128, 256], F32) for i in range(2)]; bsm = [p.buf() for _ in range(2)]
    Pb = [p.sb("Pb%d" % i, [128, 256], BF16) for i in range(4)]; bPb = [p.buf() for _ in range(4)]
    PT = [p.sb("PT%d" % i, [128, 2, 128], BF16) for i in range(4)]; bPT = [p.buf() for _ in range(4)]
    scS = [p.sb("scS%d" % i, [128, 8], F32) for i in range(6)]; bscS = [p.buf() for _ in range(6)]
    gset = []
    for i in range(5):
        gset.append(dict(sc=p.sb("scg%d" % i, [128, 8], F32), gm=p.sb("gm%d" % i, [128, 64], F32), sbm=p.sb("sbm%d" % i, [128, 64], F32),
                         top8=p.sb("top8_%d" % i, [128, 8], F32), lcols=p.sb("lcols%d" % i, [128, 66], F32), b=p.buf("gate%d" % i)))
    ycs = [p.sb("ycs%d" % i, [128, 4, 128], F32) for i in range(2)]; bycs = [p.buf() for _ in range(2)]
    yds = [p.sb("yds%d" % i, [128, 4, 128], F32) for i in range(2)]; byds = [p.buf() for _ in range(2)]
    yTs = [p.sb("yTs%d" % i, [128, 512], BF16) for i in range(4)]; byTs = [p.buf() for _ in range(4)]

    _psP = p.ps("psP", [128, 512], F32); _bpsP = p.pbuf(); psP = [_psP, _psP]; bpsP = [_bpsP, _bpsP]
    psV = p.ps("psV", [128, 512], F32); bpsV = p.pbuf()
    psS = [p.ps("psS%d" % i, [128, 512], F32) for i in range(2)]; bpsS = [p.pbuf() for _ in range(2)]
    psT = [p.ps("psT%d" % i, [128, 8, 128], BF16) for i in range(2)]; bpsT = [p.pbuf() for _ in range(2)]
    psO = [p.ps("psO%d" % i, [128, 512], F32) for i in range(2)]; bpsO = [p.pbuf() for _ in range(2)]

    NTILE = L // 512

    def load_x(ti):
        s = ti % 2
        load_x_cb(res, ti, xt[s], bxt[s])

    load_x(0)
    rot = 0
    kbase = 0
    for ti in range(NTILE):
        s = ti % 2
        x_ = xt[s]
        if ti + 1 < NTILE:
            load_x(ti + 1)
        t0 = ti * 512
        tsl = slice(t0, t0 + 512)
        dsts = [(qcT[:], bqc, 0.125), (kkc[:, tsl], bkvt[ti], 1.0), (qdT[:], bqd, 0.125), (kkd[:, 0, tsl], bkvt[ti], 1.0), (kkd[:, 1, tsl], bkvt[ti], 1.0)]
        for pi, (dst, bdst, scl) in enumerate(dsts):
            pp = psP[pi % 2]
            for c in range(8):
                p.pe(lambda e, pp=pp, c=c, pi=pi, x_=x_: e.matmul(pp[:], lhsT=wC[:, c, pi * 128:(pi + 1) * 128], rhs=x_[:, c, :],
                                                                  start=(c == 0), stop=(c == 7)), reads=[bwC, bxt[s]], writes=[bpsP[pi % 2]])
            p.act(lambda e, pp=pp, dst=dst, scl=scl: e.activation(out=dst, in_=pp[:], func=AF.Copy, scale=scl), reads=[bpsP[pi % 2]], writes=[bdst])
            if pi == 2:
                p.dve(lambda e, pp=pp: e.tensor_scalar(out=sq[:], in0=pp[:], scalar1=-1.0, scalar2=None, op0=ALU.mult), reads=[bpsP[pi % 2]], writes=[bsq])
                p.dve(lambda e, pp=pp: e.tensor_tensor(out=sq[:], in0=sq[:], in1=pp[:], op=ALU.max), reads=[bpsP[pi % 2], bsq], writes=[bsq])
                p.dve(lambda e: e.tensor_scalar(out=qab[:], in0=sq[:], scalar1=0.125, scalar2=None, op0=ALU.mult), reads=[bsq], writes=[bqab])
            if pi >= 3:
                hh_ = pi - 3
                p.dve(lambda e, pp=pp: e.tensor_scalar(out=sq[:], in0=pp[:], scalar1=-1.0, scalar2=None, op0=ALU.mult), reads=[bpsP[pi % 2]], writes=[bsq])
                p.dve(lambda e, pp=pp: e.tensor_tensor(out=sq[:], in0=sq[:], in1=pp[:], op=ALU.max), reads=[bpsP[pi % 2], bsq], writes=[bsq])
                p.dve(lambda e: e.max(out=k8[:], in_=sq[:]), reads=[bsq], writes=[bkm])
                p.dve(lambda e, hh_=hh_: e.tensor_tensor(out=kmx[:, hh_:hh_ + 1], in0=kmx[:, hh_:hh_ + 1], in1=k8[:, 0:1], op=ALU.max), reads=[bkm], writes=[bkm])
                p.dve(lambda e, hh_=hh_: e.tensor_copy(out=kab[:, hh_, 0:1], in_=kmx[:, hh_:hh_ + 1]), reads=[bkm], writes=[bkm])
                p.dve(lambda e, hh_=hh_: e.tensor_copy(out=kab[:, hh_, 1:2], in_=kmx[:, hh_:hh_ + 1]), reads=[bkm], writes=[bkm])
        for sub in range(4):
            g = ti * 4 + sub
            for c in range(8):
                p.pe(lambda e, c=c, sub=sub, x_=x_: e.matmul(psV[:, 0:128], lhsT=x_[:, c, sub * 128:(sub + 1) * 128], rhs=wC[:, c, 640:768],
                                                             start=(c == 0), stop=(c == 7)), reads=[bwC, bxt[s]], writes=[bpsV])
            p.act(lambda e, g=g: e.copy(out=vc[:, g, :], in_=psV[:, 0:64]), reads=[bpsV], writes=[bkvt[ti]])
            p.act(lambda e, g=g: e.copy(out=vd[:, g, 0:64], in_=psV[:, 64:128]), reads=[bpsV], writes=[bkvt[ti]])
        for hb in range(2):
            blk = ti * 2 + hb
            for hv in range(2):
                p.dve(lambda e, blk=blk, hv=hv: e.tensor_reduce(out=sq[:, 0:1], in_=kkd[:, hv, blk * 256:(blk + 1) * 256], axis=AX.X, op=ALU.add),
                      reads=[bkvt[ti]], writes=[bsq])
                p.dve(lambda e, blk=blk, hv=hv: e.tensor_scalar(out=kmT[:, hv, blk:blk + 1], in0=sq[:, 0:1], scalar1=1.0 / 256.0, scalar2=None, op0=ALU.mult),
                      reads=[bsq], writes=[bkm])
        yc_ = ycs[ti % 2]
        yd_ = yds[ti % 2]
        prev_kv = [bkvt[ti - 1]] if ti > 0 else []

        steps = []
        fins = {}

        def swa_step(g, sub, h, yc_=yc_):
            rows = slice(h * 64, (h + 1) * 64)
            qsl = slice(sub * 128, (sub + 1) * 128)
            if g == 0:
                k0, nk, mk = 0, 128, swm[:, 1, 128:256]
            else:
                k0, nk, mk = (g - 1) * 128, 256, swm[:, 1, :]
            nkc = nk // 128
            kvr = [bkvt[ti]] + (prev_kv if sub == 0 else [])

            def A(k):
                r2, r3 = k % 2, k % 4
                ps_, smr, pb = psS[r2], sm[r2], Pb[r3]
                scs, R = scS[k % 6], [bscS[k % 6]]
                mx, nmx, rs, es, den = [scs[:, i:i + 1] for i in range(5)]
                p.pe(lambda e: e.matmul(ps_[:, 0:nk], lhsT=qcT[rows, qsl], rhs=kkc[rows, k0:k0 + nk], start=True, stop=True),
                     reads=[bqc] + kvr, writes=[bpsS[r2]])
                p.dve(lambda e: e.tensor_tensor(out=smr[:, 0:nk], in0=ps_[:, 0:nk], in1=mk, op=ALU.add), reads=[bpsS[r2], bc], writes=[bsm[r2]])
                p.dve(lambda e: e.tensor_reduce(out=mx, in_=smr[:, 0:nk], axis=AX.X, op=ALU.max), reads=[bsm[r2]], writes=R)
                p.dve(lambda e: e.tensor_tensor(out=mx, in0=mx, in1=sink[:, h:h + 1], op=ALU.max), reads=R + [bc], writes=R)
                p.dve(lambda e: e.tensor_scalar(out=nmx, in0=mx, scalar1=-1.0, scalar2=None, op0=ALU.mult), reads=R, writes=R)
                p.act(lambda e: e.activation(out=pb[:, 0:nk], in_=smr[:, 0:nk], func=AF.Exp, bias=nmx, scale=1.0, accum_out=rs),
                      reads=[bsm[r2]] + R, writes=[bPb[r3]] + R)
                p.act(lambda e: e.activation(out=es, in_=sink[:, h:h + 1], func=AF.Exp, bias=nmx, scale=1.0), reads=R + [bc], writes=R)
                p.dve(lambda e: e.tensor_tensor(out=den, in0=rs, in1=es, op=ALU.add), reads=R, writes=R)
                p.dve(lambda e: e.reciprocal(out=den, in_=den), reads=R, writes=R)

            def B(k):
                r2, r3 = k % 2, k % 4
                pb, pt = Pb[r3], PT[r3]
                for kc in range(nkc):
                    p.pe(lambda e, kc=kc: e.transpose(out=psT[r2][:, kc, :], in_=pb[:, kc * 128:(kc + 1) * 128], identity=identb[:]),
                         reads=[bPb[r3], bidb], writes=[bpsT[r2]])
                p.dve(lambda e: e.tensor_copy(out=pt[:, 0:nkc, :], in_=psT[r2][:, 0:nkc, :]), reads=[bpsT[r2]], writes=[bPT[r3]])

            def C(k):
                r2, r3 = k % 4, k % 6
                pt = PT[r2]
                den = scS[r3][:, 4:5]
                for kc in range(nkc):
                    gk = (g - 1 + kc) if g > 0 else 0
                    p.pe(lambda e, kc=kc, gk=gk: e.matmul(psV[:, 256:320], lhsT=pt[:, kc, :], rhs=vc[:, gk, :], start=(kc == 0), stop=(kc == nkc - 1)),
                         reads=[bPT[r2]] + kvr, writes=[bpsV])
                p.act(lambda e: e.activation(out=yc_[:, sub, h * 64:(h + 1) * 64], in_=psV[:, 256:320], func=AF.Copy, scale=den),
                      reads=[bpsV, bscS[r3]], writes=[bycs[ti % 2]])
            return dict(A=A, B=B, C=C)

        def moba_prologue(g, sub, h, gi):
            rows = slice(h * 64, (h + 1) * 64)
            qsl = slice(sub * 128, (sub + 1) * 128)
            n = g // 2
            gs = gset[gi]
            G = [gs["b"]]
            mb, nmb = gs["sc"][:, 0:1], gs["sc"][:, 1:2]
            p.pe(lambda e: e.matmul(psV[:, 0:2], lhsT=qab[:, qsl], rhs=kab[:, h, :], start=True, stop=True), reads=[bqab, bkm], writes=[bpsV])
            p.dve(lambda e: e.tensor_copy(out=mb, in_=psV[:, 0:1]), reads=[bpsV], writes=G)
            p.dve(lambda e: e.tensor_scalar(out=nmb, in0=mb, scalar1=-1.0, scalar2=None, op0=ALU.mult), reads=G, writes=G)
            if n > 0:
                gm_, sbm_, top8_ = gs["gm"], gs["sbm"], gs["top8"]
                p.pe(lambda e: e.matmul(psV[:, 64:128], lhsT=qdT[:, qsl], rhs=kmT[:, h, :], start=True, stop=True), reads=[bqd, bkm], writes=[bpsV])
                fsl = slice(64 - n, 128 - n)
                p.dve(lambda e: e.tensor_tensor(out=gm_[:], in0=psV[:, 64:128], in1=fut[:, 0, fsl], op=ALU.add), reads=[bpsV, bc], writes=G)
                p.dve(lambda e: e.max(out=top8_[:], in_=gm_[:]), reads=G, writes=G)
                p.dve(lambda e: e.tensor_scalar(out=sbm_[:], in0=gm_[:], scalar1=top8_[:, 2:3], scalar2=BIGM, op0=ALU.is_ge, op1=ALU.mult), reads=G, writes=G)
                p.dve(lambda e: e.tensor_tensor(out=sbm_[:], in0=sbm_[:], in1=fut[:, 1, fsl], op=ALU.add), reads=G + [bc], writes=G)
                p.dve(lambda e: e.tensor_scalar(out=sbm_[:], in0=sbm_[:], scalar1=mb, scalar2=None, op0=ALU.subtract), reads=G, writes=G)

        def moba_step(g, sub, h, gi, jb, oi, yd_=yd_):
            rows = slice(h * 64, (h + 1) * 64)
            qsl = slice(sub * 128, (sub + 1) * 128)
            n, a = g // 2, g % 2
            own = (jb == n)
            gs = gset[gi]
            G = [gs["b"]]
            kreads = [bkvt[jb // 2]]
            po = psO[oi]

            def A(k):
                r2, r3 = k % 2, k % 4
                ps_, pb = psS[r2], Pb[r3]
                p.pe(lambda e: e.matmul(ps_[:, 0:256], lhsT=qdT[:, qsl], rhs=kkd[:, h, jb * 256:(jb + 1) * 256], start=True, stop=True),
                     reads=[bqd] + kreads, writes=[bpsS[r2]])
                if own:
                    smr = sm[r2]
                    p.dve(lambda e: e.tensor_tensor(out=smr[:], in0=ps_[:, 0:256], in1=cm[:, a, :], op=ALU.add), reads=[bpsS[r2], bc], writes=[bsm[r2]])
                    p.act(lambda e: e.activation(out=pb[:], in_=smr[:], func=AF.Exp, bias=gs["sc"][:, 1:2], scale=1.0),
                          reads=[bsm[r2]] + G, writes=[bPb[r3]])
                else:
                    p.act(lambda e: e.activation(out=pb[:], in_=ps_[:, 0:256], func=AF.Exp, bias=gs["sbm"][:, jb:jb + 1], scale=1.0),
                          reads=[bpsS[r2]] + G, writes=[bPb[r3]])

            def B(k):
                r2, r3 = k % 2, k % 4
                pb, pt = Pb[r3], PT[r3]
                for kc in range(2):
                    p.pe(lambda e, kc=kc: e.transpose(out=psT[r2][:, kc, :], in_=pb[:, kc * 128:(kc + 1) * 128], identity=identb[:]),
                         reads=[bPb[r3], bidb], writes=[bpsT[r2]])
                p.dve(lambda e: e.tensor_copy(out=pt[:], in_=psT[r2][:, 0:2, :]), reads=[bpsT[r2]], writes=[bPT[r3]])

            def C(k):
                r2 = k % 4
                pt = PT[r2]
                for kc in range(2):
                    p.pe(lambda e, kc=kc: e.matmul(po[:, 0:66], lhsT=pt[:, kc, :], rhs=vd[:, jb * 2 + kc, :], start=(jb == 0 and kc == 0), stop=(own and kc == 1)),
                         reads=[bPT[r2]] + kreads, writes=[bpsO[oi]])
                if own:
                    lsum = gs["sc"][:, 2:3]
                    p.dve(lambda e: e.reciprocal(out=lsum, in_=po[:, 64:65]), reads=G + [bpsO[oi]], writes=G)
                    p.act(lambda e: e.activation(out=yd_[:, sub, h * 64:(h + 1) * 64], in_=po[:, 0:64], func=AF.Copy, scale=lsum),
                          reads=[bpsO[oi]] + G, writes=[byds[ti % 2]])
            return dict(A=A, B=B, C=C)

        units = [(ti * 4 + sub, sub, h) for sub in range(4) for h in range(2)]
        seq = []
        for ui, (g, sub, h) in enumerate(units):
            gi = (ti * 8 + ui) % 5
            if ui == 0:
                seq.append(("pro", lambda g=g, sub=sub, h=h, gi=gi: moba_prologue(g, sub, h, gi)))
            if ui + 1 < len(units):
                g2, sub2, h2 = units[ui + 1]
                seq.append(("pro", lambda g2=g2, sub2=sub2, h2=h2, gi=gi: moba_prologue(g2, sub2, h2, (gi + 1) % 5)))
            seq.append(("step", swa_step(g, sub, h)))
            for jb in range(g // 2 + 1):
                seq.append(("step", moba_step(g, sub, h, gi, jb, (ti * 8 + ui) % 2)))
        D = 2
        stp = [x[1] for x in seq if x[0] == "step"]
        nst = len(stp)
        k = 0
        for kind, x in seq:
            if kind == "pro":
                x()
                continue
            x["A"](kbase + k)
            if k - D >= 0:
                stp[k - D]["B"](kbase + k - D)
            if k - 2 * D >= 0:
                stp[k - 2 * D]["C"](kbase + k - 2 * D)
            k += 1
        for k in range(nst, nst + 2 * D):
            if 0 <= k - D < nst:
                stp[k - D]["B"](kbase + k - D)
            if 0 <= k - 2 * D < nst:
                stp[k - 2 * D]["C"](kbase + k - 2 * D)
        kbase += nst
        for which, (src, bsrc) in enumerate(((yc_, bycs[ti % 2]), (yd_, byds[ti % 2]))):
            pp = psP[which]
            for sub in range(4):
                p.pe(lambda e, sub=sub, src=src, pp=pp: e.transpose(out=pp[:, sub * 128:(sub + 1) * 128], in_=src[:, sub, :], identity=identb_f[:]),
                     reads=[bsrc, bidf], writes=[bpsP[which]])
            k = (ti % 2) * 2 + which
            p.act(lambda e, k=k, pp=pp: e.copy(out=yTs[k][:], in_=pp[:]), reads=[bpsP[which]], writes=[byTs[k]])
            emit_y(res, ti, which, yTs[k], byTs[k])


GROUPS = [[0, 1, 2, 3], [4, 5, 6, 7]]


def build_fused(L, debug=False):
    SEG = L // 4
    NCH = SEG // 512
    TG = min(1024, SEG)
    CPG = TG // 512
    nc = bass.Bass("TRN2", target_bir_lowering=False)
    msk_d = nc.dram_tensor("rankmask", [128, 4], F32, kind="ExternalInput").ap()
    xres_d = nc.dram_tensor("xres", [SEG, 1024], F32, kind="ExternalInput").ap()
    xo_d = nc.dram_tensor("xo", [SEG, 1024], F32, kind="ExternalOutput").ap()
    exin = [nc.dram_tensor("exin%d" % i, [NCH, 4, 4, 256, 512], BF16).ap() for i in range(2)]
    exout = [nc.dram_tensor("exout%d" % i, [NCH, 4, 256, 512], BF16).ap() for i in range(2)]
    agin = nc.dram_tensor("agin", [NCH, 1024, 512], BF16).ap()
    agout = nc.dram_tensor("agout", [NCH, 4, 1024, 512], BF16).ap()
    x1loc = nc.dram_tensor("x1loc", [SEG, 1024], F32).ap()

    p = Prog(nc)
    p.debug = debug
    bexin = [[p.buf() for _ in range(NCH)] for _ in range(2)]
    bexout = [[p.buf() for _ in range(NCH)] for _ in range(2)]
    bagin = [p.buf() for _ in range(NCH)]
    bagout = [p.buf() for _ in range(NCH)]
    bx1loc = p.buf()

    def make_exchange_writer(xi):
        st = {}

        def setup():
            st["msk"] = p.sb("msk%d" % xi, [128, 4], F32)
            st["bmsk"] = p.buf()
            p.dma(st["msk"][:], msk_d, writes=[st["bmsk"]])
            st["tmp"] = [p.sb("extmp%d_%d" % (xi, i), [128, 4, 512], BF16) for i in range(2)]
            st["btmp"] = [p.buf() for _ in range(2)]
            st["k"] = 0

        def emit(ti, f0, t, b):
            if "msk" not in st:
                setup()
            k = st["k"] % 2
            st["k"] += 1
            tmp, btmp = st["tmp"][k], st["btmp"][k]
            for q in range(4):
                p.pool(lambda e, q=q, tmp=tmp: e.tensor_scalar(out=tmp[:, q, :], in0=t[:], scalar1=st["msk"][:, q:q + 1], scalar2=0.0,
                                                               op0=ALU.mult, op1=ALU.add), reads=[b, st["bmsk"]], writes=[btmp])
            d, c = ti // NCH, ti % NCH
            p.dma(exin[xi][c, d, :, f0:f0 + 128, :].rearrange("q f t -> f q t"), tmp[:], reads=[btmp], writes=[bexin[xi][c]])

        def finish():
            for c in range(NCH):
                p.collective("ReduceScatter", ALU.add, GROUPS,
                             ins=[exin[xi][c].rearrange("d q f t -> (d q f) t")], outs=[exout[xi][c].rearrange("q f t -> (q f) t")],
                             reads=[bexin[xi][c]], writes=[bexout[xi][c]])
        return emit, finish

    def make_load_yt(xi):
        def load_yt(res, g, yt, byt):
            for cc in range(CPG):
                c = g * CPG + cc
                for half in range(2):
                    p.dma(yt[:, half * 4:(half + 1) * 4, cc * 512:(cc + 1) * 512],
                          exout[xi][c, :, half * 128:(half + 1) * 128, :].rearrange("q f t -> f q t"),
                          reads=[bexout[xi][c]], writes=[byt], eng=("sp" if half == 0 else "act"))
        return load_yt

    p.prefix = "A_"
    emitA, finA = make_exchange_writer(0)
    phase_mixa(p, nc, "a_", L, lambda res, ti, t, b: emitA(ti, 0, t, b), lambda res, ti, t, b: emitA(ti, 128, t, b))
    finA()
    p.end_phase()

    p.prefix = "T0_"
    st0 = {}

    def load_xres0(res, g, st, dst, bdst):
        t0 = g * TG + st * 128
        p.dma(dst, xres_d[t0:t0 + 128, :], writes=[bdst], eng="act")

    def store_out0(res, g, st, o, bo):
        if "xT" not in st0:
            st0["xT"] = [p.sb("x1Ts%d" % i, [128, 8, 128], BF16) for i in range(2)]
            st0["bxT"] = [p.buf() for _ in range(2)]
            st0["k"] = 0
        t0 = g * TG + st * 128
        p.dma(x1loc[t0:t0 + 128, :], o[:], reads=[bo], writes=[bx1loc])
        k = st0["k"] % 2
        st0["k"] += 1
        xT, bxT = st0["xT"][k], st0["bxT"][k]
        psA, bpsA, identf, bidf = res["psA"], res["bpsA"], res["identf"], res["bidf"]
        for hf in range(2):
            pa = psA[hf]
            for c4 in range(4):
                c = hf * 4 + c4
                p.pe(lambda e, pa=pa, c4=c4, c=c: e.transpose(out=pa[:, c4 * 128:(c4 + 1) * 128], in_=o[:, c * 128:(c + 1) * 128], identity=identf[:]),
                     reads=[bo, bidf], writes=[bpsA[hf]])
            p.act(lambda e, pa=pa, hf=hf, xT=xT: e.copy(out=xT[:, hf * 4:(hf + 1) * 4, :], in_=pa[:].rearrange("p (c t) -> p c t", c=4)),
                  reads=[bpsA[hf]], writes=[bxT])
        chunk, toff = t0 // 512, t0 % 512
        p.dma(agin[chunk].rearrange("(dc p) t -> p dc t", p=128)[:, :, toff:toff + 128], xT[:], reads=[bxT], writes=[bagin[chunk]])
        if toff == 384:
            p.collective("AllGather", ALU.bypass, GROUPS, ins=[agin[chunk]], outs=[agout[chunk].rearrange("r d t -> (r d) t")],
                         reads=[bagin[chunk]], writes=[bagout[chunk]])

    phase_tail(p, nc, "t0_", SEG, True, make_load_yt(0), load_xres0, store_out0, TG)
    p.end_phase()

    p.prefix = "C_"
    emitC, finC = make_exchange_writer(1)

    def load_xc(res, ti, dst, bdst):
        r, c = ti // NCH, ti % NCH
        p.dma(dst[:], agout[c, r].rearrange("(dc p) t -> p dc t", p=128), reads=[bagout[c]], writes=[bdst])

    phase_mixc(p, nc, "c_", L, load_xc, lambda res, ti, which, t, b: emitC(ti, which * 128, t, b))
    finC()
    p.end_phase()

    p.prefix = "T1_"

    def load_xres1(res, g, st, dst, bdst):
        t0 = g * TG + st * 128
        p.dma(dst, x1loc[t0:t0 + 128, :], reads=[bx1loc], writes=[bdst], eng="act")

    def store_out1(res, g, st, o, bo):
        t0 = g * TG + st * 128
        p.dma(xo_d[t0:t0 + 128, :], o[:], reads=[bo])

    phase_tail(p, nc, "t1_", SEG, False, make_load_yt(1), load_xres1, store_out1, TG)
    if debug:
        for name, src, bufs in (("dbg_exout0", exout[0], bexout[0]), ("dbg_exout1", exout[1], bexout[1]), ("dbg_agout", agout, bagout),
                                ("dbg_exin0", exin[0], bexin[0])):
            d = nc.dram_tensor(name, list(src.shape), BF16, kind="ExternalOutput").ap()
            for c in range(NCH):
                p.dma(d[c], src[c], reads=[bufs[c]])
        d = nc.dram_tensor("dbg_x1loc", [SEG, 1024], F32, kind="ExternalOutput").ap()
        p.dma(d, x1loc, reads=[bx1loc])
    p.finish()
    return nc

import ml_dtypes

_BF = ml_dtypes.bfloat16
_PROGS = {}


def _prog(key, fn):
    if key not in _PROGS:
        _PROGS[key] = fn()
    return _PROGS[key]


def mixa_inputs(inp, j, xT):
    W = inp['ev_w_in'][0]
    sl = slice(128 * j, 128 * (j + 1))
    wA = np.concatenate([W[:, 0:512][:, sl], W[:, 512:1024][:, sl], W[:, 1024:1536][:, sl], W[:, 1536:2048][:, sl], W[:, 2048:2560][:, sl]], axis=1)
    lbl = np.ascontiguousarray(inp['hgrn_lb_logits'][:, sl].T)
    ng = np.ascontiguousarray(np.broadcast_to(inp['ev_a_norm'][0, sl][None], (128, 128)))
    G0 = 8 * j
    are, aim, ldt = inp['ev_s5_a_re'][0], inp['ev_s5_a_im'][0], inp['ev_s5_log_dt'][0]
    sps = np.zeros((128, 3, 4), np.float32)
    spw = np.zeros((128, 3, 4, 128), np.float32)
    bpad = np.zeros((128, 2, 4, 128), np.float32)
    cpad = np.zeros((128, 2, 4, 128), np.float32)
    for i in range(4):
        for gl in range(2):
            g = G0 + 2 * i + gl
            glc = 2 * i + gl
            sps[gl * 64:(gl + 1) * 64, 0, i] = are[g]
            sps[gl * 64:(gl + 1) * 64, 1, i] = aim[g]
            sps[gl * 64:(gl + 1) * 64, 2, i] = ldt[g]
            spw[:, 0, i, gl * 64:(gl + 1) * 64] = are[g][None]
            spw[:, 1, i, gl * 64:(gl + 1) * 64] = aim[g][None]
            spw[:, 2, i, gl * 64:(gl + 1) * 64] = ldt[g]
            bpad[glc * 16:(glc + 1) * 16, 0, i, gl * 64:(gl + 1) * 64] = inp['ev_s5_b_re'][0, g].T
            bpad[glc * 16:(glc + 1) * 16, 1, i, gl * 64:(gl + 1) * 64] = inp['ev_s5_b_im'][0, g].T
            cpad[gl * 64:(gl + 1) * 64, 0, i, glc * 16:(glc + 1) * 16] = inp['ev_s5_c_re'][0, g].T
            cpad[gl * 64:(gl + 1) * 64, 1, i, glc * 16:(glc + 1) * 16] = inp['ev_s5_c_im'][0, g].T
    dsk = np.ascontiguousarray(inp['ev_s5_d'][0, sl][:, None])
    s_ = np.arange(128)[:, None]
    c_ = np.arange(128)[None, :]
    tri = ((s_ // 64 == c_ // 64) & (s_ <= c_)).astype(np.float32)
    m01 = np.broadcast_to((np.arange(512) % 64 != 0).astype(np.float32)[None], (128, 512))
    return {"xT": xT, "wA": np.ascontiguousarray(wA), "lbl": lbl, "ng": ng, "sps": sps, "spw": spw.reshape(128, 3, 512),
            "bpad": bpad.reshape(128, 2, 512), "cpad": cpad.reshape(128, 2, 512), "dsk": dsk, "tri": tri, "m01": np.ascontiguousarray(m01)}


def mixc_inputs(inp, j, xT):
    W = inp['od_w_in'][0]
    kv = j // 2
    qc = W[:, 128 * j:128 * (j + 1)]
    kc = W[:, 512 + 64 * kv:512 + 64 * (kv + 1)]
    vc = W[:, 640 + 64 * kv:640 + 64 * (kv + 1)]
    qd = W[:, 768 + 128 * j:768 + 128 * (j + 1)]
    kd = W[:, 1280 + 64 * kv:1280 + 64 * (kv + 1)]
    vd = W[:, 1408 + 64 * kv:1408 + 64 * (kv + 1)]
    z = np.zeros_like(kd)
    wC = np.concatenate([qc, kc, kc, qd, kd, z, z, kd, vc, vd], axis=1)
    sink = np.ascontiguousarray(np.broadcast_to(inp['od_sinks'][0, 2 * j:2 * j + 2][None], (128, 2)))
    q = np.arange(128)[:, None]
    k = np.arange(256)[None, :]
    band = np.where(((k < 128) & (k > q)) | ((k >= 128) & (k - 128 <= q)), 0.0, -BIGM).astype(np.float32)
    swm = np.stack([band, band], axis=1)
    cm = np.stack([np.where(k <= q + 128 * a, 0.0, -BIGM) for a in range(2)], axis=1).astype(np.float32)
    i = np.arange(128)[None, :]
    f0 = np.where(i < 64, 0.0, -BIGM) * np.ones((128, 1))
    fut = np.stack([f0, f0 - BIGM], axis=1).astype(np.float32)
    return {"xT": xT, "wC": np.ascontiguousarray(wC), "sink": sink, "swm": np.ascontiguousarray(swm), "cm": np.ascontiguousarray(cm),
            "fut": np.ascontiguousarray(fut)}


def tail_inputs(inp, layer, xres, yT, w_out, w_glu):
    lnp = np.ascontiguousarray(np.stack([np.broadcast_to(inp[k][layer], (128, 1024)) for k in ['ln1_g', 'ln1_b', 'ln2_g', 'ln2_b']]).astype(np.float32))
    w_r = np.ascontiguousarray(np.concatenate([inp['moe_w_group'][layer], inp['moe_w_expert'][layer]], axis=1))
    b_r = np.ascontiguousarray(np.broadcast_to(np.concatenate([inp['moe_b_group'][layer], inp['moe_b_expert'][layer]])[None], (128, 20)).astype(np.float32))
    m = {"xres": xres, "yT": yT, "w_out": w_out, "lnp": lnp, "w_r": w_r, "b_r": b_r,
         "w_gu": inp['moe_w_gate_up'][layer], "w_dn": inp['moe_w_down'][layer]}
    if w_glu is not None:
        m["w_glu"] = w_glu
    return m


def fused_inputs(inp, x, xTb, b, r, SEG):
    m = {}
    for k, v in mixa_inputs(inp, r, xTb).items():
        m["a_" + k] = v
    for k, v in mixc_inputs(inp, r, None).items():
        if k != "xT":
            m["c_" + k] = v
    for k, v in tail_inputs(inp, 0, None, None, inp['ev_w_out'][0], inp['ev_s5_w_glu'][0]).items():
        if k not in ("xres", "yT"):
            m["t0_" + k] = v
    for k, v in tail_inputs(inp, 1, None, None, inp['od_w_out'][0], None).items():
        if k not in ("xres", "yT"):
            m["t1_" + k] = v
    msk = np.zeros((128, 4), np.float32)
    msk[:, r] = 1.0
    m["rankmask"] = msk
    m["xres"] = np.ascontiguousarray(x[b, r * SEG:(r + 1) * SEG])
    return m


def kernel(**inputs):
    inp = {k: np.ascontiguousarray(np.asarray(v)) for k, v in inputs.items()}
    x = inp['x']
    B, L, D = x.shape
    SEG = L // 4
    cores = list(range(8))
    xT = [np.ascontiguousarray(x[b].T) for b in range(B)]
    nc = _prog(("F", L), lambda: build_fused(L))
    maps = [fused_inputs(inp, x, xT[c // 4], c // 4, c % 4, SEG) for c in cores]
    res = run_bass_kernel_spmd(nc, maps, core_ids=cores).results
    out = np.stack([np.concatenate([np.asarray(res[b * 4 + s]["xo"]) for s in range(4)], axis=0) for b in range(B)])
    return out.astype(np.float32)
```

```python
import numpy as np
from contextlib import ExitStack
import concourse.bass as bass
import concourse.mybir as mybir
from concourse.bass_utils import run_bass_kernel_spmd

dt = mybir.dt
F32 = dt.float32
BF16 = dt.bfloat16
I32 = dt.int32
U32 = dt.uint32
AF = mybir.ActivationFunctionType
ALU = mybir.AluOpType
AX = mybir.AxisListType


class Buf:
    __slots__ = ("name", "lw", "rd", "excl")

    def __init__(self, name, excl=False):
        self.name = name
        self.lw = None
        self.rd = {}
        self.excl = excl


class Prog:
    ENG = ["pe", "dve", "act", "pool", "sp"]
    NDMA = 32

    def __init__(self, nc, same_engine_sync=True):
        self.nc = nc
        self.es = ExitStack()
        self.ops = {e: [] for e in self.ENG}
        self.cnt = {e: 0 for e in self.ENG}
        self.waited = {e: {} for e in self.ENG}
        self.dma_cnt = [0] * self.NDMA
        self.dma_rr = 0
        self.same = same_engine_sync
        self.sems = {}
        self.gen = 0
        self.semkey = {}
        for e in ["pe", "dve", "act", "pool"]:
            self.semkey[e] = e
            self.sems[e] = self.es.enter_context(nc.semaphore("s_" + e))
        for j in range(self.NDMA):
            self.sems["d%d" % j] = self.es.enter_context(nc.semaphore("s_d%d" % j))
        self.nbuf = 0
        self.scope = ExitStack()
        self.ncc = 0
        self.prefix = ""

    def sb(self, name, shape, dtype):
        return self.scope.enter_context(self.nc.sbuf_tensor("sb_" + self.prefix + name, list(shape), dtype))

    def ps(self, name, shape, dtype):
        return self.scope.enter_context(self.nc.psum_tensor("pp_" + self.prefix + name, list(shape), dtype))

    def debug_dump(self, name, ap, shape, dtype, reads):
        if not getattr(self, "debug", False):
            return
        d = self.nc.dram_tensor("dbg_" + name, list(shape), dtype, kind="ExternalOutput").ap()
        self.dma(d, ap, reads=reads)

    def barrier(self):
        targets = []
        for j in range(self.NDMA):
            if self.dma_cnt[j] > 0:
                targets.append(("d%d" % j, self.dma_cnt[j]))
        for e in ["pe", "dve", "act", "pool"]:
            if self.cnt[e] > 0:
                targets.append((self.semkey[e], self.cnt[e]))
        for k in self.sems:
            if k.startswith("cc"):
                targets.append((k, 1))
        for e in self.ENG:
            waits = []
            for k, v in targets:
                if k == self.semkey.get(e):
                    continue
                if self.waited[e].get(k, 0) >= v:
                    continue
                waits.append((k, v))
                self.waited[e][k] = v
            if waits:
                self.ops[e].append((waits, None, None, False))

    def end_phase(self):
        self.barrier()
        self.scope.close()
        self.scope = ExitStack()
        self.gen += 1
        for e in ["pe", "dve", "act", "pool"]:
            k = "%s@%d" % (e, self.gen)
            self.semkey[e] = k
            self.sems[k] = self.es.enter_context(self.nc.semaphore("s_%s_%d" % (e, self.gen)))
            self.cnt[e] = 0

    def collective(self, kind, alu, groups, ins, outs, reads=(), writes=()):
        k = "cc%d" % self.ncc
        self.ncc += 1
        self.sems[k] = self.es.enter_context(self.nc.semaphore("s_" + k))
        eng = "pool"
        waits = {}

        def need(kk, v):
            if v <= 0 or self.waited[eng].get(kk, 0) >= v:
                return
            if waits.get(kk, 0) < v:
                waits[kk] = v
        for b in reads:
            if b.lw is not None:
                need(*b.lw)
        for b in writes:
            if b.lw is not None:
                need(*b.lw)
            for kk, v in b.rd.items():
                need(kk, v)
        for kk, v in waits.items():
            self.waited[eng][kk] = v
        tok = (k, 1)
        self.ops[eng].append((list(waits.items()), lambda e: e.collective_compute(kind, alu, replica_groups=groups, ins=ins, outs=outs), tok, "cc"))
        for b in reads:
            if b.rd.get(tok[0], 0) < tok[1]:
                b.rd[tok[0]] = tok[1]
        for b in writes:
            b.lw = tok
            b.rd = {}
        return tok

    def buf(self, name=None, excl=False):
        self.nbuf += 1
        return Buf(name or ("b%d" % self.nbuf), excl)

    def pbuf(self, name=None):
        return self.buf(name, True)

    def op(self, eng, emit, reads=(), writes=(), dma=False):
        waits = {}
        ex = [b for b in reads if b.excl]
        if ex:
            writes = list(writes) + ex
            reads = [b for b in reads if not b.excl]

        def need(k, v):
            if v <= 0:
                return
            if k == self.semkey.get(eng) and (eng == "pe" or not self.same):
                return
            if self.waited[eng].get(k, 0) >= v:
                return
            if waits.get(k, 0) < v:
                waits[k] = v

        for b in reads:
            if b.lw is not None:
                need(*b.lw)
        for b in writes:
            if b.lw is not None:
                need(*b.lw)
            for k, v in b.rd.items():
                need(k, v)
        if dma:
            j = self.dma_rr
            self.dma_rr = (self.dma_rr + 1) % self.NDMA
            k = "d%d" % j
            need(k, self.dma_cnt[j])
            self.dma_cnt[j] += 16
            tok = (k, self.dma_cnt[j])
        else:
            self.cnt[eng] += 1
            tok = (self.semkey[eng], self.cnt[eng])
        for k, v in waits.items():
            self.waited[eng][k] = v
        self.ops[eng].append((list(waits.items()), emit, tok, dma))
        for b in reads:
            if b.rd.get(tok[0], 0) < tok[1]:
                b.rd[tok[0]] = tok[1]
        for b in writes:
            b.lw = tok
            b.rd = {}
        return tok

    def pe(self, emit, reads=(), writes=()):
        return self.op("pe", emit, reads, writes)

    def dve(self, emit, reads=(), writes=()):
        return self.op("dve", emit, reads, writes)

    def act(self, emit, reads=(), writes=()):
        return self.op("act", emit, reads, writes)

    def pool(self, emit, reads=(), writes=()):
        return self.op("pool", emit, reads, writes)

    def dma(self, out, in_, reads=(), writes=(), eng="sp", **kw):
        return self.op(eng, lambda e: e.dma_start(out=out, in_=in_, **kw), reads, writes, dma=True)

    def finish(self):
        waits = []
        for j in range(self.NDMA):
            if self.dma_cnt[j] > 0:
                waits.append(("d%d" % j, self.dma_cnt[j]))
        for e in ["pe", "dve", "act", "pool"]:
            if self.cnt[e] > 0:
                waits.append((self.semkey[e], self.cnt[e]))
        for k in self.sems:
            if k.startswith("cc"):
                waits.append((k, 1))
        self.ops["sp"].append((waits, None, None, False))
        nc = self.nc
        sems = self.sems
        ops = self.ops

        def run(name, e):
            for waits, emit, tok, dma in ops[name]:
                for k, v in waits:
                    e.wait_ge(sems[k], v)
                if emit is None:
                    continue
                ins = emit(e)
                ins.then_inc(sems[tok[0]], 16 if dma is True else 1)

        with nc.Block() as block:
            @block.tensor
            def _(e):
                run("pe", e)

            @block.vector
            def _(e):
                run("dve", e)

            @block.scalar
            def _(e):
                run("act", e)

            @block.gpsimd
            def _(e):
                run("pool", e)

            @block.sync
            def _(e):
                run("sp", e)
        self.scope.close()
        self.es.close()


DN_ALPHA = 4.0 ** 0.25
LN_EPS = 1e-5
BIG = 30000.0


def make_ident(p, dtype, name):
    ident = p.sb(name, [128, 128], dtype)
    b = p.buf(name)
    p.pool(lambda e: e.memset(ident[:], 0.0), writes=[b])
    p.pool(lambda e: e.affine_select(out=ident[:], in_=ident[:], compare_op=ALU.not_equal, fill=1.0,
                                     base=0, pattern=[[-1, 128]], channel_multiplier=1), reads=[b], writes=[b])
    return ident, b


def layer_norm_tm(p, src, dst, gam, bet, bsrc, bdst, bconst, tmp, btmp, eps, key):
    stats, mv, rstd = tmp
    p.dve(lambda e: e.bn_stats(out=stats[:, 0, :], in_=src[:, 0:512]), reads=[bsrc], writes=[btmp])
    p.dve(lambda e: e.bn_stats(out=stats[:, 1, :], in_=src[:, 512:1024]), reads=[bsrc], writes=[btmp])
    p.dve(lambda e: e.bn_aggr(out=mv[:], in_=stats[:].rearrange("p a b -> p (a b)")), reads=[btmp], writes=[btmp])
    p.act(lambda e: e.activation(out=rstd[:], in_=mv[:, 1:2], func=AF.Sqrt, bias=eps, scale=1.0), reads=[btmp], writes=[btmp])
    p.dve(lambda e: e.reciprocal(out=rstd[:], in_=rstd[:]), reads=[btmp], writes=[btmp])
    p.dve(lambda e: e.tensor_scalar(out=dst, in0=src, scalar1=mv[:, 0:1], scalar2=rstd[:, 0:1],
                                    op0=ALU.subtract, op1=ALU.mult), reads=[bsrc, btmp], writes=[bdst])
    p.dve(lambda e: e.tensor_tensor(out=dst, in0=dst, in1=gam, op=ALU.mult), reads=[bdst, bconst], writes=[bdst])
    p.dve(lambda e: e.tensor_tensor(out=dst, in0=dst, in1=bet, op=ALU.add), reads=[bdst, bconst], writes=[bdst])


def build_tail(NT, glu, TG=1024):
    nc = bass.Bass("TRN2", target_bir_lowering=False)
    D = 1024
    xres = nc.dram_tensor("xres", [NT, D], F32, kind="ExternalInput").ap()
    yT = nc.dram_tensor("yT", [D, NT], BF16, kind="ExternalInput").ap()
    xo = nc.dram_tensor("xo", [NT, D], F32, kind="ExternalOutput").ap()
    p = Prog(nc)

    def load_yt(res, g, yt, byt):
        p.dma(yt[:], yT[:, g * TG:(g + 1) * TG].rearrange("(c p) t -> p c t", p=128), writes=[byt])

    def load_xres(res, g, st, dst, bdst):
        t0 = g * TG + st * 128
        p.dma(dst, xres[t0:t0 + 128, :], writes=[bdst], eng="act")

    def store_out(res, g, st, o, bo):
        t0 = g * TG + st * 128
        p.dma(xo[t0:t0 + 128, :], o[:], reads=[bo])

    phase_tail(p, nc, "", NT, glu, load_yt, load_xres, store_out, TG)
    p.finish()
    return nc


def phase_tail(p, nc, pre, NT, glu, load_yt, load_xres, store_out, TG=1024):
    D = 1024
    NG = NT // TG
    NST = TG // 128
    if glu:
        w_glu = nc.dram_tensor(pre + "w_glu", [512, 1024], F32, kind="ExternalInput").ap()
    w_out = nc.dram_tensor(pre + "w_out", [D, D], F32, kind="ExternalInput").ap()
    lnp = nc.dram_tensor(pre + "lnp", [4, 128, D], F32, kind="ExternalInput").ap()
    w_r = nc.dram_tensor(pre + "w_r", [D, 20], F32, kind="ExternalInput").ap()
    b_r = nc.dram_tensor(pre + "b_r", [128, 20], F32, kind="ExternalInput").ap()
    w_gu = nc.dram_tensor(pre + "w_gu", [16, D, 512], F32, kind="ExternalInput").ap()
    w_dn = nc.dram_tensor(pre + "w_dn", [16, 256, D], F32, kind="ExternalInput").ap()
    identf, bidf = make_ident(p, F32, "identf")
    identb = p.sb("identb", [128, 128], BF16)
    bidb = p.buf()
    p.dve(lambda e: e.tensor_copy(out=identb[:], in_=identf[:]), reads=[bidf], writes=[bidb])
    wout = p.sb("wout", [128, 8, D], BF16)
    bwout = p.buf()
    p.dma(wout[:], w_out.rearrange("(c p) f -> p c f", p=128), writes=[bwout], eng="pool")
    if glu:
        wglu = p.sb("wglu", [128, 4, 1024], BF16)
        bwglu = p.buf()
        p.dma(wglu[:], w_glu.rearrange("(c p) f -> p c f", p=128), writes=[bwglu], eng="pool")
    lns = p.sb("lns", [128, 4, D], F32)
    bln = p.buf()
    p.dma(lns[:], lnp.rearrange("a p d -> p a d"), writes=[bln])
    wr = p.sb("wr", [128, 8, 20], BF16)
    bwr = p.buf()
    p.dma(wr[:], w_r.rearrange("(c p) f -> p c f", p=128), writes=[bwr], eng="pool")
    br_ = p.sb("br", [128, 20], F32)
    bbr = p.buf()
    p.dma(br_[:], b_r, writes=[bbr])

    yt = p.sb("yt", [128, 8, TG], BF16)
    byt = p.buf()
    if glu:
        yglu = p.sb("yglu", [128, 4, TG], BF16)
        byglu = [p.buf() for _ in range(4)]
        sig = p.sb("sig", [128, 512], F32)
        bsig = p.buf()
    acc = p.sb("acc", [128, NST, D], F32)
    bacc = [p.buf() for _ in range(NST)]
    x1T = p.sb("x1T", [128, 8, TG], BF16)
    bx1T = [p.buf() for _ in range(NST)]
    gates = p.sb("gates", [128, NST, 16], F32)
    bgates = [p.buf() for _ in range(NST)]
    gub = [p.sb("gub%d" % i, [128, 8, 512], BF16) for i in range(2)]
    bgub = [p.buf() for _ in range(2)]
    dnb = [p.sb("dnb%d" % i, [128, 2, D], BF16) for i in range(2)]
    bdnb = [p.buf() for _ in range(2)]
    sg = [p.sb("sg%d" % i, [128, 256], F32) for i in range(2)]
    bsg = [p.buf() for _ in range(2)]
    hh = [p.sb("hh%d" % i, [128, 256], BF16) for i in range(3)]
    bhh = [p.buf() for _ in range(3)]
    hT = [p.sb("hT%d" % i, [128, 2, 128], BF16) for i in range(3)]
    bhT = [p.buf() for _ in range(3)]
    stats = p.sb("stats", [128, 2, 6], F32)
    mv = p.sb("mv", [128, 2], F32)
    rstd = p.sb("rstd", [128, 1], F32)
    btmp = p.buf()
    rt = p.sb("rt", [128, 80], F32)
    brt = p.buf()
    top8 = p.sb("top8", [128, 8], F32)
    xout = [p.sb("xout%d" % i, [128, D], F32) for i in range(2)]
    bxout = [p.buf() for _ in range(2)]

    psA = [p.ps("psA%d" % i, [128, 512], F32) for i in range(2)]
    bpsA = [p.pbuf() for _ in range(2)]
    psT = [p.ps("psT%d" % i, [128, 8, 128], BF16) for i in range(2)]
    bpsT = [p.pbuf() for _ in range(2)]
    psY = [p.ps("psY%d" % i, [128, 1024], F32) for i in range(2)]
    bpsY = [p.pbuf() for _ in range(2)]

    res = dict(psA=psA, bpsA=bpsA, identf=identf, bidf=bidf, TG=TG, NST=NST)
    expert_steps = [(g, e) for g in range(NG) for e in range(16)]

    def load_expert(idx):
        g, e = expert_steps[idx]
        s = idx % 2
        p.dma(gub[s][:], w_gu[e].rearrange("(c p) f -> p c f", p=128), writes=[bgub[s]], eng="pool")
        p.dma(dnb[s][:], w_dn[e].rearrange("(c p) f -> p c f", p=128), writes=[bdnb[s]], eng="pool")

    load_expert(0)
    rot = 0
    mk0 = 0
    for g in range(NG):
        t0 = g * TG
        load_yt(res, g, yt, byt)
        for st in range(NST):
            load_xres(res, g, st, acc[:, st, :], bacc[st])
        if glu:
            for half in range(TG // 512):
                ts = slice(half * 512, (half + 1) * 512)
                for f in range(4):
                    for which in range(2):
                        col0 = which * 512 + f * 128
                        for k in range(4):
                            p.pe(lambda e, which=which, col0=col0, k=k, ts=ts: e.matmul(
                                psA[which][:], lhsT=wglu[:, k, col0:col0 + 128], rhs=yt[:, 4 + k, ts],
                                start=(k == 0), stop=(k == 3)), reads=[bwglu, byt], writes=[bpsA[which]])
                    p.act(lambda e: e.activation(out=sig[:], in_=psA[1][:], func=AF.Sigmoid), reads=[bpsA[1]], writes=[bsig])
                    p.dve(lambda e, f=f, ts=ts: e.tensor_tensor(out=yglu[:, f, ts], in0=psA[0][:], in1=sig[:], op=ALU.mult),
                          reads=[bpsA[0], bsig], writes=[byglu[f]])
        for st in range(NST):
            tsl = slice(st * 128, (st + 1) * 128)
            py = psY[st % 2]
            bpy = bpsY[st % 2]
            for half in range(2):
                for c in range(8):
                    if glu and c >= 4:
                        lhs = yglu[:, c - 4, tsl]
                        rb = byglu[c - 4]
                    else:
                        lhs = yt[:, c, tsl]
                        rb = byt
                    p.pe(lambda e, lhs=lhs, c=c, half=half, py=py: e.matmul(
                        py[:, half * 512:(half + 1) * 512], lhsT=lhs, rhs=wout[:, c, half * 512:(half + 1) * 512],
                        start=(c == 0), stop=(c == 7)), reads=[rb, bwout], writes=[bpy])
            a = acc[:, st, :]
            p.dve(lambda e, a=a, py=py: e.scalar_tensor_tensor(out=a, in0=a, scalar=DN_ALPHA, in1=py[:],
                                                                 op0=ALU.mult, op1=ALU.add), reads=[bacc[st], bpy], writes=[bacc[st]])
            layer_norm_tm(p, a, a, lns[:, 0, :], lns[:, 1, :], bacc[st], bacc[st], bln, (stats, mv, rstd), btmp, LN_EPS, "ln1")
            for hf in range(2):
                pa = psA[hf]
                for c4 in range(4):
                    c = hf * 4 + c4
                    p.pe(lambda e, pa=pa, c4=c4, c=c, a=a: e.transpose(out=pa[:, c4 * 128:(c4 + 1) * 128], in_=a[:, c * 128:(c + 1) * 128],
                                                                       identity=identf[:]), reads=[bacc[st], bidf], writes=[bpsA[hf]])
                p.act(lambda e, pa=pa, hf=hf, tsl=tsl: e.copy(out=x1T[:, hf * 4:(hf + 1) * 4, tsl], in_=pa[:].rearrange("p (c t) -> p c t", c=4)),
                      reads=[bpsA[hf]], writes=[bx1T[st]])
            p.act(lambda e, a=a: e.activation(out=a, in_=a, func=AF.Copy, scale=DN_ALPHA), reads=[bacc[st]], writes=[bacc[st]])
            pr = psT[st % 2]
            pl = psY[(st + 1) % 2]
            bpl = bpsY[(st + 1) % 2]
            for c in range(8):
                p.pe(lambda e, c=c, pl=pl, tsl=tsl: e.matmul(pl[:, 0:20], lhsT=x1T[:, c, tsl], rhs=wr[:, c, :],
                                                             start=(c == 0), stop=(c == 7)), reads=[bx1T[st], bwr], writes=[bpl])
            lg = rt[:, 0:20]
            p.dve(lambda e, pl=pl: e.tensor_tensor(out=lg, in0=pl[:, 0:20], in1=br_[:], op=ALU.add), reads=[bpl, bbr], writes=[brt])
            gmax = rt[:, 20:21]
            ngmax = rt[:, 21:22]
            sume = rt[:, 22:23]
            gtop = rt[:, 23:24]
            eg = rt[:, 24:28]
            oh = rt[:, 28:32]
            em = rt[:, 32:48]
            dd = rt[:, 48:49]
            ex = rt[:, 49:50]
            w1 = rt[:, 50:51]
            w2 = rt[:, 51:52]
            t2 = rt[:, 56:72]
            R = [brt]
            p.dve(lambda e: e.tensor_reduce(out=gmax, in_=lg[:, 0:4], axis=AX.X, op=ALU.max), reads=R, writes=R)
            p.dve(lambda e: e.tensor_scalar(out=ngmax, in0=gmax, scalar1=-1.0, scalar2=None, op0=ALU.mult), reads=R, writes=R)
            p.act(lambda e: e.activation(out=eg, in_=lg[:, 0:4], func=AF.Exp, bias=ngmax, scale=1.0, accum_out=sume), reads=R, writes=R)
            p.dve(lambda e: e.reciprocal(out=gtop, in_=sume), reads=R, writes=R)
            p.dve(lambda e: e.tensor_scalar(out=oh, in0=lg[:, 0:4], scalar1=gmax, scalar2=BIG, op0=ALU.is_equal, op1=ALU.mult), reads=R, writes=R)
            p.dve(lambda e: e.tensor_scalar(out=oh, in0=oh, scalar1=-BIG, scalar2=None, op0=ALU.add), reads=R, writes=R)
            for gi in range(4):
                p.dve(lambda e, gi=gi: e.tensor_scalar(out=em[:, gi * 4:(gi + 1) * 4], in0=lg[:, 4 + gi * 4:8 + gi * 4],
                                                       scalar1=oh[:, gi:gi + 1], scalar2=None, op0=ALU.add), reads=R, writes=R)
            p.dve(lambda e: e.max(out=top8[:], in_=em), reads=R, writes=R)
            p.dve(lambda e: e.tensor_tensor(out=dd, in0=top8[:, 1:2], in1=top8[:, 0:1], op=ALU.subtract), reads=R, writes=R)
            p.act(lambda e: e.activation(out=ex, in_=dd, func=AF.Exp), reads=R, writes=R)
            p.dve(lambda e: e.tensor_scalar(out=w1, in0=ex, scalar1=1.0, scalar2=None, op0=ALU.add), reads=R, writes=R)
            p.dve(lambda e: e.reciprocal(out=w1, in_=w1), reads=R, writes=R)
            p.dve(lambda e: e.tensor_tensor(out=w2, in0=ex, in1=w1, op=ALU.mult), reads=R, writes=R)
            p.dve(lambda e: e.tensor_tensor(out=w1, in0=w1, in1=gtop, op=ALU.mult), reads=R, writes=R)
            p.dve(lambda e: e.tensor_tensor(out=w2, in0=w2, in1=gtop, op=ALU.mult), reads=R, writes=R)
            gt = gates[:, st, :]
            p.dve(lambda e, gt=gt: e.tensor_scalar(out=gt, in0=em, scalar1=top8[:, 0:1], scalar2=w1, op0=ALU.is_equal, op1=ALU.mult),
                  reads=R, writes=[bgates[st]])
            p.dve(lambda e: e.tensor_scalar(out=t2, in0=em, scalar1=top8[:, 1:2], scalar2=w2,
                                            op0=ALU.is_equal, op1=ALU.mult), reads=R, writes=R)
            p.dve(lambda e, gt=gt: e.tensor_tensor(out=gt, in0=gt, in1=t2, op=ALU.add), reads=R + [bgates[st]], writes=[bgates[st]])
        def moe_step(e_i, st, s):
            tsl = slice(st * 128, (st + 1) * 128)

            def A(k):
                r, r3 = k % 2, k % 3
                pa = psA[r]
                for c in range(8):
                    p.pe(lambda e, c=c: e.matmul(pa[:], lhsT=x1T[:, c, tsl], rhs=gub[s][:, c, :], start=(c == 0), stop=(c == 7)),
                         reads=[bx1T[st], bgub[s]], writes=[bpsA[r]])
                p.act(lambda e: e.activation(out=sg[r][:], in_=pa[:, 0:256], func=AF.Silu), reads=[bpsA[r]], writes=[bsg[r]])
                p.dve(lambda e: e.scalar_tensor_tensor(out=hh[r3][:], in0=pa[:, 256:512], scalar=gates[:, st, e_i:e_i + 1], in1=sg[r][:],
                                                       op0=ALU.mult, op1=ALU.mult), reads=[bpsA[r], bsg[r], bgates[st]], writes=[bhh[r3]])

            def B(k):
                r, r3 = k % 2, k % 3
                for kk in range(2):
                    p.pe(lambda e, kk=kk: e.transpose(out=psT[r][:, kk, :], in_=hh[r3][:, kk * 128:(kk + 1) * 128], identity=identb[:]),
                         reads=[bhh[r3], bidb], writes=[bpsT[r]])
                p.act(lambda e: e.copy(out=hT[r3][:], in_=psT[r][:, 0:2, :]), reads=[bpsT[r]], writes=[bhT[r3]])

            def C(k):
                r, r3 = k % 2, k % 3
                py = psY[r]
                for half in range(2):
                    for kk in range(2):
                        p.pe(lambda e, half=half, kk=kk: e.matmul(py[:, half * 512:(half + 1) * 512], lhsT=hT[r3][:, kk, :],
                                                                 rhs=dnb[s][:, kk, half * 512:(half + 1) * 512], start=(kk == 0), stop=(kk == 1)),
                             reads=[bhT[r3], bdnb[s]], writes=[bpsY[r]])
                a = acc[:, st, :]
                p.dve(lambda e: e.tensor_tensor(out=a, in0=a, in1=py[:], op=ALU.add), reads=[bacc[st], bpsY[r]], writes=[bacc[st]])
            return dict(A=A, B=B, C=C)

        msteps = []
        for e_i in range(16):
            idx = g * 16 + e_i
            for st in range(NST):
                msteps.append((idx, st, moe_step(e_i, st, idx % 2)))
        nst_ = len(msteps)
        ld_at = min(2, NST - 1)
        for k in range(nst_ + 2):
            if k < nst_:
                idx, st, sd = msteps[k]
                if st == ld_at and idx + 1 < len(expert_steps):
                    load_expert(idx + 1)
                sd["A"](mk0 + k)
            if 0 <= k - 1 < nst_:
                msteps[k - 1][2]["B"](mk0 + k - 1)
            if 0 <= k - 2 < nst_:
                msteps[k - 2][2]["C"](mk0 + k - 2)
        mk0 += nst_
        for st in range(NST):
            a = acc[:, st, :]
            o = xout[st % 2]
            layer_norm_tm(p, a, o[:], lns[:, 2, :], lns[:, 3, :], bacc[st], bxout[st % 2], bln, (stats, mv, rstd), btmp, LN_EPS, "ln2")
            store_out(res, g, st, o, bxout[st % 2])

import math

RMS_EPS = 1e-6
TWO_PI = 2.0 * math.pi


def s5_lambda(p, pre, shape, ar, ai, ldt, breads, T=None, extra=()):
    F = shape[1]
    if T is None:
        T = p.sb(pre + "_t", [128, 8, F], F32)
    Ti = p.sb(pre + "_ti", [128, F], I32)
    b = p.buf(pre)
    dtt, mag, th, t, kf, r, c1, s = [T[:, i, :] for i in range(8)]
    lre = p.sb(pre + "_lre", [128, F], F32)
    lim = p.sb(pre + "_lim", [128, F], F32)
    R = [b] + list(extra)
    p.act(lambda e: e.activation(out=dtt, in_=ldt, func=AF.Exp), reads=breads, writes=R)
    p.dve(lambda e: e.tensor_tensor(out=mag, in0=dtt, in1=ar, op=ALU.mult), reads=R + breads, writes=R)
    p.act(lambda e: e.activation(out=mag, in_=mag, func=AF.Exp), reads=R, writes=R)
    p.dve(lambda e: e.tensor_tensor(out=th, in0=dtt, in1=ai, op=ALU.mult), reads=R + breads, writes=R)
    for which, dst in ((0, lim), (1, lre)):
        p.dve(lambda e, which=which: e.tensor_scalar(out=t, in0=th, scalar1=1.0 / TWO_PI, scalar2=0.25 * which,
                                                     op0=ALU.mult, op1=ALU.add), reads=R, writes=R)
        p.dve(lambda e: e.tensor_copy(out=Ti[:], in_=t), reads=R, writes=R)
        p.dve(lambda e: e.tensor_copy(out=kf, in_=Ti[:]), reads=R, writes=R)
        p.dve(lambda e: e.tensor_tensor(out=r, in0=t, in1=kf, op=ALU.subtract), reads=R, writes=R)
        p.dve(lambda e: e.tensor_scalar(out=c1, in0=r, scalar1=0.5, scalar2=None, op0=ALU.is_gt), reads=R, writes=R)
        p.dve(lambda e: e.tensor_tensor(out=r, in0=r, in1=c1, op=ALU.subtract), reads=R, writes=R)
        p.dve(lambda e: e.tensor_scalar(out=c1, in0=r, scalar1=-0.5, scalar2=None, op0=ALU.is_lt), reads=R, writes=R)
        p.dve(lambda e: e.tensor_tensor(out=r, in0=r, in1=c1, op=ALU.add), reads=R, writes=R)
        p.act(lambda e: e.activation(out=s, in_=r, func=AF.Sin, scale=TWO_PI), reads=R, writes=R)
        p.dve(lambda e, dst=dst: e.tensor_tensor(out=dst[:], in0=s, in1=mag, op=ALU.mult), reads=R, writes=R)
    return lre, lim, b


def build_mixa(L, TS5=2048):
    nc = bass.Bass("TRN2", target_bir_lowering=False)
    yaT_d = nc.dram_tensor("yaT", [128, L], BF16, kind="ExternalOutput").ap()
    ysT_d = nc.dram_tensor("ysT", [128, L], BF16, kind="ExternalOutput").ap()
    p = Prog(nc)

    def emit_ya(res, ti, t, b):
        p.dma(yaT_d[:, ti * 512:(ti + 1) * 512], t[:], reads=[b])

    def emit_ys(res, ti, t, b):
        p.dma(ysT_d[:, ti * 512:(ti + 1) * 512], t[:], reads=[b])

    phase_mixa(p, nc, "", L, emit_ya, emit_ys, TS5)
    p.finish()
    return nc


def phase_mixa(p, nc, pre, L, emit_ya, emit_ys, TS5=2048):
    xT = nc.dram_tensor(pre + "xT", [1024, L], F32, kind="ExternalInput").ap()
    wA_d = nc.dram_tensor(pre + "wA", [1024, 640], F32, kind="ExternalInput").ap()
    lbl_d = nc.dram_tensor(pre + "lbl", [128, 3], F32, kind="ExternalInput").ap()
    ng_d = nc.dram_tensor(pre + "ng", [128, 128], F32, kind="ExternalInput").ap()
    sps_d = nc.dram_tensor(pre + "sps", [128, 3, 4], F32, kind="ExternalInput").ap()
    spw_d = nc.dram_tensor(pre + "spw", [128, 3, 512], F32, kind="ExternalInput").ap()
    bpad_d = nc.dram_tensor(pre + "bpad", [128, 2, 512], F32, kind="ExternalInput").ap()
    cpad_d = nc.dram_tensor(pre + "cpad", [128, 2, 512], F32, kind="ExternalInput").ap()
    dsk_d = nc.dram_tensor(pre + "dsk", [128, 1], F32, kind="ExternalInput").ap()
    tri_d = nc.dram_tensor(pre + "tri", [128, 128], F32, kind="ExternalInput").ap()
    m01_d = nc.dram_tensor(pre + "m01", [128, 512], F32, kind="ExternalInput").ap()
    res = {}
    identf, bidf = make_ident(p, F32, "identf")
    wA = p.sb("wA", [128, 8, 640], BF16)
    bwA = p.buf()
    p.dma(wA[:], wA_d.rearrange("(c p) f -> p c f", p=128), writes=[bwA], eng="pool")
    cst = p.sb("cst", [128, 3 + 128 + 12 + 1 + 128 + 512 + 8], F32)
    bc = p.buf("cst")
    lbl = cst[:, 0:3]
    ng = cst[:, 3:131]
    sps = cst[:, 131:143].rearrange("p (a b) -> p a b", a=3)
    dsk = cst[:, 143:144]
    tri = cst[:, 144:272]
    m01 = cst[:, 272:784]
    misc = cst[:, 784:792]
    p.dma(lbl, lbl_d, writes=[bc])
    p.dma(ng, ng_d, writes=[bc])
    p.dma(sps, sps_d, writes=[bc])
    p.dma(dsk, dsk_d, writes=[bc])
    p.dma(tri, tri_d, writes=[bc])
    p.dma(m01, m01_d, writes=[bc])
    spw = p.sb("spw", [128, 3, 512], F32)
    p.dma(spw[:], spw_d, writes=[bc])
    bpad = p.sb("bpad", [128, 2, 512], F32)
    p.dma(bpad[:], bpad_d, writes=[bc])
    cpad = p.sb("cpad", [128, 2, 512], F32)
    p.dma(cpad[:], cpad_d, writes=[bc])
    lbe = misc[:, 0:3]
    lbs = misc[:, 3:4]
    lb = misc[:, 4:5]
    oml = misc[:, 5:6]
    bm = p.buf("misc")
    p.act(lambda e: e.activation(out=lbe, in_=lbl, func=AF.Exp, accum_out=lbs), reads=[bc], writes=[bm])
    p.dve(lambda e: e.reciprocal(out=lbs, in_=lbs), reads=[bm], writes=[bm])
    p.dve(lambda e: e.tensor_tensor(out=lb, in0=lbe[:, 0:1], in1=lbs, op=ALU.mult), reads=[bm], writes=[bm])
    p.dve(lambda e: e.tensor_scalar(out=oml, in0=lb, scalar1=-1.0, scalar2=1.0, op0=ALU.mult, op1=ALU.add), reads=[bm], writes=[bm])

    dre = p.sb("dre", [128, 4, TS5], F32)
    dim_ = p.sb("dim", [128, 4, TS5], F32)
    bd = [p.buf("d%d" % i) for i in range(4)]
    assert TS5 >= 1024
    ls_re, ls_im, bls = s5_lambda(p, "ls", [128, 4], sps[:, 0, :], sps[:, 1, :], sps[:, 2, :], [bc])
    lw_re, lw_im, blw = s5_lambda(p, "lw", [128, 512], spw[:, 0, :], spw[:, 1, :], spw[:, 2, :], [bc],
                                  T=dre[:].rearrange("p a t -> p (a t)")[:, 0:4096].rearrange("p (a f) -> p a f", a=8), extra=bd)
    W = dim_[:].rearrange("p a t -> p (a t)")[:, 0:4096].rearrange("p (a f) -> p a f", a=8)
    bW = p.buf("wtmp")
    xr, den, t1, t2, fr, fi, o1, o2 = [W[:, i, :] for i in range(8)]
    arw, aiw = spw[:, 0, :], spw[:, 1, :]
    RW = [bW, blw, bc]
    p.dve(lambda e: e.tensor_scalar(out=xr, in0=lw_re[:], scalar1=-1.0, scalar2=None, op0=ALU.add), reads=RW, writes=[bW] + bd)
    p.dve(lambda e: e.tensor_tensor(out=den, in0=arw, in1=arw, op=ALU.mult), reads=RW, writes=[bW])
    p.dve(lambda e: e.tensor_tensor(out=t1, in0=aiw, in1=aiw, op=ALU.mult), reads=RW, writes=[bW])
    p.dve(lambda e: e.tensor_tensor(out=den, in0=den, in1=t1, op=ALU.add), reads=RW, writes=[bW])
    p.dve(lambda e: e.reciprocal(out=den, in_=den), reads=RW, writes=[bW])
    p.dve(lambda e: e.tensor_tensor(out=t1, in0=xr, in1=arw, op=ALU.mult), reads=RW, writes=[bW])
    p.dve(lambda e: e.tensor_tensor(out=t2, in0=lw_im[:], in1=aiw, op=ALU.mult), reads=RW, writes=[bW])
    p.dve(lambda e: e.tensor_tensor(out=fr, in0=t1, in1=t2, op=ALU.add), reads=RW, writes=[bW])
    p.dve(lambda e: e.tensor_tensor(out=fr, in0=fr, in1=den, op=ALU.mult), reads=RW, writes=[bW])
    p.dve(lambda e: e.tensor_tensor(out=t1, in0=lw_im[:], in1=arw, op=ALU.mult), reads=RW, writes=[bW])
    p.dve(lambda e: e.tensor_tensor(out=t2, in0=xr, in1=aiw, op=ALU.mult), reads=RW, writes=[bW])
    p.dve(lambda e: e.tensor_tensor(out=fi, in0=t1, in1=t2, op=ALU.subtract), reads=RW, writes=[bW])
    p.dve(lambda e: e.tensor_tensor(out=fi, in0=fi, in1=den, op=ALU.mult), reads=RW, writes=[bW])
    wB = p.sb("wB", [128, 2, 512], BF16)
    wC = p.sb("wC", [128, 2, 512], BF16)
    bwB = p.buf("wB")
    bre, bim = bpad[:, 0, :], bpad[:, 1, :]
    p.dve(lambda e: e.tensor_tensor(out=o1, in0=fr, in1=bre, op=ALU.mult), reads=RW, writes=[bW])
    p.dve(lambda e: e.tensor_tensor(out=o2, in0=fi, in1=bim, op=ALU.mult), reads=RW, writes=[bW])
    p.dve(lambda e: e.tensor_tensor(out=wB[:, 0, :], in0=o1, in1=o2, op=ALU.subtract), reads=RW, writes=[bwB])
    p.dve(lambda e: e.tensor_tensor(out=o1, in0=fr, in1=bim, op=ALU.mult), reads=RW, writes=[bW])
    p.dve(lambda e: e.tensor_tensor(out=o2, in0=fi, in1=bre, op=ALU.mult), reads=RW, writes=[bW])
    p.dve(lambda e: e.tensor_tensor(out=wB[:, 1, :], in0=o1, in1=o2, op=ALU.add), reads=RW, writes=[bwB])
    p.dve(lambda e: e.tensor_copy(out=wC[:, 0, :], in_=cpad[:, 0, :]), reads=[bc], writes=[bwB])
    p.dve(lambda e: e.tensor_scalar(out=wC[:, 1, :], in0=cpad[:, 1, :], scalar1=-1.0, scalar2=None, op0=ALU.mult), reads=[bc, bW], writes=[bwB] + bd)

    NLEV = 3
    lamp = p.sb("lamp", [128, NLEV, 3, 4], F32)
    ltmp = p.sb("ltmp", [128, 4, 4], F32)
    blam = p.buf("lam")
    RL = [blam, bls]
    p.dve(lambda e: e.tensor_copy(out=lamp[:, 0, 0, :], in_=ls_re[:]), reads=RL, writes=[blam])
    p.dve(lambda e: e.tensor_copy(out=lamp[:, 0, 1, :], in_=ls_im[:]), reads=RL, writes=[blam])
    for lev in range(1, NLEV):
        p.dve(lambda e, lev=lev: e.tensor_copy(out=lamp[:, lev, 0:2, :], in_=lamp[:, lev - 1, 0:2, :]), reads=RL, writes=[blam])
        for _ in range(4):
            a = lamp[:, lev, 0, :]
            b_ = lamp[:, lev, 1, :]
            p.dve(lambda e, a=a: e.tensor_tensor(out=ltmp[:, 0, :], in0=a, in1=a, op=ALU.mult), reads=RL, writes=[blam])
            p.dve(lambda e, b_=b_: e.tensor_tensor(out=ltmp[:, 1, :], in0=b_, in1=b_, op=ALU.mult), reads=RL, writes=[blam])
            p.dve(lambda e, a=a, b_=b_: e.tensor_tensor(out=ltmp[:, 2, :], in0=a, in1=b_, op=ALU.mult), reads=RL, writes=[blam])
            p.dve(lambda e, a=a: e.tensor_tensor(out=a, in0=ltmp[:, 0, :], in1=ltmp[:, 1, :], op=ALU.subtract), reads=RL, writes=[blam])
            p.dve(lambda e, b_=b_: e.tensor_scalar(out=b_, in0=ltmp[:, 2, :], scalar1=2.0, scalar2=None, op0=ALU.mult), reads=RL, writes=[blam])
    for lev in range(NLEV):
        p.dve(lambda e, lev=lev: e.tensor_scalar(out=lamp[:, lev, 2, :], in0=lamp[:, lev, 1, :], scalar1=-1.0, scalar2=None, op0=ALU.mult),
              reads=RL, writes=[blam])

    xt = [p.sb("xt%d" % i, [128, 8, 512], BF16) for i in range(2)]
    bxt = [p.buf() for _ in range(2)]
    H = p.sb("hg", [128, 9, 512], F32)
    bH = p.buf("hg")
    f_, lf, kk, bb, eb, enb, qq, kinv, ktT = [H[:, i, :] for i in range(9)]
    qdec = p.sb("qdec", [128, 512], BF16)
    kinvb = p.sb("kinvb", [128, 512], BF16)
    bqk = p.buf("qk")
    vb = p.sb("vb", [128, 4, 128], BF16)
    gn = p.sb("gn", [128, 4, 128], F32)
    bvg = [p.buf() for _ in range(4)]
    kt = p.sb("kt", [128, 4, 128], BF16)
    bkt = [p.buf() for _ in range(4)]
    attT = [p.sb("attT%d" % i, [128, 128], BF16) for i in range(2)]
    battT = [p.buf() for _ in range(2)]
    yasb = [p.sb("yasb%d" % i, [128, 4, 128], F32) for i in range(2)]
    yaTs = [p.sb("yaTs%d" % i, [128, 512], BF16) for i in range(2)]
    byaTs = [p.buf() for _ in range(2)]
    byasb = [p.buf() for _ in range(2)]
    S = p.sb("S", [128, 128], F32)
    bS = p.buf("S")
    Sb = [p.sb("Sb%d" % i, [128, 128], BF16) for i in range(2)]
    bSb = [p.buf() for _ in range(2)]
    osc = p.sb("osc", [128, 128], F32)
    om = p.sb("om", [128, 2], F32)
    bo = p.buf("o")
    p.dve(lambda e: e.memset(S[:], 0.0), writes=[bS])
    p.dve(lambda e: e.memset(Sb[1][:], 0.0), writes=[bSb[1]])
    NB1 = TS5 // 16
    assert NB1 % 16 == 0 or NB1 <= 16
    uT = p.sb("uT", [128, TS5], F32)
    buT = p.buf("uT")
    uTb = [p.sb("uTb%d" % i, [128, 512], BF16) for i in range(2)]
    buTb = [p.buf() for _ in range(2)]
    nb_levels = []
    n = TS5
    while n > 16:
        n //= 16
        nb_levels.append(n)
    Ebufs = []
    for li, nbl in enumerate(nb_levels):
        Ebufs.append((p.sb("Ere%d" % li, [128, 4, nbl + 1], F32), p.sb("Eim%d" % li, [128, 4, nbl + 1], F32)))
    carry = p.sb("carry", [128, 2, 4], F32)
    p.dve(lambda e: e.memset(carry[:], 0.0), writes=bd)
    stmp = p.sb("stmp", [128, 4, 2, max(NB1, 16)], F32)
    hb = [p.sb("hb%d" % i, [128, 2, 4, 512], BF16) for i in range(2)]
    bhb = [p.buf() for _ in range(2)]
    zs = p.sb("zs", [128, 512], F32)
    bzs = p.buf()
    ysb = [p.sb("ysb%d" % i, [128, 512], BF16) for i in range(2)]
    bysb = [p.buf() for _ in range(2)]

    psQ = p.ps("psQ", [128, 512], F32); bpsQ = p.pbuf()
    psF = p.ps("psF", [128, 512], F32); bpsF = p.pbuf()
    psU = p.ps("psU", [128, 512], F32); bpsU = p.pbuf()
    psVG = p.ps("psVG", [128, 2, 256], F32); _b = p.pbuf(); bpsVG = [_b, _b]
    psD1 = p.ps("psD", [128, 512], F32); psD = [psD1, psD1]; _b = p.pbuf(); bpsD = [_b, _b]
    psS = p.ps("psS", [128, 4, 128], F32); _b = p.pbuf(); bpsS = [_b, _b]
    psO = p.ps("psO", [128, 4, 128], F32); _b = p.pbuf(); bpsO = [_b, _b]
    psM = p.ps("psM", [128, 4, 128], F32); _b = p.pbuf(); bpsM = [_b] * 4

    def cstep(i, lev, dst_re, dst_im, prev_re, prev_im, n, add_re=None, add_im=None):
        ar = lamp[:, lev, 0, i:i + 1]
        ai = lamp[:, lev, 1, i:i + 1]
        nai = lamp[:, lev, 2, i:i + 1]
        if add_re is None:
            add_re, add_im = dst_re, dst_im
        ta = stmp[:, i, 0, 0:n]
        tb = stmp[:, i, 1, 0:n]
        R = [bd[i], blam]
        p.dve(lambda e: e.scalar_tensor_tensor(out=ta, in0=prev_im, scalar=nai, in1=add_re, op0=ALU.mult, op1=ALU.add), reads=R, writes=[bd[i]])
        p.dve(lambda e: e.scalar_tensor_tensor(out=tb, in0=prev_re, scalar=ai, in1=add_im, op0=ALU.mult, op1=ALU.add), reads=R, writes=[bd[i]])
        p.dve(lambda e: e.scalar_tensor_tensor(out=dst_re, in0=prev_re, scalar=ar, in1=ta, op0=ALU.mult, op1=ALU.add), reads=R, writes=[bd[i]])
        p.dve(lambda e: e.scalar_tensor_tensor(out=dst_im, in0=prev_im, scalar=ar, in1=tb, op0=ALU.mult, op1=ALU.add), reads=R, writes=[bd[i]])

    def cscan(lev, Xre, Xim, n, hin_re, hin_im):
        if n <= 16:
            for t in range(n):
                for i in range(4):
                    pr = hin_re(i) if t == 0 else Xre(i)[:, t - 1:t]
                    pi_ = hin_im(i) if t == 0 else Xim(i)[:, t - 1:t]
                    cstep(i, lev, Xre(i)[:, t:t + 1], Xim(i)[:, t:t + 1], pr, pi_, 1)
            return
        nb = n // 16
        Ere, Eim = Ebufs[lev]
        for i in range(4):
            p.dve(lambda e, i=i: e.tensor_copy(out=Ere[:, i, 0:1], in_=hin_re(i)), reads=[bd[i]], writes=[bd[i]])
            p.dve(lambda e, i=i: e.tensor_copy(out=Eim[:, i, 0:1], in_=hin_im(i)), reads=[bd[i]], writes=[bd[i]])
            p.dve(lambda e, i=i: e.tensor_copy(out=Ere[:, i, 1:nb + 1], in_=Xre(i)[:, 0:n:16]), reads=[bd[i]], writes=[bd[i]])
            p.dve(lambda e, i=i: e.tensor_copy(out=Eim[:, i, 1:nb + 1], in_=Xim(i)[:, 0:n:16]), reads=[bd[i]], writes=[bd[i]])
        for r in range(1, 16):
            for i in range(4):
                cstep(i, lev, Ere[:, i, 1:nb + 1], Eim[:, i, 1:nb + 1], Ere[:, i, 1:nb + 1], Eim[:, i, 1:nb + 1], nb,
                      add_re=Xre(i)[:, r:n:16], add_im=Xim(i)[:, r:n:16])
        cscan(lev + 1, lambda i: Ere[:, i, 1:nb + 1], lambda i: Eim[:, i, 1:nb + 1], nb,
              lambda i: Ere[:, i, 0:1], lambda i: Eim[:, i, 0:1])
        for r in range(16):
            for i in range(4):
                if r == 0:
                    pr, pi_ = Ere[:, i, 0:nb], Eim[:, i, 0:nb]
                else:
                    pr, pi_ = Xre(i)[:, r - 1:n:16], Xim(i)[:, r - 1:n:16]
                cstep(i, lev, Xre(i)[:, r:n:16], Xim(i)[:, r:n:16], pr, pi_, nb)

    NTILE = L // 512
    TPS = TS5 // 512

    def load_x(ti):
        s = ti % 2
        p.dma(xt[s][:], xT[:, ti * 512:(ti + 1) * 512].rearrange("(c p) t -> p c t", p=128), writes=[bxt[s]], eng="pool")

    load_x(0)
    chunk_idx = 0
    for ti in range(NTILE):
        s = ti % 2
        x_ = xt[s]
        if ti + 1 < NTILE:
            load_x(ti + 1)
        t0 = ti * 512
        tl = (ti % TPS) * 512
        for (ps_, bps_, c0) in ((psQ, bpsQ, 0), (psF, bpsF, 128), (psU, bpsU, 512)):
            for c in range(8):
                p.pe(lambda e, ps_=ps_, c=c, c0=c0, x_=x_: e.matmul(ps_[:], lhsT=wA[:, c, c0:c0 + 128], rhs=x_[:, c, :],
                                                                    start=(c == 0), stop=(c == 7)), reads=[bwA, bxt[s]], writes=[bps_])
        RH = [bH]
        p.act(lambda e: e.activation(out=f_, in_=psF[:], func=AF.Sigmoid), reads=[bpsF], writes=RH)
        p.dve(lambda e: e.tensor_scalar(out=f_, in0=f_, scalar1=oml, scalar2=lb, op0=ALU.mult, op1=ALU.add), reads=RH + [bm], writes=RH)
        p.act(lambda e: e.activation(out=lf, in_=f_, func=AF.Ln), reads=RH, writes=RH)
        p.dve(lambda e: e.tensor_scalar(out=kk, in0=f_, scalar1=-1.0, scalar2=1.0, op0=ALU.mult, op1=ALU.add), reads=RH, writes=RH)
        p.dve(lambda e: e.tensor_tensor_scan(out=bb, data0=m01, data1=lf, initial=0.0, op0=ALU.mult, op1=ALU.add), reads=RH + [bc], writes=RH)
        p.act(lambda e: e.activation(out=eb, in_=bb, func=AF.Exp), reads=RH, writes=RH)
        p.act(lambda e: e.activation(out=enb, in_=bb, func=AF.Exp, scale=-1.0), reads=RH, writes=RH)
        p.act(lambda e: e.activation(out=qq, in_=psQ[:], func=AF.Silu), reads=[bpsQ], writes=RH)
        p.dve(lambda e: e.tensor_tensor(out=qdec[:], in0=qq, in1=eb, op=ALU.mult), reads=RH, writes=[bqk])
        p.dve(lambda e: e.tensor_tensor(out=kinv, in0=kk, in1=enb, op=ALU.mult), reads=RH, writes=RH)
        p.act(lambda e: e.copy(out=kinvb[:], in_=kinv), reads=RH, writes=[bqk])
        eb3 = eb.rearrange("p (c s) -> p c s", s=64)
        p.dve(lambda e: e.tensor_tensor(out=ktT.rearrange("p (c s) -> p c s", s=64), in0=kinv.rearrange("p (c s) -> p c s", s=64),
                                        in1=eb3[:, :, 63:64].to_broadcast([128, 8, 64]), op=ALU.mult), reads=RH, writes=RH)
        sb5 = ti % 2
        p.act(lambda e, tl=tl: e.copy(out=uT[:, tl:tl + 512], in_=psU[:]), reads=[bpsU], writes=[buT])
        p.dve(lambda e, sb5=sb5: e.tensor_copy(out=uTb[sb5][:], in_=psU[:]), reads=[bpsU], writes=[buTb[sb5]])
        k = 0
        for i in range(4):
            for ri in range(2):
                pd = psD[k % 2]
                p.pe(lambda e, pd=pd, i=i, ri=ri, sb5=sb5: e.matmul(pd[:], lhsT=wB[:, ri, i * 128:(i + 1) * 128], rhs=uTb[sb5][:],
                                                                    start=True, stop=True), reads=[bwB, buTb[sb5]], writes=[bpsD[k % 2]])
                dst = (dre if ri == 0 else dim_)[:, i, tl:tl + 512]
                p.act(lambda e, pd=pd, dst=dst: e.copy(out=dst, in_=pd[:]), reads=[bpsD[k % 2]], writes=[bd[i]])
                k += 1
        for sub in range(4):
            tsl = slice(sub * 128, (sub + 1) * 128)
            h2 = sub % 2
            for c in range(8):
                p.pe(lambda e, c=c, tsl=tsl, h2=h2, x_=x_: e.matmul(psVG[:, h2, :], lhsT=x_[:, c, tsl], rhs=wA[:, c, 256:512],
                                                                    start=(c == 0), stop=(c == 7)), reads=[bwA, bxt[s]], writes=[bpsVG[h2]])
            p.act(lambda e, sub=sub, h2=h2: e.copy(out=vb[:, sub, :], in_=psVG[:, h2, 0:128]), reads=[bpsVG[h2]], writes=[bvg[sub]])
            p.act(lambda e, sub=sub, h2=h2: e.activation(out=gn[:, sub, :], in_=psVG[:, h2, 128:256], func=AF.Silu), reads=[bpsVG[h2]], writes=[bvg[sub]])
            p.dve(lambda e, sub=sub: e.tensor_tensor(out=gn[:, sub, :], in0=gn[:, sub, :], in1=ng, op=ALU.mult), reads=[bvg[sub], bc], writes=[bvg[sub]])
            mi = sub % 2
            p.pe(lambda e, mi=mi, tsl=tsl: e.transpose(out=psM[:, mi, :], in_=ktT[:, tsl], identity=identf[:]), reads=RH + [bidf], writes=[bpsM[mi]])
            p.act(lambda e, mi=mi, sub=sub: e.copy(out=kt[:, sub, :], in_=psM[:, mi, :]), reads=[bpsM[mi]], writes=[bkt[sub]])
        ys_ = yasb[ti % 2]
        for sub in range(4):
            tsl = slice(sub * 128, (sub + 1) * 128)
            ai_ = sub % 2
            mi = 2 + sub % 2
            p.pe(lambda e, mi=mi, tsl=tsl: e.matmul(psM[:, mi, :], lhsT=kinvb[:, tsl], rhs=qdec[:, tsl], start=True, stop=True),
                 reads=[bqk], writes=[bpsM[mi]])
            p.dve(lambda e, mi=mi, ai_=ai_: e.tensor_tensor(out=attT[ai_][:], in0=psM[:, mi, :], in1=tri, op=ALU.mult),
                  reads=[bpsM[mi], bc], writes=[battT[ai_]])
            oi = sub % 2
            p.pe(lambda e, oi=oi, ai_=ai_, sub=sub: e.matmul(psO[:, oi, :], lhsT=attT[ai_][:], rhs=vb[:, sub, :], start=True, stop=False),
                 reads=[battT[ai_], bvg[sub]], writes=[bpsO[oi]])
            for hc in range(2):
                rows = slice(hc * 64, (hc + 1) * 64)
                tch = slice(sub * 128 + hc * 64, sub * 128 + (hc + 1) * 64)
                sprev = (chunk_idx + 1) % 2
                snew = chunk_idx % 2
                p.pe(lambda e, oi=oi, rows=rows, tch=tch, sprev=sprev, hc=hc: e.matmul(
                    psO[rows, oi, :], lhsT=qdec[:, tch], rhs=Sb[sprev][:], start=False, stop=(hc == 1)),
                    reads=[bqk, bSb[sprev]], writes=[bpsO[oi]])
                di = chunk_idx % 2
                p.pe(lambda e, di=di, rows=rows, sub=sub: e.matmul(psS[:, di, :], lhsT=kt[rows, sub, :], rhs=vb[rows, sub, :], start=True, stop=True),
                     reads=[bkt[sub], bvg[sub]], writes=[bpsS[di]])
                cpos = sub * 128 + hc * 64 + 63
                p.dve(lambda e, di=di, cpos=cpos: e.scalar_tensor_tensor(out=S[:], in0=S[:], scalar=eb[:, cpos:cpos + 1], in1=psS[:, di, :],
                                                                         op0=ALU.mult, op1=ALU.add), reads=[bS, bpsS[di]] + RH, writes=[bS])
                p.act(lambda e, snew=snew: e.copy(out=Sb[snew][:], in_=S[:]), reads=[bS], writes=[bSb[snew]])
                chunk_idx += 1
            p.act(lambda e, oi=oi: e.activation(out=osc[:], in_=psO[:, oi, :], func=AF.Square, accum_out=om[:, 0:1]), reads=[bpsO[oi]], writes=[bo])
            p.act(lambda e: e.activation(out=om[:, 1:2], in_=om[:, 0:1], func=AF.Sqrt, scale=1.0 / 128.0, bias=RMS_EPS), reads=[bo], writes=[bo])
            p.dve(lambda e: e.reciprocal(out=om[:, 1:2], in_=om[:, 1:2]), reads=[bo], writes=[bo])
            p.dve(lambda e, oi=oi, sub=sub, ys_=ys_: e.scalar_tensor_tensor(out=ys_[:, sub, :], in0=psO[:, oi, :], scalar=om[:, 1:2], in1=gn[:, sub, :],
                                                                         op0=ALU.mult, op1=ALU.mult), reads=[bpsO[oi], bo, bvg[sub]], writes=[byasb[ti % 2]])
        for sub in range(4):
            p.pe(lambda e, sub=sub, ys_=ys_: e.transpose(out=psM[:, sub, :], in_=ys_[:, sub, :], identity=identf[:]),
                 reads=[byasb[ti % 2], bidf], writes=[bpsM[0]])
        yt_ = yaTs[ti % 2]
        p.act(lambda e, yt_=yt_: e.copy(out=yt_[:], in_=psM[:].rearrange("p a b -> p (a b)")), reads=[bpsM[0]], writes=[byaTs[ti % 2]])
        emit_ya(res, ti, yt_, byaTs[ti % 2])
        if (ti + 1) % TPS == 0:
            sup0 = (ti + 1 - TPS) * 512
            cscan(0, lambda i: dre[:, i, :], lambda i: dim_[:, i, :], TS5,
                  lambda i: carry[:, 0, i:i + 1], lambda i: carry[:, 1, i:i + 1])
            for i in range(4):
                p.dve(lambda e, i=i: e.tensor_copy(out=carry[:, 0, i:i + 1], in_=dre[:, i, TS5 - 1:TS5]), reads=[bd[i]], writes=[bd[i]])
                p.dve(lambda e, i=i: e.tensor_copy(out=carry[:, 1, i:i + 1], in_=dim_[:, i, TS5 - 1:TS5]), reads=[bd[i]], writes=[bd[i]])
            for ch in range(TPS):
                csl = slice(ch * 512, (ch + 1) * 512)
                hs = ch % 2
                for i in range(4):
                    p.act(lambda e, i=i, hs=hs, csl=csl: e.copy(out=hb[hs][:, 0, i, :], in_=dre[:, i, csl]), reads=[bd[i]], writes=[bhb[hs]])
                    p.act(lambda e, i=i, hs=hs, csl=csl: e.copy(out=hb[hs][:, 1, i, :], in_=dim_[:, i, csl]), reads=[bd[i]], writes=[bhb[hs]])
                k = 0
                for i in range(4):
                    for ri in range(2):
                        p.pe(lambda e, i=i, ri=ri, hs=hs, k=k: e.matmul(psQ[:], lhsT=wC[:, ri, i * 128:(i + 1) * 128], rhs=hb[hs][:, ri, i, :],
                                                                        start=(k == 0), stop=(k == 7)), reads=[bwB, bhb[hs]], writes=[bpsQ])
                        k += 1
                p.dve(lambda e, csl=csl: e.scalar_tensor_tensor(out=zs[:], in0=uT[:, csl], scalar=dsk, in1=psQ[:], op0=ALU.mult, op1=ALU.add),
                      reads=[buT, bc, bpsQ], writes=[bzs])
                p.act(lambda e, hs=hs: e.activation(out=ysb[hs][:], in_=zs[:], func=AF.Gelu_apprx_tanh), reads=[bzs], writes=[bysb[hs]])
                emit_ys(res, (sup0 // 512) + ch, ysb[hs], bysb[hs])


BIGM = 30000.0
DEBUG_MIXC = False


def build_mixc(L):
    nc = bass.Bass("TRN2", target_bir_lowering=False)
    xT = nc.dram_tensor("xT", [1024, L], F32, kind="ExternalInput").ap()
    ycT_d = nc.dram_tensor("ycT", [128, L], BF16, kind="ExternalOutput").ap()
    ydT_d = nc.dram_tensor("ydT", [128, L], BF16, kind="ExternalOutput").ap()
    p = Prog(nc)
    p.debug = DEBUG_MIXC

    def load_x(res, ti, dst, bdst):
        p.dma(dst[:], xT[:, ti * 512:(ti + 1) * 512].rearrange("(c p) t -> p c t", p=128), writes=[bdst], eng="pool")

    def emit_y(res, ti, which, t, b):
        d = ycT_d if which == 0 else ydT_d
        p.dma(d[:, ti * 512:(ti + 1) * 512], t[:], reads=[b])

    phase_mixc(p, nc, "", L, load_x, emit_y)
    p.finish()
    return nc


def phase_mixc(p, nc, pre, L, load_x_cb, emit_y):
    NBLK = L // 256
    wC_d = nc.dram_tensor(pre + "wC", [1024, 768], F32, kind="ExternalInput").ap()
    sink_d = nc.dram_tensor(pre + "sink", [128, 2], F32, kind="ExternalInput").ap()
    swm_d = nc.dram_tensor(pre + "swm", [128, 2, 256], F32, kind="ExternalInput").ap()
    cm_d = nc.dram_tensor(pre + "cm", [128, 2, 256], F32, kind="ExternalInput").ap()
    fut_d = nc.dram_tensor(pre + "fut", [128, 2, 128], F32, kind="ExternalInput").ap()
    res = {}
    identb_f, bidf = make_ident(p, F32, "identf")
    identb = p.sb("identb", [128, 128], BF16)
    bidb = p.buf()
    p.dve(lambda e: e.tensor_copy(out=identb[:], in_=identb_f[:]), reads=[bidf], writes=[bidb])
    wC = p.sb("wC", [128, 8, 768], BF16)
    bwC = p.buf()
    p.dma(wC[:], wC_d.rearrange("(c p) f -> p c f", p=128), writes=[bwC], eng="pool")
    cst = p.sb("cst", [128, 2 + 512 + 512 + 256], F32)
    bc = p.buf("cst")
    sink = cst[:, 0:2]
    swm = cst[:, 2:514].rearrange("p (a k) -> p a k", a=2)
    cm = cst[:, 514:1026].rearrange("p (a k) -> p a k", a=2)
    fut = cst[:, 1026:1282].rearrange("p (a k) -> p a k", a=2)
    p.dma(sink, sink_d, writes=[bc])
    p.dma(swm, swm_d, writes=[bc])
    p.dma(cm, cm_d, writes=[bc])
    p.dma(fut, fut_d, writes=[bc])
    ones = p.sb("ones", [128, 128], F32)
    bones = p.buf()
    p.dve(lambda e: e.memset(ones[:], 1.0), writes=[bones])

    qcT = p.sb("qcT", [128, 512], BF16); bqc = p.buf()
    qdT = p.sb("qdT", [128, 512], BF16); bqd = p.buf()
    kkc = p.sb("kkc", [128, L], BF16)
    kkd = p.sb("kkd", [128, 2, L], BF16)
    vc = p.sb("vc", [128, L // 128, 64], BF16)
    vd = p.sb("vd", [128, L // 128, 66], BF16)
    bkv = p.buf("kv")
    bkvt = [p.buf("kv%d" % i) for i in range(L // 512)]
    kmT = p.sb("kmT", [128, 2, 64], BF16)
    bkm = p.buf("km")
    p.dve(lambda e: e.memset(kmT[:], 0.0), writes=[bkm])
    kmx = p.sb("kmx", [128, 4], F32)
    p.dve(lambda e: e.memset(kmx[:], 0.0), writes=[bkm])
    p.dve(lambda e: e.memset(vd[:, :, 64:66], 1.0), writes=[bkvt[0]])
    xt = [p.sb("xt%d" % i, [128, 8, 512], BF16) for i in range(2)]
    bxt = [p.buf() for _ in range(2)]
    sq = p.sb("sq", [128, 512], F32); bsq = p.buf()
    qab = p.sb("qab", [128, 512], BF16)
    bqab = p.buf()
    kab = p.sb("kab", [128, 2, 2], BF16)
    k8 = p.sb("k8", [128, 8], F32)
    sm = [p.sb("sm%d" % i, [128, 256], F32) for i in range(2)]; bsm = [p.buf() for _ in range(2)]
    Pb = [p.sb("Pb%d" % i, [128, 512], BF16) for i in range(4)]; bPb = [p.buf() for _ in range(4)]
    Dring = p.sb("Dring", [128, 8, 128], BF16); bD = [p.buf() for _ in range(8)]
    dcnt = [0]
    PT = [p.sb("PT%d" % i, [128, 4, 128], BF16) for i in range(4)]; bPT = [p.buf() for _ in range(4)]
    scS = [p.sb("scS%d" % i, [128, 8], F32) for i in range(6)]; bscS = [p.buf() for _ in range(6)]
    gset = []
    for i in range(5):
        gset.append(dict(sc=p.sb("scg%d" % i, [128, 8], F32), gm=p.sb("gm%d" % i, [128, 64], F32), sel=p.sb("sel%d" % i, [128, 64], F32),
                         top8=p.sb("top8_%d" % i, [128, 8], F32), lcols=p.sb("lcols%d" % i, [128, 66], F32), b=p.buf("gate%d" % i)))
    ycs = [p.sb("ycs%d" % i, [128, 4, 128], F32) for i in range(2)]; bycs = [p.buf() for _ in range(2)]
    yds = [p.sb("yds%d" % i, [128, 4, 128], F32) for i in range(2)]; byds = [p.buf() for _ in range(2)]
    yTs = [p.sb("yTs%d" % i, [128, 512], BF16) for i in range(4)]; byTs = [p.buf() for _ in range(4)]

    _psP = p.ps("psP", [128, 512], F32); _bpsP = p.pbuf(); psP = [_psP, _psP]; bpsP = [_bpsP, _bpsP]
    psV = p.ps("psV", [128, 512], F32); bpsV = p.pbuf()
    psS = [p.ps("psS%d" % i, [128, 512], F32) for i in range(2)]; bpsS = [p.pbuf() for _ in range(2)]
    psT = [p.ps("psT%d" % i, [128, 4, 128], F32) for i in range(2)]; bpsT = [p.pbuf() for _ in range(2)]
    psO = [p.ps("psO%d" % i, [128, 512], F32) for i in range(2)]; bpsO = [p.pbuf() for _ in range(2)]

    NTILE = L // 512

    def load_x(ti):
        s = ti % 2
        load_x_cb(res, ti, xt[s], bxt[s])

    load_x(0)
    rot = 0
    kbase = 0
    for ti in range(NTILE):
        s = ti % 2
        x_ = xt[s]
        if ti + 1 < NTILE:
            load_x(ti + 1)
        t0 = ti * 512
        tsl = slice(t0, t0 + 512)
        dsts = [(qcT[:], bqc, 0.125), (kkc[:, tsl], bkvt[ti], 1.0), (qdT[:], bqd, 0.125), (kkd[:, 0, tsl], bkvt[ti], 1.0), (kkd[:, 1, tsl], bkvt[ti], 1.0)]
        for pi, (dst, bdst, scl) in enumerate(dsts):
            pp = psP[pi % 2]
            for c in range(8):
                p.pe(lambda e, pp=pp, c=c, pi=pi, x_=x_: e.matmul(pp[:], lhsT=wC[:, c, pi * 128:(pi + 1) * 128], rhs=x_[:, c, :],
                                                                  start=(c == 0), stop=(c == 7)), reads=[bwC, bxt[s]], writes=[bpsP[pi % 2]])
            p.act(lambda e, pp=pp, dst=dst, scl=scl: e.activation(out=dst, in_=pp[:], func=AF.Copy, scale=scl), reads=[bpsP[pi % 2]], writes=[bdst])
            if pi == 2:
                p.dve(lambda e, pp=pp: e.tensor_scalar(out=sq[:], in0=pp[:], scalar1=-1.0, scalar2=None, op0=ALU.mult), reads=[bpsP[pi % 2]], writes=[bsq])
                p.dve(lambda e, pp=pp: e.tensor_tensor(out=sq[:], in0=sq[:], in1=pp[:], op=ALU.max), reads=[bpsP[pi % 2], bsq], writes=[bsq])
                p.dve(lambda e: e.tensor_scalar(out=qab[:], in0=sq[:], scalar1=0.125, scalar2=None, op0=ALU.mult), reads=[bsq], writes=[bqab])
            if pi >= 3:
                hh_ = pi - 3
                p.dve(lambda e, pp=pp: e.tensor_scalar(out=sq[:], in0=pp[:], scalar1=-1.0, scalar2=None, op0=ALU.mult), reads=[bpsP[pi % 2]], writes=[bsq])
                p.dve(lambda e, pp=pp: e.tensor_tensor(out=sq[:], in0=sq[:], in1=pp[:], op=ALU.max), reads=[bpsP[pi % 2], bsq], writes=[bsq])
                p.dve(lambda e: e.max(out=k8[:], in_=sq[:]), reads=[bsq], writes=[bkm])
                p.dve(lambda e, hh_=hh_: e.tensor_tensor(out=kmx[:, hh_:hh_ + 1], in0=kmx[:, hh_:hh_ + 1], in1=k8[:, 0:1], op=ALU.max), reads=[bkm], writes=[bkm])
                p.dve(lambda e, hh_=hh_: e.tensor_copy(out=kab[:, hh_, 0:1], in_=kmx[:, hh_:hh_ + 1]), reads=[bkm], writes=[bkm])
                p.dve(lambda e, hh_=hh_: e.tensor_copy(out=kab[:, hh_, 1:2], in_=kmx[:, hh_:hh_ + 1]), reads=[bkm], writes=[bkm])
        for sub in range(4):
            g = ti * 4 + sub
            for c in range(8):
                p.pe(lambda e, c=c, sub=sub, x_=x_: e.matmul(psV[:, 0:128], lhsT=x_[:, c, sub * 128:(sub + 1) * 128], rhs=wC[:, c, 640:768],
                                                             start=(c == 0), stop=(c == 7)), reads=[bwC, bxt[s]], writes=[bpsV])
            p.act(lambda e, g=g: e.copy(out=vc[:, g, :], in_=psV[:, 0:64]), reads=[bpsV], writes=[bkvt[ti]])
            p.act(lambda e, g=g: e.copy(out=vd[:, g, 0:64], in_=psV[:, 64:128]), reads=[bpsV], writes=[bkvt[ti]])
        for hb in range(2):
            blk = ti * 2 + hb
            for hv in range(2):
                p.dve(lambda e, blk=blk, hv=hv: e.tensor_reduce(out=sq[:, 0:1], in_=kkd[:, hv, blk * 256:(blk + 1) * 256], axis=AX.X, op=ALU.add),
                      reads=[bkvt[ti]], writes=[bsq])
                p.dve(lambda e, blk=blk, hv=hv: e.tensor_scalar(out=kmT[:, hv, blk:blk + 1], in0=sq[:, 0:1], scalar1=1.0 / 256.0, scalar2=None, op0=ALU.mult),
                      reads=[bsq], writes=[bkm])
        yc_ = ycs[ti % 2]
        yd_ = yds[ti % 2]
        prev_kv = [bkvt[ti - 1]] if ti > 0 else []

        steps = []
        fins = {}

        def swa_step(g, sub, h, yc_=yc_):
            rows = slice(h * 64, (h + 1) * 64)
            qsl = slice(sub * 128, (sub + 1) * 128)
            if g == 0:
                k0, nk, mk = 0, 128, swm[:, 1, 128:256]
            else:
                k0, nk, mk = (g - 1) * 128, 256, swm[:, 1, :]
            nkc = nk // 128
            kvr = [bkvt[ti]] + (prev_kv if sub == 0 else [])

            def A(k):
                r2, r3 = k % 2, k % 4
                ps_, smr, pb = psS[r2], sm[r2], Pb[r3]
                scs, R = scS[k % 6], [bscS[k % 6]]
                mx, nmx, rs, es, den = [scs[:, i:i + 1] for i in range(5)]
                p.pe(lambda e: e.matmul(ps_[:, 0:nk], lhsT=qcT[rows, qsl], rhs=kkc[rows, k0:k0 + nk], start=True, stop=True),
                     reads=[bqc] + kvr, writes=[bpsS[r2]])
                p.dve(lambda e: e.tensor_tensor(out=smr[:, 0:nk], in0=ps_[:, 0:nk], in1=mk, op=ALU.add), reads=[bpsS[r2], bc], writes=[bsm[r2]])
                p.dve(lambda e: e.tensor_reduce(out=mx, in_=smr[:, 0:nk], axis=AX.X, op=ALU.max), reads=[bsm[r2]], writes=R)
                p.dve(lambda e: e.tensor_tensor(out=mx, in0=mx, in1=sink[:, h:h + 1], op=ALU.max), reads=R + [bc], writes=R)
                p.dve(lambda e: e.tensor_scalar(out=nmx, in0=mx, scalar1=-1.0, scalar2=None, op0=ALU.mult), reads=R, writes=R)
                p.act(lambda e: e.activation(out=pb[:, 0:nk], in_=smr[:, 0:nk], func=AF.Exp, bias=nmx, scale=1.0, accum_out=rs),
                      reads=[bsm[r2]] + R, writes=[bPb[r3]] + R)
                p.act(lambda e: e.activation(out=es, in_=sink[:, h:h + 1], func=AF.Exp, bias=nmx, scale=1.0), reads=R + [bc], writes=R)
                p.dve(lambda e: e.tensor_tensor(out=den, in0=rs, in1=es, op=ALU.add), reads=R, writes=R)
                p.dve(lambda e: e.reciprocal(out=den, in_=den), reads=R, writes=R)

            def B(k):
                r2, r3 = k % 2, k % 4
                pb, pt = Pb[r3], PT[r3]
                for kc in range(nkc):
                    p.pe(lambda e, kc=kc: e.matmul(psT[r2][:, kc, :], lhsT=pb[:, kc * 128:(kc + 1) * 128], rhs=identb[:], start=True, stop=True),
                         reads=[bPb[r3], bidb], writes=[bpsT[r2]])
                p.dve(lambda e: e.tensor_copy(out=pt[:, 0:nkc, :], in_=psT[r2][:, 0:nkc, :]), reads=[bpsT[r2]], writes=[bPT[r3]])

            def C(k):
                r2, r3 = k % 4, k % 6
                pt = PT[r2]
                den = scS[r3][:, 4:5]
                for kc in range(nkc):
                    gk = (g - 1 + kc) if g > 0 else 0
                    p.pe(lambda e, kc=kc, gk=gk: e.matmul(psV[:, 256:320], lhsT=pt[:, kc, :], rhs=vc[:, gk, :], start=(kc == 0), stop=(kc == nkc - 1)),
                         reads=[bPT[r2]] + kvr, writes=[bpsV])
                p.act(lambda e: e.activation(out=yc_[:, sub, h * 64:(h + 1) * 64], in_=psV[:, 256:320], func=AF.Copy, scale=den),
                      reads=[bpsV, bscS[r3]], writes=[bycs[ti % 2]])
            return dict(A=A, B=B, C=C)

        def moba_prologue(g, sub, h, gi):
            rows = slice(h * 64, (h + 1) * 64)
            qsl = slice(sub * 128, (sub + 1) * 128)
            n = g // 2
            gs = gset[gi]
            G = [gs["b"]]
            mb, nmb = gs["sc"][:, 0:1], gs["sc"][:, 1:2]
            p.pe(lambda e: e.matmul(psV[:, 0:2], lhsT=qab[:, qsl], rhs=kab[:, h, :], start=True, stop=True), reads=[bqab, bkm], writes=[bpsV])
            p.dve(lambda e: e.tensor_copy(out=mb, in_=psV[:, 0:1]), reads=[bpsV], writes=G)
            p.dve(lambda e: e.tensor_scalar(out=nmb, in0=mb, scalar1=-1.0, scalar2=None, op0=ALU.mult), reads=G, writes=G)
            if n > 0:
                gm_, sel_, top8_ = gs["gm"], gs["sel"], gs["top8"]
                p.pe(lambda e: e.matmul(psV[:, 64:128], lhsT=qdT[:, qsl], rhs=kmT[:, h, :], start=True, stop=True), reads=[bqd, bkm], writes=[bpsV])
                fsl = slice(64 - n, 128 - n)
                p.dve(lambda e: e.tensor_tensor(out=gm_[:], in0=psV[:, 64:128], in1=fut[:, 0, fsl], op=ALU.add), reads=[bpsV, bc], writes=G)
                p.dve(lambda e: e.max(out=top8_[:], in_=gm_[:]), reads=G, writes=G)
                p.dve(lambda e: e.tensor_scalar(out=sel_[:], in0=gm_[:], scalar1=top8_[:, 2:3], scalar2=None, op0=ALU.is_ge), reads=G, writes=G)

        def moba_step(g, sub, h, gi, jb0, nb, oi, yd_=yd_):
            rows = slice(h * 64, (h + 1) * 64)
            qsl = slice(sub * 128, (sub + 1) * 128)
            n, a = g // 2, g % 2
            own = (jb0 == n)
            gs = gset[gi]
            G = [gs["b"]]
            kreads = [bkvt[(jb0 + i) // 2] for i in range(nb)]
            po = psO[oi]
            W = nb * 256
            nc_ = nb * 2
            slots = []

            def A(k):
                r2, r3 = k % 2, k % 4
                ps_, pb = psS[r2], Pb[r3]
                p.pe(lambda e: e.matmul(ps_[:, 0:W], lhsT=qdT[:, qsl], rhs=kkd[:, h, jb0 * 256:jb0 * 256 + W], start=True, stop=True),
                     reads=[bqd] + kreads, writes=[bpsS[r2]])
                if own:
                    smr = sm[r2]
                    p.dve(lambda e: e.tensor_tensor(out=smr[:], in0=ps_[:, 0:256], in1=cm[:, a, :], op=ALU.add), reads=[bpsS[r2], bc], writes=[bsm[r2]])
                    p.act(lambda e: e.activation(out=pb[:, 0:256], in_=smr[:], func=AF.Exp, bias=gs["sc"][:, 1:2], scale=1.0),
                          reads=[bsm[r2]] + G, writes=[bPb[r3]])
                else:
                    p.act(lambda e: e.activation(out=pb[:, 0:W], in_=ps_[:, 0:W], func=AF.Exp, bias=gs["sc"][:, 1:2], scale=1.0),
                          reads=[bpsS[r2]] + G, writes=[bPb[r3]])
                    for i in range(nb):
                        sl_ = dcnt[0] % 8
                        dcnt[0] += 1
                        slots.append(sl_)
                        jb = jb0 + i
                        p.pool(lambda e, sl_=sl_, jb=jb: e.tensor_scalar(out=Dring[:, sl_, :], in0=identb[:], scalar1=gs["sel"][:, jb:jb + 1], scalar2=0.0,
                                                                        op0=ALU.mult, op1=ALU.add), reads=G + [bidb], writes=[bD[sl_]])

            def B(k):
                r2, r3 = k % 2, k % 4
                pb, pt = Pb[r3], PT[r3]
                for c in range(nc_):
                    if own:
                        rhs_, rb = identb[:], []
                    else:
                        rhs_, rb = Dring[:, slots[c // 2], :], [bD[slots[c // 2]]]
                    p.pe(lambda e, c=c, rhs_=rhs_: e.matmul(psT[r2][:, c, :], lhsT=pb[:, c * 128:(c + 1) * 128], rhs=rhs_, start=True, stop=True),
                         reads=[bPb[r3], bidb] + rb, writes=[bpsT[r2]])
                p.dve(lambda e: e.tensor_copy(out=pt[:, 0:nc_, :], in_=psT[r2][:, 0:nc_, :]), reads=[bpsT[r2]], writes=[bPT[r3]])

            def C(k):
                r2 = k % 4
                pt = PT[r2]
                for c in range(nc_):
                    p.pe(lambda e, c=c: e.matmul(po[:, 0:66], lhsT=pt[:, c, :], rhs=vd[:, jb0 * 2 + c, :], start=(jb0 == 0 and c == 0), stop=(own and c == nc_ - 1)),
                         reads=[bPT[r2]] + kreads, writes=[bpsO[oi]])
                if own:
                    lsum = gs["sc"][:, 2:3]
                    p.dve(lambda e: e.reciprocal(out=lsum, in_=po[:, 64:65]), reads=G + [bpsO[oi]], writes=G)
                    p.act(lambda e: e.activation(out=yd_[:, sub, h * 64:(h + 1) * 64], in_=po[:, 0:64], func=AF.Copy, scale=lsum),
                          reads=[bpsO[oi]] + G, writes=[byds[ti % 2]])
            return dict(A=A, B=B, C=C)

        units = [(ti * 4 + sub, sub, h) for sub in range(4) for h in range(2)]
        seq = []
        for ui, (g, sub, h) in enumerate(units):
            gi = (ti * 8 + ui) % 5
            if ui == 0:
                seq.append(("pro", lambda g=g, sub=sub, h=h, gi=gi: moba_prologue(g, sub, h, gi)))
            if ui + 1 < len(units):
                g2, sub2, h2 = units[ui + 1]
                seq.append(("pro", lambda g2=g2, sub2=sub2, h2=h2, gi=gi: moba_prologue(g2, sub2, h2, (gi + 1) % 5)))
            seq.append(("step", swa_step(g, sub, h)))
            n_ = g // 2
            jb = 0
            while jb < n_:
                nb = 2 if jb + 1 < n_ else 1
                seq.append(("step", moba_step(g, sub, h, gi, jb, nb, (ti * 8 + ui) % 2)))
                jb += nb
            seq.append(("step", moba_step(g, sub, h, gi, n_, 1, (ti * 8 + ui) % 2)))
        D = 2
        stp = [x[1] for x in seq if x[0] == "step"]
        nst = len(stp)
        k = 0
        for kind, x in seq:
            if kind == "pro":
                x()
                continue
            x["A"](kbase + k)
            if k - D >= 0:
                stp[k - D]["B"](kbase + k - D)
            if k - 2 * D >= 0:
                stp[k - 2 * D]["C"](kbase + k - 2 * D)
            k += 1
        for k in range(nst, nst + 2 * D):
            if 0 <= k - D < nst:
                stp[k - D]["B"](kbase + k - D)
            if 0 <= k - 2 * D < nst:
                stp[k - 2 * D]["C"](kbase + k - 2 * D)
        kbase += nst
        for which, (src, bsrc) in enumerate(((yc_, bycs[ti % 2]), (yd_, byds[ti % 2]))):
            pp = psP[which]
            for sub in range(4):
                p.pe(lambda e, sub=sub, src=src, pp=pp: e.transpose(out=pp[:, sub * 128:(sub + 1) * 128], in_=src[:, sub, :], identity=identb_f[:]),
                     reads=[bsrc, bidf], writes=[bpsP[which]])
            k = (ti % 2) * 2 + which
            p.act(lambda e, k=k, pp=pp: e.copy(out=yTs[k][:], in_=pp[:]), reads=[bpsP[which]], writes=[byTs[k]])
            emit_y(res, ti, which, yTs[k], byTs[k])


GROUPS = [[0, 1, 2, 3], [4, 5, 6, 7]]


def build_fused(L, debug=False):
    SEG = L // 4
    NCH = SEG // 512
    TG = min(1024, SEG)
    CPG = TG // 512
    nc = bass.Bass("TRN2", target_bir_lowering=False)
    msk_d = nc.dram_tensor("rankmask", [128, 4], F32, kind="ExternalInput").ap()
    xres_d = nc.dram_tensor("xres", [SEG, 1024], F32, kind="ExternalInput").ap()
    xo_d = nc.dram_tensor("xo", [SEG, 1024], F32, kind="ExternalOutput").ap()
    exin = [nc.dram_tensor("exin%d" % i, [NCH, 4, 4, 256, 512], BF16).ap() for i in range(2)]
    exout = [nc.dram_tensor("exout%d" % i, [NCH, 4, 256, 512], BF16).ap() for i in range(2)]
    agin = nc.dram_tensor("agin", [NCH, 1024, 512], BF16).ap()
    agout = nc.dram_tensor("agout", [NCH, 4, 1024, 512], BF16).ap()
    x1loc = nc.dram_tensor("x1loc", [SEG, 1024], F32).ap()

    p = Prog(nc)
    p.debug = debug
    bexin = [[p.buf() for _ in range(NCH)] for _ in range(2)]
    bexout = [[p.buf() for _ in range(NCH)] for _ in range(2)]
    bagin = [p.buf() for _ in range(NCH)]
    bagout = [p.buf() for _ in range(NCH)]
    bx1loc = p.buf()

    def make_exchange_writer(xi):
        st = {}

        def setup():
            st["msk"] = p.sb("msk%d" % xi, [128, 4], F32)
            st["bmsk"] = p.buf()
            p.dma(st["msk"][:], msk_d, writes=[st["bmsk"]])
            st["tmp"] = [p.sb("extmp%d_%d" % (xi, i), [128, 4, 512], BF16) for i in range(2)]
            st["btmp"] = [p.buf() for _ in range(2)]
            st["k"] = 0

        def emit(ti, f0, t, b):
            if "msk" not in st:
                setup()
            k = st["k"] % 2
            st["k"] += 1
            tmp, btmp = st["tmp"][k], st["btmp"][k]
            for q in range(4):
                p.pool(lambda e, q=q, tmp=tmp: e.tensor_scalar(out=tmp[:, q, :], in0=t[:], scalar1=st["msk"][:, q:q + 1], scalar2=0.0,
                                                               op0=ALU.mult, op1=ALU.add), reads=[b, st["bmsk"]], writes=[btmp])
            d, c = ti // NCH, ti % NCH
            p.dma(exin[xi][c, d, :, f0:f0 + 128, :].rearrange("q f t -> f q t"), tmp[:], reads=[btmp], writes=[bexin[xi][c]])

        def finish():
            for c in range(NCH):
                p.collective("ReduceScatter", ALU.add, GROUPS,
                             ins=[exin[xi][c].rearrange("d q f t -> (d q f) t")], outs=[exout[xi][c].rearrange("q f t -> (q f) t")],
                             reads=[bexin[xi][c]], writes=[bexout[xi][c]])
        return emit, finish

    def make_load_yt(xi):
        def load_yt(res, g, yt, byt):
            for cc in range(CPG):
                c = g * CPG + cc
                for half in range(2):
                    p.dma(yt[:, half * 4:(half + 1) * 4, cc * 512:(cc + 1) * 512],
                          exout[xi][c, :, half * 128:(half + 1) * 128, :].rearrange("q f t -> f q t"),
                          reads=[bexout[xi][c]], writes=[byt], eng=("sp" if half == 0 else "act"))
        return load_yt

    p.prefix = "A_"
    emitA, finA = make_exchange_writer(0)
    phase_mixa(p, nc, "a_", L, lambda res, ti, t, b: emitA(ti, 0, t, b), lambda res, ti, t, b: emitA(ti, 128, t, b))
    finA()
    p.end_phase()

    p.prefix = "T0_"
    st0 = {}

    def load_xres0(res, g, st, dst, bdst):
        t0 = g * TG + st * 128
        p.dma(dst, xres_d[t0:t0 + 128, :], writes=[bdst], eng="act")

    def store_out0(res, g, st, o, bo):
        if "xT" not in st0:
            st0["xT"] = [p.sb("x1Ts%d" % i, [128, 8, 128], BF16) for i in range(2)]
            st0["bxT"] = [p.buf() for _ in range(2)]
            st0["k"] = 0
        t0 = g * TG + st * 128
        p.dma(x1loc[t0:t0 + 128, :], o[:], reads=[bo], writes=[bx1loc])
        k = st0["k"] % 2
        st0["k"] += 1
        xT, bxT = st0["xT"][k], st0["bxT"][k]
        psA, bpsA, identf, bidf = res["psA"], res["bpsA"], res["identf"], res["bidf"]
        for hf in range(2):
            pa = psA[hf]
            for c4 in range(4):
                c = hf * 4 + c4
                p.pe(lambda e, pa=pa, c4=c4, c=c: e.transpose(out=pa[:, c4 * 128:(c4 + 1) * 128], in_=o[:, c * 128:(c + 1) * 128], identity=identf[:]),
                     reads=[bo, bidf], writes=[bpsA[hf]])
            p.act(lambda e, pa=pa, hf=hf, xT=xT: e.copy(out=xT[:, hf * 4:(hf + 1) * 4, :], in_=pa[:].rearrange("p (c t) -> p c t", c=4)),
                  reads=[bpsA[hf]], writes=[bxT])
        chunk, toff = t0 // 512, t0 % 512
        p.dma(agin[chunk].rearrange("(dc p) t -> p dc t", p=128)[:, :, toff:toff + 128], xT[:], reads=[bxT], writes=[bagin[chunk]])
        if toff == 384:
            p.collective("AllGather", ALU.bypass, GROUPS, ins=[agin[chunk]], outs=[agout[chunk].rearrange("r d t -> (r d) t")],
                         reads=[bagin[chunk]], writes=[bagout[chunk]])

    phase_tail(p, nc, "t0_", SEG, True, make_load_yt(0), load_xres0, store_out0, TG)
    p.end_phase()

    p.prefix = "C_"
    emitC, finC = make_exchange_writer(1)

    def load_xc(res, ti, dst, bdst):
        r, c = ti // NCH, ti % NCH
        p.dma(dst[:], agout[c, r].rearrange("(dc p) t -> p dc t", p=128), reads=[bagout[c]], writes=[bdst])

    phase_mixc(p, nc, "c_", L, load_xc, lambda res, ti, which, t, b: emitC(ti, which * 128, t, b))
    finC()
    p.end_phase()

    p.prefix = "T1_"

    def load_xres1(res, g, st, dst, bdst):
        t0 = g * TG + st * 128
        p.dma(dst, x1loc[t0:t0 + 128, :], reads=[bx1loc], writes=[bdst], eng="act")

    def store_out1(res, g, st, o, bo):
        t0 = g * TG + st * 128
        p.dma(xo_d[t0:t0 + 128, :], o[:], reads=[bo])

    phase_tail(p, nc, "t1_", SEG, False, make_load_yt(1), load_xres1, store_out1, TG)
    if debug:
        for name, src, bufs in (("dbg_exout0", exout[0], bexout[0]), ("dbg_exout1", exout[1], bexout[1]), ("dbg_agout", agout, bagout),
                                ("dbg_exin0", exin[0], bexin[0])):
            d = nc.dram_tensor(name, list(src.shape), BF16, kind="ExternalOutput").ap()
            for c in range(NCH):
                p.dma(d[c], src[c], reads=[bufs[c]])
        d = nc.dram_tensor("dbg_x1loc", [SEG, 1024], F32, kind="ExternalOutput").ap()
        p.dma(d, x1loc, reads=[bx1loc])
    p.finish()
    return nc

import ml_dtypes

_BF = ml_dtypes.bfloat16
_PROGS = {}


def _prog(key, fn):
    if key not in _PROGS:
        _PROGS[key] = fn()
    return _PROGS[key]


def mixa_inputs(inp, j, xT):
    W = inp['ev_w_in'][0]
    sl = slice(128 * j, 128 * (j + 1))
    wA = np.concatenate([W[:, 0:512][:, sl], W[:, 512:1024][:, sl], W[:, 1024:1536][:, sl], W[:, 1536:2048][:, sl], W[:, 2048:2560][:, sl]], axis=1)
    lbl = np.ascontiguousarray(inp['hgrn_lb_logits'][:, sl].T)
    ng = np.ascontiguousarray(np.broadcast_to(inp['ev_a_norm'][0, sl][None], (128, 128)))
    G0 = 8 * j
    are, aim, ldt = inp['ev_s5_a_re'][0], inp['ev_s5_a_im'][0], inp['ev_s5_log_dt'][0]
    sps = np.zeros((128, 3, 4), np.float32)
    spw = np.zeros((128, 3, 4, 128), np.float32)
    bpad = np.zeros((128, 2, 4, 128), np.float32)
    cpad = np.zeros((128, 2, 4, 128), np.float32)
    for i in range(4):
        for gl in range(2):
            g = G0 + 2 * i + gl
            glc = 2 * i + gl
            sps[gl * 64:(gl + 1) * 64, 0, i] = are[g]
            sps[gl * 64:(gl + 1) * 64, 1, i] = aim[g]
            sps[gl * 64:(gl + 1) * 64, 2, i] = ldt[g]
            spw[:, 0, i, gl * 64:(gl + 1) * 64] = are[g][None]
            spw[:, 1, i, gl * 64:(gl + 1) * 64] = aim[g][None]
            spw[:, 2, i, gl * 64:(gl + 1) * 64] = ldt[g]
            bpad[glc * 16:(glc + 1) * 16, 0, i, gl * 64:(gl + 1) * 64] = inp['ev_s5_b_re'][0, g].T
            bpad[glc * 16:(glc + 1) * 16, 1, i, gl * 64:(gl + 1) * 64] = inp['ev_s5_b_im'][0, g].T
            cpad[gl * 64:(gl + 1) * 64, 0, i, glc * 16:(glc + 1) * 16] = inp['ev_s5_c_re'][0, g].T
            cpad[gl * 64:(gl + 1) * 64, 1, i, glc * 16:(glc + 1) * 16] = inp['ev_s5_c_im'][0, g].T
    dsk = np.ascontiguousarray(inp['ev_s5_d'][0, sl][:, None])
    s_ = np.arange(128)[:, None]
    c_ = np.arange(128)[None, :]
    tri = ((s_ // 64 == c_ // 64) & (s_ <= c_)).astype(np.float32)
    m01 = np.broadcast_to((np.arange(512) % 64 != 0).astype(np.float32)[None], (128, 512))
    return {"xT": xT, "wA": np.ascontiguousarray(wA), "lbl": lbl, "ng": ng, "sps": sps, "spw": spw.reshape(128, 3, 512),
            "bpad": bpad.reshape(128, 2, 512), "cpad": cpad.reshape(128, 2, 512), "dsk": dsk, "tri": tri, "m01": np.ascontiguousarray(m01)}


def mixc_inputs(inp, j, xT):
    W = inp['od_w_in'][0]
    kv = j // 2
    qc = W[:, 128 * j:128 * (j + 1)]
    kc = W[:, 512 + 64 * kv:512 + 64 * (kv + 1)]
    vc = W[:, 640 + 64 * kv:640 + 64 * (kv + 1)]
    qd = W[:, 768 + 128 * j:768 + 128 * (j + 1)]
    kd = W[:, 1280 + 64 * kv:1280 + 64 * (kv + 1)]
    vd = W[:, 1408 + 64 * kv:1408 + 64 * (kv + 1)]
    z = np.zeros_like(kd)
    wC = np.concatenate([qc, kc, kc, qd, kd, z, z, kd, vc, vd], axis=1)
    sink = np.ascontiguousarray(np.broadcast_to(inp['od_sinks'][0, 2 * j:2 * j + 2][None], (128, 2)))
    q = np.arange(128)[:, None]
    k = np.arange(256)[None, :]
    band = np.where(((k < 128) & (k > q)) | ((k >= 128) & (k - 128 <= q)), 0.0, -BIGM).astype(np.float32)
    swm = np.stack([band, band], axis=1)
    cm = np.stack([np.where(k <= q + 128 * a, 0.0, -BIGM) for a in range(2)], axis=1).astype(np.float32)
    i = np.arange(128)[None, :]
    f0 = np.where(i < 64, 0.0, -BIGM) * np.ones((128, 1))
    fut = np.stack([f0, f0 - BIGM], axis=1).astype(np.float32)
    return {"xT": xT, "wC": np.ascontiguousarray(wC), "sink": sink, "swm": np.ascontiguousarray(swm), "cm": np.ascontiguousarray(cm),
            "fut": np.ascontiguousarray(fut)}


def tail_inputs(inp, layer, xres, yT, w_out, w_glu):
    lnp = np.ascontiguousarray(np.stack([np.broadcast_to(inp[k][layer], (128, 1024)) for k in ['ln1_g', 'ln1_b', 'ln2_g', 'ln2_b']]).astype(np.float32))
    w_r = np.ascontiguousarray(np.concatenate([inp['moe_w_group'][layer], inp['moe_w_expert'][layer]], axis=1))
    b_r = np.ascontiguousarray(np.broadcast_to(np.concatenate([inp['moe_b_group'][layer], inp['moe_b_expert'][layer]])[None], (128, 20)).astype(np.float32))
    m = {"xres": xres, "yT": yT, "w_out": w_out, "lnp": lnp, "w_r": w_r, "b_r": b_r,
         "w_gu": inp['moe_w_gate_up'][layer], "w_dn": inp['moe_w_down'][layer]}
    if w_glu is not None:
        m["w_glu"] = w_glu
    return m


def fused_inputs(inp, x, xTb, b, r, SEG):
    m = {}
    for k, v in mixa_inputs(inp, r, xTb).items():
        m["a_" + k] = v
    for k, v in mixc_inputs(inp, r, None).items():
        if k != "xT":
            m["c_" + k] = v
    for k, v in tail_inputs(inp, 0, None, None, inp['ev_w_out'][0], inp['ev_s5_w_glu'][0]).items():
        if k not in ("xres", "yT"):
            m["t0_" + k] = v
    for k, v in tail_inputs(inp, 1, None, None, inp['od_w_out'][0], None).items():
        if k not in ("xres", "yT"):
            m["t1_" + k] = v
    msk = np.zeros((128, 4), np.float32)
    msk[:, r] = 1.0
    m["rankmask"] = msk
    m["xres"] = np.ascontiguousarray(x[b, r * SEG:(r + 1) * SEG])
    return m


def kernel(**inputs):
    inp = {k: np.ascontiguousarray(np.asarray(v)) for k, v in inputs.items()}
    x = inp['x']
    B, L, D = x.shape
    SEG = L // 4
    cores = list(range(8))
    xT = [np.ascontiguousarray(x[b].T) for b in range(B)]
    nc = _prog(("F", L), lambda: build_fused(L))
    maps = [fused_inputs(inp, x, xT[c // 4], c // 4, c % 4, SEG) for c in cores]
    res = run_bass_kernel_spmd(nc, maps, core_ids=cores).results
    out = np.stack([np.concatenate([np.asarray(res[b * 4 + s]["xo"]) for s in range(4)], axis=0) for b in range(B)])
    return out.astype(np.float32)
```

```python
import numpy as np
from contextlib import ExitStack
import concourse.bass as bass
import concourse.mybir as mybir
from concourse.bass_utils import run_bass_kernel_spmd

dt = mybir.dt
F32 = dt.float32
BF16 = dt.bfloat16
I32 = dt.int32
U32 = dt.uint32
AF = mybir.ActivationFunctionType
ALU = mybir.AluOpType
AX = mybir.AxisListType


class Buf:
    __slots__ = ("name", "lw", "rd", "excl")

    def __init__(self, name, excl=False):
        self.name = name
        self.lw = None
        self.rd = {}
        self.excl = excl


class Prog:
    ENG = ["pe", "dve", "act", "pool", "sp"]
    NDMA = 32

    def __init__(self, nc, same_engine_sync=True):
        self.nc = nc
        self.es = ExitStack()
        self.ops = {e: [] for e in self.ENG}
        self.cnt = {e: 0 for e in self.ENG}
        self.waited = {e: {} for e in self.ENG}
        self.dma_cnt = [0] * self.NDMA
        self.dma_rr = 0
        self.same = same_engine_sync
        self.sems = {}
        self.gen = 0
        self.semkey = {}
        for e in ["pe", "dve", "act", "pool"]:
            self.semkey[e] = e
            self.sems[e] = self.es.enter_context(nc.semaphore("s_" + e))
        for j in range(self.NDMA):
            self.sems["d%d" % j] = self.es.enter_context(nc.semaphore("s_d%d" % j))
        self.nbuf = 0
        self.scope = ExitStack()
        self.ncc = 0
        self.prefix = ""

    def sb(self, name, shape, dtype):
        return self.scope.enter_context(self.nc.sbuf_tensor("sb_" + self.prefix + name, list(shape), dtype))

    def ps(self, name, shape, dtype):
        return self.scope.enter_context(self.nc.psum_tensor("pp_" + self.prefix + name, list(shape), dtype))

    def debug_dump(self, name, ap, shape, dtype, reads):
        if not getattr(self, "debug", False):
            return
        d = self.nc.dram_tensor("dbg_" + name, list(shape), dtype, kind="ExternalOutput").ap()
        self.dma(d, ap, reads=reads)

    def barrier(self):
        targets = []
        for j in range(self.NDMA):
            if self.dma_cnt[j] > 0:
                targets.append(("d%d" % j, self.dma_cnt[j]))
        for e in ["pe", "dve", "act", "pool"]:
            if self.cnt[e] > 0:
                targets.append((self.semkey[e], self.cnt[e]))
        for k in self.sems:
            if k.startswith("cc"):
                targets.append((k, 1))
        for e in self.ENG:
            waits = []
            for k, v in targets:
                if k == self.semkey.get(e):
                    continue
                if self.waited[e].get(k, 0) >= v:
                    continue
                waits.append((k, v))
                self.waited[e][k] = v
            if waits:
                self.ops[e].append((waits, None, None, False))

    def end_phase(self):
        self.barrier()
        self.scope.close()
        self.scope = ExitStack()
        self.gen += 1
        for e in ["pe", "dve", "act", "pool"]:
            k = "%s@%d" % (e, self.gen)
            self.semkey[e] = k
            self.sems[k] = self.es.enter_context(self.nc.semaphore("s_%s_%d" % (e, self.gen)))
            self.cnt[e] = 0

    def collective(self, kind, alu, groups, ins, outs, reads=(), writes=()):
        k = "cc%d" % self.ncc
        self.ncc += 1
        self.sems[k] = self.es.enter_context(self.nc.semaphore("s_" + k))
        eng = "pool"
        waits = {}

        def need(kk, v):
            if v <= 0 or self.waited[eng].get(kk, 0) >= v:
                return
            if waits.get(kk, 0) < v:
                waits[kk] = v
        for b in reads:
            if b.lw is not None:
                need(*b.lw)
        for b in writes:
            if b.lw is not None:
                need(*b.lw)
            for kk, v in b.rd.items():
                need(kk, v)
        for kk, v in waits.items():
            self.waited[eng][kk] = v
        tok = (k, 1)
        self.ops[eng].append((list(waits.items()), lambda e: e.collective_compute(kind, alu, replica_groups=groups, ins=ins, outs=outs), tok, "cc"))
        for b in reads:
            if b.rd.get(tok[0], 0) < tok[1]:
                b.rd[tok[0]] = tok[1]
        for b in writes:
            b.lw = tok
            b.rd = {}
        return tok

    def buf(self, name=None, excl=False):
        self.nbuf += 1
        return Buf(name or ("b%d" % self.nbuf), excl)

    def pbuf(self, name=None):
        return self.buf(name, True)

    def op(self, eng, emit, reads=(), writes=(), dma=False):
        waits = {}
        ex = [b for b in reads if b.excl]
        if ex:
            writes = list(writes) + ex
            reads = [b for b in reads if not b.excl]

        def need(k, v):
            if v <= 0:
                return
            if k == self.semkey.get(eng) and (eng == "pe" or not self.same):
                return
            if self.waited[eng].get(k, 0) >= v:
                return
            if waits.get(k, 0) < v:
                waits[k] = v

        for b in reads:
            if b.lw is not None:
                need(*b.lw)
        for b in writes:
            if b.lw is not None:
                need(*b.lw)
            for k, v in b.rd.items():
                need(k, v)
        if dma:
            j = self.dma_rr
            self.dma_rr = (self.dma_rr + 1) % self.NDMA
            k = "d%d" % j
            need(k, self.dma_cnt[j])
            self.dma_cnt[j] += 16
            tok = (k, self.dma_cnt[j])
        else:
            self.cnt[eng] += 1
            tok = (self.semkey[eng], self.cnt[eng])
        for k, v in waits.items():
            self.waited[eng][k] = v
        self.ops[eng].append((list(waits.items()), emit, tok, dma))
        for b in reads:
            if b.rd.get(tok[0], 0) < tok[1]:
                b.rd[tok[0]] = tok[1]
        for b in writes:
            b.lw = tok
            b.rd = {}
        return tok

    def pe(self, emit, reads=(), writes=()):
        return self.op("pe", emit, reads, writes)

    def dve(self, emit, reads=(), writes=()):
        return self.op("dve", emit, reads, writes)

    def act(self, emit, reads=(), writes=()):
        return self.op("act", emit, reads, writes)

    def pool(self, emit, reads=(), writes=()):
        return self.op("pool", emit, reads, writes)

    def dma(self, out, in_, reads=(), writes=(), eng="sp", **kw):
        return self.op(eng, lambda e: e.dma_start(out=out, in_=in_, **kw), reads, writes, dma=True)

    def finish(self):
        waits = []
        for j in range(self.NDMA):
            if self.dma_cnt[j] > 0:
                waits.append(("d%d" % j, self.dma_cnt[j]))
        for e in ["pe", "dve", "act", "pool"]:
            if self.cnt[e] > 0:
                waits.append((self.semkey[e], self.cnt[e]))
        for k in self.sems:
            if k.startswith("cc"):
                waits.append((k, 1))
        self.ops["sp"].append((waits, None, None, False))
        nc = self.nc
        sems = self.sems
        ops = self.ops

        def run(name, e):
            for waits, emit, tok, dma in ops[name]:
                for k, v in waits:
                    e.wait_ge(sems[k], v)
                if emit is None:
                    continue
                ins = emit(e)
                ins.then_inc(sems[tok[0]], 16 if dma is True else 1)

        with nc.Block() as block:
            @block.tensor
            def _(e):
                run("pe", e)

            @block.vector
            def _(e):
                run("dve", e)

            @block.scalar
            def _(e):
                run("act", e)

            @block.gpsimd
            def _(e):
                run("pool", e)

            @block.sync
            def _(e):
                run("sp", e)
        self.scope.close()
        self.es.close()


DN_ALPHA = 4.0 ** 0.25
LN_EPS = 1e-5
BIG = 30000.0


def make_ident(p, dtype, name):
    ident = p.sb(name, [128, 128], dtype)
    b = p.buf(name)
    p.pool(lambda e: e.memset(ident[:], 0.0), writes=[b])
    p.pool(lambda e: e.affine_select(out=ident[:], in_=ident[:], compare_op=ALU.not_equal, fill=1.0,
                                     base=0, pattern=[[-1, 128]], channel_multiplier=1), reads=[b], writes=[b])
    return ident, b


def layer_norm_tm(p, src, dst, gam, bet, bsrc, bdst, bconst, tmp, btmp, eps, key):
    stats, mv, rstd = tmp
    p.dve(lambda e: e.bn_stats(out=stats[:, 0, :], in_=src[:, 0:512]), reads=[bsrc], writes=[btmp])
    p.dve(lambda e: e.bn_stats(out=stats[:, 1, :], in_=src[:, 512:1024]), reads=[bsrc], writes=[btmp])
    p.dve(lambda e: e.bn_aggr(out=mv[:], in_=stats[:].rearrange("p a b -> p (a b)")), reads=[btmp], writes=[btmp])
    p.act(lambda e: e.activation(out=rstd[:], in_=mv[:, 1:2], func=AF.Sqrt, bias=eps, scale=1.0), reads=[btmp], writes=[btmp])
    p.dve(lambda e: e.reciprocal(out=rstd[:], in_=rstd[:]), reads=[btmp], writes=[btmp])
    p.dve(lambda e: e.tensor_scalar(out=dst, in0=src, scalar1=mv[:, 0:1], scalar2=rstd[:, 0:1],
                                    op0=ALU.subtract, op1=ALU.mult), reads=[bsrc, btmp], writes=[bdst])
    p.dve(lambda e: e.tensor_tensor(out=dst, in0=dst, in1=gam, op=ALU.mult), reads=[bdst, bconst], writes=[bdst])
    p.dve(lambda e: e.tensor_tensor(out=dst, in0=dst, in1=bet, op=ALU.add), reads=[bdst, bconst], writes=[bdst])


def build_tail(NT, glu, TG=1024):
    nc = bass.Bass("TRN2", target_bir_lowering=False)
    D = 1024
    xres = nc.dram_tensor("xres", [NT, D], F32, kind="ExternalInput").ap()
    yT = nc.dram_tensor("yT", [D, NT], BF16, kind="ExternalInput").ap()
    xo = nc.dram_tensor("xo", [NT, D], F32, kind="ExternalOutput").ap()
    p = Prog(nc)

    def load_yt(res, g, yt, byt):
        p.dma(yt[:], yT[:, g * TG:(g + 1) * TG].rearrange("(c p) t -> p c t", p=128), writes=[byt])

    def load_xres(res, g, st, dst, bdst):
        t0 = g * TG + st * 128
        p.dma(dst, xres[t0:t0 + 128, :], writes=[bdst], eng="act")

    def store_out(res, g, st, o, bo):
        t0 = g * TG + st * 128
        p.dma(xo[t0:t0 + 128, :], o[:], reads=[bo])

    phase_tail(p, nc, "", NT, glu, load_yt, load_xres, store_out, TG)
    p.finish()
    return nc


def phase_tail(p, nc, pre, NT, glu, load_yt, load_xres, store_out, TG=1024):
    D = 1024
    NG = NT // TG
    NST = TG // 128
    if glu:
        w_glu = nc.dram_tensor(pre + "w_glu", [512, 1024], F32, kind="ExternalInput").ap()
    w_out = nc.dram_tensor(pre + "w_out", [D, D], F32, kind="ExternalInput").ap()
    lnp = nc.dram_tensor(pre + "lnp", [4, 128, D], F32, kind="ExternalInput").ap()
    w_r = nc.dram_tensor(pre + "w_r", [D, 20], F32, kind="ExternalInput").ap()
    b_r = nc.dram_tensor(pre + "b_r", [128, 20], F32, kind="ExternalInput").ap()
    w_gu = nc.dram_tensor(pre + "w_gu", [16, D, 512], F32, kind="ExternalInput").ap()
    w_dn = nc.dram_tensor(pre + "w_dn", [16, 256, D], F32, kind="ExternalInput").ap()
    identf, bidf = make_ident(p, F32, "identf")
    identb = p.sb("identb", [128, 128], BF16)
    bidb = p.buf()
    p.dve(lambda e: e.tensor_copy(out=identb[:], in_=identf[:]), reads=[bidf], writes=[bidb])
    wout = p.sb("wout", [128, 8, D], BF16)
    bwout = p.buf()
    p.dma(wout[:], w_out.rearrange("(c p) f -> p c f", p=128), writes=[bwout], eng="pool")
    if glu:
        wglu = p.sb("wglu", [128, 4, 1024], BF16)
        bwglu = p.buf()
        p.dma(wglu[:], w_glu.rearrange("(c p) f -> p c f", p=128), writes=[bwglu], eng="pool")
    lns = p.sb("lns", [128, 4, D], F32)
    bln = p.buf()
    p.dma(lns[:], lnp.rearrange("a p d -> p a d"), writes=[bln])
    wr = p.sb("wr", [128, 8, 20], BF16)
    bwr = p.buf()
    p.dma(wr[:], w_r.rearrange("(c p) f -> p c f", p=128), writes=[bwr], eng="pool")
    br_ = p.sb("br", [128, 20], F32)
    bbr = p.buf()
    p.dma(br_[:], b_r, writes=[bbr])

    yt = p.sb("yt", [128, 8, TG], BF16)
    byt = p.buf()
    if glu:
        yglu = p.sb("yglu", [128, 4, TG], BF16)
        byglu = [p.buf() for _ in range(4)]
        sig = p.sb("sig", [128, 512], F32)
        bsig = p.buf()
    acc = p.sb("acc", [128, NST, D], F32)
    bacc = [p.buf() for _ in range(NST)]
    x1T = p.sb("x1T", [128, 8, TG], BF16)
    bx1T = [p.buf() for _ in range(NST)]
    gates = p.sb("gates", [128, NST, 16], F32)
    bgates = [p.buf() for _ in range(NST)]
    gub = [p.sb("gub%d" % i, [128, 8, 512], BF16) for i in range(2)]
    bgub = [p.buf() for _ in range(2)]
    dnb = [p.sb("dnb%d" % i, [128, 2, D], BF16) for i in range(2)]
    bdnb = [p.buf() for _ in range(2)]
    sg = [p.sb("sg%d" % i, [128, 256], F32) for i in range(2)]
    bsg = [p.buf() for _ in range(2)]
    hh = [p.sb("hh%d" % i, [128, 256], BF16) for i in range(3)]
    bhh = [p.buf() for _ in range(3)]
    hT = [p.sb("hT%d" % i, [128, 2, 128], BF16) for i in range(3)]
    bhT = [p.buf() for _ in range(3)]
    stats = p.sb("stats", [128, 2, 6], F32)
    mv = p.sb("mv", [128, 2], F32)
    rstd = p.sb("rstd", [128, 1], F32)
    btmp = p.buf()
    rt = p.sb("rt", [128, 80], F32)
    brt = p.buf()
    top8 = p.sb("top8", [128, 8], F32)
    xout = [p.sb("xout%d" % i, [128, D], F32) for i in range(2)]
    bxout = [p.buf() for _ in range(2)]

    psA = [p.ps("psA%d" % i, [128, 512], F32) for i in range(2)]
    bpsA = [p.pbuf() for _ in range(2)]
    psT = [p.ps("psT%d" % i, [128, 8, 128], BF16) for i in range(2)]
    bpsT = [p.pbuf() for _ in range(2)]
    psY = [p.ps("psY%d" % i, [128, 1024], F32) for i in range(2)]
    bpsY = [p.pbuf() for _ in range(2)]

    res = dict(psA=psA, bpsA=bpsA, identf=identf, bidf=bidf, TG=TG, NST=NST)
    expert_steps = [(g, e) for g in range(NG) for e in range(16)]

    def load_expert(idx):
        g, e = expert_steps[idx]
        s = idx % 2
        p.dma(gub[s][:], w_gu[e].rearrange("(c p) f -> p c f", p=128), writes=[bgub[s]], eng="pool")
        p.dma(dnb[s][:], w_dn[e].rearrange("(c p) f -> p c f", p=128), writes=[bdnb[s]], eng="pool")

    load_expert(0)
    rot = 0
    mk0 = 0
    for g in range(NG):
        t0 = g * TG
        load_yt(res, g, yt, byt)
        for st in range(NST):
            load_xres(res, g, st, acc[:, st, :], bacc[st])
        if glu:
            for half in range(TG // 512):
                ts = slice(half * 512, (half + 1) * 512)
                for f in range(4):
                    for which in range(2):
                        col0 = which * 512 + f * 128
                        for k in range(4):
                            p.pe(lambda e, which=which, col0=col0, k=k, ts=ts: e.matmul(
                                psA[which][:], lhsT=wglu[:, k, col0:col0 + 128], rhs=yt[:, 4 + k, ts],
                                start=(k == 0), stop=(k == 3)), reads=[bwglu, byt], writes=[bpsA[which]])
                    p.act(lambda e: e.activation(out=sig[:], in_=psA[1][:], func=AF.Sigmoid), reads=[bpsA[1]], writes=[bsig])
                    p.dve(lambda e, f=f, ts=ts: e.tensor_tensor(out=yglu[:, f, ts], in0=psA[0][:], in1=sig[:], op=ALU.mult),
                          reads=[bpsA[0], bsig], writes=[byglu[f]])
        for st in range(NST):
            tsl = slice(st * 128, (st + 1) * 128)
            py = psY[st % 2]
            bpy = bpsY[st % 2]
            for half in range(2):
                for c in range(8):
                    if glu and c >= 4:
                        lhs = yglu[:, c - 4, tsl]
                        rb = byglu[c - 4]
                    else:
                        lhs = yt[:, c, tsl]
                        rb = byt
                    p.pe(lambda e, lhs=lhs, c=c, half=half, py=py: e.matmul(
                        py[:, half * 512:(half + 1) * 512], lhsT=lhs, rhs=wout[:, c, half * 512:(half + 1) * 512],
                        start=(c == 0), stop=(c == 7)), reads=[rb, bwout], writes=[bpy])
            a = acc[:, st, :]
            p.dve(lambda e, a=a, py=py: e.scalar_tensor_tensor(out=a, in0=a, scalar=DN_ALPHA, in1=py[:],
                                                                 op0=ALU.mult, op1=ALU.add), reads=[bacc[st], bpy], writes=[bacc[st]])
            layer_norm_tm(p, a, a, lns[:, 0, :], lns[:, 1, :], bacc[st], bacc[st], bln, (stats, mv, rstd), btmp, LN_EPS, "ln1")
            for hf in range(2):
                pa = psA[hf]
                for c4 in range(4):
                    c = hf * 4 + c4
                    p.pe(lambda e, pa=pa, c4=c4, c=c, a=a: e.transpose(out=pa[:, c4 * 128:(c4 + 1) * 128], in_=a[:, c * 128:(c + 1) * 128],
                                                                       identity=identf[:]), reads=[bacc[st], bidf], writes=[bpsA[hf]])
                p.act(lambda e, pa=pa, hf=hf, tsl=tsl: e.copy(out=x1T[:, hf * 4:(hf + 1) * 4, tsl], in_=pa[:].rearrange("p (c t) -> p c t", c=4)),
                      reads=[bpsA[hf]], writes=[bx1T[st]])
            p.act(lambda e, a=a: e.activation(out=a, in_=a, func=AF.Copy, scale=DN_ALPHA), reads=[bacc[st]], writes=[bacc[st]])
            pr = psT[st % 2]
            pl = psY[(st + 1) % 2]
            bpl = bpsY[(st + 1) % 2]
            for c in range(8):
                p.pe(lambda e, c=c, pl=pl, tsl=tsl: e.matmul(pl[:, 0:20], lhsT=x1T[:, c, tsl], rhs=wr[:, c, :],
                                                             start=(c == 0), stop=(c == 7)), reads=[bx1T[st], bwr], writes=[bpl])
            lg = rt[:, 0:20]
            p.dve(lambda e, pl=pl: e.tensor_tensor(out=lg, in0=pl[:, 0:20], in1=br_[:], op=ALU.add), reads=[bpl, bbr], writes=[brt])
            gmax = rt[:, 20:21]
            ngmax = rt[:, 21:22]
            sume = rt[:, 22:23]
            gtop = rt[:, 23:24]
            eg = rt[:, 24:28]
            oh = rt[:, 28:32]
            em = rt[:, 32:48]
            dd = rt[:, 48:49]
            ex = rt[:, 49:50]
            w1 = rt[:, 50:51]
            w2 = rt[:, 51:52]
            t2 = rt[:, 56:72]
            R = [brt]
            p.dve(lambda e: e.tensor_reduce(out=gmax, in_=lg[:, 0:4], axis=AX.X, op=ALU.max), reads=R, writes=R)
            p.dve(lambda e: e.tensor_scalar(out=ngmax, in0=gmax, scalar1=-1.0, scalar2=None, op0=ALU.mult), reads=R, writes=R)
            p.act(lambda e: e.activation(out=eg, in_=lg[:, 0:4], func=AF.Exp, bias=ngmax, scale=1.0, accum_out=sume), reads=R, writes=R)
            p.dve(lambda e: e.reciprocal(out=gtop, in_=sume), reads=R, writes=R)
            p.dve(lambda e: e.tensor_scalar(out=oh, in0=lg[:, 0:4], scalar1=gmax, scalar2=BIG, op0=ALU.is_equal, op1=ALU.mult), reads=R, writes=R)
            p.dve(lambda e: e.tensor_scalar(out=oh, in0=oh, scalar1=-BIG, scalar2=None, op0=ALU.add), reads=R, writes=R)
            for gi in range(4):
                p.dve(lambda e, gi=gi: e.tensor_scalar(out=em[:, gi * 4:(gi + 1) * 4], in0=lg[:, 4 + gi * 4:8 + gi * 4],
                                                       scalar1=oh[:, gi:gi + 1], scalar2=None, op0=ALU.add), reads=R, writes=R)
            p.dve(lambda e: e.max(out=top8[:], in_=em), reads=R, writes=R)
            p.dve(lambda e: e.tensor_tensor(out=dd, in0=top8[:, 1:2], in1=top8[:, 0:1], op=ALU.subtract), reads=R, writes=R)
            p.act(lambda e: e.activation(out=ex, in_=dd, func=AF.Exp), reads=R, writes=R)
            p.dve(lambda e: e.tensor_scalar(out=w1, in0=ex, scalar1=1.0, scalar2=None, op0=ALU.add), reads=R, writes=R)
            p.dve(lambda e: e.reciprocal(out=w1, in_=w1), reads=R, writes=R)
            p.dve(lambda e: e.tensor_tensor(out=w2, in0=ex, in1=w1, op=ALU.mult), reads=R, writes=R)
            p.dve(lambda e: e.tensor_tensor(out=w1, in0=w1, in1=gtop, op=ALU.mult), reads=R, writes=R)
            p.dve(lambda e: e.tensor_tensor(out=w2, in0=w2, in1=gtop, op=ALU.mult), reads=R, writes=R)
            gt = gates[:, st, :]
            p.dve(lambda e, gt=gt: e.tensor_scalar(out=gt, in0=em, scalar1=top8[:, 0:1], scalar2=w1, op0=ALU.is_equal, op1=ALU.mult),
                  reads=R, writes=[bgates[st]])
            p.dve(lambda e: e.tensor_scalar(out=t2, in0=em, scalar1=top8[:, 1:2], scalar2=w2,
                                            op0=ALU.is_equal, op1=ALU.mult), reads=R, writes=R)
            p.dve(lambda e, gt=gt: e.tensor_tensor(out=gt, in0=gt, in1=t2, op=ALU.add), reads=R + [bgates[st]], writes=[bgates[st]])
        def moe_step(e_i, st, s):
            tsl = slice(st * 128, (st + 1) * 128)

            def A(k):
                r, r3 = k % 2, k % 3
                pa = psA[r]
                for c in range(8):
                    p.pe(lambda e, c=c: e.matmul(pa[:], lhsT=x1T[:, c, tsl], rhs=gub[s][:, c, :], start=(c == 0), stop=(c == 7)),
                         reads=[bx1T[st], bgub[s]], writes=[bpsA[r]])
                p.act(lambda e: e.activation(out=sg[r][:], in_=pa[:, 0:256], func=AF.Silu), reads=[bpsA[r]], writes=[bsg[r]])
                p.dve(lambda e: e.scalar_tensor_tensor(out=hh[r3][:], in0=pa[:, 256:512], scalar=gates[:, st, e_i:e_i + 1], in1=sg[r][:],
                                                       op0=ALU.mult, op1=ALU.mult), reads=[bpsA[r], bsg[r], bgates[st]], writes=[bhh[r3]])

            def B(k):
                r, r3 = k % 2, k % 3
                for kk in range(2):
                    p.pe(lambda e, kk=kk: e.transpose(out=psT[r][:, kk, :], in_=hh[r3][:, kk * 128:(kk + 1) * 128], identity=identb[:]),
                         reads=[bhh[r3], bidb], writes=[bpsT[r]])
                p.act(lambda e: e.copy(out=hT[r3][:], in_=psT[r][:, 0:2, :]), reads=[bpsT[r]], writes=[bhT[r3]])

            def C(k):
                r, r3 = k % 2, k % 3
                py = psY[r]
                for half in range(2):
                    for kk in range(2):
                        p.pe(lambda e, half=half, kk=kk: e.matmul(py[:, half * 512:(half + 1) * 512], lhsT=hT[r3][:, kk, :],
                                                                 rhs=dnb[s][:, kk, half * 512:(half + 1) * 512], start=(kk == 0), stop=(kk == 1)),
                             reads=[bhT[r3], bdnb[s]], writes=[bpsY[r]])
                a = acc[:, st, :]
                p.dve(lambda e: e.tensor_tensor(out=a, in0=a, in1=py[:], op=ALU.add), reads=[bacc[st], bpsY[r]], writes=[bacc[st]])
            return dict(A=A, B=B, C=C)

        msteps = []
        for e_i in range(16):
            idx = g * 16 + e_i
            for st in range(NST):
                msteps.append((idx, st, moe_step(e_i, st, idx % 2)))
        nst_ = len(msteps)
        ld_at = min(2, NST - 1)
        for k in range(nst_ + 2):
            if k < nst_:
                idx, st, sd = msteps[k]
                if st == ld_at and idx + 1 < len(expert_steps):
                    load_expert(idx + 1)
                sd["A"](mk0 + k)
            if 0 <= k - 1 < nst_:
                msteps[k - 1][2]["B"](mk0 + k - 1)
            if 0 <= k - 2 < nst_:
                msteps[k - 2][2]["C"](mk0 + k - 2)
        mk0 += nst_
        for st in range(NST):
            a = acc[:, st, :]
            o = xout[st % 2]
            layer_norm_tm(p, a, o[:], lns[:, 2, :], lns[:, 3, :], bacc[st], bxout[st % 2], bln, (stats, mv, rstd), btmp, LN_EPS, "ln2")
            store_out(res, g, st, o, bxout[st % 2])

import math

RMS_EPS = 1e-6
TWO_PI = 2.0 * math.pi


def s5_lambda(p, pre, shape, ar, ai, ldt, breads, T=None, extra=()):
    F = shape[1]
    if T is None:
        T = p.sb(pre + "_t", [128, 8, F], F32)
    Ti = p.sb(pre + "_ti", [128, F], I32)
    b = p.buf(pre)
    dtt, mag, th, t, kf, r, c1, s = [T[:, i, :] for i in range(8)]
    lre = p.sb(pre + "_lre", [128, F], F32)
    lim = p.sb(pre + "_lim", [128, F], F32)
    R = [b] + list(extra)
    p.act(lambda e: e.activation(out=dtt, in_=ldt, func=AF.Exp), reads=breads, writes=R)
    p.dve(lambda e: e.tensor_tensor(out=mag, in0=dtt, in1=ar, op=ALU.mult), reads=R + breads, writes=R)
    p.act(lambda e: e.activation(out=mag, in_=mag, func=AF.Exp), reads=R, writes=R)
    p.dve(lambda e: e.tensor_tensor(out=th, in0=dtt, in1=ai, op=ALU.mult), reads=R + breads, writes=R)
    for which, dst in ((0, lim), (1, lre)):
        p.dve(lambda e, which=which: e.tensor_scalar(out=t, in0=th, scalar1=1.0 / TWO_PI, scalar2=0.25 * which,
                                                     op0=ALU.mult, op1=ALU.add), reads=R, writes=R)
        p.dve(lambda e: e.tensor_copy(out=Ti[:], in_=t), reads=R, writes=R)
        p.dve(lambda e: e.tensor_copy(out=kf, in_=Ti[:]), reads=R, writes=R)
        p.dve(lambda e: e.tensor_tensor(out=r, in0=t, in1=kf, op=ALU.subtract), reads=R, writes=R)
        p.dve(lambda e: e.tensor_scalar(out=c1, in0=r, scalar1=0.5, scalar2=None, op0=ALU.is_gt), reads=R, writes=R)
        p.dve(lambda e: e.tensor_tensor(out=r, in0=r, in1=c1, op=ALU.subtract), reads=R, writes=R)
        p.dve(lambda e: e.tensor_scalar(out=c1, in0=r, scalar1=-0.5, scalar2=None, op0=ALU.is_lt), reads=R, writes=R)
        p.dve(lambda e: e.tensor_tensor(out=r, in0=r, in1=c1, op=ALU.add), reads=R, writes=R)
        p.act(lambda e: e.activation(out=s, in_=r, func=AF.Sin, scale=TWO_PI), reads=R, writes=R)
        p.dve(lambda e, dst=dst: e.tensor_tensor(out=dst[:], in0=s, in1=mag, op=ALU.mult), reads=R, writes=R)
    return lre, lim, b


def build_mixa(L, TS5=2048):
    nc = bass.Bass("TRN2", target_bir_lowering=False)
    yaT_d = nc.dram_tensor("yaT", [128, L], BF16, kind="ExternalOutput").ap()
    ysT_d = nc.dram_tensor("ysT", [128, L], BF16, kind="ExternalOutput").ap()
    p = Prog(nc)

    def emit_ya(res, ti, t, b):
        p.dma(yaT_d[:, ti * 512:(ti + 1) * 512], t[:], reads=[b])

    def emit_ys(res, ti, t, b):
        p.dma(ysT_d[:, ti * 512:(ti + 1) * 512], t[:], reads=[b])

    phase_mixa(p, nc, "", L, emit_ya, emit_ys, TS5)
    p.finish()
    return nc


def phase_mixa(p, nc, pre, L, emit_ya, emit_ys, TS5=2048):
    xT = nc.dram_tensor(pre + "xT", [1024, L], F32, kind="ExternalInput").ap()
    wA_d = nc.dram_tensor(pre + "wA", [1024, 640], F32, kind="ExternalInput").ap()
    lbl_d = nc.dram_tensor(pre + "lbl", [128, 3], F32, kind="ExternalInput").ap()
    ng_d = nc.dram_tensor(pre + "ng", [128, 128], F32, kind="ExternalInput").ap()
    sps_d = nc.dram_tensor(pre + "sps", [128, 3, 4], F32, kind="ExternalInput").ap()
    spw_d = nc.dram_tensor(pre + "spw", [128, 3, 512], F32, kind="ExternalInput").ap()
    bpad_d = nc.dram_tensor(pre + "bpad", [128, 2, 512], F32, kind="ExternalInput").ap()
    cpad_d = nc.dram_tensor(pre + "cpad", [128, 2, 512], F32, kind="ExternalInput").ap()
    dsk_d = nc.dram_tensor(pre + "dsk", [128, 1], F32, kind="ExternalInput").ap()
    tri_d = nc.dram_tensor(pre + "tri", [128, 128], F32, kind="ExternalInput").ap()
    m01_d = nc.dram_tensor(pre + "m01", [128, 512], F32, kind="ExternalInput").ap()
    res = {}
    identf, bidf = make_ident(p, F32, "identf")
    wA = p.sb("wA", [128, 8, 640], BF16)
    bwA = p.buf()
    p.dma(wA[:], wA_d.rearrange("(c p) f -> p c f", p=128), writes=[bwA], eng="pool")
    cst = p.sb("cst", [128, 3 + 128 + 12 + 1 + 128 + 512 + 8], F32)
    bc = p.buf("cst")
    lbl = cst[:, 0:3]
    ng = cst[:, 3:131]
    sps = cst[:, 131:143].rearrange("p (a b) -> p a b", a=3)
    dsk = cst[:, 143:144]
    tri = cst[:, 144:272]
    m01 = cst[:, 272:784]
    misc = cst[:, 784:792]
    p.dma(lbl, lbl_d, writes=[bc])
    p.dma(ng, ng_d, writes=[bc])
    p.dma(sps, sps_d, writes=[bc])
    p.dma(dsk, dsk_d, writes=[bc])
    p.dma(tri, tri_d, writes=[bc])
    p.dma(m01, m01_d, writes=[bc])
    spw = p.sb("spw", [128, 3, 512], F32)
    p.dma(spw[:], spw_d, writes=[bc])
    bpad = p.sb("bpad", [128, 2, 512], F32)
    p.dma(bpad[:], bpad_d, writes=[bc])
    cpad = p.sb("cpad", [128, 2, 512], F32)
    p.dma(cpad[:], cpad_d, writes=[bc])
    lbe = misc[:, 0:3]
    lbs = misc[:, 3:4]
    lb = misc[:, 4:5]
    oml = misc[:, 5:6]
    bm = p.buf("misc")
    p.act(lambda e: e.activation(out=lbe, in_=lbl, func=AF.Exp, accum_out=lbs), reads=[bc], writes=[bm])
    p.dve(lambda e: e.reciprocal(out=lbs, in_=lbs), reads=[bm], writes=[bm])
    p.dve(lambda e: e.tensor_tensor(out=lb, in0=lbe[:, 0:1], in1=lbs, op=ALU.mult), reads=[bm], writes=[bm])
    p.dve(lambda e: e.tensor_scalar(out=oml, in0=lb, scalar1=-1.0, scalar2=1.0, op0=ALU.mult, op1=ALU.add), reads=[bm], writes=[bm])

    dre = p.sb("dre", [128, 4, TS5], F32)
    dim_ = p.sb("dim", [128, 4, TS5], F32)
    bd = [p.buf("d%d" % i) for i in range(4)]
    assert TS5 >= 1024
    ls_re, ls_im, bls = s5_lambda(p, "ls", [128, 4], sps[:, 0, :], sps[:, 1, :], sps[:, 2, :], [bc])
    lw_re, lw_im, blw = s5_lambda(p, "lw", [128, 512], spw[:, 0, :], spw[:, 1, :], spw[:, 2, :], [bc],
                                  T=dre[:].rearrange("p a t -> p (a t)")[:, 0:4096].rearrange("p (a f) -> p a f", a=8), extra=bd)
    W = dim_[:].rearrange("p a t -> p (a t)")[:, 0:4096].rearrange("p (a f) -> p a f", a=8)
    bW = p.buf("wtmp")
    xr, den, t1, t2, fr, fi, o1, o2 = [W[:, i, :] for i in range(8)]
    arw, aiw = spw[:, 0, :], spw[:, 1, :]
    RW = [bW, blw, bc]
    p.dve(lambda e: e.tensor_scalar(out=xr, in0=lw_re[:], scalar1=-1.0, scalar2=None, op0=ALU.add), reads=RW, writes=[bW] + bd)
    p.dve(lambda e: e.tensor_tensor(out=den, in0=arw, in1=arw, op=ALU.mult), reads=RW, writes=[bW])
    p.dve(lambda e: e.tensor_tensor(out=t1, in0=aiw, in1=aiw, op=ALU.mult), reads=RW, writes=[bW])
    p.dve(lambda e: e.tensor_tensor(out=den, in0=den, in1=t1, op=ALU.add), reads=RW, writes=[bW])
    p.dve(lambda e: e.reciprocal(out=den, in_=den), reads=RW, writes=[bW])
    p.dve(lambda e: e.tensor_tensor(out=t1, in0=xr, in1=arw, op=ALU.mult), reads=RW, writes=[bW])
    p.dve(lambda e: e.tensor_tensor(out=t2, in0=lw_im[:], in1=aiw, op=ALU.mult), reads=RW, writes=[bW])
    p.dve(lambda e: e.tensor_tensor(out=fr, in0=t1, in1=t2, op=ALU.add), reads=RW, writes=[bW])
    p.dve(lambda e: e.tensor_tensor(out=fr, in0=fr, in1=den, op=ALU.mult), reads=RW, writes=[bW])
    p.dve(lambda e: e.tensor_tensor(out=t1, in0=lw_im[:], in1=arw, op=ALU.mult), reads=RW, writes=[bW])
    p.dve(lambda e: e.tensor_tensor(out=t2, in0=xr, in1=aiw, op=ALU.mult), reads=RW, writes=[bW])
    p.dve(lambda e: e.tensor_tensor(out=fi, in0=t1, in1=t2, op=ALU.subtract), reads=RW, writes=[bW])
    p.dve(lambda e: e.tensor_tensor(out=fi, in0=fi, in1=den, op=ALU.mult), reads=RW, writes=[bW])
    wB = p.sb("wB", [128, 2, 512], BF16)
    wC = p.sb("wC", [128, 2, 512], BF16)
    bwB = p.buf("wB")
    bre, bim = bpad[:, 0, :], bpad[:, 1, :]
    p.dve(lambda e: e.tensor_tensor(out=o1, in0=fr, in1=bre, op=ALU.mult), reads=RW, writes=[bW])
    p.dve(lambda e: e.tensor_tensor(out=o2, in0=fi, in1=bim, op=ALU.mult), reads=RW, writes=[bW])
    p.dve(lambda e: e.tensor_tensor(out=wB[:, 0, :], in0=o1, in1=o2, op=ALU.subtract), reads=RW, writes=[bwB])
    p.dve(lambda e: e.tensor_tensor(out=o1, in0=fr, in1=bim, op=ALU.mult), reads=RW, writes=[bW])
    p.dve(lambda e: e.tensor_tensor(out=o2, in0=fi, in1=bre, op=ALU.mult), reads=RW, writes=[bW])
    p.dve(lambda e: e.tensor_tensor(out=wB[:, 1, :], in0=o1, in1=o2, op=ALU.add), reads=RW, writes=[bwB])
    p.dve(lambda e: e.tensor_copy(out=wC[:, 0, :], in_=cpad[:, 0, :]), reads=[bc], writes=[bwB])
    p.dve(lambda e: e.tensor_scalar(out=wC[:, 1, :], in0=cpad[:, 1, :], scalar1=-1.0, scalar2=None, op0=ALU.mult), reads=[bc, bW], writes=[bwB] + bd)

    NLEV = 3
    lamp = p.sb("lamp", [128, NLEV, 3, 4], F32)
    ltmp = p.sb("ltmp", [128, 4, 4], F32)
    blam = p.buf("lam")
    RL = [blam, bls]
    p.dve(lambda e: e.tensor_copy(out=lamp[:, 0, 0, :], in_=ls_re[:]), reads=RL, writes=[blam])
    p.dve(lambda e: e.tensor_copy(out=lamp[:, 0, 1, :], in_=ls_im[:]), reads=RL, writes=[blam])
    for lev in range(1, NLEV):
        p.dve(lambda e, lev=lev: e.tensor_copy(out=lamp[:, lev, 0:2, :], in_=lamp[:, lev - 1, 0:2, :]), reads=RL, writes=[blam])
        for _ in range(4):
            a = lamp[:, lev, 0, :]
            b_ = lamp[:, lev, 1, :]
            p.dve(lambda e, a=a: e.tensor_tensor(out=ltmp[:, 0, :], in0=a, in1=a, op=ALU.mult), reads=RL, writes=[blam])
            p.dve(lambda e, b_=b_: e.tensor_tensor(out=ltmp[:, 1, :], in0=b_, in1=b_, op=ALU.mult), reads=RL, writes=[blam])
            p.dve(lambda e, a=a, b_=b_: e.tensor_tensor(out=ltmp[:, 2, :], in0=a, in1=b_, op=ALU.mult), reads=RL, writes=[blam])
            p.dve(lambda e, a=a: e.tensor_tensor(out=a, in0=ltmp[:, 0, :], in1=ltmp[:, 1, :], op=ALU.subtract), reads=RL, writes=[blam])
            p.dve(lambda e, b_=b_: e.tensor_scalar(out=b_, in0=ltmp[:, 2, :], scalar1=2.0, scalar2=None, op0=ALU.mult), reads=RL, writes=[blam])
    for lev in range(NLEV):
        p.dve(lambda e, lev=lev: e.tensor_scalar(out=lamp[:, lev, 2, :], in0=lamp[:, lev, 1, :], scalar1=-1.0, scalar2=None, op0=ALU.mult),
              reads=RL, writes=[blam])

    xt = [p.sb("xt%d" % i, [128, 8, 512], BF16) for i in range(2)]
    bxt = [p.buf() for _ in range(2)]
    H = p.sb("hg", [128, 9, 512], F32)
    bH = p.buf("hg")
    f_, lf, kk, bb, eb, enb, qq, kinv, ktT = [H[:, i, :] for i in range(9)]
    qdec = p.sb("qdec", [128, 512], BF16)
    kinvb = p.sb("kinvb", [128, 512], BF16)
    bqk = p.buf("qk")
    vb = p.sb("vb", [128, 4, 128], BF16)
    gn = p.sb("gn", [128, 4, 128], F32)
    bvg = [p.buf() for _ in range(4)]
    kt = p.sb("kt", [128, 4, 128], BF16)
    bkt = [p.buf() for _ in range(4)]
    attT = [p.sb("attT%d" % i, [128, 128], BF16) for i in range(2)]
    battT = [p.buf() for _ in range(2)]
    yasb = [p.sb("yasb%d" % i, [128, 4, 128], F32) for i in range(2)]
    yaTs = [p.sb("yaTs%d" % i, [128, 512], BF16) for i in range(2)]
    byaTs = [p.buf() for _ in range(2)]
    byasb = [p.buf() for _ in range(2)]
    S = p.sb("S", [128, 128], F32)
    bS = p.buf("S")
    Sb = [p.sb("Sb%d" % i, [128, 128], BF16) for i in range(2)]
    bSb = [p.buf() for _ in range(2)]
    osc = p.sb("osc", [128, 128], F32)
    om = p.sb("om", [128, 2], F32)
    bo = p.buf("o")
    p.dve(lambda e: e.memset(S[:], 0.0), writes=[bS])
    p.dve(lambda e: e.memset(Sb[1][:], 0.0), writes=[bSb[1]])
    NB1 = TS5 // 16
    assert NB1 % 16 == 0 or NB1 <= 16
    uT = p.sb("uT", [128, TS5], F32)
    buT = p.buf("uT")
    uTb = [p.sb("uTb%d" % i, [128, 512], BF16) for i in range(2)]
    buTb = [p.buf() for _ in range(2)]
    nb_levels = []
    n = TS5
    while n > 16:
        n //= 16
        nb_levels.append(n)
    Ebufs = []
    for li, nbl in enumerate(nb_levels):
        Ebufs.append((p.sb("Ere%d" % li, [128, 4, nbl + 1], F32), p.sb("Eim%d" % li, [128, 4, nbl + 1], F32)))
    carry = p.sb("carry", [128, 2, 4], F32)
    p.dve(lambda e: e.memset(carry[:], 0.0), writes=bd)
    stmp = p.sb("stmp", [128, 4, 2, max(NB1, 16)], F32)
    hb = [p.sb("hb%d" % i, [128, 2, 4, 512], BF16) for i in range(2)]
    bhb = [p.buf() for _ in range(2)]
    zs = p.sb("zs", [128, 512], F32)
    bzs = p.buf()
    ysb = [p.sb("ysb%d" % i, [128, 512], BF16) for i in range(2)]
    bysb = [p.buf() for _ in range(2)]

    psQ = p.ps("psQ", [128, 512], F32); bpsQ = p.pbuf()
    psF = p.ps("psF", [128, 512], F32); bpsF = p.pbuf()
    psU = p.ps("psU", [128, 512], F32); bpsU = p.pbuf()
    psVG = p.ps("psVG", [128, 2, 256], F32); _b = p.pbuf(); bpsVG = [_b, _b]
    psD1 = p.ps("psD", [128, 512], F32); psD = [psD1, psD1]; _b = p.pbuf(); bpsD = [_b, _b]
    psS = p.ps("psS", [128, 4, 128], F32); _b = p.pbuf(); bpsS = [_b, _b]
    psO = p.ps("psO", [128, 4, 128], F32); _b = p.pbuf(); bpsO = [_b, _b]
    psM = p.ps("psM", [128, 4, 128], F32); _b = p.pbuf(); bpsM = [_b] * 4

    def cstep(i, lev, dst_re, dst_im, prev_re, prev_im, n, add_re=None, add_im=None):
        ar = lamp[:, lev, 0, i:i + 1]
        ai = lamp[:, lev, 1, i:i + 1]
        nai = lamp[:, lev, 2, i:i + 1]
        if add_re is None:
            add_re, add_im = dst_re, dst_im
        ta = stmp[:, i, 0, 0:n]
        tb = stmp[:, i, 1, 0:n]
        R = [bd[i], blam]
        p.dve(lambda e: e.scalar_tensor_tensor(out=ta, in0=prev_im, scalar=nai, in1=add_re, op0=ALU.mult, op1=ALU.add), reads=R, writes=[bd[i]])
        p.dve(lambda e: e.scalar_tensor_tensor(out=tb, in0=prev_re, scalar=ai, in1=add_im, op0=ALU.mult, op1=ALU.add), reads=R, writes=[bd[i]])
        p.dve(lambda e: e.scalar_tensor_tensor(out=dst_re, in0=prev_re, scalar=ar, in1=ta, op0=ALU.mult, op1=ALU.add), reads=R, writes=[bd[i]])
        p.dve(lambda e: e.scalar_tensor_tensor(out=dst_im, in0=prev_im, scalar=ar, in1=tb, op0=ALU.mult, op1=ALU.add), reads=R, writes=[bd[i]])

    def cscan(lev, Xre, Xim, n, hin_re, hin_im):
        if n <= 16:
            for t in range(n):
                for i in range(4):
                    pr = hin_re(i) if t == 0 else Xre(i)[:, t - 1:t]
                    pi_ = hin_im(i) if t == 0 else Xim(i)[:, t - 1:t]
                    cstep(i, lev, Xre(i)[:, t:t + 1], Xim(i)[:, t:t + 1], pr, pi_, 1)
            return
        nb = n // 16
        Ere, Eim = Ebufs[lev]
        for i in range(4):
            p.dve(lambda e, i=i: e.tensor_copy(out=Ere[:, i, 0:1], in_=hin_re(i)), reads=[bd[i]], writes=[bd[i]])
            p.dve(lambda e, i=i: e.tensor_copy(out=Eim[:, i, 0:1], in_=hin_im(i)), reads=[bd[i]], writes=[bd[i]])
            p.dve(lambda e, i=i: e.tensor_copy(out=Ere[:, i, 1:nb + 1], in_=Xre(i)[:, 0:n:16]), reads=[bd[i]], writes=[bd[i]])
            p.dve(lambda e, i=i: e.tensor_copy(out=Eim[:, i, 1:nb + 1], in_=Xim(i)[:, 0:n:16]), reads=[bd[i]], writes=[bd[i]])
        for r in range(1, 16):
            for i in range(4):
                cstep(i, lev, Ere[:, i, 1:nb + 1], Eim[:, i, 1:nb + 1], Ere[:, i, 1:nb + 1], Eim[:, i, 1:nb + 1], nb,
                      add_re=Xre(i)[:, r:n:16], add_im=Xim(i)[:, r:n:16])
        cscan(lev + 1, lambda i: Ere[:, i, 1:nb + 1], lambda i: Eim[:, i, 1:nb + 1], nb,
              lambda i: Ere[:, i, 0:1], lambda i: Eim[:, i, 0:1])
        for r in range(16):
            for i in range(4):
                if r == 0:
                    pr, pi_ = Ere[:, i, 0:nb], Eim[:, i, 0:nb]
                else:
                    pr, pi_ = Xre(i)[:, r - 1:n:16], Xim(i)[:, r - 1:n:16]
                cstep(i, lev, Xre(i)[:, r:n:16], Xim(i)[:, r:n:16], pr, pi_, nb)

    NTILE = L // 512
    TPS = TS5 // 512

    def load_x(ti):
        s = ti % 2
        p.dma(xt[s][:], xT[:, ti * 512:(ti + 1) * 512].rearrange("(c p) t -> p c t", p=128), writes=[bxt[s]], eng="pool")

    load_x(0)
    chunk_idx = 0
    for ti in range(NTILE):
        s = ti % 2
        x_ = xt[s]
        if ti + 1 < NTILE:
            load_x(ti + 1)
        t0 = ti * 512
        tl = (ti % TPS) * 512
        for (ps_, bps_, c0) in ((psQ, bpsQ, 0), (psF, bpsF, 128), (psU, bpsU, 512)):
            for c in range(8):
                p.pe(lambda e, ps_=ps_, c=c, c0=c0, x_=x_: e.matmul(ps_[:], lhsT=wA[:, c, c0:c0 + 128], rhs=x_[:, c, :],
                                                                    start=(c == 0), stop=(c == 7)), reads=[bwA, bxt[s]], writes=[bps_])
        RH = [bH]
        p.act(lambda e: e.activation(out=f_, in_=psF[:], func=AF.Sigmoid), reads=[bpsF], writes=RH)
        p.dve(lambda e: e.tensor_scalar(out=f_, in0=f_, scalar1=oml, scalar2=lb, op0=ALU.mult, op1=ALU.add), reads=RH + [bm], writes=RH)
        p.act(lambda e: e.activation(out=lf, in_=f_, func=AF.Ln), reads=RH, writes=RH)
        p.dve(lambda e: e.tensor_scalar(out=kk, in0=f_, scalar1=-1.0, scalar2=1.0, op0=ALU.mult, op1=ALU.add), reads=RH, writes=RH)
        p.dve(lambda e: e.tensor_tensor_scan(out=bb, data0=m01, data1=lf, initial=0.0, op0=ALU.mult, op1=ALU.add), reads=RH + [bc], writes=RH)
        p.act(lambda e: e.activation(out=eb, in_=bb, func=AF.Exp), reads=RH, writes=RH)
        p.act(lambda e: e.activation(out=enb, in_=bb, func=AF.Exp, scale=-1.0), reads=RH, writes=RH)
        p.act(lambda e: e.activation(out=qq, in_=psQ[:], func=AF.Silu), reads=[bpsQ], writes=RH)
        p.dve(lambda e: e.tensor_tensor(out=qdec[:], in0=qq, in1=eb, op=ALU.mult), reads=RH, writes=[bqk])
        p.dve(lambda e: e.tensor_tensor(out=kinv, in0=kk, in1=enb, op=ALU.mult), reads=RH, writes=RH)
        p.act(lambda e: e.copy(out=kinvb[:], in_=kinv), reads=RH, writes=[bqk])
        eb3 = eb.rearrange("p (c s) -> p c s", s=64)
        p.dve(lambda e: e.tensor_tensor(out=ktT.rearrange("p (c s) -> p c s", s=64), in0=kinv.rearrange("p (c s) -> p c s", s=64),
                                        in1=eb3[:, :, 63:64].to_broadcast([128, 8, 64]), op=ALU.mult), reads=RH, writes=RH)
        sb5 = ti % 2
        p.act(lambda e, tl=tl: e.copy(out=uT[:, tl:tl + 512], in_=psU[:]), reads=[bpsU], writes=[buT])
        p.dve(lambda e, sb5=sb5: e.tensor_copy(out=uTb[sb5][:], in_=psU[:]), reads=[bpsU], writes=[buTb[sb5]])
        k = 0
        for i in range(4):
            for ri in range(2):
                pd = psD[k % 2]
                p.pe(lambda e, pd=pd, i=i, ri=ri, sb5=sb5: e.matmul(pd[:], lhsT=wB[:, ri, i * 128:(i + 1) * 128], rhs=uTb[sb5][:],
                                                                    start=True, stop=True), reads=[bwB, buTb[sb5]], writes=[bpsD[k % 2]])
                dst = (dre if ri == 0 else dim_)[:, i, tl:tl + 512]
                p.act(lambda e, pd=pd, dst=dst: e.copy(out=dst, in_=pd[:]), reads=[bpsD[k % 2]], writes=[bd[i]])
                k += 1
        for sub in range(4):
            tsl = slice(sub * 128, (sub + 1) * 128)
            h2 = sub % 2
            for c in range(8):
                p.pe(lambda e, c=c, tsl=tsl, h2=h2, x_=x_: e.matmul(psVG[:, h2, :], lhsT=x_[:, c, tsl], rhs=wA[:, c, 256:512],
                                                                    start=(c == 0), stop=(c == 7)), reads=[bwA, bxt[s]], writes=[bpsVG[h2]])
            p.act(lambda e, sub=sub, h2=h2: e.copy(out=vb[:, sub, :], in_=psVG[:, h2, 0:128]), reads=[bpsVG[h2]], writes=[bvg[sub]])
            p.act(lambda e, sub=sub, h2=h2: e.activation(out=gn[:, sub, :], in_=psVG[:, h2, 128:256], func=AF.Silu), reads=[bpsVG[h2]], writes=[bvg[sub]])
            p.dve(lambda e, sub=sub: e.tensor_tensor(out=gn[:, sub, :], in0=gn[:, sub, :], in1=ng, op=ALU.mult), reads=[bvg[sub], bc], writes=[bvg[sub]])
            mi = sub % 2
            p.pe(lambda e, mi=mi, tsl=tsl: e.transpose(out=psM[:, mi, :], in_=ktT[:, tsl], identity=identf[:]), reads=RH + [bidf], writes=[bpsM[mi]])
            p.act(lambda e, mi=mi, sub=sub: e.copy(out=kt[:, sub, :], in_=psM[:, mi, :]), reads=[bpsM[mi]], writes=[bkt[sub]])
        ys_ = yasb[ti % 2]
        for sub in range(4):
            tsl = slice(sub * 128, (sub + 1) * 128)
            ai_ = sub % 2
            mi = 2 + sub % 2
            p.pe(lambda e, mi=mi, tsl=tsl: e.matmul(psM[:, mi, :], lhsT=kinvb[:, tsl], rhs=qdec[:, tsl], start=True, stop=True),
                 reads=[bqk], writes=[bpsM[mi]])
            p.dve(lambda e, mi=mi, ai_=ai_: e.tensor_tensor(out=attT[ai_][:], in0=psM[:, mi, :], in1=tri, op=ALU.mult),
                  reads=[bpsM[mi], bc], writes=[battT[ai_]])
            oi = sub % 2
            p.pe(lambda e, oi=oi, ai_=ai_, sub=sub: e.matmul(psO[:, oi, :], lhsT=attT[ai_][:], rhs=vb[:, sub, :], start=True, stop=False),
                 reads=[battT[ai_], bvg[sub]], writes=[bpsO[oi]])
            for hc in range(2):
                rows = slice(hc * 64, (hc + 1) * 64)
                tch = slice(sub * 128 + hc * 64, sub * 128 + (hc + 1) * 64)
                sprev = (chunk_idx + 1) % 2
                snew = chunk_idx % 2
                p.pe(lambda e, oi=oi, rows=rows, tch=tch, sprev=sprev, hc=hc: e.matmul(
                    psO[rows, oi, :], lhsT=qdec[:, tch], rhs=Sb[sprev][:], start=False, stop=(hc == 1)),
                    reads=[bqk, bSb[sprev]], writes=[bpsO[oi]])
                di = chunk_idx % 2
                p.pe(lambda e, di=di, rows=rows, sub=sub: e.matmul(psS[:, di, :], lhsT=kt[rows, sub, :], rhs=vb[rows, sub, :], start=True, stop=True),
                     reads=[bkt[sub], bvg[sub]], writes=[bpsS[di]])
                cpos = sub * 128 + hc * 64 + 63
                p.dve(lambda e, di=di, cpos=cpos: e.scalar_tensor_tensor(out=S[:], in0=S[:], scalar=eb[:, cpos:cpos + 1], in1=psS[:, di, :],
                                                                         op0=ALU.mult, op1=ALU.add), reads=[bS, bpsS[di]] + RH, writes=[bS])
                p.act(lambda e, snew=snew: e.copy(out=Sb[snew][:], in_=S[:]), reads=[bS], writes=[bSb[snew]])
                chunk_idx += 1
            p.act(lambda e, oi=oi: e.activation(out=osc[:], in_=psO[:, oi, :], func=AF.Square, accum_out=om[:, 0:1]), reads=[bpsO[oi]], writes=[bo])
            p.act(lambda e: e.activation(out=om[:, 1:2], in_=om[:, 0:1], func=AF.Sqrt, scale=1.0 / 128.0, bias=RMS_EPS), reads=[bo], writes=[bo])
            p.dve(lambda e: e.reciprocal(out=om[:, 1:2], in_=om[:, 1:2]), reads=[bo], writes=[bo])
            p.dve(lambda e, oi=oi, sub=sub, ys_=ys_: e.scalar_tensor_tensor(out=ys_[:, sub, :], in0=psO[:, oi, :], scalar=om[:, 1:2], in1=gn[:, sub, :],
                                                                         op0=ALU.mult, op1=ALU.mult), reads=[bpsO[oi], bo, bvg[sub]], writes=[byasb[ti % 2]])
        for sub in range(4):
            p.pe(lambda e, sub=sub, ys_=ys_: e.transpose(out=psM[:, sub, :], in_=ys_[:, sub, :], identity=identf[:]),
                 reads=[byasb[ti % 2], bidf], writes=[bpsM[0]])
        yt_ = yaTs[ti % 2]
        p.act(lambda e, yt_=yt_: e.copy(out=yt_[:], in_=psM[:].rearrange("p a b -> p (a b)")), reads=[bpsM[0]], writes=[byaTs[ti % 2]])
        emit_ya(res, ti, yt_, byaTs[ti % 2])
        if (ti + 1) % TPS == 0:
            sup0 = (ti + 1 - TPS) * 512
            cscan(0, lambda i: dre[:, i, :], lambda i: dim_[:, i, :], TS5,
                  lambda i: carry[:, 0, i:i + 1], lambda i: carry[:, 1, i:i + 1])
            for i in range(4):
                p.dve(lambda e, i=i: e.tensor_copy(out=carry[:, 0, i:i + 1], in_=dre[:, i, TS5 - 1:TS5]), reads=[bd[i]], writes=[bd[i]])
                p.dve(lambda e, i=i: e.tensor_copy(out=carry[:, 1, i:i + 1], in_=dim_[:, i, TS5 - 1:TS5]), reads=[bd[i]], writes=[bd[i]])
            for ch in range(TPS):
                csl = slice(ch * 512, (ch + 1) * 512)
                hs = ch % 2
                for i in range(4):
                    p.act(lambda e, i=i, hs=hs, csl=csl: e.copy(out=hb[hs][:, 0, i, :], in_=dre[:, i, csl]), reads=[bd[i]], writes=[bhb[hs]])
                    p.act(lambda e, i=i, hs=hs, csl=csl: e.copy(out=hb[hs][:, 1, i, :], in_=dim_[:, i, csl]), reads=[bd[i]], writes=[bhb[hs]])
                k = 0
                for i in range(4):
                    for ri in range(2):
                        p.pe(lambda e, i=i, ri=ri, hs=hs, k=k: e.matmul(psQ[:], lhsT=wC[:, ri, i * 128:(i + 1) * 128], rhs=hb[hs][:, ri, i, :],
                                                                        start=(k == 0), stop=(k == 7)), reads=[bwB, bhb[hs]], writes=[bpsQ])
                        k += 1
                p.dve(lambda e, csl=csl: e.scalar_tensor_tensor(out=zs[:], in0=uT[:, csl], scalar=dsk, in1=psQ[:], op0=ALU.mult, op1=ALU.add),
                      reads=[buT, bc, bpsQ], writes=[bzs])
                p.act(lambda e, hs=hs: e.activation(out=ysb[hs][:], in_=zs[:], func=AF.Gelu_apprx_tanh), reads=[bzs], writes=[bysb[hs]])
                emit_ys(res, (sup0 // 512) + ch, ysb[hs], bysb[hs])


BIGM = 30000.0
DEBUG_MIXC = False


def build_mixc(L):
    nc = bass.Bass("TRN2", target_bir_lowering=False)
    xT = nc.dram_tensor("xT", [1024, L], F32, kind="ExternalInput").ap()
    ycT_d = nc.dram_tensor("ycT", [128, L], BF16, kind="ExternalOutput").ap()
    ydT_d = nc.dram_tensor("ydT", [128, L], BF16, kind="ExternalOutput").ap()
    p = Prog(nc)
    p.debug = DEBUG_MIXC

    def load_x(res, ti, dst, bdst):
        p.dma(dst[:], xT[:, ti * 512:(ti + 1) * 512].rearrange("(c p) t -> p c t", p=128), writes=[bdst], eng="pool")

    def emit_y(res, ti, which, t, b):
        d = ycT_d if which == 0 else ydT_d
        p.dma(d[:, ti * 512:(ti + 1) * 512], t[:], reads=[b])

    phase_mixc(p, nc, "", L, load_x, emit_y)
    p.finish()
    return nc


def phase_mixc(p, nc, pre, L, load_x_cb, emit_y):
    NBLK = L // 256
    wC_d = nc.dram_tensor(pre + "wC", [1024, 768], F32, kind="ExternalInput").ap()
    sink_d = nc.dram_tensor(pre + "sink", [128, 2], F32, kind="ExternalInput").ap()
    swm_d = nc.dram_tensor(pre + "swm", [128, 2, 256], F32, kind="ExternalInput").ap()
    cm_d = nc.dram_tensor(pre + "cm", [128, 2, 256], F32, kind="ExternalInput").ap()
    fut_d = nc.dram_tensor(pre + "fut", [128, 2, 128], F32, kind="ExternalInput").ap()
    res = {}
    identb_f, bidf = make_ident(p, F32, "identf")
    identb = p.sb("identb", [128, 128], BF16)
    bidb = p.buf()
    p.dve(lambda e: e.tensor_copy(out=identb[:], in_=identb_f[:]), reads=[bidf], writes=[bidb])
    wC = p.sb("wC", [128, 8, 768], BF16)
    bwC = p.buf()
    p.dma(wC[:], wC_d.rearrange("(c p) f -> p c f", p=128), writes=[bwC], eng="pool")
    cst = p.sb("cst", [128, 2 + 512 + 512 + 256], F32)
    bc = p.buf("cst")
    sink = cst[:, 0:2]
    swm = cst[:, 2:514].rearrange("p (a k) -> p a k", a=2)
    cm = cst[:, 514:1026].rearrange("p (a k) -> p a k", a=2)
    fut = cst[:, 1026:1282].rearrange("p (a k) -> p a k", a=2)
    p.dma(sink, sink_d, writes=[bc])
    p.dma(swm, swm_d, writes=[bc])
    p.dma(cm, cm_d, writes=[bc])
    p.dma(fut, fut_d, writes=[bc])
    ones = p.sb("ones", [128, 128], F32)
    bones = p.buf()
    p.dve(lambda e: e.memset(ones[:], 1.0), writes=[bones])

    qcT = p.sb("qcT", [128, 512], BF16); bqc = p.buf()
    qdT = p.sb("qdT", [128, 512], BF16); bqd = p.buf()
    kkc = p.sb("kkc", [128, L], BF16)
    kkd = p.sb("kkd", [128, 2, L], BF16)
    vc = p.sb("vc", [128, L // 128, 64], BF16)
    vd = p.sb("vd", [128, L // 128, 66], BF16)
    bkv = p.buf("kv")
    bkvt = [p.buf("kv%d" % i) for i in range(L // 512)]
    kmT = p.sb("kmT", [128, 2, 64], BF16)
    bkm = p.buf("km")
    p.dve(lambda e: e.memset(kmT[:], 0.0), writes=[bkm])
    kmx = p.sb("kmx", [128, 4], F32)
    p.dve(lambda e: e.memset(kmx[:], 0.0), writes=[bkm])
    p.dve(lambda e: e.memset(vd[:, :, 64:66], 1.0), writes=[bkvt[0]])
    xt = [p.sb("xt%d" % i, [128, 8, 512], BF16) for i in range(2)]
    bxt = [p.buf() for _ in range(2)]
    sq = p.sb("sq", [128, 512], F32); bsq = p.buf()
    qab = p.sb("qab", [128, 512], BF16)
    bqab = p.buf()
    kab = p.sb("kab", [128, 2, 2], BF16)
    k8 = p.sb("k8", [128, 8], F32)
    sm = [p.sb("sm%d" % i, [128, 256], F32) for i in range(2)]; bsm = [p.buf() for _ in range(2)]
    Pb = [p.sb("Pb%d" % i, [128, 512], BF16) for i in range(4)]; bPb = [p.buf() for _ in range(4)]
    Dring = p.sb("Dring", [128, 8, 128], BF16); bD = [p.buf() for _ in range(8)]
    dcnt = [0]
    PT = [p.sb("PT%d" % i, [128, 4, 128], BF16) for i in range(4)]; bPT = [p.buf() for _ in range(4)]
    scS = [p.sb("scS%d" % i, [128, 8], F32) for i in range(6)]; bscS = [p.buf() for _ in range(6)]
    gset = []
    for i in range(5):
        gset.append(dict(sc=p.sb("scg%d" % i, [128, 8], F32), gm=p.sb("gm%d" % i, [128, 64], F32), sel=p.sb("sel%d" % i, [128, 64], F32),
                         top8=p.sb("top8_%d" % i, [128, 8], F32), lcols=p.sb("lcols%d" % i, [128, 66], F32), b=p.buf("gate%d" % i)))
    ycs = [p.sb("ycs%d" % i, [128, 4, 128], F32) for i in range(2)]; bycs = [p.buf() for _ in range(2)]
    yds = [p.sb("yds%d" % i, [128, 4, 128], F32) for i in range(2)]; byds = [p.buf() for _ in range(2)]
    yTs = [p.sb("yTs%d" % i, [128, 512], BF16) for i in range(4)]; byTs = [p.buf() for _ in range(4)]

    _psP = p.ps("psP", [128, 512], F32); _bpsP = p.pbuf(); psP = [_psP, _psP]; bpsP = [_bpsP, _bpsP]
    psV = p.ps("psV", [128, 512], F32); bpsV = p.pbuf()
    psS = [p.ps("psS%d" % i, [128, 512], F32) for i in range(2)]; bpsS = [p.pbuf() for _ in range(2)]
    psT = [p.ps("psT%d" % i, [128, 4, 128], F32) for i in range(2)]; bpsT = [p.pbuf() for _ in range(2)]
    psO = [p.ps("psO%d" % i, [128, 512], F32) for i in range(2)]; bpsO = [p.pbuf() for _ in range(2)]

    NTILE = L // 512

    def load_x(ti):
        s = ti % 2
        load_x_cb(res, ti, xt[s], bxt[s])

    load_x(0)
    rot = 0
    kbase = 0
    for ti in range(NTILE):
        s = ti % 2
        x_ = xt[s]
        if ti + 1 < NTILE:
            load_x(ti + 1)
        t0 = ti * 512
        tsl = slice(t0, t0 + 512)
        dsts = [(qcT[:], bqc, 0.125), (kkc[:, tsl], bkvt[ti], 1.0), (qdT[:], bqd, 0.125), (kkd[:, 0, tsl], bkvt[ti], 1.0), (kkd[:, 1, tsl], bkvt[ti], 1.0)]
        for pi, (dst, bdst, scl) in enumerate(dsts):
            pp = psP[pi % 2]
            for c in range(8):
                p.pe(lambda e, pp=pp, c=c, pi=pi, x_=x_: e.matmul(pp[:], lhsT=wC[:, c, pi * 128:(pi + 1) * 128], rhs=x_[:, c, :],
                                                                  start=(c == 0), stop=(c == 7)), reads=[bwC, bxt[s]], writes=[bpsP[pi % 2]])
            p.act(lambda e, pp=pp, dst=dst, scl=scl: e.activation(out=dst, in_=pp[:], func=AF.Copy, scale=scl), reads=[bpsP[pi % 2]], writes=[bdst])
            if pi == 2:
                p.dve(lambda e, pp=pp: e.tensor_scalar(out=sq[:], in0=pp[:], scalar1=-1.0, scalar2=None, op0=ALU.mult), reads=[bpsP[pi % 2]], writes=[bsq])
                p.dve(lambda e, pp=pp: e.tensor_tensor(out=sq[:], in0=sq[:], in1=pp[:], op=ALU.max), reads=[bpsP[pi % 2], bsq], writes=[bsq])
                p.dve(lambda e: e.tensor_scalar(out=qab[:], in0=sq[:], scalar1=0.125, scalar2=None, op0=ALU.mult), reads=[bsq], writes=[bqab])
            if pi >= 3:
                hh_ = pi - 3
                p.dve(lambda e, pp=pp: e.tensor_scalar(out=sq[:], in0=pp[:], scalar1=-1.0, scalar2=None, op0=ALU.mult), reads=[bpsP[pi % 2]], writes=[bsq])
                p.dve(lambda e, pp=pp: e.tensor_tensor(out=sq[:], in0=sq[:], in1=pp[:], op=ALU.max), reads=[bpsP[pi % 2], bsq], writes=[bsq])
                p.dve(lambda e: e.max(out=k8[:], in_=sq[:]), reads=[bsq], writes=[bkm])
                p.dve(lambda e, hh_=hh_: e.tensor_tensor(out=kmx[:, hh_:hh_ + 1], in0=kmx[:, hh_:hh_ + 1], in1=k8[:, 0:1], op=ALU.max), reads=[bkm], writes=[bkm])
                p.dve(lambda e, hh_=hh_: e.tensor_copy(out=kab[:, hh_, 0:1], in_=kmx[:, hh_:hh_ + 1]), reads=[bkm], writes=[bkm])
                p.dve(lambda e, hh_=hh_: e.tensor_copy(out=kab[:, hh_, 1:2], in_=kmx[:, hh_:hh_ + 1]), reads=[bkm], writes=[bkm])
        for sub in range(4):
            g = ti * 4 + sub
            for c in range(8):
                p.pe(lambda e, c=c, sub=sub, x_=x_: e.matmul(psV[:, 0:128], lhsT=x_[:, c, sub * 128:(sub + 1) * 128], rhs=wC[:, c, 640:768],
                                                             start=(c == 0), stop=(c == 7)), reads=[bwC, bxt[s]], writes=[bpsV])
            p.act(lambda e, g=g: e.copy(out=vc[:, g, :], in_=psV[:, 0:64]), reads=[bpsV], writes=[bkvt[ti]])
            p.act(lambda e, g=g: e.copy(out=vd[:, g, 0:64], in_=psV[:, 64:128]), reads=[bpsV], writes=[bkvt[ti]])
        for hb in range(2):
            blk = ti * 2 + hb
            for hv in range(2):
                p.dve(lambda e, blk=blk, hv=hv: e.tensor_reduce(out=sq[:, 0:1], in_=kkd[:, hv, blk * 256:(blk + 1) * 256], axis=AX.X, op=ALU.add),
                      reads=[bkvt[ti]], writes=[bsq])
                p.dve(lambda e, blk=blk, hv=hv: e.tensor_scalar(out=kmT[:, hv, blk:blk + 1], in0=sq[:, 0:1], scalar1=1.0 / 256.0, scalar2=None, op0=ALU.mult),
                      reads=[bsq], writes=[bkm])
        yc_ = ycs[ti % 2]
        yd_ = yds[ti % 2]
        prev_kv = [bkvt[ti - 1]] if ti > 0 else []

        steps = []
        fins = {}

        def swa_step(g, sub, h, yc_=yc_):
            rows = slice(h * 64, (h + 1) * 64)
            qsl = slice(sub * 128, (sub + 1) * 128)
            if g == 0:
                k0, nk, mk = 0, 128, swm[:, 1, 128:256]
            else:
                k0, nk, mk = (g - 1) * 128, 256, swm[:, 1, :]
            nkc = nk // 128
            kvr = [bkvt[ti]] + (prev_kv if sub == 0 else [])

            def A(k):
                r2, r3 = k % 2, k % 4
                ps_, smr, pb = psS[r2], sm[r2], Pb[r3]
                scs, R = scS[k % 6], [bscS[k % 6]]
                mx, nmx, rs, es, den = [scs[:, i:i + 1] for i in range(5)]
                p.pe(lambda e: e.matmul(ps_[:, 0:nk], lhsT=qcT[rows, qsl], rhs=kkc[rows, k0:k0 + nk], start=True, stop=True),
                     reads=[bqc] + kvr, writes=[bpsS[r2]])
                p.dve(lambda e: e.tensor_tensor(out=smr[:, 0:nk], in0=ps_[:, 0:nk], in1=mk, op=ALU.add), reads=[bpsS[r2], bc], writes=[bsm[r2]])
                p.dve(lambda e: e.tensor_reduce(out=mx, in_=smr[:, 0:nk], axis=AX.X, op=ALU.max), reads=[bsm[r2]], writes=R)
                p.dve(lambda e: e.tensor_tensor(out=mx, in0=mx, in1=sink[:, h:h + 1], op=ALU.max), reads=R + [bc], writes=R)
                p.dve(lambda e: e.tensor_scalar(out=nmx, in0=mx, scalar1=-1.0, scalar2=None, op0=ALU.mult), reads=R, writes=R)
                p.act(lambda e: e.activation(out=pb[:, 0:nk], in_=smr[:, 0:nk], func=AF.Exp, bias=nmx, scale=1.0, accum_out=rs),
                      reads=[bsm[r2]] + R, writes=[bPb[r3]] + R)
                p.act(lambda e: e.activation(out=es, in_=sink[:, h:h + 1], func=AF.Exp, bias=nmx, scale=1.0), reads=R + [bc], writes=R)
                p.dve(lambda e: e.tensor_tensor(out=den, in0=rs, in1=es, op=ALU.add), reads=R, writes=R)
                p.dve(lambda e: e.reciprocal(out=den, in_=den), reads=R, writes=R)

            def B(k):
                r2, r3 = k % 2, k % 4
                pb, pt = Pb[r3], PT[r3]
                for kc in range(nkc):
                    p.pe(lambda e, kc=kc: e.matmul(psT[r2][:, kc, :], lhsT=pb[:, kc * 128:(kc + 1) * 128], rhs=identb[:], start=True, stop=True),
                         reads=[bPb[r3], bidb], writes=[bpsT[r2]])
                p.dve(lambda e: e.tensor_copy(out=pt[:, 0:nkc, :], in_=psT[r2][:, 0:nkc, :]), reads=[bpsT[r2]], writes=[bPT[r3]])

            def C(k):
                r2, r3 = k % 4, k % 6
                pt = PT[r2]
                den = scS[r3][:, 4:5]
                for kc in range(nkc):
                    gk = (g - 1 + kc) if g > 0 else 0
                    p.pe(lambda e, kc=kc, gk=gk: e.matmul(psV[:, 256:320], lhsT=pt[:, kc, :], rhs=vc[:, gk, :], start=(kc == 0), stop=(kc == nkc - 1)),
                         reads=[bPT[r2]] + kvr, writes=[bpsV])
                p.act(lambda e: e.activation(out=yc_[:, sub, h * 64:(h + 1) * 64], in_=psV[:, 256:320], func=AF.Copy, scale=den),
                      reads=[bpsV, bscS[r3]], writes=[bycs[ti % 2]])
            return dict(A=A, B=B, C=C)

        def moba_prologue(g, sub, h, gi):
            rows = slice(h * 64, (h + 1) * 64)
            qsl = slice(sub * 128, (sub + 1) * 128)
            n = g // 2
            gs = gset[gi]
            G = [gs["b"]]
            mb, nmb = gs["sc"][:, 0:1], gs["sc"][:, 1:2]
            p.pe(lambda e: e.matmul(psV[:, 0:2], lhsT=qab[:, qsl], rhs=kab[:, h, :], start=True, stop=True), reads=[bqab, bkm], writes=[bpsV])
            p.dve(lambda e: e.tensor_copy(out=mb, in_=psV[:, 0:1]), reads=[bpsV], writes=G)
            p.dve(lambda e: e.tensor_scalar(out=nmb, in0=mb, scalar1=-1.0, scalar2=None, op0=ALU.mult), reads=G, writes=G)
            if n > 0:
                gm_, sel_, top8_ = gs["gm"], gs["sel"], gs["top8"]
                p.pe(lambda e: e.matmul(psV[:, 64:128], lhsT=qdT[:, qsl], rhs=kmT[:, h, :], start=True, stop=True), reads=[bqd, bkm], writes=[bpsV])
                fsl = slice(64 - n, 128 - n)
                p.dve(lambda e: e.tensor_tensor(out=gm_[:], in0=psV[:, 64:128], in1=fut[:, 0, fsl], op=ALU.add), reads=[bpsV, bc], writes=G)
                p.dve(lambda e: e.max(out=top8_[:], in_=gm_[:]), reads=G, writes=G)
                p.dve(lambda e: e.tensor_scalar(out=sel_[:], in0=gm_[:], scalar1=top8_[:, 2:3], scalar2=None, op0=ALU.is_ge), reads=G, writes=G)

        def moba_step(g, sub, h, gi, jb0, nb, oi, yd_=yd_):
            rows = slice(h * 64, (h + 1) * 64)
            qsl = slice(sub * 128, (sub + 1) * 128)
            n, a = g // 2, g % 2
            own = (jb0 == n)
            gs = gset[gi]
            G = [gs["b"]]
            kreads = [bkvt[(jb0 + i) // 2] for i in range(nb)]
            po = psO[oi]
            W = nb * 256
            nc_ = nb * 2
            slots = []

            def A(k):
                r2, r3 = k % 2, k % 4
                ps_, pb = psS[r2], Pb[r3]
                p.pe(lambda e: e.matmul(ps_[:, 0:W], lhsT=qdT[:, qsl], rhs=kkd[:, h, jb0 * 256:jb0 * 256 + W], start=True, stop=True),
                     reads=[bqd] + kreads, writes=[bpsS[r2]])
                if own:
                    smr = sm[r2]
                    p.dve(lambda e: e.tensor_tensor(out=smr[:], in0=ps_[:, 0:256], in1=cm[:, a, :], op=ALU.add), reads=[bpsS[r2], bc], writes=[bsm[r2]])
                    p.act(lambda e: e.activation(out=pb[:, 0:256], in_=smr[:], func=AF.Exp, bias=gs["sc"][:, 1:2], scale=1.0),
                          reads=[bsm[r2]] + G, writes=[bPb[r3]])
                else:
                    p.act(lambda e: e.activation(out=pb[:, 0:W], in_=ps_[:, 0:W], func=AF.Exp, bias=gs["sc"][:, 1:2], scale=1.0),
                          reads=[bpsS[r2]] + G, writes=[bPb[r3]])
                    for i in range(nb):
                        sl_ = dcnt[0] % 8
                        dcnt[0] += 1
                        slots.append(sl_)
                        jb = jb0 + i
                        p.pool(lambda e, sl_=sl_, jb=jb: e.tensor_scalar(out=Dring[:, sl_, :], in0=identb[:], scalar1=gs["sel"][:, jb:jb + 1], scalar2=0.0,
                                                                        op0=ALU.mult, op1=ALU.add), reads=G + [bidb], writes=[bD[sl_]])

            def B(k):
                r2, r3 = k % 2, k % 4
                pb, pt = Pb[r3], PT[r3]
                for c in range(nc_):
                    if own:
                        rhs_, rb = identb[:], []
                    else:
                        rhs_, rb = Dring[:, slots[c // 2], :], [bD[slots[c // 2]]]
                    p.pe(lambda e, c=c, rhs_=rhs_: e.matmul(psT[r2][:, c, :], lhsT=pb[:, c * 128:(c + 1) * 128], rhs=rhs_, start=True, stop=True),
                         reads=[bPb[r3], bidb] + rb, writes=[bpsT[r2]])
                p.dve(lambda e: e.tensor_copy(out=pt[:, 0:nc_, :], in_=psT[r2][:, 0:nc_, :]), reads=[bpsT[r2]], writes=[bPT[r3]])

            def C(k):
                r2 = k % 4
                pt = PT[r2]
                for c in range(nc_):
                    p.pe(lambda e, c=c: e.matmul(po[:, 0:66], lhsT=pt[:, c, :], rhs=vd[:, jb0 * 2 + c, :], start=(jb0 == 0 and c == 0), stop=(own and c == nc_ - 1)),
                         reads=[bPT[r2]] + kreads, writes=[bpsO[oi]])
                if own:
                    lsum = gs["sc"][:, 2:3]
                    p.dve(lambda e: e.reciprocal(out=lsum, in_=po[:, 64:65]), reads=G + [bpsO[oi]], writes=G)
                    p.act(lambda e: e.activation(out=yd_[:, sub, h * 64:(h + 1) * 64], in_=po[:, 0:64], func=AF.Copy, scale=lsum),
                          reads=[bpsO[oi]] + G, writes=[byds[ti % 2]])
            return dict(A=A, B=B, C=C)

        units = [(ti * 4 + sub, sub, h) for sub in range(4) for h in range(2)]
        seq = []
        for ui, (g, sub, h) in enumerate(units):
            gi = (ti * 8 + ui) % 5
            if ui == 0:
                seq.append(("pro", lambda g=g, sub=sub, h=h, gi=gi: moba_prologue(g, sub, h, gi)))
            if ui + 1 < len(units):
                g2, sub2, h2 = units[ui + 1]
                seq.append(("pro", lambda g2=g2, sub2=sub2, h2=h2, gi=gi: moba_prologue(g2, sub2, h2, (gi + 1) % 5)))
            seq.append(("step", swa_step(g, sub, h)))
            n_ = g // 2
            jb = 0
            while jb < n_:
                nb = 2 if jb + 1 < n_ else 1
                seq.append(("step", moba_step(g, sub, h, gi, jb, nb, (ti * 8 + ui) % 2)))
                jb += nb
            seq.append(("step", moba_step(g, sub, h, gi, n_, 1, (ti * 8 + ui) % 2)))
        D = 2
        stp = [x[1] for x in seq if x[0] == "step"]
        nst = len(stp)
        k = 0
        for kind, x in seq:
            if kind == "pro":
                x()
                continue
            x["A"](kbase + k)
            if k - D >= 0:
                stp[k - D]["B"](kbase + k - D)
            if k - 2 * D >= 0:
                stp[k - 2 * D]["C"](kbase + k - 2 * D)
            k += 1
        for k in range(nst, nst + 2 * D):
            if 0 <= k - D < nst:
                stp[k - D]["B"](kbase + k - D)
            if 0 <= k - 2 * D < nst:
                stp[k - 2 * D]["C"](kbase + k - 2 * D)
        kbase += nst
        for which, (src, bsrc) in enumerate(((yc_, bycs[ti % 2]), (yd_, byds[ti % 2]))):
            pp = psP[which]
            for sub in range(4):
                p.pe(lambda e, sub=sub, src=src, pp=pp: e.transpose(out=pp[:, sub * 128:(sub + 1) * 128], in_=src[:, sub, :], identity=identb_f[:]),
                     reads=[bsrc, bidf], writes=[bpsP[which]])
            k = (ti % 2) * 2 + which
            p.act(lambda e, k=k, pp=pp: e.copy(out=yTs[k][:], in_=pp[:]), reads=[bpsP[which]], writes=[byTs[k]])
            emit_y(res, ti, which, yTs[k], byTs[k])


GROUPS = [[0, 1, 2, 3], [4, 5, 6, 7]]


def build_fused(L, debug=False):
    SEG = L // 4
    NCH = SEG // 512
    TG = min(1024, SEG)
    CPG = TG // 512
    nc = bass.Bass("TRN2", target_bir_lowering=False)
    msk_d = nc.dram_tensor("rankmask", [128, 4], F32, kind="ExternalInput").ap()
    xres_d = nc.dram_tensor("xres", [SEG, 1024], F32, kind="ExternalInput").ap()
    xo_d = nc.dram_tensor("xo", [SEG, 1024], F32, kind="ExternalOutput").ap()
    exin = [nc.dram_tensor("exin%d" % i, [NCH, 4, 4, 256, 512], BF16).ap() for i in range(2)]
    exout = [nc.dram_tensor("exout%d" % i, [NCH, 4, 256, 512], BF16).ap() for i in range(2)]
    agin = nc.dram_tensor("agin", [NCH, 1024, 512], BF16).ap()
    agout = nc.dram_tensor("agout", [NCH, 4, 1024, 512], BF16).ap()
    x1loc = nc.dram_tensor("x1loc", [SEG, 1024], F32).ap()

    p = Prog(nc)
    p.debug = debug
    bexin = [[p.buf() for _ in range(NCH)] for _ in range(2)]
    bexout = [[p.buf() for _ in range(NCH)] for _ in range(2)]
    bagin = [p.buf() for _ in range(NCH)]
    bagout = [p.buf() for _ in range(NCH)]
    bx1loc = p.buf()

    def make_exchange_writer(xi):
        st = {}

        def setup():
            st["msk"] = p.sb("msk%d" % xi, [128, 4], F32)
            st["bmsk"] = p.buf()
            p.dma(st["msk"][:], msk_d, writes=[st["bmsk"]])
            st["tmp"] = [p.sb("extmp%d_%d" % (xi, i), [128, 4, 512], BF16) for i in range(2)]
            st["btmp"] = [p.buf() for _ in range(2)]
            st["k"] = 0

        def emit(ti, f0, t, b):
            if "msk" not in st:
                setup()
            k = st["k"] % 2
            st["k"] += 1
            tmp, btmp = st["tmp"][k], st["btmp"][k]
            for q in range(4):
                p.pool(lambda e, q=q, tmp=tmp: e.tensor_scalar(out=tmp[:, q, :], in0=t[:], scalar1=st["msk"][:, q:q + 1], scalar2=0.0,
                                                               op0=ALU.mult, op1=ALU.add), reads=[b, st["bmsk"]], writes=[btmp])
            d, c = ti // NCH, ti % NCH
            p.dma(exin[xi][c, d, :, f0:f0 + 128, :].rearrange("q f t -> f q t"), tmp[:], reads=[btmp], writes=[bexin[xi][c]])
            st.setdefault("cnt", {})
            st["cnt"][c] = st["cnt"].get(c, 0) + 1
            if st["cnt"][c] == 8:
                p.collective("ReduceScatter", ALU.add, GROUPS,
                             ins=[exin[xi][c].rearrange("d q f t -> (d q f) t")], outs=[exout[xi][c].rearrange("q f t -> (q f) t")],
                             reads=[bexin[xi][c]], writes=[bexout[xi][c]])

        def finish():
            assert all(st["cnt"].get(c, 0) == 8 for c in range(NCH)), st["cnt"]
        return emit, finish

    def make_load_yt(xi):
        def load_yt(res, g, yt, byt):
            for cc in range(CPG):
                c = g * CPG + cc
                for half in range(2):
                    p.dma(yt[:, half * 4:(half + 1) * 4, cc * 512:(cc + 1) * 512],
                          exout[xi][c, :, half * 128:(half + 1) * 128, :].rearrange("q f t -> f q t"),
                          reads=[bexout[xi][c]], writes=[byt], eng=("sp" if half == 0 else "act"))
        return load_yt

    p.prefix = "A_"
    emitA, finA = make_exchange_writer(0)
    phase_mixa(p, nc, "a_", L, lambda res, ti, t, b: emitA(ti, 0, t, b), lambda res, ti, t, b: emitA(ti, 128, t, b))
    finA()
    p.end_phase()

    p.prefix = "T0_"
    st0 = {}

    def load_xres0(res, g, st, dst, bdst):
        t0 = g * TG + st * 128
        p.dma(dst, xres_d[t0:t0 + 128, :], writes=[bdst], eng="act")

    def store_out0(res, g, st, o, bo):
        if "xT" not in st0:
            st0["xT"] = [p.sb("x1Ts%d" % i, [128, 8, 128], BF16) for i in range(2)]
            st0["bxT"] = [p.buf() for _ in range(2)]
            st0["k"] = 0
        t0 = g * TG + st * 128
        p.dma(x1loc[t0:t0 + 128, :], o[:], reads=[bo], writes=[bx1loc])
        k = st0["k"] % 2
        st0["k"] += 1
        xT, bxT = st0["xT"][k], st0["bxT"][k]
        psA, bpsA, identf, bidf = res["psA"], res["bpsA"], res["identf"], res["bidf"]
        for hf in range(2):
            pa = psA[hf]
            for c4 in range(4):
                c = hf * 4 + c4
                p.pe(lambda e, pa=pa, c4=c4, c=c: e.transpose(out=pa[:, c4 * 128:(c4 + 1) * 128], in_=o[:, c * 128:(c + 1) * 128], identity=identf[:]),
                     reads=[bo, bidf], writes=[bpsA[hf]])
            p.act(lambda e, pa=pa, hf=hf, xT=xT: e.copy(out=xT[:, hf * 4:(hf + 1) * 4, :], in_=pa[:].rearrange("p (c t) -> p c t", c=4)),
                  reads=[bpsA[hf]], writes=[bxT])
        chunk, toff = t0 // 512, t0 % 512
        p.dma(agin[chunk].rearrange("(dc p) t -> p dc t", p=128)[:, :, toff:toff + 128], xT[:], reads=[bxT], writes=[bagin[chunk]])
        if toff == 384:
            p.collective("AllGather", ALU.bypass, GROUPS, ins=[agin[chunk]], outs=[agout[chunk].rearrange("r d t -> (r d) t")],
                         reads=[bagin[chunk]], writes=[bagout[chunk]])

    phase_tail(p, nc, "t0_", SEG, True, make_load_yt(0), load_xres0, store_out0, TG)
    p.end_phase()

    p.prefix = "C_"
    emitC, finC = make_exchange_writer(1)

    def load_xc(res, ti, dst, bdst):
        r, c = ti // NCH, ti % NCH
        p.dma(dst[:], agout[c, r].rearrange("(dc p) t -> p dc t", p=128), reads=[bagout[c]], writes=[bdst])

    phase_mixc(p, nc, "c_", L, load_xc, lambda res, ti, which, t, b: emitC(ti, which * 128, t, b))
    finC()
    p.end_phase()

    p.prefix = "T1_"

    def load_xres1(res, g, st, dst, bdst):
        t0 = g * TG + st * 128
        p.dma(dst, x1loc[t0:t0 + 128, :], reads=[bx1loc], writes=[bdst], eng="act")

    def store_out1(res, g, st, o, bo):
        t0 = g * TG + st * 128
        p.dma(xo_d[t0:t0 + 128, :], o[:], reads=[bo])

    phase_tail(p, nc, "t1_", SEG, False, make_load_yt(1), load_xres1, store_out1, TG)
    if debug:
        for name, src, bufs in (("dbg_exout0", exout[0], bexout[0]), ("dbg_exout1", exout[1], bexout[1]), ("dbg_agout", agout, bagout),
                                ("dbg_exin0", exin[0], bexin[0])):
            d = nc.dram_tensor(name, list(src.shape), BF16, kind="ExternalOutput").ap()
            for c in range(NCH):
                p.dma(d[c], src[c], reads=[bufs[c]])
        d = nc.dram_tensor("dbg_x1loc", [SEG, 1024], F32, kind="ExternalOutput").ap()
        p.dma(d, x1loc, reads=[bx1loc])
    p.finish()
    return nc

import ml_dtypes

_BF = ml_dtypes.bfloat16
_PROGS = {}


def _prog(key, fn):
    if key not in _PROGS:
        _PROGS[key] = fn()
    return _PROGS[key]


def mixa_inputs(inp, j, xT):
    W = inp['ev_w_in'][0]
    sl = slice(128 * j, 128 * (j + 1))
    wA = np.concatenate([W[:, 0:512][:, sl], W[:, 512:1024][:, sl], W[:, 1024:1536][:, sl], W[:, 1536:2048][:, sl], W[:, 2048:2560][:, sl]], axis=1)
    lbl = np.ascontiguousarray(inp['hgrn_lb_logits'][:, sl].T)
    ng = np.ascontiguousarray(np.broadcast_to(inp['ev_a_norm'][0, sl][None], (128, 128)))
    G0 = 8 * j
    are, aim, ldt = inp['ev_s5_a_re'][0], inp['ev_s5_a_im'][0], inp['ev_s5_log_dt'][0]
    sps = np.zeros((128, 3, 4), np.float32)
    spw = np.zeros((128, 3, 4, 128), np.float32)
    bpad = np.zeros((128, 2, 4, 128), np.float32)
    cpad = np.zeros((128, 2, 4, 128), np.float32)
    for i in range(4):
        for gl in range(2):
            g = G0 + 2 * i + gl
            glc = 2 * i + gl
            sps[gl * 64:(gl + 1) * 64, 0, i] = are[g]
            sps[gl * 64:(gl + 1) * 64, 1, i] = aim[g]
            sps[gl * 64:(gl + 1) * 64, 2, i] = ldt[g]
            spw[:, 0, i, gl * 64:(gl + 1) * 64] = are[g][None]
            spw[:, 1, i, gl * 64:(gl + 1) * 64] = aim[g][None]
            spw[:, 2, i, gl * 64:(gl + 1) * 64] = ldt[g]
            bpad[glc * 16:(glc + 1) * 16, 0, i, gl * 64:(gl + 1) * 64] = inp['ev_s5_b_re'][0, g].T
            bpad[glc * 16:(glc + 1) * 16, 1, i, gl * 64:(gl + 1) * 64] = inp['ev_s5_b_im'][0, g].T
            cpad[gl * 64:(gl + 1) * 64, 0, i, glc * 16:(glc + 1) * 16] = inp['ev_s5_c_re'][0, g].T
            cpad[gl * 64:(gl + 1) * 64, 1, i, glc * 16:(glc + 1) * 16] = inp['ev_s5_c_im'][0, g].T
    dsk = np.ascontiguousarray(inp['ev_s5_d'][0, sl][:, None])
    s_ = np.arange(128)[:, None]
    c_ = np.arange(128)[None, :]
    tri = ((s_ // 64 == c_ // 64) & (s_ <= c_)).astype(np.float32)
    m01 = np.broadcast_to((np.arange(512) % 64 != 0).astype(np.float32)[None], (128, 512))
    return {"xT": xT, "wA": np.ascontiguousarray(wA), "lbl": lbl, "ng": ng, "sps": sps, "spw": spw.reshape(128, 3, 512),
            "bpad": bpad.reshape(128, 2, 512), "cpad": cpad.reshape(128, 2, 512), "dsk": dsk, "tri": tri, "m01": np.ascontiguousarray(m01)}


def mixc_inputs(inp, j, xT):
    W = inp['od_w_in'][0]
    kv = j // 2
    qc = W[:, 128 * j:128 * (j + 1)]
    kc = W[:, 512 + 64 * kv:512 + 64 * (kv + 1)]
    vc = W[:, 640 + 64 * kv:640 + 64 * (kv + 1)]
    qd = W[:, 768 + 128 * j:768 + 128 * (j + 1)]
    kd = W[:, 1280 + 64 * kv:1280 + 64 * (kv + 1)]
    vd = W[:, 1408 + 64 * kv:1408 + 64 * (kv + 1)]
    z = np.zeros_like(kd)
    wC = np.concatenate([qc, kc, kc, qd, kd, z, z, kd, vc, vd], axis=1)
    sink = np.ascontiguousarray(np.broadcast_to(inp['od_sinks'][0, 2 * j:2 * j + 2][None], (128, 2)))
    q = np.arange(128)[:, None]
    k = np.arange(256)[None, :]
    band = np.where(((k < 128) & (k > q)) | ((k >= 128) & (k - 128 <= q)), 0.0, -BIGM).astype(np.float32)
    swm = np.stack([band, band], axis=1)
    cm = np.stack([np.where(k <= q + 128 * a, 0.0, -BIGM) for a in range(2)], axis=1).astype(np.float32)
    i = np.arange(128)[None, :]
    f0 = np.where(i < 64, 0.0, -BIGM) * np.ones((128, 1))
    fut = np.stack([f0, f0 - BIGM], axis=1).astype(np.float32)
    return {"xT": xT, "wC": np.ascontiguousarray(wC), "sink": sink, "swm": np.ascontiguousarray(swm), "cm": np.ascontiguousarray(cm),
            "fut": np.ascontiguousarray(fut)}


def tail_inputs(inp, layer, xres, yT, w_out, w_glu):
    lnp = np.ascontiguousarray(np.stack([np.broadcast_to(inp[k][layer], (128, 1024)) for k in ['ln1_g', 'ln1_b', 'ln2_g', 'ln2_b']]).astype(np.float32))
    w_r = np.ascontiguousarray(np.concatenate([inp['moe_w_group'][layer], inp['moe_w_expert'][layer]], axis=1))
    b_r = np.ascontiguousarray(np.broadcast_to(np.concatenate([inp['moe_b_group'][layer], inp['moe_b_expert'][layer]])[None], (128, 20)).astype(np.float32))
    m = {"xres": xres, "yT": yT, "w_out": w_out, "lnp": lnp, "w_r": w_r, "b_r": b_r,
         "w_gu": inp['moe_w_gate_up'][layer], "w_dn": inp['moe_w_down'][layer]}
    if w_glu is not None:
        m["w_glu"] = w_glu
    return m


def fused_inputs(inp, x, xTb, b, r, SEG):
    m = {}
    for k, v in mixa_inputs(inp, r, xTb).items():
        m["a_" + k] = v
    for k, v in mixc_inputs(inp, r, None).items():
        if k != "xT":
            m["c_" + k] = v
    for k, v in tail_inputs(inp, 0, None, None, inp['ev_w_out'][0], inp['ev_s5_w_glu'][0]).items():
        if k not in ("xres", "yT"):
            m["t0_" + k] = v
    for k, v in tail_inputs(inp, 1, None, None, inp['od_w_out'][0], None).items():
        if k not in ("xres", "yT"):
            m["t1_" + k] = v
    msk = np.zeros((128, 4), np.float32)
    msk[:, r] = 1.0
    m["rankmask"] = msk
    m["xres"] = np.ascontiguousarray(x[b, r * SEG:(r + 1) * SEG])
    return m


def kernel(**inputs):
    inp = {k: np.ascontiguousarray(np.asarray(v)) for k, v in inputs.items()}
    x = inp['x']
    B, L, D = x.shape
    SEG = L // 4
    cores = list(range(8))
    xT = [np.ascontiguousarray(x[b].T) for b in range(B)]
    nc = _prog(("F", L), lambda: build_fused(L))
    maps = [fused_inputs(inp, x, xT[c // 4], c // 4, c % 4, SEG) for c in cores]
    res = run_bass_kernel_spmd(nc, maps, core_ids=cores).results
    out = np.stack([np.concatenate([np.asarray(res[b * 4 + s]["xo"]) for s in range(4)], axis=0) for b in range(B)])
    return out.astype(np.float32)
```
